# Optimizing a Trainium2 kernel written in Bass

```python
import jax, jax.numpy as jnp
from jax import lax
import numpy as np

D_MODEL = 1024
BATCH = 8
SEQ = 4096
DEPTH = 1

CTX_LEN = 256
GRID_W = 64
MIX_WIDTH = D_MODEL
ATTN_WIDTH = MIX_WIDTH // 2
LRU_WIDTH = MIX_WIDTH - ATTN_WIDTH
HEAD_DIM = 64
N_ATTN_HEADS = ATTN_WIDTH // HEAD_DIM
LRU_BLOCK = 64
N_LRU_BLOCKS = LRU_WIDTH // LRU_BLOCK
IN_WIDTH = 4 * ATTN_WIDTH + 2 * LRU_WIDTH
NA_ROWS_MAX = 8
NA_COLS = 16
CONV_WIDTH = 4
LRU_C = 8.0
ROPE_BASE = 10000.0
EPS = 1e-6
NEG_INF = -1e30

kernel_name = "hymba_na_rglru_prefix_block"


def rms_norm(x, g):
    x32 = x.astype(jnp.float32)
    y = x32 * lax.rsqrt(jnp.mean(x32 * x32, axis=-1, keepdims=True) + EPS)
    return (y * g.astype(jnp.float32)).astype(x.dtype)


def split_heads(t):
    b, l, _ = t.shape
    return t.reshape(b, l, -1, HEAD_DIM).transpose(0, 2, 1, 3)


def merge_heads(t):
    b, h, l, d = t.shape
    return t.transpose(0, 2, 1, 3).reshape(b, l, h * d)


def _rotate(x, pos):
    half = x.shape[-1] // 2
    inv = ROPE_BASE ** (-jnp.arange(half, dtype=jnp.float32) / half)
    ang = pos.astype(jnp.float32)[:, None] * inv[None, :]
    cos, sin = jnp.cos(ang), jnp.sin(ang)
    x1, x2 = x[..., :half], x[..., half:]
    return jnp.concatenate([x1 * cos - x2 * sin, x2 * cos + x1 * sin], axis=-1).astype(x.dtype)


def axial_rope(x, pos_r, pos_c):
    d = x.shape[-1] // 2
    return jnp.concatenate([_rotate(x[..., :d], pos_r), _rotate(x[..., d:], pos_c)], axis=-1)


def neighbourhood_attention(q_rot, q_plain, k_rot, v, k_ctx, v_ctx, rpb):
    b, h, s, hd = q_rot.shape
    rows = s // GRID_W
    kh = min(NA_ROWS_MAX, rows)
    scale = hd ** -0.5
    grid = lambda t: t.reshape(b, h, rows, GRID_W, hd)
    qr, qp, kr, vg = grid(q_rot), grid(q_plain), grid(k_rot), grid(v)
    cols = jnp.arange(GRID_W)
    col_start = jnp.clip(cols - NA_COLS // 2, 0, GRID_W - NA_COLS)
    col_mask = (cols[None, :] >= col_start[:, None]) & (cols[None, :] < col_start[:, None] + NA_COLS)
    dc_idx = jnp.clip(cols[None, :] - cols[:, None], -(NA_COLS - 1), NA_COLS - 1) + NA_COLS - 1

    def row_step(r):
        r0 = jnp.clip(r - kh // 2, 0, rows - kh)
        q_r = lax.dynamic_index_in_dim(qr, r, axis=2, keepdims=False)
        qp_r = lax.dynamic_index_in_dim(qp, r, axis=2, keepdims=False)
        k_band = lax.dynamic_slice_in_dim(kr, r0, kh, axis=2)
        v_band = lax.dynamic_slice_in_dim(vg, r0, kh, axis=2)
        s_loc = jnp.einsum('bhqd,bhjkd->bhqjk', q_r, k_band,
                           preferred_element_type=jnp.float32) * scale
        dr_idx = r0 + jnp.arange(kh) - r + NA_ROWS_MAX - 1
        bias = rpb[:, dr_idx[None, :, None], dc_idx[:, None, :]].astype(jnp.float32)
        s_loc = jnp.where(col_mask[:, None, :], s_loc + bias, NEG_INF)
        s_ctx = jnp.einsum('bhqd,bhcd->bhqc', qp_r, k_ctx,
                           preferred_element_type=jnp.float32) * scale
        n_loc = kh * GRID_W
        sc = jnp.concatenate([s_loc.reshape(b, h, GRID_W, n_loc), s_ctx], axis=-1)
        p = jax.nn.softmax(sc, axis=-1).astype(v.dtype)
        p_loc = p[..., :n_loc].reshape(b, h, GRID_W, kh, GRID_W)
        return (jnp.einsum('bhqjk,bhjkd->bhqd', p_loc, v_band)
                + jnp.einsum('bhqc,bhcd->bhqd', p[..., n_loc:], v_ctx))

    out = lax.map(row_step, jnp.arange(rows))
    return out.transpose(1, 2, 0, 3, 4).reshape(b, h, s, hd)


def context_attention(q, k, v):
    sc = jnp.einsum('bhqd,bhkd->bhqk', q, k, preferred_element_type=jnp.float32) * (q.shape[-1] ** -0.5)
    p = jax.nn.softmax(sc, axis=-1).astype(v.dtype)
    return jnp.einsum('bhqk,bhkd->bhqd', p, v)


def depthwise_conv(u, w, bias):
    k = w.shape[0]
    lo = (k - 1) // 2
    y = lax.conv_general_dilated(u, w[:, None, :].astype(u.dtype), window_strides=(1,),
                                 padding=[(lo, k - 1 - lo)],
                                 dimension_numbers=('NWC', 'WIO', 'NWC'),
                                 feature_group_count=u.shape[-1])
    return y + bias


def rglru_coeffs(u, w_a, b_a, w_x, b_x, lam):
    bsz, l, ch = u.shape
    u32 = u.astype(jnp.float32)
    ub = u32.reshape(bsz, l, N_LRU_BLOCKS, LRU_BLOCK)
    r = jax.nn.sigmoid(jnp.einsum('blnc,ncd->blnd', ub, w_a.astype(jnp.float32)).reshape(bsz, l, ch) + b_a)
    i = jax.nn.sigmoid(jnp.einsum('blnc,ncd->blnd', ub, w_x.astype(jnp.float32)).reshape(bsz, l, ch) + b_x)
    log_a = -LRU_C * r * jax.nn.softplus(-lam.astype(jnp.float32))
    a = jnp.exp(log_a)
    mult = jnp.sqrt(-jnp.expm1(2.0 * log_a))
    return a, mult * (i * u32)


def linear_scan(a, b, h0, reverse):
    def comb(lhs, rhs):
        return lhs[0] * rhs[0], rhs[0] * lhs[1] + rhs[1]
    a_cum, h = lax.associative_scan(comb, (a, b), axis=1, reverse=reverse)
    return h if h0 is None else h + a_cum * h0[:, None, :]


def rglru_bidirectional(u, u_c, w_a, b_a, w_x, b_x, lam):
    ys, hcs = [], []
    for d, reverse in ((0, False), (1, True)):
        a_c, b_c = rglru_coeffs(u_c, w_a[d], b_a[d], w_x[d], b_x[d], lam[d])
        h_c = linear_scan(a_c, b_c, None, reverse)
        h_end = h_c[:, 0] if reverse else h_c[:, -1]
        a, b = rglru_coeffs(u, w_a[d], b_a[d], w_x[d], b_x[d], lam[d])
        ys.append(linear_scan(a, b, h_end, reverse))
        hcs.append(h_c)
    return (ys[0] + ys[1]).astype(u.dtype), hcs


def setup_inputs(seed: int = 0) -> dict:
    key = jax.random.key(seed)
    ks = jax.random.split(key, 24)
    nrm = lambda k, shape, s: jax.random.normal(k, shape, jnp.float32) * s
    u = jax.random.uniform(ks[18], (DEPTH, 2, LRU_WIDTH), jnp.float32, 0.9, 0.999)
    a0 = u ** (1.0 / LRU_C)
    return {
        "x": nrm(ks[0], (BATCH, SEQ, D_MODEL), 1.0),
        "c": nrm(ks[1], (BATCH, D_MODEL), 1.0),
        "ctx": nrm(ks[2], (BATCH, CTX_LEN, D_MODEL), 1.0),
        "c_ctx": nrm(ks[3], (D_MODEL,), 1.0),
        "norm_g": 1.0 + nrm(ks[4], (DEPTH, D_MODEL), 0.02),
        "w_mod": nrm(ks[5], (DEPTH, D_MODEL, 3 * D_MODEL), D_MODEL ** -0.5),
        "b_mod": nrm(ks[6], (DEPTH, 3 * D_MODEL), 0.02),
        "w_in": nrm(ks[7], (DEPTH, D_MODEL, IN_WIDTH), D_MODEL ** -0.5),
        "w_out": nrm(ks[8], (DEPTH, MIX_WIDTH, D_MODEL), MIX_WIDTH ** -0.5),
        "q_norm_g": 1.0 + nrm(ks[9], (DEPTH, HEAD_DIM), 0.02),
        "k_norm_g": 1.0 + nrm(ks[10], (DEPTH, HEAD_DIM), 0.02),
        "rpb": nrm(ks[11], (DEPTH, N_ATTN_HEADS, 2 * NA_ROWS_MAX - 1, 2 * NA_COLS - 1), 0.5),
        "conv_w": nrm(ks[12], (DEPTH, CONV_WIDTH, LRU_WIDTH), CONV_WIDTH ** -0.5),
        "conv_b": nrm(ks[13], (DEPTH, LRU_WIDTH), 0.02),
        "lru_wa": nrm(ks[14], (DEPTH, 2, N_LRU_BLOCKS, LRU_BLOCK, LRU_BLOCK), LRU_BLOCK ** -0.5),
        "lru_ba": nrm(ks[15], (DEPTH, 2, LRU_WIDTH), 0.1),
        "lru_wx": nrm(ks[16], (DEPTH, 2, N_LRU_BLOCKS, LRU_BLOCK, LRU_BLOCK), LRU_BLOCK ** -0.5),
        "lru_bx": nrm(ks[17], (DEPTH, 2, LRU_WIDTH), 0.1),
        "lru_lam": jnp.log(a0) - jnp.log1p(-a0),
    }


def reference(x, c, ctx, c_ctx, norm_g, w_mod, b_mod, w_in, w_out, q_norm_g, k_norm_g, rpb,
              conv_w, conv_b, lru_wa, lru_ba, lru_wx, lru_bx, lru_lam):
    bsz, s, _ = x.shape
    t = jnp.arange(s)
    pos_r, pos_c = t // GRID_W, t % GRID_W
    aw, lw = ATTN_WIDTH, LRU_WIDTH
    silu_c = jax.nn.silu(c)
    silu_cc = jax.nn.silu(c_ctx)
    for layer in range(DEPTH):
        shift, scale, gate = jnp.split(silu_c @ w_mod[layer] + b_mod[layer], 3, axis=-1)
        shift_c, scale_c, gate_c = jnp.split(silu_cc @ w_mod[layer] + b_mod[layer], 3, axis=-1)
        h = rms_norm(x, norm_g[layer]) * (1.0 + scale[:, None, :]) + shift[:, None, :]
        hc = rms_norm(ctx, norm_g[layer]) * (1.0 + scale_c) + shift_c
        w = w_in[layer]
        q, k, v, z_a, u, z_l = jnp.split(h @ w, [aw, 2 * aw, 3 * aw, 4 * aw, 4 * aw + lw], axis=-1)
        k_c, v_c = jnp.split(hc @ w[:, aw:3 * aw], 2, axis=-1)
        u_c = hc @ w[:, 4 * aw:4 * aw + lw]

        qn = rms_norm(split_heads(q), q_norm_g[layer])
        kn = rms_norm(split_heads(k), k_norm_g[layer])
        kcn = rms_norm(split_heads(k_c), k_norm_g[layer])
        vch = split_heads(v_c)
        attn = neighbourhood_attention(axial_rope(qn, pos_r, pos_c), qn, axial_rope(kn, pos_r, pos_c),
                                       split_heads(v), kcn, vch, rpb[layer])
        attn = merge_heads(attn)

        uc = depthwise_conv(u, conv_w[layer], conv_b[layer])
        ucc = depthwise_conv(u_c, conv_w[layer], conv_b[layer])
        y, hc_states = rglru_bidirectional(uc, ucc, lru_wa[layer], lru_ba[layer], lru_wx[layer],
                                           lru_bx[layer], lru_lam[layer])

        mix = jnp.concatenate([attn * jax.nn.silu(z_a), y * jax.nn.silu(z_l)], axis=-1) @ w_out[layer]

        if layer + 1 < DEPTH:
            q_c = hc @ w[:, :aw]
            z_a_c = hc @ w[:, 3 * aw:4 * aw]
            z_l_c = hc @ w[:, 4 * aw + lw:]
            attn_c = merge_heads(context_attention(rms_norm(split_heads(q_c), q_norm_g[layer]), kcn, vch))
            y_c = (hc_states[0] + hc_states[1]).astype(ctx.dtype)
            mix_c = jnp.concatenate([attn_c * jax.nn.silu(z_a_c), y_c * jax.nn.silu(z_l_c)], axis=-1) @ w_out[layer]
            ctx = ctx + gate_c * mix_c

        x = x + gate[:, None, :] * mix
    return x
```

```python
import numpy as np
from contextlib import ExitStack
import concourse.bass as bass
import concourse.mybir as mybir
from concourse.bass_utils import run_bass_kernel_spmd

F32 = mybir.dt.float32
BF16 = mybir.dt.bfloat16
ALU = mybir.AluOpType
AF = mybir.ActivationFunctionType

CFG = {"score_banks": [3, 4, 5], "n_et": 3, "n_pt": 4, "vbank": 3, "window": 200, "rope_add": "pool", "n_qk": 2, "a2_act": 2, "conv_act": 0, "scan_cost": 2.2, "slack": 0.0, "vevac": "act", "lru_pair": 1}
SEQ = 4096
DM = 1024
CTX = 256
EPS = 1e-6
NJ = 14
FW = NJ * 64


class _Op:
    __slots__ = ("eng", "fn", "args", "kwargs", "preds", "dur", "idx", "tab", "seq", "dsem", "dval", "fin")


_ACT_TAB = {}


def _act_tab(func):
    n = str(func)
    if "Exp" in n:
        return ("exp", "ln")
    if "Tanh" in n:
        return ("exp",)
    if "Ln" in n:
        return ("ln",)
    if "Sqrt" in n:
        return ("sqrt",)
    if "Silu" in n:
        return ("silu",)
    if "Sigmoid" in n:
        return ("sigmoid",)
    return None


class Sched:
    WINDOW = 48

    def __init__(self, nc, es):
        self.nc = nc
        self.engs = {"pe": nc.tensor, "act": nc.scalar, "dve": nc.vector, "pool": nc.gpsimd, "sp": nc.sync}
        self.csem = {e: es.enter_context(nc.semaphore("cs_" + e)) for e in ("pe", "act", "dve", "pool")}
        self.cnt = {e: 0 for e in self.csem}
        self.NDS = 12
        self.dsem = [es.enter_context(nc.semaphore("ds%d" % i)) for i in range(self.NDS)]
        self.dcnt = [0] * self.NDS
        self.dnext = 0
        self.seen = {e: {} for e in self.engs}
        self.ops = []
        self.lastw = {}
        self.readers = {}
        self.last_tab = "none"

    def _est(self, eng, args, kwargs, fn=None):
        out = kwargs.get("out", args[0] if args else None)
        try:
            n = out.free_size()
        except Exception:
            n = 512
        if eng == "pe":
            return 64.0 + n * 0.42
        if eng == "act":
            return max(200.0, 100.0 + n * 0.88)
        if eng == "dve":
            if "scan" in getattr(fn, "__name__", ""):
                return 100.0 + n * CFG["scan_cost"]
            return max(160.0, 60.0 + n * 1.1)
        if eng == "pool":
            return 250.0 + n * 4.5
        try:
            nb = out.nbytes()
        except Exception:
            nb = 65536
        return float(nb)

    def _add(self, eng, fn, r, w, args, kwargs):
        o = _Op()
        o.eng = eng; o.fn = fn; o.args = args; o.kwargs = kwargs; o.idx = len(self.ops)
        o.dur = self._est(eng, args, kwargs, fn)
        o.tab = _act_tab(kwargs.get("func")) if eng == "act" else None
        preds = {}
        for k in r:
            p = self.lastw.get(k)
            if p is not None:
                need = not (p.eng == eng and eng == "pe")
                preds[p.idx] = preds.get(p.idx, False) or need
        for k in w:
            p = self.lastw.get(k)
            if p is not None:
                need = not (p.eng == eng and eng == "pe")
                preds[p.idx] = preds.get(p.idx, False) or need
            for p in self.readers.get(k, ()):
                need = not (p.eng == eng and eng == "pe")
                preds[p.idx] = preds.get(p.idx, False) or need
        o.preds = preds
        for k in r:
            self.readers.setdefault(k, []).append(o)
        for k in w:
            self.lastw[k] = o
            self.readers[k] = []
        self.ops.append(o)
        return o

    def op(self, eng, fn, r, w, *args, **kwargs):
        return self._add(eng, fn, r, w, args, kwargs)

    def dma(self, out, in_, r, w, **kwargs):
        kwargs = dict(kwargs); kwargs["out"] = out; kwargs["in_"] = in_
        return self._add("sp", self.nc.sync.dma_start, r, w, (), kwargs)

    def _wait(self, eng, key, val):
        if self.seen[eng].get(key, 0) >= val:
            return
        sem = self.csem[key[1]] if key[0] == "c" else self.dsem[key[1]]
        self.engs[eng].wait_ge(sem, val)
        self.seen[eng][key] = val

    def flush(self):
        ops = self.ops
        if not ops:
            return
        pend = {e: [] for e in self.engs}
        for o in ops:
            o.fin = None
            pend[o.eng].append(o)
        pos = {e: 0 for e in self.engs}
        free = {e: 0.0 for e in self.engs}
        order = {e: [] for e in self.engs}
        done = {e: set() for e in self.engs}
        remaining = len(ops)
        last_tab = self.last_tab
        dma_free = 0.0
        succ = [[] for _ in ops]
        for o in ops:
            for pi in o.preds:
                succ[pi].append(o.idx)
        rank = [0.0] * len(ops)
        for o in reversed(ops):
            m = 0.0
            for si in succ[o.idx]:
                if rank[si] > m:
                    m = rank[si]
            rank[o.idx] = m + (o.dur if o.eng != "sp" else 2500.0)
        slack = CFG.get("slack", 0.0)
        while remaining:
            best = None
            for e in self.engs:
                lst = pend[e]; p0 = pos[e]
                cnt = 0; i = p0
                cands = []
                while i < len(lst) and cnt < CFG["window"]:
                    o = lst[i]; i += 1
                    if o.idx in done[e]:
                        continue
                    cnt += 1
                    rt = 0.0; ok = True
                    for pi in o.preds:
                        f = ops[pi].fin
                        if f is None:
                            ok = False; break
                        if ops[pi].eng != e:
                            f += 200.0
                        if f > rt:
                            rt = f
                    if not ok:
                        continue
                    st = max(free[e], rt)
                    pen = 0.0
                    if e == "act" and o.tab is not None and last_tab not in o.tab:
                        pen = 1400.0
                    cands.append((st + pen, o, st, pen))
                if not cands:
                    continue
                m = min(c[0] for c in cands)
                pick = None
                for c in cands:
                    if c[0] <= m + slack:
                        if pick is None or (rank[c[1].idx], -c[1].idx) > (rank[pick[1].idx], -pick[1].idx):
                            pick = c
                key = (pick[0], pick[1].idx)
                if best is None or key < best[0]:
                    best = (key, e, pick[1], pick[2], pick[3])
            _, e, o, st, pen = best
            if e == "sp":
                free[e] = st + 300.0
                xs = max(st + 300.0, dma_free)
                dma_free = xs + o.dur / 200.0
                o.fin = dma_free + 1800.0
            else:
                o.fin = st + pen + o.dur
                free[e] = o.fin
            if e == "act" and o.tab is not None and last_tab not in o.tab:
                last_tab = o.tab[0]
            order[e].append(o)
            done[e].add(o.idx)
            while pos[e] < len(pend[e]) and pend[e][pos[e]].idx in done[e]:
                pos[e] += 1
            remaining -= 1
        self.last_tab = last_tab
        for e in self.csem:
            c = self.cnt[e]
            for o in order[e]:
                c += 1; o.seq = c
        dn = self.dnext; dc = list(self.dcnt)
        for o in order["sp"]:
            i = dn % self.NDS; dn += 1
            dc[i] += 16
            o.dsem = i; o.dval = dc[i]
        for e in self.engs:
            for o in order[e]:
                reqs = {}
                for pi, need in o.preds.items():
                    if not need:
                        continue
                    p = ops[pi]
                    if p.eng == "sp":
                        k_ = ("d", p.dsem); v_ = p.dval
                    else:
                        k_ = ("c", p.eng); v_ = p.seq
                    if reqs.get(k_, 0) < v_:
                        reqs[k_] = v_
                for k_, v_ in reqs.items():
                    self._wait(e, k_, v_)
                if e == "sp":
                    i = o.dsem
                    if o.dval > 16:
                        self._wait(e, ("d", i), o.dval - 16)
                    ins = o.fn(*o.args, **o.kwargs)
                    ins.then_inc(self.dsem[i], 16)
                else:
                    ins = o.fn(*o.args, **o.kwargs)
                    ins.then_inc(self.csem[e], 1)
        for e in self.csem:
            self.cnt[e] += len(order[e])
        self.dnext = dn; self.dcnt = dc
        self.ops = []
        self.lastw = {}
        self.readers = {}

    def barrier(self):
        self.flush()
        for eng in self.engs:
            for e in self.csem:
                if e != eng and self.cnt[e] > 0:
                    self._wait(eng, ("c", e), self.cnt[e])
            for i in range(self.NDS):
                if self.dcnt[i] > 0:
                    self._wait(eng, ("d", i), self.dcnt[i])

    def final_wait(self):
        self.flush()
        for i in range(self.NDS):
            if self.dcnt[i] > 0:
                self._wait("sp", ("d", i), self.dcnt[i])


def _host_consts():
    p = np.arange(128)
    f = p % 64
    i16 = (f % 16).astype(np.float32)
    inv = (np.float32(10000.0) ** (-(i16) / np.float32(16.0))).astype(np.float32)
    t = np.arange(SEQ)
    r = (t // 64).astype(np.float32)
    c = (t % 64).astype(np.float32)
    pos = np.where((f < 32)[:, None], r[None, :], c[None, :]).astype(np.float32)
    ang = (pos * inv[:, None]).astype(np.float32)
    cosT = np.cos(ang).astype(np.float32)
    sgn = np.where((f % 32) < 16, -1.0, 1.0).astype(np.float32)
    sinT = (np.sin(ang) * sgn[:, None]).astype(np.float32)
    partner = np.where((f % 32) < 16, p + 16, p - 16)
    cm = np.zeros((128, 384), np.float32)
    cm[p, p] = 1.0
    cm[partner, 128 + p] = 1.0
    blk = (p[:, None] // 64) == (p[None, :] // 64)
    cm[:, 256:384] = blk.astype(np.float32) / 64.0
    sel = np.zeros((3, 384), np.float32)
    for rr in range(3):
        sel[rr, rr * 128:(rr + 1) * 128] = 1.0
    krl = (p // 64)[:, None, None]
    kc = (p % 64)[:, None, None]
    jj = np.arange(NJ)[None, :, None]
    qc = np.arange(64)[None, None, :]
    dr = 6 - jj + krl
    cs = np.clip(qc - 8, 0, 48)
    colm = (kc >= cs) & (kc < cs + 16)
    MC = np.broadcast_to(colm, (128, NJ, 64)).astype(np.float32)
    MI = (colm & (dr >= -4) & (dr <= 3)).astype(np.float32)
    masks = np.concatenate([MC.reshape(128, FW), MI.reshape(128, FW)], axis=1).astype(np.float32)
    dridx = np.broadcast_to(dr + 7, (128, NJ, 64))
    dcidx = np.broadcast_to(np.clip(kc - qc, -15, 15) + 15, (128, NJ, 64))
    return cosT, sinT, cm, sel, masks, dridx, dcidx


def build_nc():
    nc = bass.Bass("TRN2", target_bir_lowering=False)

    def din(name, shape, dt=F32):
        return nc.dram_tensor(name, list(shape), dt, kind="ExternalInput").ap()

    x = din("x", [SEQ, DM]); ctx = din("ctx", [CTX, DM]); cv = din("cv", [128, 16])
    norm_g = din("norm_g", [1, DM]); w_mod = din("w_mod", [DM, 3 * DM]); b_mod = din("b_mod", [1, 3 * DM])
    w_in = din("w_in", [DM, 3 * DM]); w_out = din("w_out", [DM, DM])
    gqk = din("gqk", [128, 2]); plru = din("plru", [4, 128, 16]); wbd = din("wbd", [4, 128, 512])
    rpbG = din("rpbG", [8, 128, FW]); cosT = din("cosT", [128, SEQ]); sinT = din("sinT", [128, SEQ])
    cmat = din("cmat", [128, 384]); sel = din("sel", [3, 384]); masks = din("masks", [128, 2 * FW])
    out = nc.dram_tensor("out", [SEQ, DM], F32, kind="ExternalOutput").ap()
    mixD = nc.dram_tensor("mixD", [DM, SEQ], BF16, kind="Internal").ap()
    vD = nc.dram_tensor("vD", [4, 128, 32 * 256], BF16, kind="Internal").ap()
    woD = nc.dram_tensor("woD", [DM, DM], BF16, kind="Internal").ap()
    w_in_v = w_in.rearrange("(kc p) n -> p kc n", p=128)

    with ExitStack() as es:
        S = Sched(nc, es)
        V, A, G, T = nc.vector, nc.scalar, nc.gpsimd, nc.tensor

        uid = [0]

        def sbt(stack, name, shape, dt):
            uid[0] += 1
            return stack.enter_context(nc.sbuf_tensor("%s_%d" % (name, uid[0]), list(shape), dt))

        hT = sbt(es, "hT", [128, 8 * SEQ], BF16)
        hcT = sbt(es, "hcT", [128, 8 * CTX], BF16)
        cm = sbt(es, "cm", [128, 384], BF16)
        gq = sbt(es, "gq", [128, 2], F32)
        wst = [sbt(es, "wst%d" % i, [128, 1024], F32) for i in range(2)]
        wbf = [sbt(es, "wbf%d" % i, [128, 1024], BF16) for i in range(4)]
        PS = [es.enter_context(nc.psum_tensor("ps%d" % i, [128, 512], F32)) for i in range(6)]
        ident = cm[:, 0:128]; Rm = cm[:, 128:256]; bones = cm[:, 256:384]
        wslot = [0]

        def load_wblock(c0):
            s = wslot[0]; wslot[0] += 1
            st = wst[s % 2]; wb = wbf[s % 4]
            S.dma(st[:].rearrange("p (kc n) -> p kc n", kc=8), w_in_v[:, :, c0:c0 + 128], [], [("wst", s % 2)])
            S.op("dve", V.tensor_copy, [("wst", s % 2)], [("wbf", s % 4)], out=wb[:], in_=st[:])
            return wb, ("wbf", s % 4)

        def proj_fm(wb, wkey, src, src_w, col0, ncols, bank):
            for kc in range(8):
                S.op("pe", T.matmul, [wkey, "hT"], [("ps", bank)], PS[bank][:, 0:ncols],
                     wb[:, kc * 128:(kc + 1) * 128], src[:, kc * src_w + col0: kc * src_w + col0 + ncols],
                     start=(kc == 0), stop=(kc == 7))

        with ExitStack() as e1:
            PT = [e1.enter_context(nc.psum_tensor("pt%d" % i, [128, 1024], BF16)) for i in range(2)]
            cm_f = sbt(e1, "cm_f", [128, 384], F32); gate_bc = sbt(e1, "gate_bc", [128, DM], F32)
            cvt = sbt(e1, "cvt", [128, 16], F32); scv = sbt(e1, "scv", [128, 16], F32)
            M3 = sbt(e1, "M3", [3, 3 * DM], F32); b2 = sbt(e1, "b2", [2, 3 * DM], F32)
            sel_t = sbt(e1, "sel_t", [3, 384], F32)
            wmb = [sbt(e1, "wm%d" % i, [128, 1536], F32) for i in range(4)]
            cols = sbt(e1, "cols", [128, 48], F32); mcol = sbt(e1, "mcol", [128, 32], F32)
            wob = [sbt(e1, "wob%d" % i, [128, DM], BF16) for i in range(2)]
            xtb = [sbt(e1, "xt%d" % i, [128, DM], F32) for i in range(3)]
            hbb = [sbt(e1, "hb%d" % i, [128, DM], BF16) for i in range(2)]
            junk = sbt(e1, "junk", [128, DM], BF16)
            ssall = sbt(e1, "ssall", [128, 40], F32); rtall = sbt(e1, "rtall", [128, 40], F32)
            rsall = sbt(e1, "rsall", [128, 40], F32)

            S.dma(cvt[:], cv, [], ["cvt"]); S.dma(cm_f[:], cmat, [], ["cm_f"])
            S.dma(sel_t[:], sel, [], ["sel"])
            S.dma(gq[:], gqk, [], ["gq"])
            S.op("dve", V.tensor_copy, ["cm_f"], ["cm"], out=cm[:], in_=cm_f[:])
            S.op("act", A.activation, ["cvt"], ["scv"], out=scv[:], in_=cvt[:], func=AF.Silu)
            S.op("dve", V.memset, [], ["M3"], M3[:], 0.0)
            S.op("dve", V.memset, [], ["ssall"], ssall[:], 0.0)
            S.dma(M3[2:3, 0:DM], norm_g, [], ["M3"])
            S.dma(b2[0:1, :], b_mod, [], ["b2"]); S.dma(b2[1:2, :], b_mod, [], ["b2"])
            for kc in range(8):
                for hf in range(2):
                    wi = (2 * kc + hf) % 4
                    wm = wmb[wi]
                    S.dma(wm[:], w_mod[kc * 128:(kc + 1) * 128, hf * 1536:(hf + 1) * 1536], [], [("wm", wi)])
                    for n3 in range(3):
                        n = hf * 3 + n3
                        S.op("pe", T.matmul, ["scv", ("wm", wi)], [("ps", n)], PS[n][0:2, :],
                             scv[:, 2 * kc:2 * kc + 2], wm[:, n3 * 512:(n3 + 1) * 512], start=(kc == 0), stop=(kc == 7))
            for n in range(6):
                S.op("dve", V.tensor_tensor, [("ps", n), "b2"], ["M3"], out=M3[0:2, n * 512:(n + 1) * 512],
                     in0=PS[n][0:2, :], in1=b2[0:2, n * 512:(n + 1) * 512], op=ALU.add)
            for j in range(2):
                b = j
                S.op("pe", T.matmul, ["sel", "M3"], [("ps", b)], PS[b][:, :], sel_t[0:3, 0:128], M3[0:3, 2 * DM + j * 512: 2 * DM + (j + 1) * 512],
                     start=True, stop=True)
                S.op("act", A.activation, [("ps", b)], ["gate_bc"], out=gate_bc[:, j * 512:(j + 1) * 512], in_=PS[b][:, :], func=AF.Identity)
            for kc in range(8):
                S.dma(wst[kc % 2][:], w_out[kc * 128:(kc + 1) * 128, :], [], [("wst", kc % 2)])
                S.op("dve", V.tensor_tensor, [("wst", kc % 2), "gate_bc"], [("wob", kc % 2)], out=wob[kc % 2][:], in0=wst[kc % 2][:], in1=gate_bc[:], op=ALU.mult)
                S.dma(woD[kc * 128:(kc + 1) * 128, :], wob[kc % 2][:], [("wob", kc % 2)], [])
            i3 = bass.AP(sel_t[0:3, 0:1].tensor, sel_t[0:3, 0:1].offset, [[sel_t[0:3, 0:1].ap[0][0], 3], [128, 3]])
            for sidx in range(2):
                for kc in range(8):
                    c = (sidx * 8 + kc) * 3
                    S.op("pe", T.matmul, ["sel", "M3"], [("ps", 2)], PS[2][:, c:c + 3],
                         M3[0:3, sidx * DM + kc * 128: sidx * DM + (kc + 1) * 128], i3, start=True, stop=True)
            S.op("dve", V.tensor_copy, [("ps", 2)], ["cols"], out=cols[:, 0:48], in_=PS[2][:, 0:48])

            def colv(sidx, r):
                base = cols[:, sidx * 24 + r: sidx * 24 + r + 1]
                return bass.AP(base.tensor, base.offset, [[base.ap[0][0], 128], [3, 8]])

            S.op("dve", V.scalar_tensor_tensor, ["cols"], ["mcol"], out=mcol[:, 0:8], in0=colv(1, 0), scalar=1.0, in1=colv(0, 2), op0=ALU.add, op1=ALU.mult)
            S.op("dve", V.tensor_copy, ["cols"], ["mcol"], out=mcol[:, 8:16], in_=colv(0, 0))
            S.op("dve", V.scalar_tensor_tensor, ["cols"], ["mcol"], out=mcol[:, 16:24], in0=colv(1, 1), scalar=1.0, in1=colv(0, 2), op0=ALU.add, op1=ALU.mult)
            S.op("dve", V.tensor_copy, ["cols"], ["mcol"], out=mcol[:, 24:32], in_=colv(0, 1))

            def modulate(dst, dstw, c0, n, off, key, eng):
                for kc in range(8):
                    sl = dst[:, kc * dstw + c0: kc * dstw + c0 + n]
                    if True:
                        S.op("dve", V.tensor_scalar, [key, "mcol"], [key], out=sl, in0=sl, scalar1=mcol[:, off + kc:off + kc + 1],
                             scalar2=mcol[:, off + 8 + kc:off + 9 + kc], op0=ALU.mult, op1=ALU.add)
                    else:
                        S.op("act", A.activation, [key, "mcol"], [key], out=sl, in_=sl, func=AF.Identity, scale=mcol[:, off + kc:off + kc + 1],
                             bias=mcol[:, off + 8 + kc:off + 9 + kc])

            tiles = [(ctx[i * 128:(i + 1) * 128, :], hcT, CTX, i * 128, ("hcT", 0)) for i in range(2)]
            tiles += [(x[i * 128:(i + 1) * 128, :], hT, SEQ, i * 128, ("hT", i // 8)) for i in range(32)]
            for idx, (src, dst, dstw, col, hkey) in enumerate(tiles):
                xt = xtb[idx % 3]; xk = ("xt", idx % 3)
                S.dma(xt[:], src, [], [xk])
                S.op("act", A.activation, [xk, "ssall"], ["junk", ("ss", idx)], out=junk[:], in_=xt[:], func=AF.Square,
                     accum_out=ssall[:, idx:idx + 1])
                S.op("act", A.activation, [("ss", idx)], [("rt", idx)], out=rtall[:, idx:idx + 1],
                     in_=ssall[:, idx:idx + 1], func=AF.Sqrt, scale=1.0 / DM, bias=EPS)
                S.op("dve", V.reciprocal, [("rt", idx)], [("rs", idx)], out=rsall[:, idx:idx + 1], in_=rtall[:, idx:idx + 1])
                hb = hbb[idx % 2]
                S.op("act", A.activation, [xk, ("rs", idx)], [("hb", idx % 2)], out=hb[:], in_=xt[:], func=AF.Identity, scale=rsall[:, idx:idx + 1])
                pt = PT[idx % 2]
                for kc in range(8):
                    S.op("pe", T.transpose, [("hb", idx % 2), "cm"], [("pt", idx % 2)], out=pt[:, kc * 128:(kc + 1) * 128],
                         in_=hb[:, kc * 128:(kc + 1) * 128], identity=ident)
                dst_ap = dst[:].rearrange("p (kc t) -> p kc t", kc=8)[:, :, col:col + 128]
                S.op("dve", V.tensor_copy, [("pt", idx % 2)], [hkey], out=dst_ap, in_=pt[:].rearrange("p (kc t) -> p kc t", kc=8))
                if idx == 1:
                    modulate(hcT, CTX, 0, CTX, 16, ("hcT", 0), 0)
                elif idx >= 2 and (idx - 2) % 8 == 7:
                    g4 = (idx - 2) // 8
                    modulate(hT, SEQ, g4 * 1024, 1024, 0, ("hT", g4), g4)
            S.flush()

        def lru_phase():
            S.barrier()
            with ExitStack() as e2:
                NB = 1 if CFG["lru_pair"] else 2
                NWS = 4 if CFG["lru_pair"] else 2
                T0s = [sbt(e2, "T0_%d" % i, [128, SEQ], F32) for i in range(NB)]
                szls = [sbt(e2, "szl_%d" % i, [128, SEQ], BF16) for i in range(NB)]
                T1 = sbt(e2, "T1", [128, SEQ], F32)
                ucbf = sbt(e2, "ucbf", [128, SEQ], BF16)
                QW = 1024
                BAs = [sbt(e2, "BA%d" % i, [128, QW], F32) for i in range(NWS)]
                BBs = [sbt(e2, "BB%d" % i, [128, QW], F32) for i in range(NWS)]
                BCs = [sbt(e2, "BC%d" % i, [128, QW], F32) for i in range(NWS)]
                trb = [sbt(e2, "trb%d" % i, [128, 512], F32) for i in range(2)]
                tib = [sbt(e2, "tib%d" % i, [128, 512], F32) for i in range(2)]
                cx0 = sbt(e2, "cx0", [128, CTX], F32); cx1 = sbt(e2, "cx1", [128, CTX], F32)
                cxbf = sbt(e2, "cxbf", [128, CTX], BF16)
                prms = [sbt(e2, "prm%d" % i, [128, 16], F32) for i in range(2)]
                prhs = [sbt(e2, "prh%d" % i, [128, 16], F32) for i in range(2)]
                clts = [sbt(e2, "clt%d" % i, [128, 8], F32) for i in range(2)]
                gw_f = sbt(e2, "gw_f", [128, 512], F32)
                gws = [sbt(e2, "gw%d" % i, [128, 512], BF16) for i in range(2)]
                carry = sbt(e2, "carry", [128, 2], F32)
                wvb = [sbt(e2, "wvb%d" % i, [128, 1024], BF16) for i in range(4)]
                vstage = sbt(e2, "vstage", [128, 1024], BF16)
                psv = e2.enter_context(nc.psum_tensor("psv", [128, 512], F32))
                S.op("pool", G.memset, [], ["vstage"], vstage[:], 1.0)
                for pr_ in range(4):
                    S.dma(wst[pr_ % 2][:].rearrange("p (kc n) -> p kc n", kc=8), w_in_v[:, :, 1024 + 128 * pr_:1024 + 128 * (pr_ + 1)], [], [("wst", pr_ % 2)])
                    S.op("dve", V.tensor_copy, [("wst", pr_ % 2)], [("wvb", pr_)], out=wvb[pr_][:], in_=wst[pr_ % 2][:])
                vD_t = vD.rearrange("r p c -> p r c")
                vst3 = vstage[:].rearrange("p (r c) -> p r c", r=4)
                vb_ = vstage[:]
                vst4 = bass.AP(vb_.tensor, vb_.offset, [[vb_.ap[0][0], 128], [256, 4], [192, 2], [1, 64]])
                pvb_ = psv[:, :]
                psv4 = bass.AP(pvb_.tensor, pvb_.offset, [[pvb_.ap[0][0], 128], [128, 4], [64, 2], [1, 64]])

                def v_tiles(t0, t1):
                    for t_ in range(t0, t1):
                        for pr_ in range(4):
                            for kc in range(8):
                                S.op("pe", T.matmul, [("wvb", pr_), "hT"], ["psv"], psv[:, pr_ * 128:(pr_ + 1) * 128],
                                     hT[:, kc * SEQ + t_ * 128: kc * SEQ + (t_ + 1) * 128], wvb[pr_][:, kc * 128:(kc + 1) * 128],
                                     start=(kc == 0), stop=(kc == 7))
                        if CFG["vevac"] == "act":
                            S.op("act", A.activation, ["psv"], ["vstage"], out=vst4, in_=psv4, func=AF.Identity)
                        else:
                            S.op("dve", V.tensor_copy, ["psv"], ["vstage"], out=vst4, in_=psv4)
                        S.dma(vD_t[:, :, t_ * 256:(t_ + 1) * 256], vst3, ["vstage"], [])

                def rev(ap2d, n):
                    pstep = ap2d.ap[0][0]
                    npart = ap2d.ap[0][1]
                    return bass.AP(ap2d.tensor, ap2d.offset + n - 1, [[pstep, npart], [-1, n]])

                def proj_stage(ch):
                    p = ch % 2
                    prm = prms[p]; prh = prhs[p]; clt = clts[p]; gw = gws[p]; pk = p % NB; T0 = T0s[pk]; szl = szls[pk]
                    S.dma(prm[:], plru[ch], [], [("prm", p)])
                    S.dma(gw_f[:], wbd[ch], [], ["gw_f"])
                    S.op("dve", V.tensor_copy, ["gw_f"], [("gw", p)], out=gw[:], in_=gw_f[:])
                    S.op("dve", V.tensor_scalar, [("prm", p)], [("prh", p)], out=prh[:], in0=prm[:], scalar1=0.5, scalar2=None, op0=ALU.mult)
                    for d in range(2):
                        lc = 7 + 3 * d
                        S.op("act", A.activation, [("prm", p)], [("clt", p, 4 + d)], out=clt[:, 4 + d:5 + d], in_=prm[:, lc:lc + 1], func=AF.Exp, scale=-1.0)
                        S.op("act", A.activation, [("clt", p, 4 + d)], [("clt", p, 6 + d)], out=clt[:, 6 + d:7 + d], in_=clt[:, 4 + d:5 + d], func=AF.Ln, bias=1.0)
                        S.op("dve", V.tensor_scalar, [("clt", p, 6 + d)], [("cl", p, d)], out=clt[:, 2 * d:2 * d + 1], in0=clt[:, 6 + d:7 + d],
                             scalar1=-8.0, scalar2=None, op0=ALU.mult)
                        S.op("dve", V.tensor_scalar, [("clt", p, 6 + d)], [("cl", p, d)], out=clt[:, 2 * d + 1:2 * d + 2], in0=clt[:, 6 + d:7 + d],
                             scalar1=-4.0, scalar2=None, op0=ALU.mult)
                    wu, wuk = load_wblock(2048 + 128 * ch)
                    wz, wzk = load_wblock(2560 + 128 * ch)
                    proj_fm(wu, wuk, hcT, CTX, 0, CTX, 5)
                    S.op("act", A.activation, [("ps", 5)], ["cx0"], out=cx0[:], in_=PS[5][:, 0:CTX], func=AF.Identity)
                    for g in range(8):
                        b = 4 + g % 2
                        proj_fm(wu, wuk, hT, SEQ, g * 512, 512, b)
                        S.op("act", A.activation, [("ps", b)], [("T0", pk, g)], out=T0[:, g * 512:(g + 1) * 512], in_=PS[b][:, :], func=AF.Identity)
                    for g in range(8):
                        b = 4 + g % 2
                        proj_fm(wz, wzk, hT, SEQ, g * 512, 512, b)
                        S.op("act", A.activation, [("ps", b)], [("szl", pk, g)], out=szl[:, g * 512:(g + 1) * 512], in_=PS[b][:, :], func=AF.Silu)

                def main_stage(ch, mid_hook):
                    p = ch % 2
                    prm = prms[p]; prh = prhs[p]; clt = clts[p]; gw = gws[p]; pk = p % NB; T0 = T0s[pk]; szl = szls[pk]
                    kprm = ("prm", p)

                    def conv(src, skeys, dst, dkey, lo, hi, n):
                        rk = list(skeys) + [kprm]
                        if CFG["conv_act"] and n > CTX:
                            S.op("act", A.activation, rk, [dkey], out=dst[:, lo:hi], in_=src[:, lo:hi], func=AF.Identity, scale=prm[:, 1:2], bias=prm[:, 4:5])
                        else:
                            S.op("dve", V.tensor_scalar, rk, [dkey], out=dst[:, lo:hi], in0=src[:, lo:hi], scalar1=prm[:, 1:2],
                                 scalar2=prm[:, 4:5], op0=ALU.mult, op1=ALU.add)
                        l0 = max(lo, 1)
                        S.op("dve", V.scalar_tensor_tensor, rk + [dkey], [dkey], out=dst[:, l0:hi], in0=src[:, l0 - 1:hi - 1],
                             scalar=prm[:, 0:1], in1=dst[:, l0:hi], op0=ALU.mult, op1=ALU.add)
                        h2 = min(hi, n - 1)
                        S.op("dve", V.scalar_tensor_tensor, rk + [dkey], [dkey], out=dst[:, lo:h2], in0=src[:, lo + 1:h2 + 1],
                             scalar=prm[:, 2:3], in1=dst[:, lo:h2], op0=ALU.mult, op1=ALU.add)
                        h3 = min(hi, n - 2)
                        S.op("dve", V.scalar_tensor_tensor, rk + [dkey], [dkey], out=dst[:, lo:h3], in0=src[:, lo + 2:h3 + 2],
                             scalar=prm[:, 3:4], in1=dst[:, lo:h3], op0=ALU.mult, op1=ALU.add)

                    conv(cx0, ["cx0"], cx1, "cx1", 0, CTX, CTX)
                    S.op("act", A.activation, ["cx1"], ["cxbf"], out=cxbf[:], in_=cx1[:], func=AF.Identity)
                    for g in range(8):
                        sk = [("T0", pk, j) for j in (g - 1, g, g + 1) if 0 <= j < 8]
                        conv(T0, sk, T1, ("T1", g), g * 512, (g + 1) * 512, SEQ)
                        S.op("act", A.activation, [("T1", g)], [("ucbf", g)], out=ucbf[:, g * 512:(g + 1) * 512], in_=T1[:, g * 512:(g + 1) * 512], func=AF.Identity)

                    def finish(bbuf, bkey, cbuf, ckey, n):
                        S.op("act", A.activation, [bkey], [bkey], out=bbuf[:, 0:n], in_=bbuf[:, 0:n], func=AF.Sqrt, scale=-0.25, bias=0.25)
                        S.op("dve", V.tensor_tensor, [bkey, ckey], [ckey], out=cbuf[:, 0:n], in0=cbuf[:, 0:n], in1=bbuf[:, 0:n], op=ALU.mult)

                    def gates(d, src_bf, sbkeys, src_f, sfkeys, n, abuf, akey, bbuf, bkey, cbuf, ckey, fin=True):
                        wa = gw[:, (2 * d) * 128:(2 * d + 1) * 128]; wx = gw[:, (2 * d + 1) * 128:(2 * d + 2) * 128]
                        ba_c = 5 + 3 * d; bx_c = 6 + 3 * d
                        nseg = (n + 511) // 512
                        for s_ in range(nseg):
                            w_ = min(512, n - s_ * 512); sl = slice(s_ * 512, s_ * 512 + w_)
                            i = gcnt[0] % 2; gcnt[0] += 1
                            b0 = 2 * i; b1 = 2 * i + 1
                            tr = trb[i]; ti = tib[i]
                            sbk = sbkeys[s_]; sfk = sfkeys[s_]
                            S.op("pe", T.matmul, [("gw", p), sbk], [("ps", b0)], PS[b0][:, 0:w_], wa, src_bf[:, sl], start=True, stop=True)
                            S.op("pe", T.matmul, [("gw", p), sbk], [("ps", b1)], PS[b1][:, 0:w_], wx, src_bf[:, sl], start=True, stop=True)
                            S.op("act", A.activation, [("ps", b0), ("prh", p)], [("tr", i)], out=tr[:, 0:w_], in_=PS[b0][:, 0:w_], func=AF.Tanh,
                                 scale=0.5, bias=prh[:, ba_c:ba_c + 1])
                            S.op("act", A.activation, [("ps", b1), ("prh", p)], [("ti", i)], out=ti[:, 0:w_], in_=PS[b1][:, 0:w_], func=AF.Tanh,
                                 scale=0.5, bias=prh[:, bx_c:bx_c + 1])
                            S.op("act", A.activation, [("tr", i), ("cl", p, d)], [akey], out=abuf[:, sl], in_=tr[:, 0:w_], func=AF.Exp,
                                 scale=clt[:, 2 * d + 1:2 * d + 2], bias=clt[:, 2 * d + 1:2 * d + 2])
                            if s_ % 2 < CFG["a2_act"]:
                                S.op("act", A.activation, [("tr", i), ("cl", p, d)], [bkey], out=bbuf[:, sl], in_=tr[:, 0:w_], func=AF.Exp,
                                     scale=clt[:, 2 * d:2 * d + 1], bias=clt[:, 2 * d:2 * d + 1])
                            else:
                                S.op("dve", V.tensor_tensor, [akey], [bkey], out=bbuf[:, sl], in0=abuf[:, sl], in1=abuf[:, sl], op=ALU.mult)
                            S.op("dve", V.scalar_tensor_tensor, [("ti", i), sfk], [ckey], out=cbuf[:, sl], in0=ti[:, 0:w_], scalar=1.0,
                                 in1=src_f[:, sl], op0=ALU.add, op1=ALU.mult)
                        if fin:
                            finish(bbuf, bkey, cbuf, ckey, n)

                    def wset(idx):
                        return BAs[idx], BBs[idx], BCs[idx], ("BA", idx), ("BB", idx), ("BC", idx)

                    if CFG["lru_pair"]:
                        v_tiles(ch * 8, ch * 8 + 8)
                        cs = []
                        for d in range(2):
                            ws = wset((step[0] % 2) * 2 + d)
                            gates(d, cxbf, ["cxbf"], cx1, ["cx1"], CTX, ws[0], ws[3], ws[1], ws[4], ws[2], ws[5], fin=False)
                            cs.append(ws)
                        step[0] += 1
                        for d in range(2):
                            finish(cs[d][1], cs[d][4], cs[d][2], cs[d][5], CTX)
                        for d in range(2):
                            BA, BB, BC, ka, kb, kc_ = cs[d]
                            if d == 0:
                                S.op("dve", V.tensor_tensor_scan, [ka, kc_], [kb], out=BB[:, 0:CTX], data0=BA[:, 0:CTX], data1=BC[:, 0:CTX], initial=0.0,
                                     op0=ALU.mult, op1=ALU.add)
                                S.op("dve", V.tensor_copy, [kb], [("carry", d)], out=carry[:, 0:1], in_=BB[:, CTX - 1:CTX])
                            else:
                                S.op("dve", V.tensor_tensor_scan, [ka, kc_], [kb], out=rev(BB[:, 0:CTX], CTX), data0=rev(BA[:, 0:CTX], CTX),
                                     data1=rev(BC[:, 0:CTX], CTX), initial=0.0, op0=ALU.mult, op1=ALU.add)
                                S.op("dve", V.tensor_copy, [kb], [("carry", d)], out=carry[:, 1:2], in_=BB[:, 0:1])
                        for s4 in range(4):
                            qq = (s4, 3 - s4)
                            cs = []
                            for d in range(2):
                                q = qq[d]; c0 = q * QW
                                ws = wset((step[0] % 2) * 2 + d)
                                blks = [2 * q, 2 * q + 1]
                                gates(d, ucbf[:, c0:c0 + QW], [("ucbf", j) for j in blks], T1[:, c0:c0 + QW], [("T1", j) for j in blks], QW,
                                      ws[0], ws[3], ws[1], ws[4], ws[2], ws[5], fin=False)
                                cs.append(ws)
                            step[0] += 1
                            for d in range(2):
                                finish(cs[d][1], cs[d][4], cs[d][2], cs[d][5], QW)
                            for d in range(2):
                                q = qq[d]; c0 = q * QW
                                BA, BB, BC, ka, kb, kc_ = cs[d]
                                blks = [2 * q, 2 * q + 1]
                                t0keys = [("T0", pk, j) for j in blks]
                                first = s4 < 2
                                if d == 0:
                                    if first:
                                        S.op("dve", V.tensor_tensor_scan, [ka, kc_, ("carry", 0)], t0keys, out=T0[:, c0:c0 + QW], data0=BA[:], data1=BC[:],
                                             initial=carry[:, 0:1], op0=ALU.mult, op1=ALU.add)
                                        S.op("dve", V.tensor_copy, t0keys, [("carry", 0)], out=carry[:, 0:1], in_=T0[:, c0 + QW - 1:c0 + QW])
                                    else:
                                        S.op("dve", V.tensor_tensor_scan, [ka, kc_, ("carry", 0)], [kb], out=BB[:], data0=BA[:], data1=BC[:],
                                             initial=carry[:, 0:1], op0=ALU.mult, op1=ALU.add)
                                        S.op("dve", V.tensor_copy, [kb], [("carry", 0)], out=carry[:, 0:1], in_=BB[:, QW - 1:QW])
                                else:
                                    if first:
                                        S.op("dve", V.tensor_tensor_scan, [ka, kc_, ("carry", 1)], t0keys, out=rev(T0[:, c0:c0 + QW], QW), data0=rev(BA[:], QW),
                                             data1=rev(BC[:], QW), initial=carry[:, 1:2], op0=ALU.mult, op1=ALU.add)
                                        S.op("dve", V.tensor_copy, t0keys, [("carry", 1)], out=carry[:, 1:2], in_=T0[:, c0:c0 + 1])
                                    else:
                                        S.op("dve", V.tensor_tensor_scan, [ka, kc_, ("carry", 1)], [kb], out=rev(BB[:], QW), data0=rev(BA[:], QW),
                                             data1=rev(BC[:], QW), initial=carry[:, 1:2], op0=ALU.mult, op1=ALU.add)
                                        S.op("dve", V.tensor_copy, [kb], [("carry", 1)], out=carry[:, 1:2], in_=BB[:, 0:1])
                                if not first:
                                    S.op("dve", V.tensor_tensor, [kb] + t0keys, [ka], out=BA[:], in0=BB[:], in1=T0[:, c0:c0 + QW], op=ALU.add)
                                    szk = [("szl", pk, j) for j in blks]
                                    S.op("dve", V.tensor_tensor, [ka] + szk, szk, out=szl[:, c0:c0 + QW], in0=BA[:], in1=szl[:, c0:c0 + QW],
                                         op=ALU.mult)
                        for hq in range(2):
                            S.dma(mixD[(4 + ch) * 128:(5 + ch) * 128, hq * 2048:(hq + 1) * 2048], szl[:, hq * 2048:(hq + 1) * 2048],
                                  [("szl", pk, j) for j in range(4 * hq, 4 * hq + 4)], [])
                        if mid_hook is not None:
                            mid_hook()
                        return

                    for d in range(2):
                        v_tiles(ch * 8 + d * 4, ch * 8 + d * 4 + 4)
                        si = step[0] % 2; step[0] += 1
                        BA = BAs[si]; BB = BBs[si]; BC = BCs[si]
                        ka = ("BA", si); kb = ("BB", si); kc_ = ("BC", si)
                        gates(d, cxbf, ["cxbf"], cx1, ["cx1"], CTX, BA, ka, BB, kb, BC, kc_)
                        if d == 0:
                            S.op("dve", V.tensor_tensor_scan, [ka, kc_], [kb], out=BB[:, 0:CTX], data0=BA[:, 0:CTX], data1=BC[:, 0:CTX], initial=0.0,
                                 op0=ALU.mult, op1=ALU.add)
                            S.op("dve", V.tensor_copy, [kb], [("carry", d)], out=carry[:, 0:1], in_=BB[:, CTX - 1:CTX])
                        else:
                            S.op("dve", V.tensor_tensor_scan, [ka, kc_], [kb], out=rev(BB[:, 0:CTX], CTX), data0=rev(BA[:, 0:CTX], CTX),
                                 data1=rev(BC[:, 0:CTX], CTX), initial=0.0, op0=ALU.mult, op1=ALU.add)
                            S.op("dve", V.tensor_copy, [kb], [("carry", d)], out=carry[:, 1:2], in_=BB[:, 0:1])
                        quarters = [0, 1, 2, 3] if d == 0 else [3, 2, 1, 0]
                        for q in quarters:
                            c0 = q * QW
                            si = step[0] % 2; step[0] += 1
                            BA = BAs[si]; BB = BBs[si]; BC = BCs[si]
                            ka = ("BA", si); kb = ("BB", si); kc_ = ("BC", si)
                            blks = [2 * q, 2 * q + 1]
                            gates(d, ucbf[:, c0:c0 + QW], [("ucbf", j) for j in blks], T1[:, c0:c0 + QW], [("T1", j) for j in blks], QW,
                                  BA, ka, BB, kb, BC, kc_)
                            t0keys = [("T0", pk, j) for j in blks]
                            if d == 0:
                                S.op("dve", V.tensor_tensor_scan, [ka, kc_, ("carry", 0)], t0keys, out=T0[:, c0:c0 + QW], data0=BA[:], data1=BC[:],
                                     initial=carry[:, 0:1], op0=ALU.mult, op1=ALU.add)
                                S.op("dve", V.tensor_copy, t0keys, [("carry", 0)], out=carry[:, 0:1], in_=T0[:, c0 + QW - 1:c0 + QW])
                            else:
                                S.op("dve", V.tensor_tensor_scan, [ka, kc_, ("carry", 1)], [kb], out=rev(BB[:], QW), data0=rev(BA[:], QW),
                                     data1=rev(BC[:], QW), initial=carry[:, 1:2], op0=ALU.mult, op1=ALU.add)
                                S.op("dve", V.tensor_copy, [kb], [("carry", 1)], out=carry[:, 1:2], in_=BB[:, 0:1])
                                S.op("dve", V.tensor_tensor, [kb] + t0keys, [ka], out=BA[:], in0=BB[:], in1=T0[:, c0:c0 + QW], op=ALU.add)
                                szk = [("szl", pk, j) for j in blks]
                                S.op("dve", V.tensor_tensor, [ka] + szk, szk, out=szl[:, c0:c0 + QW], in0=BA[:], in1=szl[:, c0:c0 + QW],
                                     op=ALU.mult)
                                if q % 2 == 0:
                                    hq = q // 2
                                    S.dma(mixD[(4 + ch) * 128:(5 + ch) * 128, hq * 2048:(hq + 1) * 2048], szl[:, hq * 2048:(hq + 1) * 2048],
                                          [("szl", pk, j) for j in range(4 * hq, 4 * hq + 4)], [])
                        if d == 0 and mid_hook is not None:
                            mid_hook()

                step = [0]; gcnt = [0]
                proj_stage(0)
                for ch in range(4):
                    main_stage(ch, (lambda c=ch: proj_stage(c + 1)) if ch < 3 else None)
                S.flush()

        def att_phase():
            S.barrier()
            with ExitStack() as e2:
                qrot = sbt(e2, "qrot", [128, SEQ], BF16); qpl = sbt(e2, "qpl", [128, SEQ], BF16)
                krot = sbt(e2, "krot", [128, SEQ], BF16); sza = sbt(e2, "sza", [128, SEQ], BF16)
                vaug = sbt(e2, "vaug", [128, 32 * 256], BF16); mixc = sbt(e2, "mixc", [128, SEQ], BF16)
                kcn2 = [sbt(e2, "kcn%d" % i, [128, CTX], BF16) for i in range(2)]
                vcaug2 = [sbt(e2, "vcaug%d" % i, [128, 2 * 256], BF16) for i in range(2)]
                Ffull2 = [sbt(e2, "Ffull%d" % i, [128, 2 * FW], BF16) for i in range(2)]
                Fint2 = [sbt(e2, "Fint%d" % i, [128, 2 * FW], BF16) for i in range(2)]
                cst = [sbt(e2, "cst%d" % i, [128, 512], F32) for i in range(2)]
                snt = [sbt(e2, "snt%d" % i, [128, 512], F32) for i in range(2)]
                sqb = [sbt(e2, "sqb%d" % i, [128, 512], BF16) for i in range(CFG["n_qk"])]
                qsb = [sbt(e2, "qsb%d" % i, [128, 512], F32) for i in range(CFG["n_qk"])]
                rtb = [sbt(e2, "rtb%d" % i, [128, 512], F32) for i in range(CFG["n_qk"])]
                knbf = [sbt(e2, "knbf%d" % i, [128, 512], BF16) for i in range(CFG["n_qk"])]
                r1b = [sbt(e2, "r1b%d" % i, [128, 512], F32) for i in range(CFG["n_qk"])]
                r2b = [sbt(e2, "r2b%d" % i, [128, 512], F32) for i in range(CFG["n_qk"])]
                etb = [sbt(e2, "etb%d" % i, [128, 512], BF16) for i in range(CFG["n_et"])]
                ptb = [sbt(e2, "ptb%d" % i, [128, 512], BF16) for i in range(CFG["n_pt"])]
                PSX = PS + [e2.enter_context(nc.psum_tensor("psx_%d" % i, [128, 512], F32)) for i in range(2)]
                rdb = [sbt(e2, "rdb%d" % i, [128, 256], F32) for i in range(2)]

                mk = sbt(e2, "mk", [128, 2 * FW], BF16)
                for j_ in range(4):
                    S.dma(r1b[j_ % 2][:, 0:448], masks[:, j_ * 448:(j_ + 1) * 448], [], [("r1b", j_ % 2)])
                    S.op("dve", V.tensor_copy, [("r1b", j_ % 2)], ["mk"], out=mk[:, j_ * 448:(j_ + 1) * 448], in_=r1b[j_ % 2][:, 0:448])
                for i_ in range(2):
                    S.op("pool", G.memset, [], [("vcaug", i_)], vcaug2[i_][:], 1.0)
                for pr in range(4):
                    att_body(pr, locals())
                for kc in [4, 5, 6, 7, 0, 1, 2]:
                    S.dma(hT[:, kc * SEQ:(kc + 1) * SEQ], mixD[kc * 128:(kc + 1) * 128, :], [("mixD", kc)] if kc < 4 else [], ["hT"])
                S.flush()

        def att_body(pr, L):
            if True:
                if True:
                    pass
                par = pr % 2
                qrot = L["qrot"]; qpl = L["qpl"]; krot = L["krot"]; sza = L["sza"]; vaug = L["vaug"]; mixc = L["mixc"]; mk = L["mk"]
                kcn = L["kcn2"][par]; vcaug = L["vcaug2"][par]; Ffull = L["Ffull2"][par]; Fint = L["Fint2"][par]
                cst = L["cst"]; snt = L["snt"]; sqb = L["sqb"]; rtb = L["rtb"]; knbf = L["knbf"]; qsb = L["qsb"]
                r1b = L["r1b"]; r2b = L["r2b"]; etb = L["etb"]; ptb = L["ptb"]; PSX = L["PSX"]; rdb = L["rdb"]; ttb = L["rdb"]
                kFf = ("Ffull", par); kFi = ("Fint", par); kkc = ("kcn", par); kvc = ("vcaug", par)
                for hh in range(2):
                    for hf in range(2):
                        c0 = hf * 448
                        S.dma(r1b[hf][:, 0:448], rpbG[2 * pr + hh, :, c0:c0 + 448], [], [("r1b", hf)])
                        S.op("act", A.activation, [("r1b", hf)], [("r2b", hf)], out=r2b[hf][:, 0:448], in_=r1b[hf][:, 0:448], func=AF.Exp)
                        S.op("dve", V.tensor_tensor, [("r2b", hf), "mk"], [kFf], out=Ffull[:, hh * FW + c0:hh * FW + c0 + 448], in0=r2b[hf][:, 0:448],
                             in1=mk[:, c0:c0 + 448], op=ALU.mult)
                        S.op("dve", V.tensor_tensor, [("r2b", hf), "mk"], [kFi], out=Fint[:, hh * FW + c0:hh * FW + c0 + 448], in0=r2b[hf][:, 0:448],
                             in1=mk[:, FW + c0:FW + c0 + 448], op=ALU.mult)
                wq, wqk = load_wblock(128 * pr)
                wk, wkk = load_wblock(512 + 128 * pr)
                wv, wvk = load_wblock(1024 + 128 * pr)
                wz, wzk = load_wblock(1536 + 128 * pr)
                for g in range(8):
                    b = g % 2
                    proj_fm(wz, wzk, hT, SEQ, g * 512, 512, b)
                    S.op("act", A.activation, [("ps", b)], [("sza", g)], out=sza[:, g * 512:(g + 1) * 512], in_=PS[b][:, :], func=AF.Silu)

                cnt = [0]

                def qk_path(bank, n, gcol, plain_dst, pkey, rot_dst, rkey, ts):
                    i = cnt[0] % CFG["n_qk"]; cnt[0] += 1
                    S.op("act", A.activation, [("ps", bank)], [("qsb", i)], out=qsb[i][:, 0:n], in_=PS[bank][:, 0:n], func=AF.Identity)
                    S.op("act", A.activation, [("ps", bank)], [("sqb", i)], out=sqb[i][:, 0:n], in_=PS[bank][:, 0:n], func=AF.Square)
                    S.op("pe", T.matmul, ["cm", ("sqb", i)], [("ps", 2)], PS[2][:, 0:n], bones, sqb[i][:, 0:n], start=True, stop=True)
                    S.op("act", A.activation, [("ps", 2)], [("rtb", i)], out=rtb[i][:, 0:n], in_=PS[2][:, 0:n], func=AF.Ln, bias=EPS, scale=1.0)
                    S.op("act", A.activation, [("rtb", i)], [("rtb", i)], out=rtb[i][:, 0:n], in_=rtb[i][:, 0:n], func=AF.Exp, scale=-0.5)
                    if plain_dst is None:
                        plain_dst = knbf[i][:, 0:n]; pkey = ("knbf", i)
                    S.op("dve", V.scalar_tensor_tensor, [("qsb", i), "gq", ("rtb", i)], [pkey], out=plain_dst, in0=qsb[i][:, 0:n],
                         scalar=gq[:, gcol:gcol + 1], in1=rtb[i][:, 0:n], op0=ALU.mult, op1=ALU.mult)
                    if rot_dst is None:
                        return
                    S.op("pe", T.matmul, ["cm", pkey], [("ps", 2)], PS[2][:, 0:n], Rm, plain_dst, start=True, stop=True)
                    S.op("dve", V.tensor_tensor, [pkey, ("cst", ts)], [("r1b", i)], out=r1b[i][:, 0:n], in0=plain_dst, in1=cst[ts][:, 0:n], op=ALU.mult)
                    S.op("dve", V.tensor_tensor, [("ps", 2), ("snt", ts)], [("r2b", i)], out=r2b[i][:, 0:n], in0=PS[2][:, 0:n], in1=snt[ts][:, 0:n], op=ALU.mult)
                    if CFG["rope_add"] == "pool":
                        S.op("pool", G.tensor_tensor, [("r1b", i), ("r2b", i)], [rkey], out=rot_dst, in0=r1b[i][:, 0:n], in1=r2b[i][:, 0:n], op=ALU.add)
                    else:
                        S.op("dve", V.tensor_tensor, [("r1b", i), ("r2b", i)], [rkey], out=rot_dst, in0=r1b[i][:, 0:n], in1=r2b[i][:, 0:n], op=ALU.add)

                proj_fm(wk, wkk, hcT, CTX, 0, CTX, 0)
                qk_path(0, CTX, 1, kcn[:], kkc, None, None, 0)
                for t_ in range(2):
                    for kc in range(8):
                        S.op("pe", T.matmul, [wvk, "hT"], [("ps", 3)], PS[3][:, t_ * 128:(t_ + 1) * 128],
                             hcT[:, kc * CTX + t_ * 128: kc * CTX + (t_ + 1) * 128], wv[:, kc * 128:(kc + 1) * 128], start=(kc == 0), stop=(kc == 7))
                vc3 = vcaug[:].rearrange("p (t c) -> p t c", c=256)
                pv3 = PS[3][:, 0:256].rearrange("p (t c) -> p t c", c=128)
                S.op("act", A.activation, [("ps", 3)], [kvc], out=vc3[:, :, 0:64], in_=pv3[:, :, 0:64], func=AF.Identity)
                S.op("act", A.activation, [("ps", 3)], [kvc], out=vc3[:, :, 192:256], in_=pv3[:, :, 64:128], func=AF.Identity)

                def proj_group(g):
                    ts = g % 2
                    S.dma(cst[ts][:], cosT[:, g * 512:(g + 1) * 512], [], [("cst", ts)])
                    S.dma(snt[ts][:], sinT[:, g * 512:(g + 1) * 512], [], [("snt", ts)])
                    proj_fm(wq, wqk, hT, SEQ, g * 512, 512, 0)
                    qk_path(0, 512, 0, qpl[:, g * 512:(g + 1) * 512], ("qpl", g), qrot[:, g * 512:(g + 1) * 512], ("qrot", g), ts)
                    proj_fm(wk, wkk, hT, SEQ, g * 512, 512, 1)
                    qk_path(1, 512, 1, None, None, krot[:, g * 512:(g + 1) * 512], ("krot", g), ts)
                    S.dma(vaug[:, g * 1024:(g + 1) * 1024], vD[pr, :, g * 1024:(g + 1) * 1024], [], [("vaug", g)])

                sc = [0]

                def f3(Ft, hh, jj0):
                    base = Ft[:, hh * FW + jj0 * 64: hh * FW + jj0 * 64 + 256]
                    return bass.AP(base.tensor, base.offset, [[base.ap[0][0], 128], [-128, 2], [1, 256]])

                def att_group(qg):
                    R0 = 4 * qg
                    gq_ = qg // 2
                    qs = slice(R0 * 64, R0 * 64 + 256)
                    if qg == 0:
                        tl = [(a, 6 - a, Ffull, 0, 256) for a in (0, 2, 4, 6)]
                        pairs = [(0, 1), (2, 3)]
                    elif qg == 15:
                        tl = [(a, 6 - (a - 60), Ffull, 0, 256) for a in (56, 58, 60, 62)]
                        pairs = [(0, 1), (2, 3)]
                    else:
                        qr = [(0, 128), (0, 256), (0, 256), (0, 256), (64, 256), (192, 256)]
                        tl = [(R0 - 4 + 2 * i, 10 - 2 * i, Fint, qr[i][0], qr[i][1]) for i in range(6)]
                        pairs = [(1, 2), (3, 4), (0, 5)]
                    i2 = qg % 2
                    for hh in range(2):
                        pb = 64 * hh; ob = 6 + hh; po = PSX[ob][:, 0:256]
                        for pi, (iA, iB) in enumerate(pairs):
                            tA = tl[iA]; tB = tl[iB]
                            wA = tA[4] - tA[3]; wB = tB[4] - tB[3]
                            offs = [256 - wA, 256]
                            lo_ = 256 - wA; hi_ = 256 + wB
                            sb_ = CFG["score_banks"][sc[0] % len(CFG["score_banks"])]; ke = sc[0] % CFG["n_et"]; kp = sc[0] % CFG["n_pt"]; sc[0] += 1
                            for u, tt_ in enumerate((tA, tB)):
                                au, jj0, Ft, qa, qb = tt_
                                S.op("pe", T.matmul, [("krot", au // 8), ("qrot", gq_)], [("ps", sb_)], PSX[sb_][:, offs[u]:offs[u] + (qb - qa)],
                                     krot[pb:pb + 64, au * 64:au * 64 + 128], qrot[pb:pb + 64, R0 * 64 + qa:R0 * 64 + qb], start=True, stop=True)
                            S.op("act", A.activation, [("ps", sb_)], [("et", ke)], out=etb[ke][:, lo_:hi_], in_=PSX[sb_][:, lo_:hi_], func=AF.Exp, scale=0.125)
                            if wA == 256 and wB == 256 and tB[1] == tA[1] - 2:
                                S.op("dve", V.tensor_tensor, [("et", ke), kFf, kFi], [("ptl", kp)],
                                     out=ptb[kp][:].rearrange("p (t c) -> p t c", c=256), in0=etb[ke][:].rearrange("p (t c) -> p t c", c=256),
                                     in1=f3(tA[2], hh, tA[1]), op=ALU.mult)
                            else:
                                for u, tt_ in enumerate((tA, tB)):
                                    au, jj0, Ft, qa, qb = tt_
                                    w_ = qb - qa
                                    S.op("dve", V.tensor_tensor, [("et", ke), kFf, kFi], [("ptl", kp)], out=ptb[kp][:, offs[u]:offs[u] + w_],
                                         in0=etb[ke][:, offs[u]:offs[u] + w_],
                                         in1=Ft[:, hh * FW + jj0 * 64 + qa: hh * FW + jj0 * 64 + qb], op=ALU.mult)
                            for u, tt_ in enumerate((tA, tB)):
                                au, jj0, Ft, qa, qb = tt_
                                tix = au // 2
                                S.op("pe", T.matmul, [("vaug", au // 8), ("ptl", kp)], [("ps", ob)], PSX[ob][:, qa:qb],
                                     vaug[:, tix * 256 + hh * 128: tix * 256 + hh * 128 + 128], ptb[kp][:, offs[u]:offs[u] + (qb - qa)],
                                     start=(pi == 0 and u == 0), stop=False)
                        sb_ = CFG["score_banks"][sc[0] % len(CFG["score_banks"])]; kp = sc[0] % CFG["n_pt"]; sc[0] += 1
                        for j in range(2):
                            S.op("pe", T.matmul, [kkc, ("qpl", gq_)], [("ps", sb_)], PSX[sb_][:, j * 256:(j + 1) * 256], kcn[pb:pb + 64, j * 128:(j + 1) * 128],
                                 qpl[pb:pb + 64, qs], start=True, stop=True)
                        S.op("act", A.activation, [("ps", sb_)], [("ptl", kp)], out=ptb[kp][:], in_=PSX[sb_][:, :], func=AF.Exp, scale=0.125)
                        for j in range(2):
                            S.op("pe", T.matmul, [kvc, ("ptl", kp)], [("ps", ob)], PSX[ob][:, 0:256],
                                 vcaug[:, j * 256 + hh * 128: j * 256 + hh * 128 + 128], ptb[kp][:, j * 256:(j + 1) * 256], start=False, stop=(j == 1))
                        S.op("act", A.activation, [("ps", ob)], [("rd", i2, hh)], out=rdb[i2][pb:pb + 64, :], in_=po[64 - pb:128 - pb, 0:256], func=AF.Ln)
                        S.op("act", A.activation, [("rd", i2, hh)], [("rd", i2, hh)], out=rdb[i2][pb:pb + 64, :], in_=rdb[i2][pb:pb + 64, :],
                             func=AF.Exp, scale=-1.0)
                        S.op("dve", V.tensor_tensor, [("rd", i2, hh), ("sza", gq_)], [("rd", i2, hh)], out=rdb[i2][pb:pb + 64, :], in0=rdb[i2][pb:pb + 64, :],
                             in1=sza[pb:pb + 64, qs], op=ALU.mult)
                        S.op("dve", V.tensor_tensor, [("ps", ob), ("rd", i2, hh)], [("mixc", gq_)], out=mixc[pb:pb + 64, qs], in0=po[pb:pb + 64, 0:256],
                             in1=rdb[i2][pb:pb + 64, :], op=ALU.mult)

                def mix_out(g):
                    S.dma(mixD[pr * 128:(pr + 1) * 128, g * 512:(g + 1) * 512], mixc[:, g * 512:(g + 1) * 512], [("mixc", g)], [("mixD", pr)])

                proj_group(0)
                for g in range(1, 8):
                    proj_group(g)
                    att_group(2 * g - 2)
                    att_group(2 * g - 1)
                    mix_out(g - 1)
                att_group(14)
                att_group(15)
                mix_out(7)

        lru_phase()
        att_phase()

        S.barrier()
        with ExitStack() as e3:
            wo = sbt(e3, "wo", [128, 8 * DM], BF16)
            xtb = [sbt(e3, "xo%d" % i, [128, DM], F32) for i in range(6)]
            otb = [sbt(e3, "ot%d" % i, [128, DM], F32) for i in range(3)]
            hT3 = hT[:].rearrange("p (kc t) -> p kc t", kc=8)
            mixD3 = mixD.rearrange("(kc p) t -> p kc t", p=128)
            wo3 = wo[:].rearrange("p (kc n) -> p kc n", kc=8)
            woD3 = woD.rearrange("(kc p) n -> p kc n", p=128)
            for h4 in range(2):
                S.dma(wo3[:, h4 * 4:(h4 + 1) * 4, :], woD3[:, h4 * 4:(h4 + 1) * 4, :], [], [("wo", h4)])
            for h2 in range(2):
                S.dma(hT[:, 3 * SEQ + h2 * 2048:3 * SEQ + (h2 + 1) * 2048], mixD[384:512, h2 * 2048:(h2 + 1) * 2048], [], [("mx", h2)])
            for i in range(32):
                xt = xtb[i % 6]; xk = ("xo", i % 6)
                S.dma(xt[:], x[i * 128:(i + 1) * 128, :], [], [xk])
                ot = otb[i % 3]; ok = ("ot", i % 3)
                for hf in range(2):
                    b = (2 * i + hf) % 4
                    for kc in range(8):
                        S.op("pe", T.matmul, [("mx", i // 16), ("wo", kc // 4)], [("ps", b)], PS[b][:, :], hT[:, kc * SEQ + i * 128: kc * SEQ + (i + 1) * 128],
                             wo[:, kc * DM + hf * 512: kc * DM + (hf + 1) * 512], start=(kc == 0), stop=(kc == 7))
                    S.op("dve", V.tensor_tensor, [("ps", b), xk], [ok], out=ot[:, hf * 512:(hf + 1) * 512], in0=PS[b][:, :],
                         in1=xt[:, hf * 512:(hf + 1) * 512], op=ALU.add)
                S.dma(out[i * 128:(i + 1) * 128, :], ot[:], [ok], [])
            S.final_wait()
    return nc


def kernel(x, c, ctx, c_ctx, norm_g, w_mod, b_mod, w_in, w_out, q_norm_g, k_norm_g, rpb,
           conv_w, conv_b, lru_wa, lru_ba, lru_wx, lru_bx, lru_lam):
    f = lambda a: np.ascontiguousarray(np.asarray(a, dtype=np.float32))
    x, c, ctx, c_ctx = f(x), f(c), f(ctx), f(c_ctx)
    norm_g, w_mod, b_mod, w_in, w_out = f(norm_g), f(w_mod), f(b_mod), f(w_in), f(w_out)
    q_norm_g, k_norm_g, rpb = f(q_norm_g), f(k_norm_g), f(rpb)
    conv_w, conv_b, lru_wa, lru_ba, lru_wx, lru_bx, lru_lam = (f(conv_w), f(conv_b), f(lru_wa), f(lru_ba),
                                                              f(lru_wx), f(lru_bx), f(lru_lam))
    cosT, sinT, cm, sel, masks, dridx, dcidx = _host_consts()
    rpbG = np.ascontiguousarray(rpb[0][:, dridx, dcidx].reshape(8, 128, FW))
    gqk = np.stack([np.tile(q_norm_g[0], 2), np.tile(k_norm_g[0], 2)], axis=1).astype(np.float32)
    plru = np.zeros((4, 128, 16), np.float32)
    wbd = np.zeros((4, 128, 512), np.float32)
    for ch in range(4):
        sl = slice(ch * 128, (ch + 1) * 128)
        for j in range(4):
            plru[ch, :, j] = conv_w[0, j, sl]
        plru[ch, :, 4] = conv_b[0, sl]
        for d in range(2):
            plru[ch, :, 5 + 3 * d] = lru_ba[0, d, sl]
            plru[ch, :, 6 + 3 * d] = lru_bx[0, d, sl]
            plru[ch, :, 7 + 3 * d] = lru_lam[0, d, sl]
            for m, wsrc in enumerate((lru_wa, lru_wx)):
                col = (2 * d + m) * 128
                for blk in range(2):
                    wbd[ch, blk * 64:(blk + 1) * 64, col + blk * 64: col + (blk + 1) * 64] = wsrc[0, d, 2 * ch + blk]
    common = {
        "norm_g": norm_g[0:1], "w_mod": w_mod[0], "b_mod": b_mod[0:1], "w_in": w_in[0], "w_out": w_out[0],
        "gqk": gqk, "plru": plru, "wbd": wbd, "rpbG": rpbG, "cosT": cosT, "sinT": sinT, "cmat": cm, "sel": sel,
        "masks": masks,
    }
    in_maps = []
    for b in range(8):
        cvb = np.stack([c[b], c_ctx], axis=0)
        cvl = np.ascontiguousarray(cvb.reshape(2, 8, 128).transpose(2, 1, 0).reshape(128, 16))
        m = dict(common)
        m["x"] = x[b]; m["ctx"] = ctx[b]; m["cv"] = cvl
        in_maps.append(m)
    nc = build_nc()
    res = run_bass_kernel_spmd(nc, in_maps, core_ids=list(range(8)))
    return np.stack([np.asarray(r["out"], dtype=np.float32) for r in res.results], axis=0)
```

```python
import numpy as np
from contextlib import ExitStack
import concourse.bass as bass
import concourse.mybir as mybir
from concourse.bass_utils import run_bass_kernel_spmd

F32 = mybir.dt.float32
BF16 = mybir.dt.bfloat16
ALU = mybir.AluOpType
AF = mybir.ActivationFunctionType

CFG = {"score_banks": [3, 4, 5], "n_et": 3, "n_pt": 4, "vbank": 3, "window": 200, "rope_add": "pool", "n_qk": 2, "a2_act": 2, "conv_act": 0, "scan_cost": 2.2, "slack": 0.0, "vevac": "act", "lru_pair": 1}
SEQ = 4096
DM = 1024
CTX = 256
EPS = 1e-6
NJ = 14
FW = NJ * 64


class _Op:
    __slots__ = ("eng", "fn", "args", "kwargs", "preds", "dur", "idx", "tab", "seq", "dsem", "dval", "fin")


_ACT_TAB = {}


def _act_tab(func):
    n = str(func)
    if "Exp" in n:
        return ("exp", "ln")
    if "Tanh" in n:
        return ("exp",)
    if "Ln" in n:
        return ("ln",)
    if "Sqrt" in n:
        return ("sqrt",)
    if "Silu" in n:
        return ("silu",)
    if "Sigmoid" in n:
        return ("sigmoid",)
    return None


class Sched:
    WINDOW = 48

    def __init__(self, nc, es):
        self.nc = nc
        self.engs = {"pe": nc.tensor, "act": nc.scalar, "dve": nc.vector, "pool": nc.gpsimd, "sp": nc.sync}
        self.csem = {e: es.enter_context(nc.semaphore("cs_" + e)) for e in ("pe", "act", "dve", "pool")}
        self.cnt = {e: 0 for e in self.csem}
        self.NDS = 12
        self.dsem = [es.enter_context(nc.semaphore("ds%d" % i)) for i in range(self.NDS)]
        self.dcnt = [0] * self.NDS
        self.dnext = 0
        self.seen = {e: {} for e in self.engs}
        self.ops = []
        self.lastw = {}
        self.readers = {}
        self.last_tab = "none"

    def _est(self, eng, args, kwargs, fn=None):
        out = kwargs.get("out", args[0] if args else None)
        try:
            n = out.free_size()
        except Exception:
            n = 512
        if eng == "pe":
            return 64.0 + n * 0.42
        if eng == "act":
            return max(200.0, 100.0 + n * 0.88)
        if eng == "dve":
            if "scan" in getattr(fn, "__name__", ""):
                return 100.0 + n * CFG["scan_cost"]
            return max(160.0, 60.0 + n * 1.1)
        if eng == "pool":
            return 250.0 + n * 4.5
        try:
            nb = out.nbytes()
        except Exception:
            nb = 65536
        return float(nb)

    def _add(self, eng, fn, r, w, args, kwargs):
        o = _Op()
        o.eng = eng; o.fn = fn; o.args = args; o.kwargs = kwargs; o.idx = len(self.ops)
        o.dur = self._est(eng, args, kwargs, fn)
        o.tab = _act_tab(kwargs.get("func")) if eng == "act" else None
        preds = {}
        for k in r:
            p = self.lastw.get(k)
            if p is not None:
                need = not (p.eng == eng and eng == "pe")
                preds[p.idx] = preds.get(p.idx, False) or need
        for k in w:
            p = self.lastw.get(k)
            if p is not None:
                need = not (p.eng == eng and eng == "pe")
                preds[p.idx] = preds.get(p.idx, False) or need
            for p in self.readers.get(k, ()):
                need = not (p.eng == eng and eng == "pe")
                preds[p.idx] = preds.get(p.idx, False) or need
        o.preds = preds
        for k in r:
            self.readers.setdefault(k, []).append(o)
        for k in w:
            self.lastw[k] = o
            self.readers[k] = []
        self.ops.append(o)
        return o

    def op(self, eng, fn, r, w, *args, **kwargs):
        return self._add(eng, fn, r, w, args, kwargs)

    def dma(self, out, in_, r, w, **kwargs):
        kwargs = dict(kwargs); kwargs["out"] = out; kwargs["in_"] = in_
        return self._add("sp", self.nc.sync.dma_start, r, w, (), kwargs)

    def _wait(self, eng, key, val):
        if self.seen[eng].get(key, 0) >= val:
            return
        sem = self.csem[key[1]] if key[0] == "c" else self.dsem[key[1]]
        self.engs[eng].wait_ge(sem, val)
        self.seen[eng][key] = val

    def flush(self):
        ops = self.ops
        if not ops:
            return
        pend = {e: [] for e in self.engs}
        for o in ops:
            o.fin = None
            pend[o.eng].append(o)
        pos = {e: 0 for e in self.engs}
        free = {e: 0.0 for e in self.engs}
        order = {e: [] for e in self.engs}
        done = {e: set() for e in self.engs}
        remaining = len(ops)
        last_tab = self.last_tab
        dma_free = 0.0
        succ = [[] for _ in ops]
        for o in ops:
            for pi in o.preds:
                succ[pi].append(o.idx)
        rank = [0.0] * len(ops)
        for o in reversed(ops):
            m = 0.0
            for si in succ[o.idx]:
                if rank[si] > m:
                    m = rank[si]
            rank[o.idx] = m + (o.dur if o.eng != "sp" else 2500.0)
        slack = CFG.get("slack", 0.0)
        while remaining:
            best = None
            for e in self.engs:
                lst = pend[e]; p0 = pos[e]
                cnt = 0; i = p0
                cands = []
                while i < len(lst) and cnt < CFG["window"]:
                    o = lst[i]; i += 1
                    if o.idx in done[e]:
                        continue
                    cnt += 1
                    rt = 0.0; ok = True
                    for pi in o.preds:
                        f = ops[pi].fin
                        if f is None:
                            ok = False; break
                        if ops[pi].eng != e:
                            f += 200.0
                        if f > rt:
                            rt = f
                    if not ok:
                        continue
                    st = max(free[e], rt)
                    pen = 0.0
                    if e == "act" and o.tab is not None and last_tab not in o.tab:
                        pen = 1400.0
                    cands.append((st + pen, o, st, pen))
                if not cands:
                    continue
                m = min(c[0] for c in cands)
                pick = None
                for c in cands:
                    if c[0] <= m + slack:
                        if pick is None or (rank[c[1].idx], -c[1].idx) > (rank[pick[1].idx], -pick[1].idx):
                            pick = c
                key = (pick[0], pick[1].idx)
                if best is None or key < best[0]:
                    best = (key, e, pick[1], pick[2], pick[3])
            _, e, o, st, pen = best
            if e == "sp":
                free[e] = st + 300.0
                xs = max(st + 300.0, dma_free)
                dma_free = xs + o.dur / 200.0
                o.fin = dma_free + 1800.0
            else:
                o.fin = st + pen + o.dur
                free[e] = o.fin
            if e == "act" and o.tab is not None and last_tab not in o.tab:
                last_tab = o.tab[0]
            order[e].append(o)
            done[e].add(o.idx)
            while pos[e] < len(pend[e]) and pend[e][pos[e]].idx in done[e]:
                pos[e] += 1
            remaining -= 1
        self.last_tab = last_tab
        for e in self.csem:
            c = self.cnt[e]
            for o in order[e]:
                c += 1; o.seq = c
        dn = self.dnext; dc = list(self.dcnt)
        for o in order["sp"]:
            i = dn % self.NDS; dn += 1
            dc[i] += 16
            o.dsem = i; o.dval = dc[i]
        for e in self.engs:
            for o in order[e]:
                reqs = {}
                for pi, need in o.preds.items():
                    if not need:
                        continue
                    p = ops[pi]
                    if p.eng == "sp":
                        k_ = ("d", p.dsem); v_ = p.dval
                    else:
                        k_ = ("c", p.eng); v_ = p.seq
                    if reqs.get(k_, 0) < v_:
                        reqs[k_] = v_
                for k_, v_ in reqs.items():
                    self._wait(e, k_, v_)
                if e == "sp":
                    i = o.dsem
                    if o.dval > 16:
                        self._wait(e, ("d", i), o.dval - 16)
                    ins = o.fn(*o.args, **o.kwargs)
                    ins.then_inc(self.dsem[i], 16)
                else:
                    ins = o.fn(*o.args, **o.kwargs)
                    ins.then_inc(self.csem[e], 1)
        for e in self.csem:
            self.cnt[e] += len(order[e])
        self.dnext = dn; self.dcnt = dc
        self.ops = []
        self.lastw = {}
        self.readers = {}

    def barrier(self):
        self.flush()
        for eng in self.engs:
            for e in self.csem:
                if e != eng and self.cnt[e] > 0:
                    self._wait(eng, ("c", e), self.cnt[e])
            for i in range(self.NDS):
                if self.dcnt[i] > 0:
                    self._wait(eng, ("d", i), self.dcnt[i])

    def final_wait(self):
        self.flush()
        for i in range(self.NDS):
            if self.dcnt[i] > 0:
                self._wait("sp", ("d", i), self.dcnt[i])


def _host_consts():
    p = np.arange(128)
    f = p % 64
    i16 = (f % 16).astype(np.float32)
    inv = (np.float32(10000.0) ** (-(i16) / np.float32(16.0))).astype(np.float32)
    t = np.arange(SEQ)
    r = (t // 64).astype(np.float32)
    c = (t % 64).astype(np.float32)
    pos = np.where((f < 32)[:, None], r[None, :], c[None, :]).astype(np.float32)
    ang = (pos * inv[:, None]).astype(np.float32)
    cosT = np.cos(ang).astype(np.float32)
    sgn = np.where((f % 32) < 16, -1.0, 1.0).astype(np.float32)
    sinT = (np.sin(ang) * sgn[:, None]).astype(np.float32)
    partner = np.where((f % 32) < 16, p + 16, p - 16)
    cm = np.zeros((128, 384), np.float32)
    cm[p, p] = 1.0
    cm[partner, 128 + p] = 1.0
    blk = (p[:, None] // 64) == (p[None, :] // 64)
    cm[:, 256:384] = blk.astype(np.float32) / 64.0
    sel = np.zeros((3, 384), np.float32)
    for rr in range(3):
        sel[rr, rr * 128:(rr + 1) * 128] = 1.0
    krl = (p // 64)[:, None, None]
    kc = (p % 64)[:, None, None]
    jj = np.arange(NJ)[None, :, None]
    qc = np.arange(64)[None, None, :]
    dr = 6 - jj + krl
    cs = np.clip(qc - 8, 0, 48)
    colm = (kc >= cs) & (kc < cs + 16)
    MC = np.broadcast_to(colm, (128, NJ, 64)).astype(np.float32)
    MI = (colm & (dr >= -4) & (dr <= 3)).astype(np.float32)
    masks = np.concatenate([MC.reshape(128, FW), MI.reshape(128, FW)], axis=1).astype(np.float32)
    dridx = np.broadcast_to(dr + 7, (128, NJ, 64))
    dcidx = np.broadcast_to(np.clip(kc - qc, -15, 15) + 15, (128, NJ, 64))
    return cosT, sinT, cm, sel, masks, dridx, dcidx


def build_nc():
    nc = bass.Bass("TRN2", target_bir_lowering=False)

    def din(name, shape, dt=F32):
        return nc.dram_tensor(name, list(shape), dt, kind="ExternalInput").ap()

    x = din("x", [SEQ, DM]); ctx = din("ctx", [CTX, DM]); cv = din("cv", [128, 16])
    norm_g = din("norm_g", [1, DM]); w_mod = din("w_mod", [DM, 3 * DM]); b_mod = din("b_mod", [1, 3 * DM])
    w_in = din("w_in", [DM, 3 * DM]); w_out = din("w_out", [DM, DM])
    gqk = din("gqk", [128, 2]); plru = din("plru", [4, 128, 16]); wbd = din("wbd", [4, 128, 512])
    rpbG = din("rpbG", [8, 128, FW]); cosT = din("cosT", [128, SEQ]); sinT = din("sinT", [128, SEQ])
    cmat = din("cmat", [128, 384]); sel = din("sel", [3, 384]); masks = din("masks", [128, 2 * FW])
    out = nc.dram_tensor("out", [SEQ, DM], F32, kind="ExternalOutput").ap()
    mixD = nc.dram_tensor("mixD", [DM, SEQ], BF16, kind="Internal").ap()
    vD = nc.dram_tensor("vD", [4, 128, 32 * 256], BF16, kind="Internal").ap()
    woD = nc.dram_tensor("woD", [DM, DM], BF16, kind="Internal").ap()
    w_in_v = w_in.rearrange("(kc p) n -> p kc n", p=128)

    with ExitStack() as es:
        S = Sched(nc, es)
        V, A, G, T = nc.vector, nc.scalar, nc.gpsimd, nc.tensor

        uid = [0]

        def sbt(stack, name, shape, dt):
            uid[0] += 1
            return stack.enter_context(nc.sbuf_tensor("%s_%d" % (name, uid[0]), list(shape), dt))

        hT = sbt(es, "hT", [128, 8 * SEQ], BF16)
        hcT = sbt(es, "hcT", [128, 8 * CTX], BF16)
        cm = sbt(es, "cm", [128, 384], BF16)
        gq = sbt(es, "gq", [128, 2], F32)
        wst = [sbt(es, "wst%d" % i, [128, 1024], F32) for i in range(2)]
        wbf = [sbt(es, "wbf%d" % i, [128, 1024], BF16) for i in range(4)]
        PS = [es.enter_context(nc.psum_tensor("ps%d" % i, [128, 512], F32)) for i in range(6)]
        ident = cm[:, 0:128]; Rm = cm[:, 128:256]; bones = cm[:, 256:384]
        wslot = [0]

        def load_wblock(c0):
            s = wslot[0]; wslot[0] += 1
            st = wst[s % 2]; wb = wbf[s % 4]
            S.dma(st[:].rearrange("p (kc n) -> p kc n", kc=8), w_in_v[:, :, c0:c0 + 128], [], [("wst", s % 2)])
            S.op("dve", V.tensor_copy, [("wst", s % 2)], [("wbf", s % 4)], out=wb[:], in_=st[:])
            return wb, ("wbf", s % 4)

        def proj_fm(wb, wkey, src, src_w, col0, ncols, bank):
            for kc in range(8):
                S.op("pe", T.matmul, [wkey, "hT"], [("ps", bank)], PS[bank][:, 0:ncols],
                     wb[:, kc * 128:(kc + 1) * 128], src[:, kc * src_w + col0: kc * src_w + col0 + ncols],
                     start=(kc == 0), stop=(kc == 7))

        with ExitStack() as e1:
            PT = [e1.enter_context(nc.psum_tensor("pt%d" % i, [128, 1024], BF16)) for i in range(2)]
            cm_f = sbt(e1, "cm_f", [128, 384], F32); gate_bc = sbt(e1, "gate_bc", [128, DM], F32)
            cvt = sbt(e1, "cvt", [128, 16], F32); scv = sbt(e1, "scv", [128, 16], F32)
            M3 = sbt(e1, "M3", [3, 3 * DM], F32); b2 = sbt(e1, "b2", [2, 3 * DM], F32)
            sel_t = sbt(e1, "sel_t", [3, 384], F32)
            wmb = [sbt(e1, "wm%d" % i, [128, 1536], F32) for i in range(4)]
            cols = sbt(e1, "cols", [128, 48], F32); mcol = sbt(e1, "mcol", [128, 32], F32)
            wob = [sbt(e1, "wob%d" % i, [128, DM], BF16) for i in range(2)]
            xtb = [sbt(e1, "xt%d" % i, [128, DM], F32) for i in range(3)]
            hbb = [sbt(e1, "hb%d" % i, [128, DM], BF16) for i in range(2)]
            junk = sbt(e1, "junk", [128, DM], BF16)
            ssall = sbt(e1, "ssall", [128, 40], F32); rtall = sbt(e1, "rtall", [128, 40], F32)
            rsall = sbt(e1, "rsall", [128, 40], F32)

            S.dma(cvt[:], cv, [], ["cvt"]); S.dma(cm_f[:], cmat, [], ["cm_f"])
            S.dma(sel_t[:], sel, [], ["sel"])
            S.dma(gq[:], gqk, [], ["gq"])
            S.op("dve", V.tensor_copy, ["cm_f"], ["cm"], out=cm[:], in_=cm_f[:])
            S.op("act", A.activation, ["cvt"], ["scv"], out=scv[:], in_=cvt[:], func=AF.Silu)
            S.op("dve", V.memset, [], ["M3"], M3[:], 0.0)
            S.op("dve", V.memset, [], ["ssall"], ssall[:], 0.0)
            S.dma(M3[2:3, 0:DM], norm_g, [], ["M3"])
            S.dma(b2[0:1, :], b_mod, [], ["b2"]); S.dma(b2[1:2, :], b_mod, [], ["b2"])
            for kc in range(8):
                for hf in range(2):
                    wi = (2 * kc + hf) % 4
                    wm = wmb[wi]
                    S.dma(wm[:], w_mod[kc * 128:(kc + 1) * 128, hf * 1536:(hf + 1) * 1536], [], [("wm", wi)])
                    for n3 in range(3):
                        n = hf * 3 + n3
                        S.op("pe", T.matmul, ["scv", ("wm", wi)], [("ps", n)], PS[n][0:2, :],
                             scv[:, 2 * kc:2 * kc + 2], wm[:, n3 * 512:(n3 + 1) * 512], start=(kc == 0), stop=(kc == 7))
            for n in range(6):
                S.op("dve", V.tensor_tensor, [("ps", n), "b2"], ["M3"], out=M3[0:2, n * 512:(n + 1) * 512],
                     in0=PS[n][0:2, :], in1=b2[0:2, n * 512:(n + 1) * 512], op=ALU.add)
            for j in range(2):
                b = j
                S.op("pe", T.matmul, ["sel", "M3"], [("ps", b)], PS[b][:, :], sel_t[0:3, 0:128], M3[0:3, 2 * DM + j * 512: 2 * DM + (j + 1) * 512],
                     start=True, stop=True)
                S.op("act", A.activation, [("ps", b)], ["gate_bc"], out=gate_bc[:, j * 512:(j + 1) * 512], in_=PS[b][:, :], func=AF.Identity)
            for kc in range(8):
                S.dma(wst[kc % 2][:], w_out[kc * 128:(kc + 1) * 128, :], [], [("wst", kc % 2)])
                S.op("dve", V.tensor_tensor, [("wst", kc % 2), "gate_bc"], [("wob", kc % 2)], out=wob[kc % 2][:], in0=wst[kc % 2][:], in1=gate_bc[:], op=ALU.mult)
                S.dma(woD[kc * 128:(kc + 1) * 128, :], wob[kc % 2][:], [("wob", kc % 2)], [])
            i3 = bass.AP(sel_t[0:3, 0:1].tensor, sel_t[0:3, 0:1].offset, [[sel_t[0:3, 0:1].ap[0][0], 3], [128, 3]])
            for sidx in range(2):
                for kc in range(8):
                    c = (sidx * 8 + kc) * 3
                    S.op("pe", T.matmul, ["sel", "M3"], [("ps", 2)], PS[2][:, c:c + 3],
                         M3[0:3, sidx * DM + kc * 128: sidx * DM + (kc + 1) * 128], i3, start=True, stop=True)
            S.op("dve", V.tensor_copy, [("ps", 2)], ["cols"], out=cols[:, 0:48], in_=PS[2][:, 0:48])

            def colv(sidx, r):
                base = cols[:, sidx * 24 + r: sidx * 24 + r + 1]
                return bass.AP(base.tensor, base.offset, [[base.ap[0][0], 128], [3, 8]])

            S.op("dve", V.scalar_tensor_tensor, ["cols"], ["mcol"], out=mcol[:, 0:8], in0=colv(1, 0), scalar=1.0, in1=colv(0, 2), op0=ALU.add, op1=ALU.mult)
            S.op("dve", V.tensor_copy, ["cols"], ["mcol"], out=mcol[:, 8:16], in_=colv(0, 0))
            S.op("dve", V.scalar_tensor_tensor, ["cols"], ["mcol"], out=mcol[:, 16:24], in0=colv(1, 1), scalar=1.0, in1=colv(0, 2), op0=ALU.add, op1=ALU.mult)
            S.op("dve", V.tensor_copy, ["cols"], ["mcol"], out=mcol[:, 24:32], in_=colv(0, 1))

            def modulate(dst, dstw, c0, n, off, key, eng):
                for kc in range(8):
                    sl = dst[:, kc * dstw + c0: kc * dstw + c0 + n]
                    if True:
                        S.op("dve", V.tensor_scalar, [key, "mcol"], [key], out=sl, in0=sl, scalar1=mcol[:, off + kc:off + kc + 1],
                             scalar2=mcol[:, off + 8 + kc:off + 9 + kc], op0=ALU.mult, op1=ALU.add)
                    else:
                        S.op("act", A.activation, [key, "mcol"], [key], out=sl, in_=sl, func=AF.Identity, scale=mcol[:, off + kc:off + kc + 1],
                             bias=mcol[:, off + 8 + kc:off + 9 + kc])

            tiles = [(ctx[i * 128:(i + 1) * 128, :], hcT, CTX, i * 128, ("hcT", 0)) for i in range(2)]
            tiles += [(x[i * 128:(i + 1) * 128, :], hT, SEQ, i * 128, ("hT", i // 8)) for i in range(32)]
            for idx, (src, dst, dstw, col, hkey) in enumerate(tiles):
                xt = xtb[idx % 3]; xk = ("xt", idx % 3)
                S.dma(xt[:], src, [], [xk])
                S.op("act", A.activation, [xk, "ssall"], ["junk", ("ss", idx)], out=junk[:], in_=xt[:], func=AF.Square,
                     accum_out=ssall[:, idx:idx + 1])
                S.op("act", A.activation, [("ss", idx)], [("rt", idx)], out=rtall[:, idx:idx + 1],
                     in_=ssall[:, idx:idx + 1], func=AF.Sqrt, scale=1.0 / DM, bias=EPS)
                S.op("dve", V.reciprocal, [("rt", idx)], [("rs", idx)], out=rsall[:, idx:idx + 1], in_=rtall[:, idx:idx + 1])
                hb = hbb[idx % 2]
                S.op("act", A.activation, [xk, ("rs", idx)], [("hb", idx % 2)], out=hb[:], in_=xt[:], func=AF.Identity, scale=rsall[:, idx:idx + 1])
                pt = PT[idx % 2]
                for kc in range(8):
                    S.op("pe", T.transpose, [("hb", idx % 2), "cm"], [("pt", idx % 2)], out=pt[:, kc * 128:(kc + 1) * 128],
                         in_=hb[:, kc * 128:(kc + 1) * 128], identity=ident)
                dst_ap = dst[:].rearrange("p (kc t) -> p kc t", kc=8)[:, :, col:col + 128]
                S.op("dve", V.tensor_copy, [("pt", idx % 2)], [hkey], out=dst_ap, in_=pt[:].rearrange("p (kc t) -> p kc t", kc=8))
                if idx == 1:
                    modulate(hcT, CTX, 0, CTX, 16, ("hcT", 0), 0)
                elif idx >= 2 and (idx - 2) % 8 == 7:
                    g4 = (idx - 2) // 8
                    modulate(hT, SEQ, g4 * 1024, 1024, 0, ("hT", g4), g4)
            S.flush()

        def lru_phase():
            S.barrier()
            with ExitStack() as e2:
                NB = 1 if CFG["lru_pair"] else 2
                NWS = 4 if CFG["lru_pair"] else 2
                T0s = [sbt(e2, "T0_%d" % i, [128, SEQ], F32) for i in range(NB)]
                szls = [sbt(e2, "szl_%d" % i, [128, SEQ], BF16) for i in range(NB)]
                T1 = sbt(e2, "T1", [128, SEQ], F32)
                ucbf = sbt(e2, "ucbf", [128, SEQ], BF16)
                QW = 1024
                BAs = [sbt(e2, "BA%d" % i, [128, QW], F32) for i in range(NWS)]
                BBs = [sbt(e2, "BB%d" % i, [128, QW], F32) for i in range(NWS)]
                BCs = [sbt(e2, "BC%d" % i, [128, QW], F32) for i in range(NWS)]
                trb = [sbt(e2, "trb%d" % i, [128, 512], F32) for i in range(2)]
                tib = [sbt(e2, "tib%d" % i, [128, 512], F32) for i in range(2)]
                cx0 = sbt(e2, "cx0", [128, CTX], F32); cx1 = sbt(e2, "cx1", [128, CTX], F32)
                cxbf = sbt(e2, "cxbf", [128, CTX], BF16)
                prms = [sbt(e2, "prm%d" % i, [128, 16], F32) for i in range(2)]
                prhs = [sbt(e2, "prh%d" % i, [128, 16], F32) for i in range(2)]
                clts = [sbt(e2, "clt%d" % i, [128, 8], F32) for i in range(2)]
                gw_f = sbt(e2, "gw_f", [128, 512], F32)
                gws = [sbt(e2, "gw%d" % i, [128, 512], BF16) for i in range(2)]
                carry = sbt(e2, "carry", [128, 2], F32)
                wvf = sbt(e2, "wvf", [128, 8 * 512], BF16)
                wvf3 = wvf[:].rearrange("p (kc n) -> p kc n", kc=8)
                vstage = sbt(e2, "vstage", [128, 1024], BF16)
                psv = e2.enter_context(nc.psum_tensor("psv", [128, 512], F32))
                S.op("pool", G.memset, [], ["vstage"], vstage[:], 1.0)
                for pr_ in range(4):
                    S.dma(wst[pr_ % 2][:].rearrange("p (kc n) -> p kc n", kc=8), w_in_v[:, :, 1024 + 128 * pr_:1024 + 128 * (pr_ + 1)], [], [("wst", pr_ % 2)])
                    S.op("dve", V.tensor_copy, [("wst", pr_ % 2)], ["wvf"], out=wvf3[:, :, pr_ * 128:(pr_ + 1) * 128],
                         in_=wst[pr_ % 2][:].rearrange("p (kc n) -> p kc n", kc=8))
                vD_t = vD.rearrange("r p c -> p r c")
                vst3 = vstage[:].rearrange("p (r c) -> p r c", r=4)
                vb_ = vstage[:]
                vst4 = bass.AP(vb_.tensor, vb_.offset, [[vb_.ap[0][0], 128], [256, 4], [192, 2], [1, 64]])
                pvb_ = psv[:, :]
                psv4 = bass.AP(pvb_.tensor, pvb_.offset, [[pvb_.ap[0][0], 128], [128, 4], [64, 2], [1, 64]])

                def v_tiles(t0, t1):
                    for t_ in range(t0, t1):
                        for kc in range(8):
                            S.op("pe", T.matmul, ["wvf", "hT"], ["psv"], psv[:, :],
                                 hT[:, kc * SEQ + t_ * 128: kc * SEQ + (t_ + 1) * 128], wvf[:, kc * 512:(kc + 1) * 512],
                                 start=(kc == 0), stop=(kc == 7))
                        if CFG["vevac"] == "act":
                            S.op("act", A.activation, ["psv"], ["vstage"], out=vst4, in_=psv4, func=AF.Identity)
                        else:
                            S.op("dve", V.tensor_copy, ["psv"], ["vstage"], out=vst4, in_=psv4)
                        S.dma(vD_t[:, :, t_ * 256:(t_ + 1) * 256], vst3, ["vstage"], [])

                def rev(ap2d, n):
                    pstep = ap2d.ap[0][0]
                    npart = ap2d.ap[0][1]
                    return bass.AP(ap2d.tensor, ap2d.offset + n - 1, [[pstep, npart], [-1, n]])

                def proj_stage(ch):
                    p = ch % 2
                    prm = prms[p]; prh = prhs[p]; clt = clts[p]; gw = gws[p]; pk = p % NB; T0 = T0s[pk]; szl = szls[pk]
                    S.dma(prm[:], plru[ch], [], [("prm", p)])
                    S.dma(gw_f[:], wbd[ch], [], ["gw_f"])
                    S.op("dve", V.tensor_copy, ["gw_f"], [("gw", p)], out=gw[:], in_=gw_f[:])
                    S.op("dve", V.tensor_scalar, [("prm", p)], [("prh", p)], out=prh[:], in0=prm[:], scalar1=0.5, scalar2=None, op0=ALU.mult)
                    for d in range(2):
                        lc = 7 + 3 * d
                        S.op("act", A.activation, [("prm", p)], [("clt", p, 4 + d)], out=clt[:, 4 + d:5 + d], in_=prm[:, lc:lc + 1], func=AF.Exp, scale=-1.0)
                        S.op("act", A.activation, [("clt", p, 4 + d)], [("clt", p, 6 + d)], out=clt[:, 6 + d:7 + d], in_=clt[:, 4 + d:5 + d], func=AF.Ln, bias=1.0)
                        S.op("dve", V.tensor_scalar, [("clt", p, 6 + d)], [("cl", p, d)], out=clt[:, 2 * d:2 * d + 1], in0=clt[:, 6 + d:7 + d],
                             scalar1=-8.0, scalar2=None, op0=ALU.mult)
                        S.op("dve", V.tensor_scalar, [("clt", p, 6 + d)], [("cl", p, d)], out=clt[:, 2 * d + 1:2 * d + 2], in0=clt[:, 6 + d:7 + d],
                             scalar1=-4.0, scalar2=None, op0=ALU.mult)
                    wu, wuk = load_wblock(2048 + 128 * ch)
                    wz, wzk = load_wblock(2560 + 128 * ch)
                    proj_fm(wu, wuk, hcT, CTX, 0, CTX, 5)
                    S.op("act", A.activation, [("ps", 5)], ["cx0"], out=cx0[:], in_=PS[5][:, 0:CTX], func=AF.Identity)
                    for g in range(8):
                        b = 4 + g % 2
                        proj_fm(wu, wuk, hT, SEQ, g * 512, 512, b)
                        S.op("act", A.activation, [("ps", b)], [("T0", pk, g)], out=T0[:, g * 512:(g + 1) * 512], in_=PS[b][:, :], func=AF.Identity)
                    for g in range(8):
                        b = 4 + g % 2
                        proj_fm(wz, wzk, hT, SEQ, g * 512, 512, b)
                        S.op("act", A.activation, [("ps", b)], [("szl", pk, g)], out=szl[:, g * 512:(g + 1) * 512], in_=PS[b][:, :], func=AF.Silu)

                def main_stage(ch, mid_hook):
                    p = ch % 2
                    prm = prms[p]; prh = prhs[p]; clt = clts[p]; gw = gws[p]; pk = p % NB; T0 = T0s[pk]; szl = szls[pk]
                    kprm = ("prm", p)

                    def conv(src, skeys, dst, dkey, lo, hi, n):
                        rk = list(skeys) + [kprm]
                        if CFG["conv_act"] and n > CTX:
                            S.op("act", A.activation, rk, [dkey], out=dst[:, lo:hi], in_=src[:, lo:hi], func=AF.Identity, scale=prm[:, 1:2], bias=prm[:, 4:5])
                        else:
                            S.op("dve", V.tensor_scalar, rk, [dkey], out=dst[:, lo:hi], in0=src[:, lo:hi], scalar1=prm[:, 1:2],
                                 scalar2=prm[:, 4:5], op0=ALU.mult, op1=ALU.add)
                        l0 = max(lo, 1)
                        S.op("dve", V.scalar_tensor_tensor, rk + [dkey], [dkey], out=dst[:, l0:hi], in0=src[:, l0 - 1:hi - 1],
                             scalar=prm[:, 0:1], in1=dst[:, l0:hi], op0=ALU.mult, op1=ALU.add)
                        h2 = min(hi, n - 1)
                        S.op("dve", V.scalar_tensor_tensor, rk + [dkey], [dkey], out=dst[:, lo:h2], in0=src[:, lo + 1:h2 + 1],
                             scalar=prm[:, 2:3], in1=dst[:, lo:h2], op0=ALU.mult, op1=ALU.add)
                        h3 = min(hi, n - 2)
                        S.op("dve", V.scalar_tensor_tensor, rk + [dkey], [dkey], out=dst[:, lo:h3], in0=src[:, lo + 2:h3 + 2],
                             scalar=prm[:, 3:4], in1=dst[:, lo:h3], op0=ALU.mult, op1=ALU.add)

                    conv(cx0, ["cx0"], cx1, "cx1", 0, CTX, CTX)
                    S.op("act", A.activation, ["cx1"], ["cxbf"], out=cxbf[:], in_=cx1[:], func=AF.Identity)
                    for g in range(8):
                        sk = [("T0", pk, j) for j in (g - 1, g, g + 1) if 0 <= j < 8]
                        conv(T0, sk, T1, ("T1", g), g * 512, (g + 1) * 512, SEQ)
                        S.op("act", A.activation, [("T1", g)], [("ucbf", g)], out=ucbf[:, g * 512:(g + 1) * 512], in_=T1[:, g * 512:(g + 1) * 512], func=AF.Identity)

                    def finish(bbuf, bkey, cbuf, ckey, n):
                        S.op("act", A.activation, [bkey], [bkey], out=bbuf[:, 0:n], in_=bbuf[:, 0:n], func=AF.Sqrt, scale=-0.25, bias=0.25)
                        S.op("dve", V.tensor_tensor, [bkey, ckey], [ckey], out=cbuf[:, 0:n], in0=cbuf[:, 0:n], in1=bbuf[:, 0:n], op=ALU.mult)

                    def gates(d, src_bf, sbkeys, src_f, sfkeys, n, abuf, akey, bbuf, bkey, cbuf, ckey, fin=True):
                        wa = gw[:, (2 * d) * 128:(2 * d + 1) * 128]; wx = gw[:, (2 * d + 1) * 128:(2 * d + 2) * 128]
                        ba_c = 5 + 3 * d; bx_c = 6 + 3 * d
                        nseg = (n + 511) // 512
                        for s_ in range(nseg):
                            w_ = min(512, n - s_ * 512); sl = slice(s_ * 512, s_ * 512 + w_)
                            i = gcnt[0] % 2; gcnt[0] += 1
                            b0 = 2 * i; b1 = 2 * i + 1
                            tr = trb[i]; ti = tib[i]
                            sbk = sbkeys[s_]; sfk = sfkeys[s_]
                            S.op("pe", T.matmul, [("gw", p), sbk], [("ps", b0)], PS[b0][:, 0:w_], wa, src_bf[:, sl], start=True, stop=True)
                            S.op("pe", T.matmul, [("gw", p), sbk], [("ps", b1)], PS[b1][:, 0:w_], wx, src_bf[:, sl], start=True, stop=True)
                            S.op("act", A.activation, [("ps", b0), ("prh", p)], [("tr", i)], out=tr[:, 0:w_], in_=PS[b0][:, 0:w_], func=AF.Tanh,
                                 scale=0.5, bias=prh[:, ba_c:ba_c + 1])
                            S.op("act", A.activation, [("ps", b1), ("prh", p)], [("ti", i)], out=ti[:, 0:w_], in_=PS[b1][:, 0:w_], func=AF.Tanh,
                                 scale=0.5, bias=prh[:, bx_c:bx_c + 1])
                            S.op("act", A.activation, [("tr", i), ("cl", p, d)], [akey], out=abuf[:, sl], in_=tr[:, 0:w_], func=AF.Exp,
                                 scale=clt[:, 2 * d + 1:2 * d + 2], bias=clt[:, 2 * d + 1:2 * d + 2])
                            if s_ % 2 < CFG["a2_act"]:
                                S.op("act", A.activation, [("tr", i), ("cl", p, d)], [bkey], out=bbuf[:, sl], in_=tr[:, 0:w_], func=AF.Exp,
                                     scale=clt[:, 2 * d:2 * d + 1], bias=clt[:, 2 * d:2 * d + 1])
                            else:
                                S.op("dve", V.tensor_tensor, [akey], [bkey], out=bbuf[:, sl], in0=abuf[:, sl], in1=abuf[:, sl], op=ALU.mult)
                            S.op("dve", V.scalar_tensor_tensor, [("ti", i), sfk], [ckey], out=cbuf[:, sl], in0=ti[:, 0:w_], scalar=1.0,
                                 in1=src_f[:, sl], op0=ALU.add, op1=ALU.mult)
                        if fin:
                            finish(bbuf, bkey, cbuf, ckey, n)

                    def wset(idx):
                        return BAs[idx], BBs[idx], BCs[idx], ("BA", idx), ("BB", idx), ("BC", idx)

                    if CFG["lru_pair"]:
                        v_tiles(ch * 8, ch * 8 + 8)
                        cs = []
                        for d in range(2):
                            ws = wset((step[0] % 2) * 2 + d)
                            gates(d, cxbf, ["cxbf"], cx1, ["cx1"], CTX, ws[0], ws[3], ws[1], ws[4], ws[2], ws[5], fin=False)
                            cs.append(ws)
                        step[0] += 1
                        for d in range(2):
                            finish(cs[d][1], cs[d][4], cs[d][2], cs[d][5], CTX)
                        for d in range(2):
                            BA, BB, BC, ka, kb, kc_ = cs[d]
                            if d == 0:
                                S.op("dve", V.tensor_tensor_scan, [ka, kc_], [kb], out=BB[:, 0:CTX], data0=BA[:, 0:CTX], data1=BC[:, 0:CTX], initial=0.0,
                                     op0=ALU.mult, op1=ALU.add)
                                S.op("dve", V.tensor_copy, [kb], [("carry", d)], out=carry[:, 0:1], in_=BB[:, CTX - 1:CTX])
                            else:
                                S.op("dve", V.tensor_tensor_scan, [ka, kc_], [kb], out=rev(BB[:, 0:CTX], CTX), data0=rev(BA[:, 0:CTX], CTX),
                                     data1=rev(BC[:, 0:CTX], CTX), initial=0.0, op0=ALU.mult, op1=ALU.add)
                                S.op("dve", V.tensor_copy, [kb], [("carry", d)], out=carry[:, 1:2], in_=BB[:, 0:1])
                        for s4 in range(4):
                            qq = (s4, 3 - s4)
                            cs = []
                            for d in range(2):
                                q = qq[d]; c0 = q * QW
                                ws = wset((step[0] % 2) * 2 + d)
                                blks = [2 * q, 2 * q + 1]
                                gates(d, ucbf[:, c0:c0 + QW], [("ucbf", j) for j in blks], T1[:, c0:c0 + QW], [("T1", j) for j in blks], QW,
                                      ws[0], ws[3], ws[1], ws[4], ws[2], ws[5], fin=False)
                                cs.append(ws)
                            step[0] += 1
                            for d in range(2):
                                finish(cs[d][1], cs[d][4], cs[d][2], cs[d][5], QW)
                            for d in range(2):
                                q = qq[d]; c0 = q * QW
                                BA, BB, BC, ka, kb, kc_ = cs[d]
                                blks = [2 * q, 2 * q + 1]
                                t0keys = [("T0", pk, j) for j in blks]
                                first = s4 < 2
                                if d == 0:
                                    if first:
                                        S.op("dve", V.tensor_tensor_scan, [ka, kc_, ("carry", 0)], t0keys, out=T0[:, c0:c0 + QW], data0=BA[:], data1=BC[:],
                                             initial=carry[:, 0:1], op0=ALU.mult, op1=ALU.add)
                                        S.op("dve", V.tensor_copy, t0keys, [("carry", 0)], out=carry[:, 0:1], in_=T0[:, c0 + QW - 1:c0 + QW])
                                    else:
                                        S.op("dve", V.tensor_tensor_scan, [ka, kc_, ("carry", 0)], [kb], out=BB[:], data0=BA[:], data1=BC[:],
                                             initial=carry[:, 0:1], op0=ALU.mult, op1=ALU.add)
                                        S.op("dve", V.tensor_copy, [kb], [("carry", 0)], out=carry[:, 0:1], in_=BB[:, QW - 1:QW])
                                else:
                                    if first:
                                        S.op("dve", V.tensor_tensor_scan, [ka, kc_, ("carry", 1)], t0keys, out=rev(T0[:, c0:c0 + QW], QW), data0=rev(BA[:], QW),
                                             data1=rev(BC[:], QW), initial=carry[:, 1:2], op0=ALU.mult, op1=ALU.add)
                                        S.op("dve", V.tensor_copy, t0keys, [("carry", 1)], out=carry[:, 1:2], in_=T0[:, c0:c0 + 1])
                                    else:
                                        S.op("dve", V.tensor_tensor_scan, [ka, kc_, ("carry", 1)], [kb], out=rev(BB[:], QW), data0=rev(BA[:], QW),
                                             data1=rev(BC[:], QW), initial=carry[:, 1:2], op0=ALU.mult, op1=ALU.add)
                                        S.op("dve", V.tensor_copy, [kb], [("carry", 1)], out=carry[:, 1:2], in_=BB[:, 0:1])
                                if not first:
                                    S.op("dve", V.tensor_tensor, [kb] + t0keys, [ka], out=BA[:], in0=BB[:], in1=T0[:, c0:c0 + QW], op=ALU.add)
                                    szk = [("szl", pk, j) for j in blks]
                                    S.op("dve", V.tensor_tensor, [ka] + szk, szk, out=szl[:, c0:c0 + QW], in0=BA[:], in1=szl[:, c0:c0 + QW],
                                         op=ALU.mult)
                        for hq in range(2):
                            S.dma(mixD[(4 + ch) * 128:(5 + ch) * 128, hq * 2048:(hq + 1) * 2048], szl[:, hq * 2048:(hq + 1) * 2048],
                                  [("szl", pk, j) for j in range(4 * hq, 4 * hq + 4)], [])
                        if mid_hook is not None:
                            mid_hook()
                        return

                    for d in range(2):
                        v_tiles(ch * 8 + d * 4, ch * 8 + d * 4 + 4)
                        si = step[0] % 2; step[0] += 1
                        BA = BAs[si]; BB = BBs[si]; BC = BCs[si]
                        ka = ("BA", si); kb = ("BB", si); kc_ = ("BC", si)
                        gates(d, cxbf, ["cxbf"], cx1, ["cx1"], CTX, BA, ka, BB, kb, BC, kc_)
                        if d == 0:
                            S.op("dve", V.tensor_tensor_scan, [ka, kc_], [kb], out=BB[:, 0:CTX], data0=BA[:, 0:CTX], data1=BC[:, 0:CTX], initial=0.0,
                                 op0=ALU.mult, op1=ALU.add)
                            S.op("dve", V.tensor_copy, [kb], [("carry", d)], out=carry[:, 0:1], in_=BB[:, CTX - 1:CTX])
                        else:
                            S.op("dve", V.tensor_tensor_scan, [ka, kc_], [kb], out=rev(BB[:, 0:CTX], CTX), data0=rev(BA[:, 0:CTX], CTX),
                                 data1=rev(BC[:, 0:CTX], CTX), initial=0.0, op0=ALU.mult, op1=ALU.add)
                            S.op("dve", V.tensor_copy, [kb], [("carry", d)], out=carry[:, 1:2], in_=BB[:, 0:1])
                        quarters = [0, 1, 2, 3] if d == 0 else [3, 2, 1, 0]
                        for q in quarters:
                            c0 = q * QW
                            si = step[0] % 2; step[0] += 1
                            BA = BAs[si]; BB = BBs[si]; BC = BCs[si]
                            ka = ("BA", si); kb = ("BB", si); kc_ = ("BC", si)
                            blks = [2 * q, 2 * q + 1]
                            gates(d, ucbf[:, c0:c0 + QW], [("ucbf", j) for j in blks], T1[:, c0:c0 + QW], [("T1", j) for j in blks], QW,
                                  BA, ka, BB, kb, BC, kc_)
                            t0keys = [("T0", pk, j) for j in blks]
                            if d == 0:
                                S.op("dve", V.tensor_tensor_scan, [ka, kc_, ("carry", 0)], t0keys, out=T0[:, c0:c0 + QW], data0=BA[:], data1=BC[:],
                                     initial=carry[:, 0:1], op0=ALU.mult, op1=ALU.add)
                                S.op("dve", V.tensor_copy, t0keys, [("carry", 0)], out=carry[:, 0:1], in_=T0[:, c0 + QW - 1:c0 + QW])
                            else:
                                S.op("dve", V.tensor_tensor_scan, [ka, kc_, ("carry", 1)], [kb], out=rev(BB[:], QW), data0=rev(BA[:], QW),
                                     data1=rev(BC[:], QW), initial=carry[:, 1:2], op0=ALU.mult, op1=ALU.add)
                                S.op("dve", V.tensor_copy, [kb], [("carry", 1)], out=carry[:, 1:2], in_=BB[:, 0:1])
                                S.op("dve", V.tensor_tensor, [kb] + t0keys, [ka], out=BA[:], in0=BB[:], in1=T0[:, c0:c0 + QW], op=ALU.add)
                                szk = [("szl", pk, j) for j in blks]
                                S.op("dve", V.tensor_tensor, [ka] + szk, szk, out=szl[:, c0:c0 + QW], in0=BA[:], in1=szl[:, c0:c0 + QW],
                                     op=ALU.mult)
                                if q % 2 == 0:
                                    hq = q // 2
                                    S.dma(mixD[(4 + ch) * 128:(5 + ch) * 128, hq * 2048:(hq + 1) * 2048], szl[:, hq * 2048:(hq + 1) * 2048],
                                          [("szl", pk, j) for j in range(4 * hq, 4 * hq + 4)], [])
                        if d == 0 and mid_hook is not None:
                            mid_hook()

                step = [0]; gcnt = [0]
                proj_stage(0)
                for ch in range(4):
                    main_stage(ch, (lambda c=ch: proj_stage(c + 1)) if ch < 3 else None)
                S.flush()

        def att_phase():
            S.barrier()
            with ExitStack() as e2:
                qrot = sbt(e2, "qrot", [128, SEQ], BF16); qpl = sbt(e2, "qpl", [128, SEQ], BF16)
                krot = sbt(e2, "krot", [128, SEQ], BF16); sza = sbt(e2, "sza", [128, SEQ], BF16)
                vaug = sbt(e2, "vaug", [128, 32 * 256], BF16); mixc = sbt(e2, "mixc", [128, SEQ], BF16)
                kcn2 = [sbt(e2, "kcn%d" % i, [128, CTX], BF16) for i in range(2)]
                vcaug2 = [sbt(e2, "vcaug%d" % i, [128, 2 * 256], BF16) for i in range(2)]
                Ffull2 = [sbt(e2, "Ffull%d" % i, [128, 2 * FW], BF16) for i in range(2)]
                Fint2 = [sbt(e2, "Fint%d" % i, [128, 2 * FW], BF16) for i in range(2)]
                cst = [sbt(e2, "cst%d" % i, [128, 512], F32) for i in range(2)]
                snt = [sbt(e2, "snt%d" % i, [128, 512], F32) for i in range(2)]
                sqb = [sbt(e2, "sqb%d" % i, [128, 512], BF16) for i in range(CFG["n_qk"])]
                qsb = [sbt(e2, "qsb%d" % i, [128, 512], F32) for i in range(CFG["n_qk"])]
                rtb = [sbt(e2, "rtb%d" % i, [128, 512], F32) for i in range(CFG["n_qk"])]
                knbf = [sbt(e2, "knbf%d" % i, [128, 512], BF16) for i in range(CFG["n_qk"])]
                r1b = [sbt(e2, "r1b%d" % i, [128, 512], F32) for i in range(CFG["n_qk"])]
                r2b = [sbt(e2, "r2b%d" % i, [128, 512], F32) for i in range(CFG["n_qk"])]
                etb = [sbt(e2, "etb%d" % i, [128, 512], BF16) for i in range(CFG["n_et"])]
                ptb = [sbt(e2, "ptb%d" % i, [128, 512], BF16) for i in range(CFG["n_pt"])]
                PSX = PS + [e2.enter_context(nc.psum_tensor("psx_%d" % i, [128, 512], F32)) for i in range(2)]
                rdb = [sbt(e2, "rdb%d" % i, [128, 256], F32) for i in range(2)]

                mk = sbt(e2, "mk", [128, 2 * FW], BF16)
                for j_ in range(4):
                    S.dma(r1b[j_ % 2][:, 0:448], masks[:, j_ * 448:(j_ + 1) * 448], [], [("r1b", j_ % 2)])
                    S.op("dve", V.tensor_copy, [("r1b", j_ % 2)], ["mk"], out=mk[:, j_ * 448:(j_ + 1) * 448], in_=r1b[j_ % 2][:, 0:448])
                for i_ in range(2):
                    S.op("pool", G.memset, [], [("vcaug", i_)], vcaug2[i_][:], 1.0)
                for pr in range(4):
                    att_body(pr, locals())
                for kc in [4, 5, 6, 7, 0, 1, 2]:
                    S.dma(hT[:, kc * SEQ:(kc + 1) * SEQ], mixD[kc * 128:(kc + 1) * 128, :], [("mixD", kc)] if kc < 4 else [], ["hT"])
                S.flush()

        def att_body(pr, L):
            if True:
                if True:
                    pass
                par = pr % 2
                qrot = L["qrot"]; qpl = L["qpl"]; krot = L["krot"]; sza = L["sza"]; vaug = L["vaug"]; mixc = L["mixc"]; mk = L["mk"]
                kcn = L["kcn2"][par]; vcaug = L["vcaug2"][par]; Ffull = L["Ffull2"][par]; Fint = L["Fint2"][par]
                cst = L["cst"]; snt = L["snt"]; sqb = L["sqb"]; rtb = L["rtb"]; knbf = L["knbf"]; qsb = L["qsb"]
                r1b = L["r1b"]; r2b = L["r2b"]; etb = L["etb"]; ptb = L["ptb"]; PSX = L["PSX"]; rdb = L["rdb"]; ttb = L["rdb"]
                kFf = ("Ffull", par); kFi = ("Fint", par); kkc = ("kcn", par); kvc = ("vcaug", par)
                for hh in range(2):
                    for hf in range(2):
                        c0 = hf * 448
                        S.dma(r1b[hf][:, 0:448], rpbG[2 * pr + hh, :, c0:c0 + 448], [], [("r1b", hf)])
                        S.op("act", A.activation, [("r1b", hf)], [("r2b", hf)], out=r2b[hf][:, 0:448], in_=r1b[hf][:, 0:448], func=AF.Exp)
                        S.op("dve", V.tensor_tensor, [("r2b", hf), "mk"], [kFf], out=Ffull[:, hh * FW + c0:hh * FW + c0 + 448], in0=r2b[hf][:, 0:448],
                             in1=mk[:, c0:c0 + 448], op=ALU.mult)
                        S.op("dve", V.tensor_tensor, [("r2b", hf), "mk"], [kFi], out=Fint[:, hh * FW + c0:hh * FW + c0 + 448], in0=r2b[hf][:, 0:448],
                             in1=mk[:, FW + c0:FW + c0 + 448], op=ALU.mult)
                wq, wqk = load_wblock(128 * pr)
                wk, wkk = load_wblock(512 + 128 * pr)
                wv, wvk = load_wblock(1024 + 128 * pr)
                wz, wzk = load_wblock(1536 + 128 * pr)
                for g in range(8):
                    b = g % 2
                    proj_fm(wz, wzk, hT, SEQ, g * 512, 512, b)
                    S.op("act", A.activation, [("ps", b)], [("sza", g)], out=sza[:, g * 512:(g + 1) * 512], in_=PS[b][:, :], func=AF.Silu)

                cnt = [0]

                def qk_path(bank, n, gcol, plain_dst, pkey, rot_dst, rkey, ts):
                    i = cnt[0] % CFG["n_qk"]; cnt[0] += 1
                    S.op("act", A.activation, [("ps", bank)], [("qsb", i)], out=qsb[i][:, 0:n], in_=PS[bank][:, 0:n], func=AF.Identity)
                    S.op("act", A.activation, [("ps", bank)], [("sqb", i)], out=sqb[i][:, 0:n], in_=PS[bank][:, 0:n], func=AF.Square)
                    S.op("pe", T.matmul, ["cm", ("sqb", i)], [("ps", 2)], PS[2][:, 0:n], bones, sqb[i][:, 0:n], start=True, stop=True)
                    S.op("act", A.activation, [("ps", 2)], [("rtb", i)], out=rtb[i][:, 0:n], in_=PS[2][:, 0:n], func=AF.Ln, bias=EPS, scale=1.0)
                    S.op("act", A.activation, [("rtb", i)], [("rtb", i)], out=rtb[i][:, 0:n], in_=rtb[i][:, 0:n], func=AF.Exp, scale=-0.5)
                    if plain_dst is None:
                        plain_dst = knbf[i][:, 0:n]; pkey = ("knbf", i)
                    S.op("dve", V.scalar_tensor_tensor, [("qsb", i), "gq", ("rtb", i)], [pkey], out=plain_dst, in0=qsb[i][:, 0:n],
                         scalar=gq[:, gcol:gcol + 1], in1=rtb[i][:, 0:n], op0=ALU.mult, op1=ALU.mult)
                    if rot_dst is None:
                        return
                    S.op("pe", T.matmul, ["cm", pkey], [("ps", 2)], PS[2][:, 0:n], Rm, plain_dst, start=True, stop=True)
                    S.op("dve", V.tensor_tensor, [pkey, ("cst", ts)], [("r1b", i)], out=r1b[i][:, 0:n], in0=plain_dst, in1=cst[ts][:, 0:n], op=ALU.mult)
                    S.op("dve", V.tensor_tensor, [("ps", 2), ("snt", ts)], [("r2b", i)], out=r2b[i][:, 0:n], in0=PS[2][:, 0:n], in1=snt[ts][:, 0:n], op=ALU.mult)
                    if CFG["rope_add"] == "pool":
                        S.op("pool", G.tensor_tensor, [("r1b", i), ("r2b", i)], [rkey], out=rot_dst, in0=r1b[i][:, 0:n], in1=r2b[i][:, 0:n], op=ALU.add)
                    else:
                        S.op("dve", V.tensor_tensor, [("r1b", i), ("r2b", i)], [rkey], out=rot_dst, in0=r1b[i][:, 0:n], in1=r2b[i][:, 0:n], op=ALU.add)

                proj_fm(wk, wkk, hcT, CTX, 0, CTX, 0)
                qk_path(0, CTX, 1, kcn[:], kkc, None, None, 0)
                for t_ in range(2):
                    for kc in range(8):
                        S.op("pe", T.matmul, [wvk, "hT"], [("ps", 3)], PS[3][:, t_ * 128:(t_ + 1) * 128],
                             hcT[:, kc * CTX + t_ * 128: kc * CTX + (t_ + 1) * 128], wv[:, kc * 128:(kc + 1) * 128], start=(kc == 0), stop=(kc == 7))
                vc3 = vcaug[:].rearrange("p (t c) -> p t c", c=256)
                pv3 = PS[3][:, 0:256].rearrange("p (t c) -> p t c", c=128)
                S.op("act", A.activation, [("ps", 3)], [kvc], out=vc3[:, :, 0:64], in_=pv3[:, :, 0:64], func=AF.Identity)
                S.op("act", A.activation, [("ps", 3)], [kvc], out=vc3[:, :, 192:256], in_=pv3[:, :, 64:128], func=AF.Identity)

                def proj_group(g):
                    ts = g % 2
                    S.dma(cst[ts][:], cosT[:, g * 512:(g + 1) * 512], [], [("cst", ts)])
                    S.dma(snt[ts][:], sinT[:, g * 512:(g + 1) * 512], [], [("snt", ts)])
                    proj_fm(wq, wqk, hT, SEQ, g * 512, 512, 0)
                    qk_path(0, 512, 0, qpl[:, g * 512:(g + 1) * 512], ("qpl", g), qrot[:, g * 512:(g + 1) * 512], ("qrot", g), ts)
                    proj_fm(wk, wkk, hT, SEQ, g * 512, 512, 1)
                    qk_path(1, 512, 1, None, None, krot[:, g * 512:(g + 1) * 512], ("krot", g), ts)
                    S.dma(vaug[:, g * 1024:(g + 1) * 1024], vD[pr, :, g * 1024:(g + 1) * 1024], [], [("vaug", g)])

                sc = [0]

                def f3(Ft, hh, jj0):
                    base = Ft[:, hh * FW + jj0 * 64: hh * FW + jj0 * 64 + 256]
                    return bass.AP(base.tensor, base.offset, [[base.ap[0][0], 128], [-128, 2], [1, 256]])

                def att_group(qg):
                    R0 = 4 * qg
                    gq_ = qg // 2
                    qs = slice(R0 * 64, R0 * 64 + 256)
                    if qg == 0:
                        tl = [(a, 6 - a, Ffull, 0, 256) for a in (0, 2, 4, 6)]
                        pairs = [(0, 1), (2, 3)]
                    elif qg == 15:
                        tl = [(a, 6 - (a - 60), Ffull, 0, 256) for a in (56, 58, 60, 62)]
                        pairs = [(0, 1), (2, 3)]
                    else:
                        qr = [(0, 128), (0, 256), (0, 256), (0, 256), (64, 256), (192, 256)]
                        tl = [(R0 - 4 + 2 * i, 10 - 2 * i, Fint, qr[i][0], qr[i][1]) for i in range(6)]
                        pairs = [(1, 2), (3, 4), (0, 5)]
                    i2 = qg % 2
                    for hh in range(2):
                        pb = 64 * hh; ob = 6 + hh; po = PSX[ob][:, 0:256]
                        for pi, (iA, iB) in enumerate(pairs):
                            tA = tl[iA]; tB = tl[iB]
                            wA = tA[4] - tA[3]; wB = tB[4] - tB[3]
                            offs = [256 - wA, 256]
                            lo_ = 256 - wA; hi_ = 256 + wB
                            sb_ = CFG["score_banks"][sc[0] % len(CFG["score_banks"])]; ke = sc[0] % CFG["n_et"]; kp = sc[0] % CFG["n_pt"]; sc[0] += 1
                            for u, tt_ in enumerate((tA, tB)):
                                au, jj0, Ft, qa, qb = tt_
                                S.op("pe", T.matmul, [("krot", au // 8), ("qrot", gq_)], [("ps", sb_)], PSX[sb_][:, offs[u]:offs[u] + (qb - qa)],
                                     krot[pb:pb + 64, au * 64:au * 64 + 128], qrot[pb:pb + 64, R0 * 64 + qa:R0 * 64 + qb], start=True, stop=True)
                            S.op("act", A.activation, [("ps", sb_)], [("et", ke)], out=etb[ke][:, lo_:hi_], in_=PSX[sb_][:, lo_:hi_], func=AF.Exp, scale=0.125)
                            if wA == 256 and wB == 256 and tB[1] == tA[1] - 2:
                                S.op("dve", V.tensor_tensor, [("et", ke), kFf, kFi], [("ptl", kp)],
                                     out=ptb[kp][:].rearrange("p (t c) -> p t c", c=256), in0=etb[ke][:].rearrange("p (t c) -> p t c", c=256),
                                     in1=f3(tA[2], hh, tA[1]), op=ALU.mult)
                            else:
                                for u, tt_ in enumerate((tA, tB)):
                                    au, jj0, Ft, qa, qb = tt_
                                    w_ = qb - qa
                                    S.op("dve", V.tensor_tensor, [("et", ke), kFf, kFi], [("ptl", kp)], out=ptb[kp][:, offs[u]:offs[u] + w_],
                                         in0=etb[ke][:, offs[u]:offs[u] + w_],
                                         in1=Ft[:, hh * FW + jj0 * 64 + qa: hh * FW + jj0 * 64 + qb], op=ALU.mult)
                            for u, tt_ in enumerate((tA, tB)):
                                au, jj0, Ft, qa, qb = tt_
                                tix = au // 2
                                S.op("pe", T.matmul, [("vaug", au // 8), ("ptl", kp)], [("ps", ob)], PSX[ob][:, qa:qb],
                                     vaug[:, tix * 256 + hh * 128: tix * 256 + hh * 128 + 128], ptb[kp][:, offs[u]:offs[u] + (qb - qa)],
                                     start=(pi == 0 and u == 0), stop=False)
                        sb_ = CFG["score_banks"][sc[0] % len(CFG["score_banks"])]; kp = sc[0] % CFG["n_pt"]; sc[0] += 1
                        for j in range(2):
                            S.op("pe", T.matmul, [kkc, ("qpl", gq_)], [("ps", sb_)], PSX[sb_][:, j * 256:(j + 1) * 256], kcn[pb:pb + 64, j * 128:(j + 1) * 128],
                                 qpl[pb:pb + 64, qs], start=True, stop=True)
                        S.op("act", A.activation, [("ps", sb_)], [("ptl", kp)], out=ptb[kp][:], in_=PSX[sb_][:, :], func=AF.Exp, scale=0.125)
                        for j in range(2):
                            S.op("pe", T.matmul, [kvc, ("ptl", kp)], [("ps", ob)], PSX[ob][:, 0:256],
                                 vcaug[:, j * 256 + hh * 128: j * 256 + hh * 128 + 128], ptb[kp][:, j * 256:(j + 1) * 256], start=False, stop=(j == 1))
                        S.op("act", A.activation, [("ps", ob)], [("rd", i2, hh)], out=rdb[i2][pb:pb + 64, :], in_=po[64 - pb:128 - pb, 0:256], func=AF.Ln)
                        S.op("act", A.activation, [("rd", i2, hh)], [("rd", i2, hh)], out=rdb[i2][pb:pb + 64, :], in_=rdb[i2][pb:pb + 64, :],
                             func=AF.Exp, scale=-1.0)
                        S.op("dve", V.tensor_tensor, [("rd", i2, hh), ("sza", gq_)], [("rd", i2, hh)], out=rdb[i2][pb:pb + 64, :], in0=rdb[i2][pb:pb + 64, :],
                             in1=sza[pb:pb + 64, qs], op=ALU.mult)
                        S.op("dve", V.tensor_tensor, [("ps", ob), ("rd", i2, hh)], [("mixc", gq_)], out=mixc[pb:pb + 64, qs], in0=po[pb:pb + 64, 0:256],
                             in1=rdb[i2][pb:pb + 64, :], op=ALU.mult)

                def mix_out(g):
                    S.dma(mixD[pr * 128:(pr + 1) * 128, g * 512:(g + 1) * 512], mixc[:, g * 512:(g + 1) * 512], [("mixc", g)], [("mixD", pr)])

                proj_group(0)
                for g in range(1, 8):
                    proj_group(g)
                    att_group(2 * g - 2)
                    att_group(2 * g - 1)
                    mix_out(g - 1)
                att_group(14)
                att_group(15)
                mix_out(7)

        lru_phase()
        att_phase()

        S.barrier()
        with ExitStack() as e3:
            wo = sbt(e3, "wo", [128, 8 * DM], BF16)
            xtb = [sbt(e3, "xo%d" % i, [128, DM], F32) for i in range(6)]
            otb = [sbt(e3, "ot%d" % i, [128, DM], F32) for i in range(3)]
            hT3 = hT[:].rearrange("p (kc t) -> p kc t", kc=8)
            mixD3 = mixD.rearrange("(kc p) t -> p kc t", p=128)
            wo3 = wo[:].rearrange("p (kc n) -> p kc n", kc=8)
            woD3 = woD.rearrange("(kc p) n -> p kc n", p=128)
            for h4 in range(2):
                S.dma(wo3[:, h4 * 4:(h4 + 1) * 4, :], woD3[:, h4 * 4:(h4 + 1) * 4, :], [], [("wo", h4)])
            for h2 in range(2):
                S.dma(hT[:, 3 * SEQ + h2 * 2048:3 * SEQ + (h2 + 1) * 2048], mixD[384:512, h2 * 2048:(h2 + 1) * 2048], [], [("mx", h2)])
            for i in range(32):
                xt = xtb[i % 6]; xk = ("xo", i % 6)
                S.dma(xt[:], x[i * 128:(i + 1) * 128, :], [], [xk])
                ot = otb[i % 3]; ok = ("ot", i % 3)
                for hf in range(2):
                    b = (2 * i + hf) % 4
                    for kc in range(8):
                        S.op("pe", T.matmul, [("mx", i // 16), ("wo", kc // 4)], [("ps", b)], PS[b][:, :], hT[:, kc * SEQ + i * 128: kc * SEQ + (i + 1) * 128],
                             wo[:, kc * DM + hf * 512: kc * DM + (hf + 1) * 512], start=(kc == 0), stop=(kc == 7))
                    S.op("dve", V.tensor_tensor, [("ps", b), xk], [ok], out=ot[:, hf * 512:(hf + 1) * 512], in0=PS[b][:, :],
                         in1=xt[:, hf * 512:(hf + 1) * 512], op=ALU.add)
                S.dma(out[i * 128:(i + 1) * 128, :], ot[:], [ok], [])
            S.final_wait()
    return nc


def kernel(x, c, ctx, c_ctx, norm_g, w_mod, b_mod, w_in, w_out, q_norm_g, k_norm_g, rpb,
           conv_w, conv_b, lru_wa, lru_ba, lru_wx, lru_bx, lru_lam):
    f = lambda a: np.ascontiguousarray(np.asarray(a, dtype=np.float32))
    x, c, ctx, c_ctx = f(x), f(c), f(ctx), f(c_ctx)
    norm_g, w_mod, b_mod, w_in, w_out = f(norm_g), f(w_mod), f(b_mod), f(w_in), f(w_out)
    q_norm_g, k_norm_g, rpb = f(q_norm_g), f(k_norm_g), f(rpb)
    conv_w, conv_b, lru_wa, lru_ba, lru_wx, lru_bx, lru_lam = (f(conv_w), f(conv_b), f(lru_wa), f(lru_ba),
                                                              f(lru_wx), f(lru_bx), f(lru_lam))
    cosT, sinT, cm, sel, masks, dridx, dcidx = _host_consts()
    rpbG = np.ascontiguousarray(rpb[0][:, dridx, dcidx].reshape(8, 128, FW))
    gqk = np.stack([np.tile(q_norm_g[0], 2), np.tile(k_norm_g[0], 2)], axis=1).astype(np.float32)
    plru = np.zeros((4, 128, 16), np.float32)
    wbd = np.zeros((4, 128, 512), np.float32)
    for ch in range(4):
        sl = slice(ch * 128, (ch + 1) * 128)
        for j in range(4):
            plru[ch, :, j] = conv_w[0, j, sl]
        plru[ch, :, 4] = conv_b[0, sl]
        for d in range(2):
            plru[ch, :, 5 + 3 * d] = lru_ba[0, d, sl]
            plru[ch, :, 6 + 3 * d] = lru_bx[0, d, sl]
            plru[ch, :, 7 + 3 * d] = lru_lam[0, d, sl]
            for m, wsrc in enumerate((lru_wa, lru_wx)):
                col = (2 * d + m) * 128
                for blk in range(2):
                    wbd[ch, blk * 64:(blk + 1) * 64, col + blk * 64: col + (blk + 1) * 64] = wsrc[0, d, 2 * ch + blk]
    common = {
        "norm_g": norm_g[0:1], "w_mod": w_mod[0], "b_mod": b_mod[0:1], "w_in": w_in[0], "w_out": w_out[0],
        "gqk": gqk, "plru": plru, "wbd": wbd, "rpbG": rpbG, "cosT": cosT, "sinT": sinT, "cmat": cm, "sel": sel,
        "masks": masks,
    }
    in_maps = []
    for b in range(8):
        cvb = np.stack([c[b], c_ctx], axis=0)
        cvl = np.ascontiguousarray(cvb.reshape(2, 8, 128).transpose(2, 1, 0).reshape(128, 16))
        m = dict(common)
        m["x"] = x[b]; m["ctx"] = ctx[b]; m["cv"] = cvl
        in_maps.append(m)
    nc = build_nc()
    res = run_bass_kernel_spmd(nc, in_maps, core_ids=list(range(8)))
    return np.stack([np.asarray(r["out"], dtype=np.float32) for r in res.results], axis=0)
```

```python
import numpy as np
from contextlib import ExitStack
import concourse.bass as bass
import concourse.mybir as mybir
from concourse.bass_utils import run_bass_kernel_spmd

F32 = mybir.dt.float32
BF16 = mybir.dt.bfloat16
ALU = mybir.AluOpType
AF = mybir.ActivationFunctionType

CFG = {"score_banks": [3, 4, 5], "n_et": 3, "n_pt": 4, "vbank": 3, "window": 200, "rope_add": "pool", "n_qk": 2, "a2_act": 1, "conv_act": 0, "scan_cost": 2.2, "slack": 0.0, "vevac": "act", "lru_pair": 1}
SEQ = 4096
DM = 1024
CTX = 256
EPS = 1e-6
NJ = 14
FW = NJ * 64


class _Op:
    __slots__ = ("eng", "fn", "args", "kwargs", "preds", "dur", "idx", "tab", "seq", "dsem", "dval", "fin")


_ACT_TAB = {}


def _act_tab(func):
    n = str(func)
    if "Exp" in n:
        return ("exp", "ln")
    if "Tanh" in n:
        return ("exp",)
    if "Ln" in n:
        return ("ln",)
    if "Sqrt" in n:
        return ("sqrt",)
    if "Silu" in n:
        return ("silu",)
    if "Sigmoid" in n:
        return ("sigmoid",)
    return None


class Sched:
    WINDOW = 48

    def __init__(self, nc, es):
        self.nc = nc
        self.engs = {"pe": nc.tensor, "act": nc.scalar, "dve": nc.vector, "pool": nc.gpsimd, "sp": nc.sync}
        self.csem = {e: es.enter_context(nc.semaphore("cs_" + e)) for e in ("pe", "act", "dve", "pool")}
        self.cnt = {e: 0 for e in self.csem}
        self.NDS = 12
        self.dsem = [es.enter_context(nc.semaphore("ds%d" % i)) for i in range(self.NDS)]
        self.dcnt = [0] * self.NDS
        self.dnext = 0
        self.seen = {e: {} for e in self.engs}
        self.ops = []
        self.lastw = {}
        self.readers = {}
        self.last_tab = "none"

    def _est(self, eng, args, kwargs, fn=None):
        out = kwargs.get("out", args[0] if args else None)
        try:
            n = out.free_size()
        except Exception:
            n = 512
        if eng == "pe":
            return 64.0 + n * 0.42
        if eng == "act":
            return max(200.0, 100.0 + n * 0.88)
        if eng == "dve":
            if "scan" in getattr(fn, "__name__", ""):
                return 100.0 + n * CFG["scan_cost"]
            return max(160.0, 60.0 + n * 1.1)
        if eng == "pool":
            return 250.0 + n * 4.5
        try:
            nb = out.nbytes()
        except Exception:
            nb = 65536
        return float(nb)

    def _add(self, eng, fn, r, w, args, kwargs):
        o = _Op()
        o.eng = eng; o.fn = fn; o.args = args; o.kwargs = kwargs; o.idx = len(self.ops)
        o.dur = self._est(eng, args, kwargs, fn)
        o.tab = _act_tab(kwargs.get("func")) if eng == "act" else None
        preds = {}
        for k in r:
            p = self.lastw.get(k)
            if p is not None:
                need = not (p.eng == eng and eng == "pe")
                preds[p.idx] = preds.get(p.idx, False) or need
        for k in w:
            p = self.lastw.get(k)
            if p is not None:
                need = not (p.eng == eng and eng == "pe")
                preds[p.idx] = preds.get(p.idx, False) or need
            for p in self.readers.get(k, ()):
                need = not (p.eng == eng and eng == "pe")
                preds[p.idx] = preds.get(p.idx, False) or need
        o.preds = preds
        for k in r:
            self.readers.setdefault(k, []).append(o)
        for k in w:
            self.lastw[k] = o
            self.readers[k] = []
        self.ops.append(o)
        return o

    def op(self, eng, fn, r, w, *args, **kwargs):
        return self._add(eng, fn, r, w, args, kwargs)

    def dma(self, out, in_, r, w, **kwargs):
        kwargs = dict(kwargs); kwargs["out"] = out; kwargs["in_"] = in_
        return self._add("sp", self.nc.sync.dma_start, r, w, (), kwargs)

    def _wait(self, eng, key, val):
        if self.seen[eng].get(key, 0) >= val:
            return
        sem = self.csem[key[1]] if key[0] == "c" else self.dsem[key[1]]
        self.engs[eng].wait_ge(sem, val)
        self.seen[eng][key] = val

    def flush(self):
        ops = self.ops
        if not ops:
            return
        pend = {e: [] for e in self.engs}
        for o in ops:
            o.fin = None
            pend[o.eng].append(o)
        pos = {e: 0 for e in self.engs}
        free = {e: 0.0 for e in self.engs}
        order = {e: [] for e in self.engs}
        done = {e: set() for e in self.engs}
        remaining = len(ops)
        last_tab = self.last_tab
        dma_free = 0.0
        succ = [[] for _ in ops]
        for o in ops:
            for pi in o.preds:
                succ[pi].append(o.idx)
        rank = [0.0] * len(ops)
        for o in reversed(ops):
            m = 0.0
            for si in succ[o.idx]:
                if rank[si] > m:
                    m = rank[si]
            rank[o.idx] = m + (o.dur if o.eng != "sp" else 2500.0)
        slack = CFG.get("slack", 0.0)
        while remaining:
            best = None
            for e in self.engs:
                lst = pend[e]; p0 = pos[e]
                cnt = 0; i = p0
                cands = []
                while i < len(lst) and cnt < CFG["window"]:
                    o = lst[i]; i += 1
                    if o.idx in done[e]:
                        continue
                    cnt += 1
                    rt = 0.0; ok = True
                    for pi in o.preds:
                        f = ops[pi].fin
                        if f is None:
                            ok = False; break
                        if ops[pi].eng != e:
                            f += 200.0
                        if f > rt:
                            rt = f
                    if not ok:
                        continue
                    st = max(free[e], rt)
                    pen = 0.0
                    if e == "act" and o.tab is not None and last_tab not in o.tab:
                        pen = 1400.0
                    cands.append((st + pen, o, st, pen))
                if not cands:
                    continue
                m = min(c[0] for c in cands)
                pick = None
                for c in cands:
                    if c[0] <= m + slack:
                        if pick is None or (rank[c[1].idx], -c[1].idx) > (rank[pick[1].idx], -pick[1].idx):
                            pick = c
                key = (pick[0], pick[1].idx)
                if best is None or key < best[0]:
                    best = (key, e, pick[1], pick[2], pick[3])
            _, e, o, st, pen = best
            if e == "sp":
                free[e] = st + 300.0
                xs = max(st + 300.0, dma_free)
                dma_free = xs + o.dur / 200.0
                o.fin = dma_free + 1800.0
            else:
                o.fin = st + pen + o.dur
                free[e] = o.fin
            if e == "act" and o.tab is not None and last_tab not in o.tab:
                last_tab = o.tab[0]
            order[e].append(o)
            done[e].add(o.idx)
            while pos[e] < len(pend[e]) and pend[e][pos[e]].idx in done[e]:
                pos[e] += 1
            remaining -= 1
        self.last_tab = last_tab
        for e in self.csem:
            c = self.cnt[e]
            for o in order[e]:
                c += 1; o.seq = c
        dn = self.dnext; dc = list(self.dcnt)
        for o in order["sp"]:
            i = dn % self.NDS; dn += 1
            dc[i] += 16
            o.dsem = i; o.dval = dc[i]
        for e in self.engs:
            for o in order[e]:
                reqs = {}
                for pi, need in o.preds.items():
                    if not need:
                        continue
                    p = ops[pi]
                    if p.eng == "sp":
                        k_ = ("d", p.dsem); v_ = p.dval
                    else:
                        k_ = ("c", p.eng); v_ = p.seq
                    if reqs.get(k_, 0) < v_:
                        reqs[k_] = v_
                for k_, v_ in reqs.items():
                    self._wait(e, k_, v_)
                if e == "sp":
                    i = o.dsem
                    if o.dval > 16:
                        self._wait(e, ("d", i), o.dval - 16)
                    ins = o.fn(*o.args, **o.kwargs)
                    ins.then_inc(self.dsem[i], 16)
                else:
                    ins = o.fn(*o.args, **o.kwargs)
                    ins.then_inc(self.csem[e], 1)
        for e in self.csem:
            self.cnt[e] += len(order[e])
        self.dnext = dn; self.dcnt = dc
        self.ops = []
        self.lastw = {}
        self.readers = {}

    def barrier(self):
        self.flush()
        for eng in self.engs:
            for e in self.csem:
                if e != eng and self.cnt[e] > 0:
                    self._wait(eng, ("c", e), self.cnt[e])
            for i in range(self.NDS):
                if self.dcnt[i] > 0:
                    self._wait(eng, ("d", i), self.dcnt[i])

    def final_wait(self):
        self.flush()
        for i in range(self.NDS):
            if self.dcnt[i] > 0:
                self._wait("sp", ("d", i), self.dcnt[i])


def _host_consts():
    p = np.arange(128)
    f = p % 64
    i16 = (f % 16).astype(np.float32)
    inv = (np.float32(10000.0) ** (-(i16) / np.float32(16.0))).astype(np.float32)
    t = np.arange(SEQ)
    r = (t // 64).astype(np.float32)
    c = (t % 64).astype(np.float32)
    pos = np.where((f < 32)[:, None], r[None, :], c[None, :]).astype(np.float32)
    ang = (pos * inv[:, None]).astype(np.float32)
    cosT = np.cos(ang).astype(np.float32)
    sgn = np.where((f % 32) < 16, -1.0, 1.0).astype(np.float32)
    sinT = (np.sin(ang) * sgn[:, None]).astype(np.float32)
    partner = np.where((f % 32) < 16, p + 16, p - 16)
    cm = np.zeros((128, 384), np.float32)
    cm[p, p] = 1.0
    cm[partner, 128 + p] = 1.0
    blk = (p[:, None] // 64) == (p[None, :] // 64)
    cm[:, 256:384] = blk.astype(np.float32) / 64.0
    sel = np.zeros((3, 384), np.float32)
    for rr in range(3):
        sel[rr, rr * 128:(rr + 1) * 128] = 1.0
    krl = (p // 64)[:, None, None]
    kc = (p % 64)[:, None, None]
    jj = np.arange(NJ)[None, :, None]
    qc = np.arange(64)[None, None, :]
    dr = 6 - jj + krl
    cs = np.clip(qc - 8, 0, 48)
    colm = (kc >= cs) & (kc < cs + 16)
    MC = np.broadcast_to(colm, (128, NJ, 64)).astype(np.float32)
    MI = (colm & (dr >= -4) & (dr <= 3)).astype(np.float32)
    masks = np.concatenate([MC.reshape(128, FW), MI.reshape(128, FW)], axis=1).astype(np.float32)
    dridx = np.broadcast_to(dr + 7, (128, NJ, 64))
    dcidx = np.broadcast_to(np.clip(kc - qc, -15, 15) + 15, (128, NJ, 64))
    return cosT, sinT, cm, sel, masks, dridx, dcidx


def build_nc():
    nc = bass.Bass("TRN2", target_bir_lowering=False)

    def din(name, shape, dt=F32):
        return nc.dram_tensor(name, list(shape), dt, kind="ExternalInput").ap()

    x = din("x", [SEQ, DM]); ctx = din("ctx", [CTX, DM]); cv = din("cv", [128, 16])
    norm_g = din("norm_g", [1, DM]); w_mod = din("w_mod", [DM, 3 * DM]); b_mod = din("b_mod", [1, 3 * DM])
    w_in = din("w_in", [DM, 3 * DM]); w_out = din("w_out", [DM, DM])
    gqk = din("gqk", [128, 2]); plru = din("plru", [4, 128, 16]); wbd = din("wbd", [4, 128, 512])
    rpbG = din("rpbG", [8, 128, FW]); cosT = din("cosT", [128, SEQ]); sinT = din("sinT", [128, SEQ])
    cmat = din("cmat", [128, 384]); sel = din("sel", [3, 384]); masks = din("masks", [128, 2 * FW])
    out = nc.dram_tensor("out", [SEQ, DM], F32, kind="ExternalOutput").ap()
    mixD = nc.dram_tensor("mixD", [DM, SEQ], BF16, kind="Internal").ap()
    vD = nc.dram_tensor("vD", [4, 128, 32 * 256], BF16, kind="Internal").ap()
    woD = nc.dram_tensor("woD", [DM, DM], BF16, kind="Internal").ap()
    w_in_v = w_in.rearrange("(kc p) n -> p kc n", p=128)

    with ExitStack() as es:
        S = Sched(nc, es)
        V, A, G, T = nc.vector, nc.scalar, nc.gpsimd, nc.tensor

        uid = [0]

        def sbt(stack, name, shape, dt):
            uid[0] += 1
            return stack.enter_context(nc.sbuf_tensor("%s_%d" % (name, uid[0]), list(shape), dt))

        hT = sbt(es, "hT", [128, 8 * SEQ], BF16)
        hcT = sbt(es, "hcT", [128, 8 * CTX], BF16)
        cm = sbt(es, "cm", [128, 384], BF16)
        gq = sbt(es, "gq", [128, 2], F32)
        wst = [sbt(es, "wst%d" % i, [128, 1024], F32) for i in range(2)]
        wbf = [sbt(es, "wbf%d" % i, [128, 1024], BF16) for i in range(4)]
        PS = [es.enter_context(nc.psum_tensor("ps%d" % i, [128, 512], F32)) for i in range(6)]
        ident = cm[:, 0:128]; Rm = cm[:, 128:256]; bones = cm[:, 256:384]
        wslot = [0]

        def load_wblock(c0):
            s = wslot[0]; wslot[0] += 1
            st = wst[s % 2]; wb = wbf[s % 4]
            S.dma(st[:].rearrange("p (kc n) -> p kc n", kc=8), w_in_v[:, :, c0:c0 + 128], [], [("wst", s % 2)])
            S.op("dve", V.tensor_copy, [("wst", s % 2)], [("wbf", s % 4)], out=wb[:], in_=st[:])
            return wb, ("wbf", s % 4)

        def proj_fm(wb, wkey, src, src_w, col0, ncols, bank):
            for kc in range(8):
                S.op("pe", T.matmul, [wkey, "hT"], [("ps", bank)], PS[bank][:, 0:ncols],
                     wb[:, kc * 128:(kc + 1) * 128], src[:, kc * src_w + col0: kc * src_w + col0 + ncols],
                     start=(kc == 0), stop=(kc == 7))

        with ExitStack() as e1:
            PT = [e1.enter_context(nc.psum_tensor("pt%d" % i, [128, 1024], BF16)) for i in range(2)]
            cm_f = sbt(e1, "cm_f", [128, 384], F32); gate_bc = sbt(e1, "gate_bc", [128, DM], F32)
            cvt = sbt(e1, "cvt", [128, 16], F32); scv = sbt(e1, "scv", [128, 16], F32)
            M3 = sbt(e1, "M3", [3, 3 * DM], F32); b2 = sbt(e1, "b2", [2, 3 * DM], F32)
            sel_t = sbt(e1, "sel_t", [3, 384], F32)
            wmb = [sbt(e1, "wm%d" % i, [128, 1536], F32) for i in range(4)]
            cols = sbt(e1, "cols", [128, 48], F32); mcol = sbt(e1, "mcol", [128, 32], F32)
            wob = [sbt(e1, "wob%d" % i, [128, DM], BF16) for i in range(2)]
            xtb = [sbt(e1, "xt%d" % i, [128, DM], F32) for i in range(3)]
            hbb = [sbt(e1, "hb%d" % i, [128, DM], BF16) for i in range(2)]
            junk = sbt(e1, "junk", [128, DM], BF16)
            ssall = sbt(e1, "ssall", [128, 40], F32); rtall = sbt(e1, "rtall", [128, 40], F32)
            rsall = sbt(e1, "rsall", [128, 40], F32)

            S.dma(cvt[:], cv, [], ["cvt"]); S.dma(cm_f[:], cmat, [], ["cm_f"])
            S.dma(sel_t[:], sel, [], ["sel"])
            S.dma(gq[:], gqk, [], ["gq"])
            S.op("dve", V.tensor_copy, ["cm_f"], ["cm"], out=cm[:], in_=cm_f[:])
            S.op("act", A.activation, ["cvt"], ["scv"], out=scv[:], in_=cvt[:], func=AF.Silu)
            S.op("dve", V.memset, [], ["M3"], M3[:], 0.0)
            S.op("dve", V.memset, [], ["ssall"], ssall[:], 0.0)
            S.dma(M3[2:3, 0:DM], norm_g, [], ["M3"])
            S.dma(b2[0:1, :], b_mod, [], ["b2"]); S.dma(b2[1:2, :], b_mod, [], ["b2"])
            for kc in range(8):
                for hf in range(2):
                    wi = (2 * kc + hf) % 4
                    wm = wmb[wi]
                    S.dma(wm[:], w_mod[kc * 128:(kc + 1) * 128, hf * 1536:(hf + 1) * 1536], [], [("wm", wi)])
                    for n3 in range(3):
                        n = hf * 3 + n3
                        S.op("pe", T.matmul, ["scv", ("wm", wi)], [("ps", n)], PS[n][0:2, :],
                             scv[:, 2 * kc:2 * kc + 2], wm[:, n3 * 512:(n3 + 1) * 512], start=(kc == 0), stop=(kc == 7))
            for n in range(6):
                S.op("dve", V.tensor_tensor, [("ps", n), "b2"], ["M3"], out=M3[0:2, n * 512:(n + 1) * 512],
                     in0=PS[n][0:2, :], in1=b2[0:2, n * 512:(n + 1) * 512], op=ALU.add)
            for j in range(2):
                b = j
                S.op("pe", T.matmul, ["sel", "M3"], [("ps", b)], PS[b][:, :], sel_t[0:3, 0:128], M3[0:3, 2 * DM + j * 512: 2 * DM + (j + 1) * 512],
                     start=True, stop=True)
                S.op("act", A.activation, [("ps", b)], ["gate_bc"], out=gate_bc[:, j * 512:(j + 1) * 512], in_=PS[b][:, :], func=AF.Identity)
            for kc in range(8):
                S.dma(wst[kc % 2][:], w_out[kc * 128:(kc + 1) * 128, :], [], [("wst", kc % 2)])
                S.op("dve", V.tensor_tensor, [("wst", kc % 2), "gate_bc"], [("wob", kc % 2)], out=wob[kc % 2][:], in0=wst[kc % 2][:], in1=gate_bc[:], op=ALU.mult)
                S.dma(woD[kc * 128:(kc + 1) * 128, :], wob[kc % 2][:], [("wob", kc % 2)], [])
            i3 = bass.AP(sel_t[0:3, 0:1].tensor, sel_t[0:3, 0:1].offset, [[sel_t[0:3, 0:1].ap[0][0], 3], [128, 3]])
            for sidx in range(2):
                for kc in range(8):
                    c = (sidx * 8 + kc) * 3
                    S.op("pe", T.matmul, ["sel", "M3"], [("ps", 2)], PS[2][:, c:c + 3],
                         M3[0:3, sidx * DM + kc * 128: sidx * DM + (kc + 1) * 128], i3, start=True, stop=True)
            S.op("dve", V.tensor_copy, [("ps", 2)], ["cols"], out=cols[:, 0:48], in_=PS[2][:, 0:48])

            def colv(sidx, r):
                base = cols[:, sidx * 24 + r: sidx * 24 + r + 1]
                return bass.AP(base.tensor, base.offset, [[base.ap[0][0], 128], [3, 8]])

            S.op("dve", V.scalar_tensor_tensor, ["cols"], ["mcol"], out=mcol[:, 0:8], in0=colv(1, 0), scalar=1.0, in1=colv(0, 2), op0=ALU.add, op1=ALU.mult)
            S.op("dve", V.tensor_copy, ["cols"], ["mcol"], out=mcol[:, 8:16], in_=colv(0, 0))
            S.op("dve", V.scalar_tensor_tensor, ["cols"], ["mcol"], out=mcol[:, 16:24], in0=colv(1, 1), scalar=1.0, in1=colv(0, 2), op0=ALU.add, op1=ALU.mult)
            S.op("dve", V.tensor_copy, ["cols"], ["mcol"], out=mcol[:, 24:32], in_=colv(0, 1))

            def modulate(dst, dstw, c0, n, off, key, eng):
                for kc in range(8):
                    sl = dst[:, kc * dstw + c0: kc * dstw + c0 + n]
                    if True:
                        S.op("dve", V.tensor_scalar, [key, "mcol"], [key], out=sl, in0=sl, scalar1=mcol[:, off + kc:off + kc + 1],
                             scalar2=mcol[:, off + 8 + kc:off + 9 + kc], op0=ALU.mult, op1=ALU.add)
                    else:
                        S.op("act", A.activation, [key, "mcol"], [key], out=sl, in_=sl, func=AF.Identity, scale=mcol[:, off + kc:off + kc + 1],
                             bias=mcol[:, off + 8 + kc:off + 9 + kc])

            tiles = [(ctx[i * 128:(i + 1) * 128, :], hcT, CTX, i * 128, ("hcT", 0)) for i in range(2)]
            tiles += [(x[i * 128:(i + 1) * 128, :], hT, SEQ, i * 128, ("hT", i // 8)) for i in range(32)]
            for idx, (src, dst, dstw, col, hkey) in enumerate(tiles):
                xt = xtb[idx % 3]; xk = ("xt", idx % 3)
                S.dma(xt[:], src, [], [xk])
                S.op("act", A.activation, [xk, "ssall"], ["junk", ("ss", idx)], out=junk[:], in_=xt[:], func=AF.Square,
                     accum_out=ssall[:, idx:idx + 1])
                S.op("act", A.activation, [("ss", idx)], [("rt", idx)], out=rtall[:, idx:idx + 1],
                     in_=ssall[:, idx:idx + 1], func=AF.Sqrt, scale=1.0 / DM, bias=EPS)
                S.op("dve", V.reciprocal, [("rt", idx)], [("rs", idx)], out=rsall[:, idx:idx + 1], in_=rtall[:, idx:idx + 1])
                hb = hbb[idx % 2]
                S.op("act", A.activation, [xk, ("rs", idx)], [("hb", idx % 2)], out=hb[:], in_=xt[:], func=AF.Identity, scale=rsall[:, idx:idx + 1])
                pt = PT[idx % 2]
                for kc in range(8):
                    S.op("pe", T.transpose, [("hb", idx % 2), "cm"], [("pt", idx % 2)], out=pt[:, kc * 128:(kc + 1) * 128],
                         in_=hb[:, kc * 128:(kc + 1) * 128], identity=ident)
                dst_ap = dst[:].rearrange("p (kc t) -> p kc t", kc=8)[:, :, col:col + 128]
                S.op("dve", V.tensor_copy, [("pt", idx % 2)], [hkey], out=dst_ap, in_=pt[:].rearrange("p (kc t) -> p kc t", kc=8))
                if idx == 1:
                    modulate(hcT, CTX, 0, CTX, 16, ("hcT", 0), 0)
                elif idx >= 2 and (idx - 2) % 8 == 7:
                    g4 = (idx - 2) // 8
                    modulate(hT, SEQ, g4 * 1024, 1024, 0, ("hT", g4), g4)
            S.flush()

        def lru_phase():
            S.barrier()
            with ExitStack() as e2:
                NB = 1 if CFG["lru_pair"] else 2
                NWS = 4 if CFG["lru_pair"] else 2
                T0s = [sbt(e2, "T0_%d" % i, [128, SEQ], F32) for i in range(NB)]
                szls = [sbt(e2, "szl_%d" % i, [128, SEQ], BF16) for i in range(NB)]
                T1 = sbt(e2, "T1", [128, SEQ], F32)
                ucbf = sbt(e2, "ucbf", [128, SEQ], BF16)
                QW = 1024
                BAs = [sbt(e2, "BA%d" % i, [128, QW], F32) for i in range(NWS)]
                BBs = [sbt(e2, "BB%d" % i, [128, QW], F32) for i in range(NWS)]
                BCs = [sbt(e2, "BC%d" % i, [128, QW], F32) for i in range(NWS)]
                trb = [sbt(e2, "trb%d" % i, [128, 512], F32) for i in range(2)]
                tib = [sbt(e2, "tib%d" % i, [128, 512], F32) for i in range(2)]
                cx0 = sbt(e2, "cx0", [128, CTX], F32); cx1 = sbt(e2, "cx1", [128, CTX], F32)
                cxbf = sbt(e2, "cxbf", [128, CTX], BF16)
                prms = [sbt(e2, "prm%d" % i, [128, 16], F32) for i in range(2)]
                prhs = [sbt(e2, "prh%d" % i, [128, 16], F32) for i in range(2)]
                clts = [sbt(e2, "clt%d" % i, [128, 8], F32) for i in range(2)]
                gw_f = sbt(e2, "gw_f", [128, 512], F32)
                gws = [sbt(e2, "gw%d" % i, [128, 512], BF16) for i in range(2)]
                carry = sbt(e2, "carry", [128, 2], F32)
                wvf = sbt(e2, "wvf", [128, 8 * 512], BF16)
                wvf3 = wvf[:].rearrange("p (kc n) -> p kc n", kc=8)
                vstage = sbt(e2, "vstage", [128, 1024], BF16)
                psv = e2.enter_context(nc.psum_tensor("psv", [128, 512], F32))
                S.op("pool", G.memset, [], ["vstage"], vstage[:], 1.0)
                for pr_ in range(4):
                    S.dma(wst[pr_ % 2][:].rearrange("p (kc n) -> p kc n", kc=8), w_in_v[:, :, 1024 + 128 * pr_:1024 + 128 * (pr_ + 1)], [], [("wst", pr_ % 2)])
                    S.op("dve", V.tensor_copy, [("wst", pr_ % 2)], ["wvf"], out=wvf3[:, :, pr_ * 128:(pr_ + 1) * 128],
                         in_=wst[pr_ % 2][:].rearrange("p (kc n) -> p kc n", kc=8))
                vD_t = vD.rearrange("r p c -> p r c")
                vst3 = vstage[:].rearrange("p (r c) -> p r c", r=4)
                vb_ = vstage[:]
                vst4 = bass.AP(vb_.tensor, vb_.offset, [[vb_.ap[0][0], 128], [256, 4], [192, 2], [1, 64]])
                pvb_ = psv[:, :]
                psv4 = bass.AP(pvb_.tensor, pvb_.offset, [[pvb_.ap[0][0], 128], [128, 4], [64, 2], [1, 64]])

                def v_tiles(t0, t1):
                    for t_ in range(t0, t1):
                        for kc in range(8):
                            S.op("pe", T.matmul, ["wvf", "hT"], ["psv"], psv[:, :],
                                 hT[:, kc * SEQ + t_ * 128: kc * SEQ + (t_ + 1) * 128], wvf[:, kc * 512:(kc + 1) * 512],
                                 start=(kc == 0), stop=(kc == 7))
                        if CFG["vevac"] == "act":
                            S.op("act", A.activation, ["psv"], ["vstage"], out=vst4, in_=psv4, func=AF.Identity)
                        else:
                            S.op("dve", V.tensor_copy, ["psv"], ["vstage"], out=vst4, in_=psv4)
                        S.dma(vD_t[:, :, t_ * 256:(t_ + 1) * 256], vst3, ["vstage"], [])

                def rev(ap2d, n):
                    pstep = ap2d.ap[0][0]
                    npart = ap2d.ap[0][1]
                    return bass.AP(ap2d.tensor, ap2d.offset + n - 1, [[pstep, npart], [-1, n]])

                def proj_stage(ch):
                    p = ch % 2
                    prm = prms[p]; prh = prhs[p]; clt = clts[p]; gw = gws[p]; pk = p % NB; T0 = T0s[pk]; szl = szls[pk]
                    S.dma(prm[:], plru[ch], [], [("prm", p)])
                    S.dma(gw_f[:], wbd[ch], [], ["gw_f"])
                    S.op("dve", V.tensor_copy, ["gw_f"], [("gw", p)], out=gw[:], in_=gw_f[:])
                    S.op("dve", V.tensor_scalar, [("prm", p)], [("prh", p)], out=prh[:], in0=prm[:], scalar1=0.5, scalar2=None, op0=ALU.mult)
                    for d in range(2):
                        lc = 7 + 3 * d
                        S.op("act", A.activation, [("prm", p)], [("clt", p, 4 + d)], out=clt[:, 4 + d:5 + d], in_=prm[:, lc:lc + 1], func=AF.Exp, scale=-1.0)
                        S.op("act", A.activation, [("clt", p, 4 + d)], [("clt", p, 6 + d)], out=clt[:, 6 + d:7 + d], in_=clt[:, 4 + d:5 + d], func=AF.Ln, bias=1.0)
                        S.op("dve", V.tensor_scalar, [("clt", p, 6 + d)], [("cl", p, d)], out=clt[:, 2 * d:2 * d + 1], in0=clt[:, 6 + d:7 + d],
                             scalar1=-8.0, scalar2=None, op0=ALU.mult)
                        S.op("dve", V.tensor_scalar, [("clt", p, 6 + d)], [("cl", p, d)], out=clt[:, 2 * d + 1:2 * d + 2], in0=clt[:, 6 + d:7 + d],
                             scalar1=-4.0, scalar2=None, op0=ALU.mult)
                    wu, wuk = load_wblock(2048 + 128 * ch)
                    wz, wzk = load_wblock(2560 + 128 * ch)
                    proj_fm(wu, wuk, hcT, CTX, 0, CTX, 5)
                    S.op("act", A.activation, [("ps", 5)], ["cx0"], out=cx0[:], in_=PS[5][:, 0:CTX], func=AF.Identity)
                    for g in range(8):
                        b = 4 + g % 2
                        proj_fm(wu, wuk, hT, SEQ, g * 512, 512, b)
                        S.op("act", A.activation, [("ps", b)], [("T0", pk, g)], out=T0[:, g * 512:(g + 1) * 512], in_=PS[b][:, :], func=AF.Identity)
                    for g in range(8):
                        b = 4 + g % 2
                        proj_fm(wz, wzk, hT, SEQ, g * 512, 512, b)
                        S.op("act", A.activation, [("ps", b)], [("szl", pk, g)], out=szl[:, g * 512:(g + 1) * 512], in_=PS[b][:, :], func=AF.Silu)

                def main_stage(ch, mid_hook):
                    p = ch % 2
                    prm = prms[p]; prh = prhs[p]; clt = clts[p]; gw = gws[p]; pk = p % NB; T0 = T0s[pk]; szl = szls[pk]
                    kprm = ("prm", p)

                    def conv(src, skeys, dst, dkey, lo, hi, n):
                        rk = list(skeys) + [kprm]
                        if CFG["conv_act"] and n > CTX:
                            S.op("act", A.activation, rk, [dkey], out=dst[:, lo:hi], in_=src[:, lo:hi], func=AF.Identity, scale=prm[:, 1:2], bias=prm[:, 4:5])
                        else:
                            S.op("dve", V.tensor_scalar, rk, [dkey], out=dst[:, lo:hi], in0=src[:, lo:hi], scalar1=prm[:, 1:2],
                                 scalar2=prm[:, 4:5], op0=ALU.mult, op1=ALU.add)
                        l0 = max(lo, 1)
                        S.op("dve", V.scalar_tensor_tensor, rk + [dkey], [dkey], out=dst[:, l0:hi], in0=src[:, l0 - 1:hi - 1],
                             scalar=prm[:, 0:1], in1=dst[:, l0:hi], op0=ALU.mult, op1=ALU.add)
                        h2 = min(hi, n - 1)
                        S.op("dve", V.scalar_tensor_tensor, rk + [dkey], [dkey], out=dst[:, lo:h2], in0=src[:, lo + 1:h2 + 1],
                             scalar=prm[:, 2:3], in1=dst[:, lo:h2], op0=ALU.mult, op1=ALU.add)
                        h3 = min(hi, n - 2)
                        S.op("dve", V.scalar_tensor_tensor, rk + [dkey], [dkey], out=dst[:, lo:h3], in0=src[:, lo + 2:h3 + 2],
                             scalar=prm[:, 3:4], in1=dst[:, lo:h3], op0=ALU.mult, op1=ALU.add)

                    conv(cx0, ["cx0"], cx1, "cx1", 0, CTX, CTX)
                    S.op("act", A.activation, ["cx1"], ["cxbf"], out=cxbf[:], in_=cx1[:], func=AF.Identity)
                    for g in range(8):
                        sk = [("T0", pk, j) for j in (g - 1, g, g + 1) if 0 <= j < 8]
                        conv(T0, sk, T1, ("T1", g), g * 512, (g + 1) * 512, SEQ)
                        S.op("act", A.activation, [("T1", g)], [("ucbf", g)], out=ucbf[:, g * 512:(g + 1) * 512], in_=T1[:, g * 512:(g + 1) * 512], func=AF.Identity)

                    def finish(bbuf, bkey, cbuf, ckey, n):
                        S.op("act", A.activation, [bkey], [bkey], out=bbuf[:, 0:n], in_=bbuf[:, 0:n], func=AF.Sqrt, scale=-0.25, bias=0.25)
                        S.op("dve", V.tensor_tensor, [bkey, ckey], [ckey], out=cbuf[:, 0:n], in0=cbuf[:, 0:n], in1=bbuf[:, 0:n], op=ALU.mult)

                    def gates(d, src_bf, sbkeys, src_f, sfkeys, n, abuf, akey, bbuf, bkey, cbuf, ckey, fin=True):
                        wa = gw[:, (2 * d) * 128:(2 * d + 1) * 128]; wx = gw[:, (2 * d + 1) * 128:(2 * d + 2) * 128]
                        ba_c = 5 + 3 * d; bx_c = 6 + 3 * d
                        nseg = (n + 511) // 512
                        for s_ in range(nseg):
                            w_ = min(512, n - s_ * 512); sl = slice(s_ * 512, s_ * 512 + w_)
                            i = gcnt[0] % 2; gcnt[0] += 1
                            b0 = 2 * i; b1 = 2 * i + 1
                            tr = trb[i]; ti = tib[i]
                            sbk = sbkeys[s_]; sfk = sfkeys[s_]
                            S.op("pe", T.matmul, [("gw", p), sbk], [("ps", b0)], PS[b0][:, 0:w_], wa, src_bf[:, sl], start=True, stop=True)
                            S.op("pe", T.matmul, [("gw", p), sbk], [("ps", b1)], PS[b1][:, 0:w_], wx, src_bf[:, sl], start=True, stop=True)
                            S.op("act", A.activation, [("ps", b0), ("prh", p)], [("tr", i)], out=tr[:, 0:w_], in_=PS[b0][:, 0:w_], func=AF.Tanh,
                                 scale=0.5, bias=prh[:, ba_c:ba_c + 1])
                            S.op("act", A.activation, [("ps", b1), ("prh", p)], [("ti", i)], out=ti[:, 0:w_], in_=PS[b1][:, 0:w_], func=AF.Tanh,
                                 scale=0.5, bias=prh[:, bx_c:bx_c + 1])
                            S.op("act", A.activation, [("tr", i), ("cl", p, d)], [akey], out=abuf[:, sl], in_=tr[:, 0:w_], func=AF.Exp,
                                 scale=clt[:, 2 * d + 1:2 * d + 2], bias=clt[:, 2 * d + 1:2 * d + 2])
                            if s_ % 2 < CFG["a2_act"]:
                                S.op("act", A.activation, [("tr", i), ("cl", p, d)], [bkey], out=bbuf[:, sl], in_=tr[:, 0:w_], func=AF.Exp,
                                     scale=clt[:, 2 * d:2 * d + 1], bias=clt[:, 2 * d:2 * d + 1])
                            else:
                                S.op("dve", V.tensor_tensor, [akey], [bkey], out=bbuf[:, sl], in0=abuf[:, sl], in1=abuf[:, sl], op=ALU.mult)
                            S.op("dve", V.scalar_tensor_tensor, [("ti", i), sfk], [ckey], out=cbuf[:, sl], in0=ti[:, 0:w_], scalar=1.0,
                                 in1=src_f[:, sl], op0=ALU.add, op1=ALU.mult)
                        if fin:
                            finish(bbuf, bkey, cbuf, ckey, n)

                    def wset(idx):
                        return BAs[idx], BBs[idx], BCs[idx], ("BA", idx), ("BB", idx), ("BC", idx)

                    if CFG["lru_pair"]:
                        v_tiles(ch * 8, ch * 8 + 8)
                        cs = []
                        for d in range(2):
                            ws = wset((step[0] % 2) * 2 + d)
                            gates(d, cxbf, ["cxbf"], cx1, ["cx1"], CTX, ws[0], ws[3], ws[1], ws[4], ws[2], ws[5], fin=False)
                            cs.append(ws)
                        step[0] += 1
                        for d in range(2):
                            finish(cs[d][1], cs[d][4], cs[d][2], cs[d][5], CTX)
                        for d in range(2):
                            BA, BB, BC, ka, kb, kc_ = cs[d]
                            if d == 0:
                                S.op("dve", V.tensor_tensor_scan, [ka, kc_], [kb], out=BB[:, 0:CTX], data0=BA[:, 0:CTX], data1=BC[:, 0:CTX], initial=0.0,
                                     op0=ALU.mult, op1=ALU.add)
                                S.op("dve", V.tensor_copy, [kb], [("carry", d)], out=carry[:, 0:1], in_=BB[:, CTX - 1:CTX])
                            else:
                                S.op("dve", V.tensor_tensor_scan, [ka, kc_], [kb], out=rev(BB[:, 0:CTX], CTX), data0=rev(BA[:, 0:CTX], CTX),
                                     data1=rev(BC[:, 0:CTX], CTX), initial=0.0, op0=ALU.mult, op1=ALU.add)
                                S.op("dve", V.tensor_copy, [kb], [("carry", d)], out=carry[:, 1:2], in_=BB[:, 0:1])
                        for s4 in range(4):
                            qq = (s4, 3 - s4)
                            cs = []
                            for d in range(2):
                                q = qq[d]; c0 = q * QW
                                ws = wset((step[0] % 2) * 2 + d)
                                blks = [2 * q, 2 * q + 1]
                                gates(d, ucbf[:, c0:c0 + QW], [("ucbf", j) for j in blks], T1[:, c0:c0 + QW], [("T1", j) for j in blks], QW,
                                      ws[0], ws[3], ws[1], ws[4], ws[2], ws[5], fin=False)
                                cs.append(ws)
                            step[0] += 1
                            for d in range(2):
                                finish(cs[d][1], cs[d][4], cs[d][2], cs[d][5], QW)
                            for d in range(2):
                                q = qq[d]; c0 = q * QW
                                BA, BB, BC, ka, kb, kc_ = cs[d]
                                blks = [2 * q, 2 * q + 1]
                                t0keys = [("T0", pk, j) for j in blks]
                                first = s4 < 2
                                if d == 0:
                                    if first:
                                        S.op("dve", V.tensor_tensor_scan, [ka, kc_, ("carry", 0)], t0keys, out=T0[:, c0:c0 + QW], data0=BA[:], data1=BC[:],
                                             initial=carry[:, 0:1], op0=ALU.mult, op1=ALU.add)
                                        S.op("dve", V.tensor_copy, t0keys, [("carry", 0)], out=carry[:, 0:1], in_=T0[:, c0 + QW - 1:c0 + QW])
                                    else:
                                        S.op("dve", V.tensor_tensor_scan, [ka, kc_, ("carry", 0)], [kb], out=BB[:], data0=BA[:], data1=BC[:],
                                             initial=carry[:, 0:1], op0=ALU.mult, op1=ALU.add)
                                        S.op("dve", V.tensor_copy, [kb], [("carry", 0)], out=carry[:, 0:1], in_=BB[:, QW - 1:QW])
                                else:
                                    if first:
                                        S.op("dve", V.tensor_tensor_scan, [ka, kc_, ("carry", 1)], t0keys, out=rev(T0[:, c0:c0 + QW], QW), data0=rev(BA[:], QW),
                                             data1=rev(BC[:], QW), initial=carry[:, 1:2], op0=ALU.mult, op1=ALU.add)
                                        S.op("dve", V.tensor_copy, t0keys, [("carry", 1)], out=carry[:, 1:2], in_=T0[:, c0:c0 + 1])
                                    else:
                                        S.op("dve", V.tensor_tensor_scan, [ka, kc_, ("carry", 1)], [kb], out=rev(BB[:], QW), data0=rev(BA[:], QW),
                                             data1=rev(BC[:], QW), initial=carry[:, 1:2], op0=ALU.mult, op1=ALU.add)
                                        S.op("dve", V.tensor_copy, [kb], [("carry", 1)], out=carry[:, 1:2], in_=BB[:, 0:1])
                                if not first:
                                    S.op("dve", V.tensor_tensor, [kb] + t0keys, [ka], out=BA[:], in0=BB[:], in1=T0[:, c0:c0 + QW], op=ALU.add)
                                    szk = [("szl", pk, j) for j in blks]
                                    S.op("dve", V.tensor_tensor, [ka] + szk, szk, out=szl[:, c0:c0 + QW], in0=BA[:], in1=szl[:, c0:c0 + QW],
                                         op=ALU.mult)
                        for hq in range(2):
                            S.dma(mixD[(4 + ch) * 128:(5 + ch) * 128, hq * 2048:(hq + 1) * 2048], szl[:, hq * 2048:(hq + 1) * 2048],
                                  [("szl", pk, j) for j in range(4 * hq, 4 * hq + 4)], [])
                        if mid_hook is not None:
                            mid_hook()
                        return

                    for d in range(2):
                        v_tiles(ch * 8 + d * 4, ch * 8 + d * 4 + 4)
                        si = step[0] % 2; step[0] += 1
                        BA = BAs[si]; BB = BBs[si]; BC = BCs[si]
                        ka = ("BA", si); kb = ("BB", si); kc_ = ("BC", si)
                        gates(d, cxbf, ["cxbf"], cx1, ["cx1"], CTX, BA, ka, BB, kb, BC, kc_)
                        if d == 0:
                            S.op("dve", V.tensor_tensor_scan, [ka, kc_], [kb], out=BB[:, 0:CTX], data0=BA[:, 0:CTX], data1=BC[:, 0:CTX], initial=0.0,
                                 op0=ALU.mult, op1=ALU.add)
                            S.op("dve", V.tensor_copy, [kb], [("carry", d)], out=carry[:, 0:1], in_=BB[:, CTX - 1:CTX])
                        else:
                            S.op("dve", V.tensor_tensor_scan, [ka, kc_], [kb], out=rev(BB[:, 0:CTX], CTX), data0=rev(BA[:, 0:CTX], CTX),
                                 data1=rev(BC[:, 0:CTX], CTX), initial=0.0, op0=ALU.mult, op1=ALU.add)
                            S.op("dve", V.tensor_copy, [kb], [("carry", d)], out=carry[:, 1:2], in_=BB[:, 0:1])
                        quarters = [0, 1, 2, 3] if d == 0 else [3, 2, 1, 0]
                        for q in quarters:
                            c0 = q * QW
                            si = step[0] % 2; step[0] += 1
                            BA = BAs[si]; BB = BBs[si]; BC = BCs[si]
                            ka = ("BA", si); kb = ("BB", si); kc_ = ("BC", si)
                            blks = [2 * q, 2 * q + 1]
                            gates(d, ucbf[:, c0:c0 + QW], [("ucbf", j) for j in blks], T1[:, c0:c0 + QW], [("T1", j) for j in blks], QW,
                                  BA, ka, BB, kb, BC, kc_)
                            t0keys = [("T0", pk, j) for j in blks]
                            if d == 0:
                                S.op("dve", V.tensor_tensor_scan, [ka, kc_, ("carry", 0)], t0keys, out=T0[:, c0:c0 + QW], data0=BA[:], data1=BC[:],
                                     initial=carry[:, 0:1], op0=ALU.mult, op1=ALU.add)
                                S.op("dve", V.tensor_copy, t0keys, [("carry", 0)], out=carry[:, 0:1], in_=T0[:, c0 + QW - 1:c0 + QW])
                            else:
                                S.op("dve", V.tensor_tensor_scan, [ka, kc_, ("carry", 1)], [kb], out=rev(BB[:], QW), data0=rev(BA[:], QW),
                                     data1=rev(BC[:], QW), initial=carry[:, 1:2], op0=ALU.mult, op1=ALU.add)
                                S.op("dve", V.tensor_copy, [kb], [("carry", 1)], out=carry[:, 1:2], in_=BB[:, 0:1])
                                S.op("dve", V.tensor_tensor, [kb] + t0keys, [ka], out=BA[:], in0=BB[:], in1=T0[:, c0:c0 + QW], op=ALU.add)
                                szk = [("szl", pk, j) for j in blks]
                                S.op("dve", V.tensor_tensor, [ka] + szk, szk, out=szl[:, c0:c0 + QW], in0=BA[:], in1=szl[:, c0:c0 + QW],
                                     op=ALU.mult)
                                if q % 2 == 0:
                                    hq = q // 2
                                    S.dma(mixD[(4 + ch) * 128:(5 + ch) * 128, hq * 2048:(hq + 1) * 2048], szl[:, hq * 2048:(hq + 1) * 2048],
                                          [("szl", pk, j) for j in range(4 * hq, 4 * hq + 4)], [])
                        if d == 0 and mid_hook is not None:
                            mid_hook()

                step = [0]; gcnt = [0]
                proj_stage(0)
                for ch in range(4):
                    main_stage(ch, (lambda c=ch: proj_stage(c + 1)) if ch < 3 else None)
                S.flush()

        def att_phase():
            S.barrier()
            with ExitStack() as e2:
                qrot = sbt(e2, "qrot", [128, SEQ], BF16); qpl = sbt(e2, "qpl", [128, SEQ], BF16)
                krot = sbt(e2, "krot", [128, SEQ], BF16); sza = sbt(e2, "sza", [128, SEQ], BF16)
                vaug = sbt(e2, "vaug", [128, 32 * 256], BF16); mixc = sbt(e2, "mixc", [128, SEQ], BF16)
                kcn2 = [sbt(e2, "kcn%d" % i, [128, CTX], BF16) for i in range(2)]
                vcaug2 = [sbt(e2, "vcaug%d" % i, [128, 2 * 256], BF16) for i in range(2)]
                Ffull2 = [sbt(e2, "Ffull%d" % i, [128, 2 * FW], BF16) for i in range(2)]
                Fint2 = [sbt(e2, "Fint%d" % i, [128, 2 * FW], BF16) for i in range(2)]
                cst = [sbt(e2, "cst%d" % i, [128, 512], F32) for i in range(2)]
                snt = [sbt(e2, "snt%d" % i, [128, 512], F32) for i in range(2)]
                sqb = [sbt(e2, "sqb%d" % i, [128, 512], BF16) for i in range(CFG["n_qk"])]
                qsb = [sbt(e2, "qsb%d" % i, [128, 512], F32) for i in range(CFG["n_qk"])]
                rtb = [sbt(e2, "rtb%d" % i, [128, 512], F32) for i in range(CFG["n_qk"])]
                knbf = [sbt(e2, "knbf%d" % i, [128, 512], BF16) for i in range(CFG["n_qk"])]
                r1b = [sbt(e2, "r1b%d" % i, [128, 512], F32) for i in range(CFG["n_qk"])]
                r2b = [sbt(e2, "r2b%d" % i, [128, 512], F32) for i in range(CFG["n_qk"])]
                etb = [sbt(e2, "etb%d" % i, [128, 512], BF16) for i in range(CFG["n_et"])]
                ptb = [sbt(e2, "ptb%d" % i, [128, 512], BF16) for i in range(CFG["n_pt"])]
                PSX = PS + [e2.enter_context(nc.psum_tensor("psx_%d" % i, [128, 512], F32)) for i in range(2)]
                rdb = [sbt(e2, "rdb%d" % i, [128, 256], F32) for i in range(2)]

                mk = sbt(e2, "mk", [128, 2 * FW], BF16)
                for j_ in range(4):
                    S.dma(r1b[j_ % 2][:, 0:448], masks[:, j_ * 448:(j_ + 1) * 448], [], [("r1b", j_ % 2)])
                    S.op("dve", V.tensor_copy, [("r1b", j_ % 2)], ["mk"], out=mk[:, j_ * 448:(j_ + 1) * 448], in_=r1b[j_ % 2][:, 0:448])
                for i_ in range(2):
                    S.op("pool", G.memset, [], [("vcaug", i_)], vcaug2[i_][:], 1.0)
                for pr in range(4):
                    att_body(pr, locals())
                for kc in [4, 5, 6, 7, 0, 1, 2]:
                    S.dma(hT[:, kc * SEQ:(kc + 1) * SEQ], mixD[kc * 128:(kc + 1) * 128, :], [("mixD", kc)] if kc < 4 else [], ["hT"])
                S.flush()

        def att_body(pr, L):
            if True:
                if True:
                    pass
                par = pr % 2
                qrot = L["qrot"]; qpl = L["qpl"]; krot = L["krot"]; sza = L["sza"]; vaug = L["vaug"]; mixc = L["mixc"]; mk = L["mk"]
                kcn = L["kcn2"][par]; vcaug = L["vcaug2"][par]; Ffull = L["Ffull2"][par]; Fint = L["Fint2"][par]
                cst = L["cst"]; snt = L["snt"]; sqb = L["sqb"]; rtb = L["rtb"]; knbf = L["knbf"]; qsb = L["qsb"]
                r1b = L["r1b"]; r2b = L["r2b"]; etb = L["etb"]; ptb = L["ptb"]; PSX = L["PSX"]; rdb = L["rdb"]; ttb = L["rdb"]
                kFf = ("Ffull", par); kFi = ("Fint", par); kkc = ("kcn", par); kvc = ("vcaug", par)
                for hh in range(2):
                    for hf in range(2):
                        c0 = hf * 448
                        S.dma(r1b[hf][:, 0:448], rpbG[2 * pr + hh, :, c0:c0 + 448], [], [("r1b", hf)])
                        S.op("act", A.activation, [("r1b", hf)], [("r2b", hf)], out=r2b[hf][:, 0:448], in_=r1b[hf][:, 0:448], func=AF.Exp)
                        S.op("dve", V.tensor_tensor, [("r2b", hf), "mk"], [kFf], out=Ffull[:, hh * FW + c0:hh * FW + c0 + 448], in0=r2b[hf][:, 0:448],
                             in1=mk[:, c0:c0 + 448], op=ALU.mult)
                        S.op("dve", V.tensor_tensor, [("r2b", hf), "mk"], [kFi], out=Fint[:, hh * FW + c0:hh * FW + c0 + 448], in0=r2b[hf][:, 0:448],
                             in1=mk[:, FW + c0:FW + c0 + 448], op=ALU.mult)
                wq, wqk = load_wblock(128 * pr)
                wk, wkk = load_wblock(512 + 128 * pr)
                wv, wvk = load_wblock(1024 + 128 * pr)
                wz, wzk = load_wblock(1536 + 128 * pr)
                for g in range(8):
                    b = g % 2
                    proj_fm(wz, wzk, hT, SEQ, g * 512, 512, b)
                    S.op("act", A.activation, [("ps", b)], [("sza", g)], out=sza[:, g * 512:(g + 1) * 512], in_=PS[b][:, :], func=AF.Silu)

                cnt = [0]

                def qk_path(bank, n, gcol, plain_dst, pkey, rot_dst, rkey, ts):
                    i = cnt[0] % CFG["n_qk"]; cnt[0] += 1
                    S.op("act", A.activation, [("ps", bank)], [("qsb", i)], out=qsb[i][:, 0:n], in_=PS[bank][:, 0:n], func=AF.Identity)
                    S.op("act", A.activation, [("ps", bank)], [("sqb", i)], out=sqb[i][:, 0:n], in_=PS[bank][:, 0:n], func=AF.Square)
                    S.op("pe", T.matmul, ["cm", ("sqb", i)], [("ps", 2)], PS[2][:, 0:n], bones, sqb[i][:, 0:n], start=True, stop=True)
                    S.op("act", A.activation, [("ps", 2)], [("rtb", i)], out=rtb[i][:, 0:n], in_=PS[2][:, 0:n], func=AF.Ln, bias=EPS, scale=1.0)
                    S.op("act", A.activation, [("rtb", i)], [("rtb", i)], out=rtb[i][:, 0:n], in_=rtb[i][:, 0:n], func=AF.Exp, scale=-0.5)
                    if plain_dst is None:
                        plain_dst = knbf[i][:, 0:n]; pkey = ("knbf", i)
                    S.op("dve", V.scalar_tensor_tensor, [("qsb", i), "gq", ("rtb", i)], [pkey], out=plain_dst, in0=qsb[i][:, 0:n],
                         scalar=gq[:, gcol:gcol + 1], in1=rtb[i][:, 0:n], op0=ALU.mult, op1=ALU.mult)
                    if rot_dst is None:
                        return
                    S.op("pe", T.matmul, ["cm", pkey], [("ps", 2)], PS[2][:, 0:n], Rm, plain_dst, start=True, stop=True)
                    S.op("dve", V.tensor_tensor, [pkey, ("cst", ts)], [("r1b", i)], out=r1b[i][:, 0:n], in0=plain_dst, in1=cst[ts][:, 0:n], op=ALU.mult)
                    S.op("dve", V.tensor_tensor, [("ps", 2), ("snt", ts)], [("r2b", i)], out=r2b[i][:, 0:n], in0=PS[2][:, 0:n], in1=snt[ts][:, 0:n], op=ALU.mult)
                    if CFG["rope_add"] == "pool":
                        S.op("pool", G.tensor_tensor, [("r1b", i), ("r2b", i)], [rkey], out=rot_dst, in0=r1b[i][:, 0:n], in1=r2b[i][:, 0:n], op=ALU.add)
                    else:
                        S.op("dve", V.tensor_tensor, [("r1b", i), ("r2b", i)], [rkey], out=rot_dst, in0=r1b[i][:, 0:n], in1=r2b[i][:, 0:n], op=ALU.add)

                proj_fm(wk, wkk, hcT, CTX, 0, CTX, 0)
                qk_path(0, CTX, 1, kcn[:], kkc, None, None, 0)
                for t_ in range(2):
                    for kc in range(8):
                        S.op("pe", T.matmul, [wvk, "hT"], [("ps", 3)], PS[3][:, t_ * 128:(t_ + 1) * 128],
                             hcT[:, kc * CTX + t_ * 128: kc * CTX + (t_ + 1) * 128], wv[:, kc * 128:(kc + 1) * 128], start=(kc == 0), stop=(kc == 7))
                vc3 = vcaug[:].rearrange("p (t c) -> p t c", c=256)
                pv3 = PS[3][:, 0:256].rearrange("p (t c) -> p t c", c=128)
                S.op("act", A.activation, [("ps", 3)], [kvc], out=vc3[:, :, 0:64], in_=pv3[:, :, 0:64], func=AF.Identity)
                S.op("act", A.activation, [("ps", 3)], [kvc], out=vc3[:, :, 192:256], in_=pv3[:, :, 64:128], func=AF.Identity)

                def proj_group(g):
                    ts = g % 2
                    S.dma(cst[ts][:], cosT[:, g * 512:(g + 1) * 512], [], [("cst", ts)])
                    S.dma(snt[ts][:], sinT[:, g * 512:(g + 1) * 512], [], [("snt", ts)])
                    proj_fm(wq, wqk, hT, SEQ, g * 512, 512, 0)
                    qk_path(0, 512, 0, qpl[:, g * 512:(g + 1) * 512], ("qpl", g), qrot[:, g * 512:(g + 1) * 512], ("qrot", g), ts)
                    proj_fm(wk, wkk, hT, SEQ, g * 512, 512, 1)
                    qk_path(1, 512, 1, None, None, krot[:, g * 512:(g + 1) * 512], ("krot", g), ts)
                    S.dma(vaug[:, g * 1024:(g + 1) * 1024], vD[pr, :, g * 1024:(g + 1) * 1024], [], [("vaug", g)])

                sc = [0]

                def f3(Ft, hh, jj0):
                    base = Ft[:, hh * FW + jj0 * 64: hh * FW + jj0 * 64 + 256]
                    return bass.AP(base.tensor, base.offset, [[base.ap[0][0], 128], [-128, 2], [1, 256]])

                def att_group(qg):
                    R0 = 4 * qg
                    gq_ = qg // 2
                    qs = slice(R0 * 64, R0 * 64 + 256)
                    if qg == 0:
                        tl = [(a, 6 - a, Ffull, 0, 256) for a in (0, 2, 4, 6)]
                        pairs = [(0, 1), (2, 3)]
                    elif qg == 15:
                        tl = [(a, 6 - (a - 60), Ffull, 0, 256) for a in (56, 58, 60, 62)]
                        pairs = [(0, 1), (2, 3)]
                    else:
                        qr = [(0, 128), (0, 256), (0, 256), (0, 256), (64, 256), (192, 256)]
                        tl = [(R0 - 4 + 2 * i, 10 - 2 * i, Fint, qr[i][0], qr[i][1]) for i in range(6)]
                        pairs = [(1, 2), (3, 4), (0, 5)]
                    i2 = qg % 2
                    for hh in range(2):
                        pb = 64 * hh; ob = 6 + hh; po = PSX[ob][:, 0:256]
                        for pi, (iA, iB) in enumerate(pairs):
                            tA = tl[iA]; tB = tl[iB]
                            wA = tA[4] - tA[3]; wB = tB[4] - tB[3]
                            offs = [256 - wA, 256]
                            lo_ = 256 - wA; hi_ = 256 + wB
                            sb_ = CFG["score_banks"][sc[0] % len(CFG["score_banks"])]; ke = sc[0] % CFG["n_et"]; kp = sc[0] % CFG["n_pt"]; sc[0] += 1
                            for u, tt_ in enumerate((tA, tB)):
                                au, jj0, Ft, qa, qb = tt_
                                S.op("pe", T.matmul, [("krot", au // 8), ("qrot", gq_)], [("ps", sb_)], PSX[sb_][:, offs[u]:offs[u] + (qb - qa)],
                                     krot[pb:pb + 64, au * 64:au * 64 + 128], qrot[pb:pb + 64, R0 * 64 + qa:R0 * 64 + qb], start=True, stop=True)
                            S.op("act", A.activation, [("ps", sb_)], [("et", ke)], out=etb[ke][:, lo_:hi_], in_=PSX[sb_][:, lo_:hi_], func=AF.Exp, scale=0.125)
                            if wA == 256 and wB == 256 and tB[1] == tA[1] - 2:
                                S.op("dve", V.tensor_tensor, [("et", ke), kFf, kFi], [("ptl", kp)],
                                     out=ptb[kp][:].rearrange("p (t c) -> p t c", c=256), in0=etb[ke][:].rearrange("p (t c) -> p t c", c=256),
                                     in1=f3(tA[2], hh, tA[1]), op=ALU.mult)
                            else:
                                for u, tt_ in enumerate((tA, tB)):
                                    au, jj0, Ft, qa, qb = tt_
                                    w_ = qb - qa
                                    S.op("dve", V.tensor_tensor, [("et", ke), kFf, kFi], [("ptl", kp)], out=ptb[kp][:, offs[u]:offs[u] + w_],
                                         in0=etb[ke][:, offs[u]:offs[u] + w_],
                                         in1=Ft[:, hh * FW + jj0 * 64 + qa: hh * FW + jj0 * 64 + qb], op=ALU.mult)
                            for u, tt_ in enumerate((tA, tB)):
                                au, jj0, Ft, qa, qb = tt_
                                tix = au // 2
                                S.op("pe", T.matmul, [("vaug", au // 8), ("ptl", kp)], [("ps", ob)], PSX[ob][:, qa:qb],
                                     vaug[:, tix * 256 + hh * 128: tix * 256 + hh * 128 + 128], ptb[kp][:, offs[u]:offs[u] + (qb - qa)],
                                     start=(pi == 0 and u == 0), stop=False)
                        sb_ = CFG["score_banks"][sc[0] % len(CFG["score_banks"])]; kp = sc[0] % CFG["n_pt"]; sc[0] += 1
                        for j in range(2):
                            S.op("pe", T.matmul, [kkc, ("qpl", gq_)], [("ps", sb_)], PSX[sb_][:, j * 256:(j + 1) * 256], kcn[pb:pb + 64, j * 128:(j + 1) * 128],
                                 qpl[pb:pb + 64, qs], start=True, stop=True)
                        S.op("act", A.activation, [("ps", sb_)], [("ptl", kp)], out=ptb[kp][:], in_=PSX[sb_][:, :], func=AF.Exp, scale=0.125)
                        for j in range(2):
                            S.op("pe", T.matmul, [kvc, ("ptl", kp)], [("ps", ob)], PSX[ob][:, 0:256],
                                 vcaug[:, j * 256 + hh * 128: j * 256 + hh * 128 + 128], ptb[kp][:, j * 256:(j + 1) * 256], start=False, stop=(j == 1))
                        S.op("act", A.activation, [("ps", ob)], [("rd", i2, hh)], out=rdb[i2][pb:pb + 64, :], in_=po[64 - pb:128 - pb, 0:256], func=AF.Ln)
                        S.op("act", A.activation, [("rd", i2, hh)], [("rd", i2, hh)], out=rdb[i2][pb:pb + 64, :], in_=rdb[i2][pb:pb + 64, :],
                             func=AF.Exp, scale=-1.0)
                        S.op("dve", V.tensor_tensor, [("rd", i2, hh), ("sza", gq_)], [("rd", i2, hh)], out=rdb[i2][pb:pb + 64, :], in0=rdb[i2][pb:pb + 64, :],
                             in1=sza[pb:pb + 64, qs], op=ALU.mult)
                        S.op("dve", V.tensor_tensor, [("ps", ob), ("rd", i2, hh)], [("mixc", gq_)], out=mixc[pb:pb + 64, qs], in0=po[pb:pb + 64, 0:256],
                             in1=rdb[i2][pb:pb + 64, :], op=ALU.mult)

                def mix_out(g):
                    S.dma(mixD[pr * 128:(pr + 1) * 128, g * 512:(g + 1) * 512], mixc[:, g * 512:(g + 1) * 512], [("mixc", g)], [("mixD", pr)])

                proj_group(0)
                for g in range(1, 8):
                    proj_group(g)
                    att_group(2 * g - 2)
                    att_group(2 * g - 1)
                    mix_out(g - 1)
                att_group(14)
                att_group(15)
                mix_out(7)

        lru_phase()
        att_phase()

        S.barrier()
        with ExitStack() as e3:
            wo = sbt(e3, "wo", [128, 8 * DM], BF16)
            xtb = [sbt(e3, "xo%d" % i, [128, DM], F32) for i in range(6)]
            otb = [sbt(e3, "ot%d" % i, [128, DM], F32) for i in range(3)]
            hT3 = hT[:].rearrange("p (kc t) -> p kc t", kc=8)
            mixD3 = mixD.rearrange("(kc p) t -> p kc t", p=128)
            wo3 = wo[:].rearrange("p (kc n) -> p kc n", kc=8)
            woD3 = woD.rearrange("(kc p) n -> p kc n", p=128)
            for h4 in range(2):
                S.dma(wo3[:, h4 * 4:(h4 + 1) * 4, :], woD3[:, h4 * 4:(h4 + 1) * 4, :], [], [("wo", h4)])
            for h2 in range(2):
                S.dma(hT[:, 3 * SEQ + h2 * 2048:3 * SEQ + (h2 + 1) * 2048], mixD[384:512, h2 * 2048:(h2 + 1) * 2048], [], [("mx", h2)])
            for i in range(32):
                xt = xtb[i % 6]; xk = ("xo", i % 6)
                S.dma(xt[:], x[i * 128:(i + 1) * 128, :], [], [xk])
                ot = otb[i % 3]; ok = ("ot", i % 3)
                for hf in range(2):
                    b = (2 * i + hf) % 4
                    for kc in range(8):
                        S.op("pe", T.matmul, [("mx", i // 16), ("wo", kc // 4)], [("ps", b)], PS[b][:, :], hT[:, kc * SEQ + i * 128: kc * SEQ + (i + 1) * 128],
                             wo[:, kc * DM + hf * 512: kc * DM + (hf + 1) * 512], start=(kc == 0), stop=(kc == 7))
                    S.op("dve", V.tensor_tensor, [("ps", b), xk], [ok], out=ot[:, hf * 512:(hf + 1) * 512], in0=PS[b][:, :],
                         in1=xt[:, hf * 512:(hf + 1) * 512], op=ALU.add)
                S.dma(out[i * 128:(i + 1) * 128, :], ot[:], [ok], [])
            S.final_wait()
    return nc


def kernel(x, c, ctx, c_ctx, norm_g, w_mod, b_mod, w_in, w_out, q_norm_g, k_norm_g, rpb,
           conv_w, conv_b, lru_wa, lru_ba, lru_wx, lru_bx, lru_lam):
    f = lambda a: np.ascontiguousarray(np.asarray(a, dtype=np.float32))
    x, c, ctx, c_ctx = f(x), f(c), f(ctx), f(c_ctx)
    norm_g, w_mod, b_mod, w_in, w_out = f(norm_g), f(w_mod), f(b_mod), f(w_in), f(w_out)
    q_norm_g, k_norm_g, rpb = f(q_norm_g), f(k_norm_g), f(rpb)
    conv_w, conv_b, lru_wa, lru_ba, lru_wx, lru_bx, lru_lam = (f(conv_w), f(conv_b), f(lru_wa), f(lru_ba),
                                                              f(lru_wx), f(lru_bx), f(lru_lam))
    cosT, sinT, cm, sel, masks, dridx, dcidx = _host_consts()
    rpbG = np.ascontiguousarray(rpb[0][:, dridx, dcidx].reshape(8, 128, FW))
    gqk = np.stack([np.tile(q_norm_g[0], 2), np.tile(k_norm_g[0], 2)], axis=1).astype(np.float32)
    plru = np.zeros((4, 128, 16), np.float32)
    wbd = np.zeros((4, 128, 512), np.float32)
    for ch in range(4):
        sl = slice(ch * 128, (ch + 1) * 128)
        for j in range(4):
            plru[ch, :, j] = conv_w[0, j, sl]
        plru[ch, :, 4] = conv_b[0, sl]
        for d in range(2):
            plru[ch, :, 5 + 3 * d] = lru_ba[0, d, sl]
            plru[ch, :, 6 + 3 * d] = lru_bx[0, d, sl]
            plru[ch, :, 7 + 3 * d] = lru_lam[0, d, sl]
            for m, wsrc in enumerate((lru_wa, lru_wx)):
                col = (2 * d + m) * 128
                for blk in range(2):
                    wbd[ch, blk * 64:(blk + 1) * 64, col + blk * 64: col + (blk + 1) * 64] = wsrc[0, d, 2 * ch + blk]
    common = {
        "norm_g": norm_g[0:1], "w_mod": w_mod[0], "b_mod": b_mod[0:1], "w_in": w_in[0], "w_out": w_out[0],
        "gqk": gqk, "plru": plru, "wbd": wbd, "rpbG": rpbG, "cosT": cosT, "sinT": sinT, "cmat": cm, "sel": sel,
        "masks": masks,
    }
    in_maps = []
    for b in range(8):
        cvb = np.stack([c[b], c_ctx], axis=0)
        cvl = np.ascontiguousarray(cvb.reshape(2, 8, 128).transpose(2, 1, 0).reshape(128, 16))
        m = dict(common)
        m["x"] = x[b]; m["ctx"] = ctx[b]; m["cv"] = cvl
        in_maps.append(m)
    nc = build_nc()
    res = run_bass_kernel_spmd(nc, in_maps, core_ids=list(range(8)))
    return np.stack([np.asarray(r["out"], dtype=np.float32) for r in res.results], axis=0)
```

```python
import numpy as np
from contextlib import ExitStack
import concourse.bass as bass
import concourse.mybir as mybir
from concourse.bass_utils import run_bass_kernel_spmd

F32 = mybir.dt.float32
BF16 = mybir.dt.bfloat16
ALU = mybir.AluOpType
AF = mybir.ActivationFunctionType

CFG = {"score_banks": [3, 4, 5], "n_et": 3, "n_pt": 4, "vbank": 3, "window": 200, "rope_add": "pool", "n_qk": 2, "a2_act": 1, "conv_act": 0, "scan_cost": 2.2, "slack": 0.0, "vevac": "act", "lru_pair": 1}
SEQ = 4096
DM = 1024
CTX = 256
EPS = 1e-6
NJ = 14
FW = NJ * 64


class _Op:
    __slots__ = ("eng", "fn", "args", "kwargs", "preds", "dur", "idx", "tab", "seq", "dsem", "dval", "fin")


_ACT_TAB = {}


def _act_tab(func):
    n = str(func)
    if "Exp" in n:
        return ("exp", "ln")
    if "Tanh" in n:
        return ("exp",)
    if "Ln" in n:
        return ("ln",)
    if "Sqrt" in n:
        return ("sqrt",)
    if "Silu" in n:
        return ("silu",)
    if "Sigmoid" in n:
        return ("sigmoid",)
    return None


class Sched:
    WINDOW = 48

    def __init__(self, nc, es):
        self.nc = nc
        self.engs = {"pe": nc.tensor, "act": nc.scalar, "dve": nc.vector, "pool": nc.gpsimd, "sp": nc.sync}
        self.csem = {e: es.enter_context(nc.semaphore("cs_" + e)) for e in ("pe", "act", "dve", "pool")}
        self.cnt = {e: 0 for e in self.csem}
        self.NDS = 12
        self.dsem = [es.enter_context(nc.semaphore("ds%d" % i)) for i in range(self.NDS)]
        self.dcnt = [0] * self.NDS
        self.dnext = 0
        self.seen = {e: {} for e in self.engs}
        self.ops = []
        self.lastw = {}
        self.readers = {}
        self.last_tab = "none"

    def _est(self, eng, args, kwargs, fn=None):
        out = kwargs.get("out", args[0] if args else None)
        try:
            n = out.free_size()
        except Exception:
            n = 512
        if eng == "pe":
            return 64.0 + n * 0.42
        if eng == "act":
            return max(200.0, 100.0 + n * 0.88)
        if eng == "dve":
            if "scan" in getattr(fn, "__name__", ""):
                return 100.0 + n * CFG["scan_cost"]
            return max(160.0, 60.0 + n * 1.1)
        if eng == "pool":
            return 250.0 + n * 4.5
        try:
            nb = out.nbytes()
        except Exception:
            nb = 65536
        return float(nb)

    def _add(self, eng, fn, r, w, args, kwargs):
        o = _Op()
        o.eng = eng; o.fn = fn; o.args = args; o.kwargs = kwargs; o.idx = len(self.ops)
        o.dur = self._est(eng, args, kwargs, fn)
        o.tab = _act_tab(kwargs.get("func")) if eng == "act" else None
        preds = {}
        for k in r:
            p = self.lastw.get(k)
            if p is not None:
                need = not (p.eng == eng and eng == "pe")
                preds[p.idx] = preds.get(p.idx, False) or need
        for k in w:
            p = self.lastw.get(k)
            if p is not None:
                need = not (p.eng == eng and eng == "pe")
                preds[p.idx] = preds.get(p.idx, False) or need
            for p in self.readers.get(k, ()):
                need = not (p.eng == eng and eng == "pe")
                preds[p.idx] = preds.get(p.idx, False) or need
        o.preds = preds
        for k in r:
            self.readers.setdefault(k, []).append(o)
        for k in w:
            self.lastw[k] = o
            self.readers[k] = []
        self.ops.append(o)
        return o

    def op(self, eng, fn, r, w, *args, **kwargs):
        return self._add(eng, fn, r, w, args, kwargs)

    def dma(self, out, in_, r, w, **kwargs):
        kwargs = dict(kwargs); kwargs["out"] = out; kwargs["in_"] = in_
        return self._add("sp", self.nc.sync.dma_start, r, w, (), kwargs)

    def _wait(self, eng, key, val):
        if self.seen[eng].get(key, 0) >= val:
            return
        sem = self.csem[key[1]] if key[0] == "c" else self.dsem[key[1]]
        self.engs[eng].wait_ge(sem, val)
        self.seen[eng][key] = val

    def flush(self):
        ops = self.ops
        if not ops:
            return
        pend = {e: [] for e in self.engs}
        for o in ops:
            o.fin = None
            pend[o.eng].append(o)
        pos = {e: 0 for e in self.engs}
        free = {e: 0.0 for e in self.engs}
        order = {e: [] for e in self.engs}
        done = {e: set() for e in self.engs}
        remaining = len(ops)
        last_tab = self.last_tab
        dma_free = 0.0
        succ = [[] for _ in ops]
        for o in ops:
            for pi in o.preds:
                succ[pi].append(o.idx)
        rank = [0.0] * len(ops)
        for o in reversed(ops):
            m = 0.0
            for si in succ[o.idx]:
                if rank[si] > m:
                    m = rank[si]
            rank[o.idx] = m + (o.dur if o.eng != "sp" else 2500.0)
        slack = CFG.get("slack", 0.0)
        while remaining:
            best = None
            for e in self.engs:
                lst = pend[e]; p0 = pos[e]
                cnt = 0; i = p0
                cands = []
                while i < len(lst) and cnt < CFG["window"]:
                    o = lst[i]; i += 1
                    if o.idx in done[e]:
                        continue
                    cnt += 1
                    rt = 0.0; ok = True
                    for pi in o.preds:
                        f = ops[pi].fin
                        if f is None:
                            ok = False; break
                        if ops[pi].eng != e:
                            f += 200.0
                        if f > rt:
                            rt = f
                    if not ok:
                        continue
                    st = max(free[e], rt)
                    pen = 0.0
                    if e == "act" and o.tab is not None and last_tab not in o.tab:
                        pen = 1400.0
                    cands.append((st + pen, o, st, pen))
                if not cands:
                    continue
                m = min(c[0] for c in cands)
                pick = None
                for c in cands:
                    if c[0] <= m + slack:
                        if pick is None or (rank[c[1].idx], -c[1].idx) > (rank[pick[1].idx], -pick[1].idx):
                            pick = c
                key = (pick[0], pick[1].idx)
                if best is None or key < best[0]:
                    best = (key, e, pick[1], pick[2], pick[3])
            _, e, o, st, pen = best
            if e == "sp":
                free[e] = st + 300.0
                xs = max(st + 300.0, dma_free)
                dma_free = xs + o.dur / 200.0
                o.fin = dma_free + 1800.0
            else:
                o.fin = st + pen + o.dur
                free[e] = o.fin
            if e == "act" and o.tab is not None and last_tab not in o.tab:
                last_tab = o.tab[0]
            order[e].append(o)
            done[e].add(o.idx)
            while pos[e] < len(pend[e]) and pend[e][pos[e]].idx in done[e]:
                pos[e] += 1
            remaining -= 1
        self.last_tab = last_tab
        for e in self.csem:
            c = self.cnt[e]
            for o in order[e]:
                c += 1; o.seq = c
        dn = self.dnext; dc = list(self.dcnt)
        for o in order["sp"]:
            i = dn % self.NDS; dn += 1
            dc[i] += 16
            o.dsem = i; o.dval = dc[i]
        for e in self.engs:
            for o in order[e]:
                reqs = {}
                for pi, need in o.preds.items():
                    if not need:
                        continue
                    p = ops[pi]
                    if p.eng == "sp":
                        k_ = ("d", p.dsem); v_ = p.dval
                    else:
                        k_ = ("c", p.eng); v_ = p.seq
                    if reqs.get(k_, 0) < v_:
                        reqs[k_] = v_
                for k_, v_ in reqs.items():
                    self._wait(e, k_, v_)
                if e == "sp":
                    i = o.dsem
                    if o.dval > 16:
                        self._wait(e, ("d", i), o.dval - 16)
                    ins = o.fn(*o.args, **o.kwargs)
                    ins.then_inc(self.dsem[i], 16)
                else:
                    ins = o.fn(*o.args, **o.kwargs)
                    ins.then_inc(self.csem[e], 1)
        for e in self.csem:
            self.cnt[e] += len(order[e])
        self.dnext = dn; self.dcnt = dc
        self.ops = []
        self.lastw = {}
        self.readers = {}

    def barrier(self):
        self.flush()
        for eng in self.engs:
            for e in self.csem:
                if e != eng and self.cnt[e] > 0:
                    self._wait(eng, ("c", e), self.cnt[e])
            for i in range(self.NDS):
                if self.dcnt[i] > 0:
                    self._wait(eng, ("d", i), self.dcnt[i])

    def final_wait(self):
        self.flush()
        for i in range(self.NDS):
            if self.dcnt[i] > 0:
                self._wait("sp", ("d", i), self.dcnt[i])


def _host_consts():
    p = np.arange(128)
    f = p % 64
    i16 = (f % 16).astype(np.float32)
    inv = (np.float32(10000.0) ** (-(i16) / np.float32(16.0))).astype(np.float32)
    t = np.arange(SEQ)
    r = (t // 64).astype(np.float32)
    c = (t % 64).astype(np.float32)
    pos = np.where((f < 32)[:, None], r[None, :], c[None, :]).astype(np.float32)
    ang = (pos * inv[:, None]).astype(np.float32)
    cosT = np.cos(ang).astype(np.float32)
    sgn = np.where((f % 32) < 16, -1.0, 1.0).astype(np.float32)
    sinT = (np.sin(ang) * sgn[:, None]).astype(np.float32)
    partner = np.where((f % 32) < 16, p + 16, p - 16)
    cm = np.zeros((128, 384), np.float32)
    cm[p, p] = 1.0
    cm[partner, 128 + p] = 1.0
    blk = (p[:, None] // 64) == (p[None, :] // 64)
    cm[:, 256:384] = blk.astype(np.float32) / 64.0
    sel = np.zeros((3, 384), np.float32)
    for rr in range(3):
        sel[rr, rr * 128:(rr + 1) * 128] = 1.0
    krl = (p // 64)[:, None, None]
    kc = (p % 64)[:, None, None]
    jj = np.arange(NJ)[None, :, None]
    qc = np.arange(64)[None, None, :]
    dr = 6 - jj + krl
    cs = np.clip(qc - 8, 0, 48)
    colm = (kc >= cs) & (kc < cs + 16)
    MC = np.broadcast_to(colm, (128, NJ, 64)).astype(np.float32)
    MI = (colm & (dr >= -4) & (dr <= 3)).astype(np.float32)
    masks = np.concatenate([MC.reshape(128, FW), MI.reshape(128, FW)], axis=1).astype(np.float32)
    dridx = np.broadcast_to(dr + 7, (128, NJ, 64))
    dcidx = np.broadcast_to(np.clip(kc - qc, -15, 15) + 15, (128, NJ, 64))
    return cosT, sinT, cm, sel, masks, dridx, dcidx


def build_nc():
    nc = bass.Bass("TRN2", target_bir_lowering=False)

    def din(name, shape, dt=F32):
        return nc.dram_tensor(name, list(shape), dt, kind="ExternalInput").ap()

    x = din("x", [SEQ, DM]); ctx = din("ctx", [CTX, DM]); cv = din("cv", [128, 16])
    norm_g = din("norm_g", [1, DM]); w_mod = din("w_mod", [DM, 3 * DM]); b_mod = din("b_mod", [1, 3 * DM])
    w_in = din("w_in", [DM, 3 * DM]); w_out = din("w_out", [DM, DM])
    gqk = din("gqk", [128, 2]); plru = din("plru", [4, 128, 16]); wbd = din("wbd", [4, 128, 512])
    rpbG = din("rpbG", [8, 128, FW]); cosT = din("cosT", [128, SEQ]); sinT = din("sinT", [128, SEQ])
    cmat = din("cmat", [128, 384]); sel = din("sel", [3, 384]); masks = din("masks", [128, 2 * FW])
    out = nc.dram_tensor("out", [SEQ, DM], F32, kind="ExternalOutput").ap()
    mixD = nc.dram_tensor("mixD", [DM, SEQ], BF16, kind="Internal").ap()
    vD = nc.dram_tensor("vD", [4, 128, 32 * 256], BF16, kind="Internal").ap()
    woD = nc.dram_tensor("woD", [DM, DM], BF16, kind="Internal").ap()
    gateD = nc.dram_tensor("gateD", [128, DM], F32, kind="Internal").ap()
    w_in_v = w_in.rearrange("(kc p) n -> p kc n", p=128)

    with ExitStack() as es:
        S = Sched(nc, es)
        V, A, G, T = nc.vector, nc.scalar, nc.gpsimd, nc.tensor

        uid = [0]

        def sbt(stack, name, shape, dt):
            uid[0] += 1
            return stack.enter_context(nc.sbuf_tensor("%s_%d" % (name, uid[0]), list(shape), dt))

        hT = sbt(es, "hT", [128, 8 * SEQ], BF16)
        hcT = sbt(es, "hcT", [128, 8 * CTX], BF16)
        cm = sbt(es, "cm", [128, 384], BF16)
        gq = sbt(es, "gq", [128, 2], F32)
        wst = [sbt(es, "wst%d" % i, [128, 1024], F32) for i in range(2)]
        wbf = [sbt(es, "wbf%d" % i, [128, 1024], BF16) for i in range(4)]
        PS = [es.enter_context(nc.psum_tensor("ps%d" % i, [128, 512], F32)) for i in range(6)]
        ident = cm[:, 0:128]; Rm = cm[:, 128:256]; bones = cm[:, 256:384]
        wslot = [0]

        def load_wblock(c0):
            s = wslot[0]; wslot[0] += 1
            st = wst[s % 2]; wb = wbf[s % 4]
            S.dma(st[:].rearrange("p (kc n) -> p kc n", kc=8), w_in_v[:, :, c0:c0 + 128], [], [("wst", s % 2)])
            S.op("dve", V.tensor_copy, [("wst", s % 2)], [("wbf", s % 4)], out=wb[:], in_=st[:])
            return wb, ("wbf", s % 4)

        def proj_fm(wb, wkey, src, src_w, col0, ncols, bank):
            for kc in range(8):
                S.op("pe", T.matmul, [wkey, "hT"], [("ps", bank)], PS[bank][:, 0:ncols],
                     wb[:, kc * 128:(kc + 1) * 128], src[:, kc * src_w + col0: kc * src_w + col0 + ncols],
                     start=(kc == 0), stop=(kc == 7))

        with ExitStack() as e1:
            PT = [e1.enter_context(nc.psum_tensor("pt%d" % i, [128, 1024], BF16)) for i in range(2)]
            cm_f = sbt(e1, "cm_f", [128, 384], F32); gate_bc = sbt(e1, "gate_bc", [128, DM], F32)
            cvt = sbt(e1, "cvt", [128, 16], F32); scv = sbt(e1, "scv", [128, 16], F32)
            M3 = sbt(e1, "M3", [3, 3 * DM], F32); b2 = sbt(e1, "b2", [2, 3 * DM], F32)
            sel_t = sbt(e1, "sel_t", [3, 384], F32)
            wmb = [sbt(e1, "wm%d" % i, [128, 1536], F32) for i in range(4)]
            cols = sbt(e1, "cols", [128, 48], F32); mcol = sbt(e1, "mcol", [128, 32], F32)
            wob = [sbt(e1, "wob%d" % i, [128, DM], BF16) for i in range(2)]
            xtb = [sbt(e1, "xt%d" % i, [128, DM], F32) for i in range(3)]
            hbb = [sbt(e1, "hb%d" % i, [128, DM], BF16) for i in range(2)]
            junk = sbt(e1, "junk", [128, DM], BF16)
            ssall = sbt(e1, "ssall", [128, 40], F32); rtall = sbt(e1, "rtall", [128, 40], F32)
            rsall = sbt(e1, "rsall", [128, 40], F32)

            S.dma(cvt[:], cv, [], ["cvt"]); S.dma(cm_f[:], cmat, [], ["cm_f"])
            S.dma(sel_t[:], sel, [], ["sel"])
            S.dma(gq[:], gqk, [], ["gq"])
            S.op("dve", V.tensor_copy, ["cm_f"], ["cm"], out=cm[:], in_=cm_f[:])
            S.op("act", A.activation, ["cvt"], ["scv"], out=scv[:], in_=cvt[:], func=AF.Silu)
            S.op("dve", V.memset, [], ["M3"], M3[:], 0.0)
            S.op("dve", V.memset, [], ["ssall"], ssall[:], 0.0)
            S.dma(M3[2:3, 0:DM], norm_g, [], ["M3"])
            S.dma(b2[0:1, :], b_mod, [], ["b2"]); S.dma(b2[1:2, :], b_mod, [], ["b2"])
            for kc in range(8):
                for hf in range(2):
                    wi = (2 * kc + hf) % 4
                    wm = wmb[wi]
                    S.dma(wm[:], w_mod[kc * 128:(kc + 1) * 128, hf * 1536:(hf + 1) * 1536], [], [("wm", wi)])
                    for n3 in range(3):
                        n = hf * 3 + n3
                        S.op("pe", T.matmul, ["scv", ("wm", wi)], [("ps", n)], PS[n][0:2, :],
                             scv[:, 2 * kc:2 * kc + 2], wm[:, n3 * 512:(n3 + 1) * 512], start=(kc == 0), stop=(kc == 7))
            for n in range(6):
                S.op("dve", V.tensor_tensor, [("ps", n), "b2"], ["M3"], out=M3[0:2, n * 512:(n + 1) * 512],
                     in0=PS[n][0:2, :], in1=b2[0:2, n * 512:(n + 1) * 512], op=ALU.add)
            for j in range(2):
                b = j
                S.op("pe", T.matmul, ["sel", "M3"], [("ps", b)], PS[b][:, :], sel_t[0:3, 0:128], M3[0:3, 2 * DM + j * 512: 2 * DM + (j + 1) * 512],
                     start=True, stop=True)
                S.op("act", A.activation, [("ps", b)], ["gate_bc"], out=gate_bc[:, j * 512:(j + 1) * 512], in_=PS[b][:, :], func=AF.Identity)
            S.dma(gateD, gate_bc[:], ["gate_bc"], [])
            i3 = bass.AP(sel_t[0:3, 0:1].tensor, sel_t[0:3, 0:1].offset, [[sel_t[0:3, 0:1].ap[0][0], 3], [128, 3]])
            for sidx in range(2):
                for kc in range(8):
                    c = (sidx * 8 + kc) * 3
                    S.op("pe", T.matmul, ["sel", "M3"], [("ps", 2)], PS[2][:, c:c + 3],
                         M3[0:3, sidx * DM + kc * 128: sidx * DM + (kc + 1) * 128], i3, start=True, stop=True)
            S.op("dve", V.tensor_copy, [("ps", 2)], ["cols"], out=cols[:, 0:48], in_=PS[2][:, 0:48])

            def colv(sidx, r):
                base = cols[:, sidx * 24 + r: sidx * 24 + r + 1]
                return bass.AP(base.tensor, base.offset, [[base.ap[0][0], 128], [3, 8]])

            S.op("dve", V.scalar_tensor_tensor, ["cols"], ["mcol"], out=mcol[:, 0:8], in0=colv(1, 0), scalar=1.0, in1=colv(0, 2), op0=ALU.add, op1=ALU.mult)
            S.op("dve", V.tensor_copy, ["cols"], ["mcol"], out=mcol[:, 8:16], in_=colv(0, 0))
            S.op("dve", V.scalar_tensor_tensor, ["cols"], ["mcol"], out=mcol[:, 16:24], in0=colv(1, 1), scalar=1.0, in1=colv(0, 2), op0=ALU.add, op1=ALU.mult)
            S.op("dve", V.tensor_copy, ["cols"], ["mcol"], out=mcol[:, 24:32], in_=colv(0, 1))

            def modulate(dst, dstw, c0, n, off, key, eng):
                for kc in range(8):
                    sl = dst[:, kc * dstw + c0: kc * dstw + c0 + n]
                    if True:
                        S.op("dve", V.tensor_scalar, [key, "mcol"], [key], out=sl, in0=sl, scalar1=mcol[:, off + kc:off + kc + 1],
                             scalar2=mcol[:, off + 8 + kc:off + 9 + kc], op0=ALU.mult, op1=ALU.add)
                    else:
                        S.op("act", A.activation, [key, "mcol"], [key], out=sl, in_=sl, func=AF.Identity, scale=mcol[:, off + kc:off + kc + 1],
                             bias=mcol[:, off + 8 + kc:off + 9 + kc])

            tiles = [(ctx[i * 128:(i + 1) * 128, :], hcT, CTX, i * 128, ("hcT", 0)) for i in range(2)]
            tiles += [(x[i * 128:(i + 1) * 128, :], hT, SEQ, i * 128, ("hT", i // 8)) for i in range(32)]
            for idx, (src, dst, dstw, col, hkey) in enumerate(tiles):
                xt = xtb[idx % 3]; xk = ("xt", idx % 3)
                S.dma(xt[:], src, [], [xk])
                S.op("act", A.activation, [xk, "ssall"], ["junk", ("ss", idx)], out=junk[:], in_=xt[:], func=AF.Square,
                     accum_out=ssall[:, idx:idx + 1])
                S.op("act", A.activation, [("ss", idx)], [("rt", idx)], out=rtall[:, idx:idx + 1],
                     in_=ssall[:, idx:idx + 1], func=AF.Sqrt, scale=1.0 / DM, bias=EPS)
                S.op("dve", V.reciprocal, [("rt", idx)], [("rs", idx)], out=rsall[:, idx:idx + 1], in_=rtall[:, idx:idx + 1])
                hb = hbb[idx % 2]
                S.op("act", A.activation, [xk, ("rs", idx)], [("hb", idx % 2)], out=hb[:], in_=xt[:], func=AF.Identity, scale=rsall[:, idx:idx + 1])
                pt = PT[idx % 2]
                for kc in range(8):
                    S.op("pe", T.transpose, [("hb", idx % 2), "cm"], [("pt", idx % 2)], out=pt[:, kc * 128:(kc + 1) * 128],
                         in_=hb[:, kc * 128:(kc + 1) * 128], identity=ident)
                dst_ap = dst[:].rearrange("p (kc t) -> p kc t", kc=8)[:, :, col:col + 128]
                S.op("dve", V.tensor_copy, [("pt", idx % 2)], [hkey], out=dst_ap, in_=pt[:].rearrange("p (kc t) -> p kc t", kc=8))
                if idx == 1:
                    modulate(hcT, CTX, 0, CTX, 16, ("hcT", 0), 0)
                elif idx >= 2 and (idx - 2) % 8 == 7:
                    g4 = (idx - 2) // 8
                    modulate(hT, SEQ, g4 * 1024, 1024, 0, ("hT", g4), g4)
            S.flush()

        def lru_phase():
            S.barrier()
            with ExitStack() as e2:
                NB = 1 if CFG["lru_pair"] else 2
                NWS = 4 if CFG["lru_pair"] else 2
                T0s = [sbt(e2, "T0_%d" % i, [128, SEQ], F32) for i in range(NB)]
                szls = [sbt(e2, "szl_%d" % i, [128, SEQ], BF16) for i in range(NB)]
                T1 = sbt(e2, "T1", [128, SEQ], F32)
                ucbf = sbt(e2, "ucbf", [128, SEQ], BF16)
                QW = 1024
                BAs = [sbt(e2, "BA%d" % i, [128, QW], F32) for i in range(NWS)]
                BBs = [sbt(e2, "BB%d" % i, [128, QW], F32) for i in range(NWS)]
                BCs = [sbt(e2, "BC%d" % i, [128, QW], F32) for i in range(NWS)]
                trb = [sbt(e2, "trb%d" % i, [128, 512], F32) for i in range(2)]
                tib = [sbt(e2, "tib%d" % i, [128, 512], F32) for i in range(2)]
                cx0 = sbt(e2, "cx0", [128, CTX], F32); cx1 = sbt(e2, "cx1", [128, CTX], F32)
                cxbf = sbt(e2, "cxbf", [128, CTX], BF16)
                prms = [sbt(e2, "prm%d" % i, [128, 16], F32) for i in range(2)]
                prhs = [sbt(e2, "prh%d" % i, [128, 16], F32) for i in range(2)]
                clts = [sbt(e2, "clt%d" % i, [128, 8], F32) for i in range(2)]
                gw_f = sbt(e2, "gw_f", [128, 512], F32)
                gws = [sbt(e2, "gw%d" % i, [128, 512], BF16) for i in range(2)]
                carry = sbt(e2, "carry", [128, 2], F32)
                wvf = sbt(e2, "wvf", [128, 8 * 512], BF16)
                wvf3 = wvf[:].rearrange("p (kc n) -> p kc n", kc=8)
                vstage = sbt(e2, "vstage", [128, 1024], BF16)
                psv = e2.enter_context(nc.psum_tensor("psv", [128, 512], F32))
                S.op("pool", G.memset, [], ["vstage"], vstage[:], 1.0)
                for pr_ in range(4):
                    S.dma(wst[pr_ % 2][:].rearrange("p (kc n) -> p kc n", kc=8), w_in_v[:, :, 1024 + 128 * pr_:1024 + 128 * (pr_ + 1)], [], [("wst", pr_ % 2)])
                    S.op("dve", V.tensor_copy, [("wst", pr_ % 2)], ["wvf"], out=wvf3[:, :, pr_ * 128:(pr_ + 1) * 128],
                         in_=wst[pr_ % 2][:].rearrange("p (kc n) -> p kc n", kc=8))
                vD_t = vD.rearrange("r p c -> p r c")
                vst3 = vstage[:].rearrange("p (r c) -> p r c", r=4)
                vb_ = vstage[:]
                vst4 = bass.AP(vb_.tensor, vb_.offset, [[vb_.ap[0][0], 128], [256, 4], [192, 2], [1, 64]])
                pvb_ = psv[:, :]
                psv4 = bass.AP(pvb_.tensor, pvb_.offset, [[pvb_.ap[0][0], 128], [128, 4], [64, 2], [1, 64]])

                def v_tiles(t0, t1):
                    for t_ in range(t0, t1):
                        for kc in range(8):
                            S.op("pe", T.matmul, ["wvf", "hT"], ["psv"], psv[:, :],
                                 hT[:, kc * SEQ + t_ * 128: kc * SEQ + (t_ + 1) * 128], wvf[:, kc * 512:(kc + 1) * 512],
                                 start=(kc == 0), stop=(kc == 7))
                        if CFG["vevac"] == "act":
                            S.op("act", A.activation, ["psv"], ["vstage"], out=vst4, in_=psv4, func=AF.Identity)
                        else:
                            S.op("dve", V.tensor_copy, ["psv"], ["vstage"], out=vst4, in_=psv4)
                        S.dma(vD_t[:, :, t_ * 256:(t_ + 1) * 256], vst3, ["vstage"], [])

                def rev(ap2d, n):
                    pstep = ap2d.ap[0][0]
                    npart = ap2d.ap[0][1]
                    return bass.AP(ap2d.tensor, ap2d.offset + n - 1, [[pstep, npart], [-1, n]])

                def proj_stage(ch):
                    p = ch % 2
                    prm = prms[p]; prh = prhs[p]; clt = clts[p]; gw = gws[p]; pk = p % NB; T0 = T0s[pk]; szl = szls[pk]
                    S.dma(prm[:], plru[ch], [], [("prm", p)])
                    S.dma(gw_f[:], wbd[ch], [], ["gw_f"])
                    S.op("dve", V.tensor_copy, ["gw_f"], [("gw", p)], out=gw[:], in_=gw_f[:])
                    S.op("dve", V.tensor_scalar, [("prm", p)], [("prh", p)], out=prh[:], in0=prm[:], scalar1=0.5, scalar2=None, op0=ALU.mult)
                    for d in range(2):
                        lc = 7 + 3 * d
                        S.op("act", A.activation, [("prm", p)], [("clt", p, 4 + d)], out=clt[:, 4 + d:5 + d], in_=prm[:, lc:lc + 1], func=AF.Exp, scale=-1.0)
                        S.op("act", A.activation, [("clt", p, 4 + d)], [("clt", p, 6 + d)], out=clt[:, 6 + d:7 + d], in_=clt[:, 4 + d:5 + d], func=AF.Ln, bias=1.0)
                        S.op("dve", V.tensor_scalar, [("clt", p, 6 + d)], [("cl", p, d)], out=clt[:, 2 * d:2 * d + 1], in0=clt[:, 6 + d:7 + d],
                             scalar1=-8.0, scalar2=None, op0=ALU.mult)
                        S.op("dve", V.tensor_scalar, [("clt", p, 6 + d)], [("cl", p, d)], out=clt[:, 2 * d + 1:2 * d + 2], in0=clt[:, 6 + d:7 + d],
                             scalar1=-4.0, scalar2=None, op0=ALU.mult)
                    wu, wuk = load_wblock(2048 + 128 * ch)
                    wz, wzk = load_wblock(2560 + 128 * ch)
                    proj_fm(wu, wuk, hcT, CTX, 0, CTX, 5)
                    S.op("act", A.activation, [("ps", 5)], ["cx0"], out=cx0[:], in_=PS[5][:, 0:CTX], func=AF.Identity)
                    for g in range(8):
                        b = 4 + g % 2
                        proj_fm(wu, wuk, hT, SEQ, g * 512, 512, b)
                        S.op("act", A.activation, [("ps", b)], [("T0", pk, g)], out=T0[:, g * 512:(g + 1) * 512], in_=PS[b][:, :], func=AF.Identity)
                    for g in range(8):
                        b = 4 + g % 2
                        proj_fm(wz, wzk, hT, SEQ, g * 512, 512, b)
                        S.op("act", A.activation, [("ps", b)], [("szl", pk, g)], out=szl[:, g * 512:(g + 1) * 512], in_=PS[b][:, :], func=AF.Silu)

                def main_stage(ch, mid_hook):
                    p = ch % 2
                    prm = prms[p]; prh = prhs[p]; clt = clts[p]; gw = gws[p]; pk = p % NB; T0 = T0s[pk]; szl = szls[pk]
                    kprm = ("prm", p)

                    def conv(src, skeys, dst, dkey, lo, hi, n):
                        rk = list(skeys) + [kprm]
                        if CFG["conv_act"] and n > CTX:
                            S.op("act", A.activation, rk, [dkey], out=dst[:, lo:hi], in_=src[:, lo:hi], func=AF.Identity, scale=prm[:, 1:2], bias=prm[:, 4:5])
                        else:
                            S.op("dve", V.tensor_scalar, rk, [dkey], out=dst[:, lo:hi], in0=src[:, lo:hi], scalar1=prm[:, 1:2],
                                 scalar2=prm[:, 4:5], op0=ALU.mult, op1=ALU.add)
                        l0 = max(lo, 1)
                        S.op("dve", V.scalar_tensor_tensor, rk + [dkey], [dkey], out=dst[:, l0:hi], in0=src[:, l0 - 1:hi - 1],
                             scalar=prm[:, 0:1], in1=dst[:, l0:hi], op0=ALU.mult, op1=ALU.add)
                        h2 = min(hi, n - 1)
                        S.op("dve", V.scalar_tensor_tensor, rk + [dkey], [dkey], out=dst[:, lo:h2], in0=src[:, lo + 1:h2 + 1],
                             scalar=prm[:, 2:3], in1=dst[:, lo:h2], op0=ALU.mult, op1=ALU.add)
                        h3 = min(hi, n - 2)
                        S.op("dve", V.scalar_tensor_tensor, rk + [dkey], [dkey], out=dst[:, lo:h3], in0=src[:, lo + 2:h3 + 2],
                             scalar=prm[:, 3:4], in1=dst[:, lo:h3], op0=ALU.mult, op1=ALU.add)

                    conv(cx0, ["cx0"], cx1, "cx1", 0, CTX, CTX)
                    S.op("act", A.activation, ["cx1"], ["cxbf"], out=cxbf[:], in_=cx1[:], func=AF.Identity)
                    for g in range(8):
                        sk = [("T0", pk, j) for j in (g - 1, g, g + 1) if 0 <= j < 8]
                        conv(T0, sk, T1, ("T1", g), g * 512, (g + 1) * 512, SEQ)
                        S.op("act", A.activation, [("T1", g)], [("ucbf", g)], out=ucbf[:, g * 512:(g + 1) * 512], in_=T1[:, g * 512:(g + 1) * 512], func=AF.Identity)

                    def finish(bbuf, bkey, cbuf, ckey, n):
                        S.op("act", A.activation, [bkey], [bkey], out=bbuf[:, 0:n], in_=bbuf[:, 0:n], func=AF.Sqrt, scale=-0.25, bias=0.25)
                        S.op("dve", V.tensor_tensor, [bkey, ckey], [ckey], out=cbuf[:, 0:n], in0=cbuf[:, 0:n], in1=bbuf[:, 0:n], op=ALU.mult)

                    def gates(d, src_bf, sbkeys, src_f, sfkeys, n, abuf, akey, bbuf, bkey, cbuf, ckey, fin=True):
                        wa = gw[:, (2 * d) * 128:(2 * d + 1) * 128]; wx = gw[:, (2 * d + 1) * 128:(2 * d + 2) * 128]
                        ba_c = 5 + 3 * d; bx_c = 6 + 3 * d
                        nseg = (n + 511) // 512
                        for s_ in range(nseg):
                            w_ = min(512, n - s_ * 512); sl = slice(s_ * 512, s_ * 512 + w_)
                            i = gcnt[0] % 2; gcnt[0] += 1
                            b0 = 2 * i; b1 = 2 * i + 1
                            tr = trb[i]; ti = tib[i]
                            sbk = sbkeys[s_]; sfk = sfkeys[s_]
                            S.op("pe", T.matmul, [("gw", p), sbk], [("ps", b0)], PS[b0][:, 0:w_], wa, src_bf[:, sl], start=True, stop=True)
                            S.op("pe", T.matmul, [("gw", p), sbk], [("ps", b1)], PS[b1][:, 0:w_], wx, src_bf[:, sl], start=True, stop=True)
                            S.op("act", A.activation, [("ps", b0), ("prh", p)], [("tr", i)], out=tr[:, 0:w_], in_=PS[b0][:, 0:w_], func=AF.Tanh,
                                 scale=0.5, bias=prh[:, ba_c:ba_c + 1])
                            S.op("act", A.activation, [("ps", b1), ("prh", p)], [("ti", i)], out=ti[:, 0:w_], in_=PS[b1][:, 0:w_], func=AF.Tanh,
                                 scale=0.5, bias=prh[:, bx_c:bx_c + 1])
                            S.op("act", A.activation, [("tr", i), ("cl", p, d)], [akey], out=abuf[:, sl], in_=tr[:, 0:w_], func=AF.Exp,
                                 scale=clt[:, 2 * d + 1:2 * d + 2], bias=clt[:, 2 * d + 1:2 * d + 2])
                            if s_ % 2 < CFG["a2_act"]:
                                S.op("act", A.activation, [("tr", i), ("cl", p, d)], [bkey], out=bbuf[:, sl], in_=tr[:, 0:w_], func=AF.Exp,
                                     scale=clt[:, 2 * d:2 * d + 1], bias=clt[:, 2 * d:2 * d + 1])
                            else:
                                S.op("dve", V.tensor_tensor, [akey], [bkey], out=bbuf[:, sl], in0=abuf[:, sl], in1=abuf[:, sl], op=ALU.mult)
                            S.op("dve", V.scalar_tensor_tensor, [("ti", i), sfk], [ckey], out=cbuf[:, sl], in0=ti[:, 0:w_], scalar=1.0,
                                 in1=src_f[:, sl], op0=ALU.add, op1=ALU.mult)
                        if fin:
                            finish(bbuf, bkey, cbuf, ckey, n)

                    def wset(idx):
                        return BAs[idx], BBs[idx], BCs[idx], ("BA", idx), ("BB", idx), ("BC", idx)

                    if CFG["lru_pair"]:
                        v_tiles(ch * 8, ch * 8 + 8)
                        cs = []
                        for d in range(2):
                            ws = wset((step[0] % 2) * 2 + d)
                            gates(d, cxbf, ["cxbf"], cx1, ["cx1"], CTX, ws[0], ws[3], ws[1], ws[4], ws[2], ws[5], fin=False)
                            cs.append(ws)
                        step[0] += 1
                        for d in range(2):
                            finish(cs[d][1], cs[d][4], cs[d][2], cs[d][5], CTX)
                        for d in range(2):
                            BA, BB, BC, ka, kb, kc_ = cs[d]
                            if d == 0:
                                S.op("dve", V.tensor_tensor_scan, [ka, kc_], [kb], out=BB[:, 0:CTX], data0=BA[:, 0:CTX], data1=BC[:, 0:CTX], initial=0.0,
                                     op0=ALU.mult, op1=ALU.add)
                                S.op("dve", V.tensor_copy, [kb], [("carry", d)], out=carry[:, 0:1], in_=BB[:, CTX - 1:CTX])
                            else:
                                S.op("dve", V.tensor_tensor_scan, [ka, kc_], [kb], out=rev(BB[:, 0:CTX], CTX), data0=rev(BA[:, 0:CTX], CTX),
                                     data1=rev(BC[:, 0:CTX], CTX), initial=0.0, op0=ALU.mult, op1=ALU.add)
                                S.op("dve", V.tensor_copy, [kb], [("carry", d)], out=carry[:, 1:2], in_=BB[:, 0:1])
                        for s4 in range(4):
                            qq = (s4, 3 - s4)
                            cs = []
                            for d in range(2):
                                q = qq[d]; c0 = q * QW
                                ws = wset((step[0] % 2) * 2 + d)
                                blks = [2 * q, 2 * q + 1]
                                gates(d, ucbf[:, c0:c0 + QW], [("ucbf", j) for j in blks], T1[:, c0:c0 + QW], [("T1", j) for j in blks], QW,
                                      ws[0], ws[3], ws[1], ws[4], ws[2], ws[5], fin=False)
                                cs.append(ws)
                            step[0] += 1
                            for d in range(2):
                                finish(cs[d][1], cs[d][4], cs[d][2], cs[d][5], QW)
                            for d in range(2):
                                q = qq[d]; c0 = q * QW
                                BA, BB, BC, ka, kb, kc_ = cs[d]
                                blks = [2 * q, 2 * q + 1]
                                t0keys = [("T0", pk, j) for j in blks]
                                first = s4 < 2
                                if d == 0:
                                    if first:
                                        S.op("dve", V.tensor_tensor_scan, [ka, kc_, ("carry", 0)], t0keys, out=T0[:, c0:c0 + QW], data0=BA[:], data1=BC[:],
                                             initial=carry[:, 0:1], op0=ALU.mult, op1=ALU.add)
                                        S.op("dve", V.tensor_copy, t0keys, [("carry", 0)], out=carry[:, 0:1], in_=T0[:, c0 + QW - 1:c0 + QW])
                                    else:
                                        S.op("dve", V.tensor_tensor_scan, [ka, kc_, ("carry", 0)], [kb], out=BB[:], data0=BA[:], data1=BC[:],
                                             initial=carry[:, 0:1], op0=ALU.mult, op1=ALU.add)
                                        S.op("dve", V.tensor_copy, [kb], [("carry", 0)], out=carry[:, 0:1], in_=BB[:, QW - 1:QW])
                                else:
                                    if first:
                                        S.op("dve", V.tensor_tensor_scan, [ka, kc_, ("carry", 1)], t0keys, out=rev(T0[:, c0:c0 + QW], QW), data0=rev(BA[:], QW),
                                             data1=rev(BC[:], QW), initial=carry[:, 1:2], op0=ALU.mult, op1=ALU.add)
                                        S.op("dve", V.tensor_copy, t0keys, [("carry", 1)], out=carry[:, 1:2], in_=T0[:, c0:c0 + 1])
                                    else:
                                        S.op("dve", V.tensor_tensor_scan, [ka, kc_, ("carry", 1)], [kb], out=rev(BB[:], QW), data0=rev(BA[:], QW),
                                             data1=rev(BC[:], QW), initial=carry[:, 1:2], op0=ALU.mult, op1=ALU.add)
                                        S.op("dve", V.tensor_copy, [kb], [("carry", 1)], out=carry[:, 1:2], in_=BB[:, 0:1])
                                if not first:
                                    S.op("dve", V.tensor_tensor, [kb] + t0keys, [ka], out=BA[:], in0=BB[:], in1=T0[:, c0:c0 + QW], op=ALU.add)
                                    szk = [("szl", pk, j) for j in blks]
                                    S.op("dve", V.tensor_tensor, [ka] + szk, szk, out=szl[:, c0:c0 + QW], in0=BA[:], in1=szl[:, c0:c0 + QW],
                                         op=ALU.mult)
                        for hq in range(2):
                            S.dma(mixD[(4 + ch) * 128:(5 + ch) * 128, hq * 2048:(hq + 1) * 2048], szl[:, hq * 2048:(hq + 1) * 2048],
                                  [("szl", pk, j) for j in range(4 * hq, 4 * hq + 4)], [])
                        if mid_hook is not None:
                            mid_hook()
                        return

                    for d in range(2):
                        v_tiles(ch * 8 + d * 4, ch * 8 + d * 4 + 4)
                        si = step[0] % 2; step[0] += 1
                        BA = BAs[si]; BB = BBs[si]; BC = BCs[si]
                        ka = ("BA", si); kb = ("BB", si); kc_ = ("BC", si)
                        gates(d, cxbf, ["cxbf"], cx1, ["cx1"], CTX, BA, ka, BB, kb, BC, kc_)
                        if d == 0:
                            S.op("dve", V.tensor_tensor_scan, [ka, kc_], [kb], out=BB[:, 0:CTX], data0=BA[:, 0:CTX], data1=BC[:, 0:CTX], initial=0.0,
                                 op0=ALU.mult, op1=ALU.add)
                            S.op("dve", V.tensor_copy, [kb], [("carry", d)], out=carry[:, 0:1], in_=BB[:, CTX - 1:CTX])
                        else:
                            S.op("dve", V.tensor_tensor_scan, [ka, kc_], [kb], out=rev(BB[:, 0:CTX], CTX), data0=rev(BA[:, 0:CTX], CTX),
                                 data1=rev(BC[:, 0:CTX], CTX), initial=0.0, op0=ALU.mult, op1=ALU.add)
                            S.op("dve", V.tensor_copy, [kb], [("carry", d)], out=carry[:, 1:2], in_=BB[:, 0:1])
                        quarters = [0, 1, 2, 3] if d == 0 else [3, 2, 1, 0]
                        for q in quarters:
                            c0 = q * QW
                            si = step[0] % 2; step[0] += 1
                            BA = BAs[si]; BB = BBs[si]; BC = BCs[si]
                            ka = ("BA", si); kb = ("BB", si); kc_ = ("BC", si)
                            blks = [2 * q, 2 * q + 1]
                            gates(d, ucbf[:, c0:c0 + QW], [("ucbf", j) for j in blks], T1[:, c0:c0 + QW], [("T1", j) for j in blks], QW,
                                  BA, ka, BB, kb, BC, kc_)
                            t0keys = [("T0", pk, j) for j in blks]
                            if d == 0:
                                S.op("dve", V.tensor_tensor_scan, [ka, kc_, ("carry", 0)], t0keys, out=T0[:, c0:c0 + QW], data0=BA[:], data1=BC[:],
                                     initial=carry[:, 0:1], op0=ALU.mult, op1=ALU.add)
                                S.op("dve", V.tensor_copy, t0keys, [("carry", 0)], out=carry[:, 0:1], in_=T0[:, c0 + QW - 1:c0 + QW])
                            else:
                                S.op("dve", V.tensor_tensor_scan, [ka, kc_, ("carry", 1)], [kb], out=rev(BB[:], QW), data0=rev(BA[:], QW),
                                     data1=rev(BC[:], QW), initial=carry[:, 1:2], op0=ALU.mult, op1=ALU.add)
                                S.op("dve", V.tensor_copy, [kb], [("carry", 1)], out=carry[:, 1:2], in_=BB[:, 0:1])
                                S.op("dve", V.tensor_tensor, [kb] + t0keys, [ka], out=BA[:], in0=BB[:], in1=T0[:, c0:c0 + QW], op=ALU.add)
                                szk = [("szl", pk, j) for j in blks]
                                S.op("dve", V.tensor_tensor, [ka] + szk, szk, out=szl[:, c0:c0 + QW], in0=BA[:], in1=szl[:, c0:c0 + QW],
                                     op=ALU.mult)
                                if q % 2 == 0:
                                    hq = q // 2
                                    S.dma(mixD[(4 + ch) * 128:(5 + ch) * 128, hq * 2048:(hq + 1) * 2048], szl[:, hq * 2048:(hq + 1) * 2048],
                                          [("szl", pk, j) for j in range(4 * hq, 4 * hq + 4)], [])
                        if d == 0 and mid_hook is not None:
                            mid_hook()

                step = [0]; gcnt = [0]
                proj_stage(0)
                for ch in range(4):
                    main_stage(ch, (lambda c=ch: proj_stage(c + 1)) if ch < 3 else None)
                S.flush()

        def att_phase():
            S.barrier()
            with ExitStack() as e2:
                qrot = sbt(e2, "qrot", [128, SEQ], BF16); qpl = sbt(e2, "qpl", [128, SEQ], BF16)
                krot = sbt(e2, "krot", [128, SEQ], BF16); sza = sbt(e2, "sza", [128, SEQ], BF16)
                vaug = sbt(e2, "vaug", [128, 32 * 256], BF16); mixc = sbt(e2, "mixc", [128, SEQ], BF16)
                kcn2 = [sbt(e2, "kcn%d" % i, [128, CTX], BF16) for i in range(2)]
                vcaug2 = [sbt(e2, "vcaug%d" % i, [128, 2 * 256], BF16) for i in range(2)]
                Ffull2 = [sbt(e2, "Ffull%d" % i, [128, 2 * FW], BF16) for i in range(2)]
                Fint2 = [sbt(e2, "Fint%d" % i, [128, 2 * FW], BF16) for i in range(2)]
                cst = [sbt(e2, "cst%d" % i, [128, 512], F32) for i in range(2)]
                snt = [sbt(e2, "snt%d" % i, [128, 512], F32) for i in range(2)]
                sqb = [sbt(e2, "sqb%d" % i, [128, 512], BF16) for i in range(CFG["n_qk"])]
                qsb = [sbt(e2, "qsb%d" % i, [128, 512], F32) for i in range(CFG["n_qk"])]
                rtb = [sbt(e2, "rtb%d" % i, [128, 512], F32) for i in range(CFG["n_qk"])]
                knbf = [sbt(e2, "knbf%d" % i, [128, 512], BF16) for i in range(CFG["n_qk"])]
                r1b = [sbt(e2, "r1b%d" % i, [128, 512], F32) for i in range(CFG["n_qk"])]
                r2b = [sbt(e2, "r2b%d" % i, [128, 512], F32) for i in range(CFG["n_qk"])]
                etb = [sbt(e2, "etb%d" % i, [128, 512], BF16) for i in range(CFG["n_et"])]
                ptb = [sbt(e2, "ptb%d" % i, [128, 512], BF16) for i in range(CFG["n_pt"])]
                PSX = PS + [e2.enter_context(nc.psum_tensor("psx_%d" % i, [128, 512], F32)) for i in range(2)]
                rdb = [sbt(e2, "rdb%d" % i, [128, 256], F32) for i in range(2)]

                mk = sbt(e2, "mk", [128, 2 * FW], BF16)
                for j_ in range(4):
                    S.dma(r1b[j_ % 2][:, 0:448], masks[:, j_ * 448:(j_ + 1) * 448], [], [("r1b", j_ % 2)])
                    S.op("dve", V.tensor_copy, [("r1b", j_ % 2)], ["mk"], out=mk[:, j_ * 448:(j_ + 1) * 448], in_=r1b[j_ % 2][:, 0:448])
                for i_ in range(2):
                    S.op("pool", G.memset, [], [("vcaug", i_)], vcaug2[i_][:], 1.0)
                gate_a = sbt(e2, "gate_a", [128, DM], F32); wob_a = sbt(e2, "wob_a", [128, DM], BF16)
                for pr in range(4):
                    att_body(pr, locals())
                    if pr == 0:
                        S.dma(gate_a[:], gateD, [], ["gate_a"])
                        for kc in range(8):
                            S.dma(wst[kc % 2][:], w_out[kc * 128:(kc + 1) * 128, :], [], [("wst", kc % 2)])
                            S.op("dve", V.tensor_tensor, [("wst", kc % 2), "gate_a"], ["wob_a"], out=wob_a[:], in0=wst[kc % 2][:], in1=gate_a[:], op=ALU.mult)
                            S.dma(woD[kc * 128:(kc + 1) * 128, :], wob_a[:], ["wob_a"], [])
                for kc in [4, 5, 6, 7, 0, 1, 2]:
                    S.dma(hT[:, kc * SEQ:(kc + 1) * SEQ], mixD[kc * 128:(kc + 1) * 128, :], [("mixD", kc)] if kc < 4 else [], ["hT"])
                S.flush()

        def att_body(pr, L):
            if True:
                if True:
                    pass
                par = pr % 2
                qrot = L["qrot"]; qpl = L["qpl"]; krot = L["krot"]; sza = L["sza"]; vaug = L["vaug"]; mixc = L["mixc"]; mk = L["mk"]
                kcn = L["kcn2"][par]; vcaug = L["vcaug2"][par]; Ffull = L["Ffull2"][par]; Fint = L["Fint2"][par]
                cst = L["cst"]; snt = L["snt"]; sqb = L["sqb"]; rtb = L["rtb"]; knbf = L["knbf"]; qsb = L["qsb"]
                r1b = L["r1b"]; r2b = L["r2b"]; etb = L["etb"]; ptb = L["ptb"]; PSX = L["PSX"]; rdb = L["rdb"]; ttb = L["rdb"]
                kFf = ("Ffull", par); kFi = ("Fint", par); kkc = ("kcn", par); kvc = ("vcaug", par)
                for hh in range(2):
                    for hf in range(2):
                        c0 = hf * 448
                        S.dma(r1b[hf][:, 0:448], rpbG[2 * pr + hh, :, c0:c0 + 448], [], [("r1b", hf)])
                        S.op("act", A.activation, [("r1b", hf)], [("r2b", hf)], out=r2b[hf][:, 0:448], in_=r1b[hf][:, 0:448], func=AF.Exp)
                        S.op("dve", V.tensor_tensor, [("r2b", hf), "mk"], [kFf], out=Ffull[:, hh * FW + c0:hh * FW + c0 + 448], in0=r2b[hf][:, 0:448],
                             in1=mk[:, c0:c0 + 448], op=ALU.mult)
                        S.op("dve", V.tensor_tensor, [("r2b", hf), "mk"], [kFi], out=Fint[:, hh * FW + c0:hh * FW + c0 + 448], in0=r2b[hf][:, 0:448],
                             in1=mk[:, FW + c0:FW + c0 + 448], op=ALU.mult)
                wq, wqk = load_wblock(128 * pr)
                wk, wkk = load_wblock(512 + 128 * pr)
                wv, wvk = load_wblock(1024 + 128 * pr)
                wz, wzk = load_wblock(1536 + 128 * pr)
                for g in range(8):
                    b = g % 2
                    proj_fm(wz, wzk, hT, SEQ, g * 512, 512, b)
                    S.op("act", A.activation, [("ps", b)], [("sza", g)], out=sza[:, g * 512:(g + 1) * 512], in_=PS[b][:, :], func=AF.Silu)

                cnt = [0]

                def qk_path(bank, n, gcol, plain_dst, pkey, rot_dst, rkey, ts):
                    i = cnt[0] % CFG["n_qk"]; cnt[0] += 1
                    S.op("act", A.activation, [("ps", bank)], [("qsb", i)], out=qsb[i][:, 0:n], in_=PS[bank][:, 0:n], func=AF.Identity)
                    S.op("act", A.activation, [("ps", bank)], [("sqb", i)], out=sqb[i][:, 0:n], in_=PS[bank][:, 0:n], func=AF.Square)
                    S.op("pe", T.matmul, ["cm", ("sqb", i)], [("ps", 2)], PS[2][:, 0:n], bones, sqb[i][:, 0:n], start=True, stop=True)
                    S.op("act", A.activation, [("ps", 2)], [("rtb", i)], out=rtb[i][:, 0:n], in_=PS[2][:, 0:n], func=AF.Ln, bias=EPS, scale=1.0)
                    S.op("act", A.activation, [("rtb", i)], [("rtb", i)], out=rtb[i][:, 0:n], in_=rtb[i][:, 0:n], func=AF.Exp, scale=-0.5)
                    if plain_dst is None:
                        plain_dst = knbf[i][:, 0:n]; pkey = ("knbf", i)
                    S.op("dve", V.scalar_tensor_tensor, [("qsb", i), "gq", ("rtb", i)], [pkey], out=plain_dst, in0=qsb[i][:, 0:n],
                         scalar=gq[:, gcol:gcol + 1], in1=rtb[i][:, 0:n], op0=ALU.mult, op1=ALU.mult)
                    if rot_dst is None:
                        return
                    S.op("pe", T.matmul, ["cm", pkey], [("ps", 2)], PS[2][:, 0:n], Rm, plain_dst, start=True, stop=True)
                    S.op("dve", V.tensor_tensor, [pkey, ("cst", ts)], [("r1b", i)], out=r1b[i][:, 0:n], in0=plain_dst, in1=cst[ts][:, 0:n], op=ALU.mult)
                    S.op("dve", V.tensor_tensor, [("ps", 2), ("snt", ts)], [("r2b", i)], out=r2b[i][:, 0:n], in0=PS[2][:, 0:n], in1=snt[ts][:, 0:n], op=ALU.mult)
                    if CFG["rope_add"] == "pool":
                        S.op("pool", G.tensor_tensor, [("r1b", i), ("r2b", i)], [rkey], out=rot_dst, in0=r1b[i][:, 0:n], in1=r2b[i][:, 0:n], op=ALU.add)
                    else:
                        S.op("dve", V.tensor_tensor, [("r1b", i), ("r2b", i)], [rkey], out=rot_dst, in0=r1b[i][:, 0:n], in1=r2b[i][:, 0:n], op=ALU.add)

                proj_fm(wk, wkk, hcT, CTX, 0, CTX, 0)
                qk_path(0, CTX, 1, kcn[:], kkc, None, None, 0)
                for t_ in range(2):
                    for kc in range(8):
                        S.op("pe", T.matmul, [wvk, "hT"], [("ps", 3)], PS[3][:, t_ * 128:(t_ + 1) * 128],
                             hcT[:, kc * CTX + t_ * 128: kc * CTX + (t_ + 1) * 128], wv[:, kc * 128:(kc + 1) * 128], start=(kc == 0), stop=(kc == 7))
                vc3 = vcaug[:].rearrange("p (t c) -> p t c", c=256)
                pv3 = PS[3][:, 0:256].rearrange("p (t c) -> p t c", c=128)
                S.op("act", A.activation, [("ps", 3)], [kvc], out=vc3[:, :, 0:64], in_=pv3[:, :, 0:64], func=AF.Identity)
                S.op("act", A.activation, [("ps", 3)], [kvc], out=vc3[:, :, 192:256], in_=pv3[:, :, 64:128], func=AF.Identity)

                def proj_group(g):
                    ts = g % 2
                    S.dma(cst[ts][:], cosT[:, g * 512:(g + 1) * 512], [], [("cst", ts)])
                    S.dma(snt[ts][:], sinT[:, g * 512:(g + 1) * 512], [], [("snt", ts)])
                    proj_fm(wq, wqk, hT, SEQ, g * 512, 512, 0)
                    qk_path(0, 512, 0, qpl[:, g * 512:(g + 1) * 512], ("qpl", g), qrot[:, g * 512:(g + 1) * 512], ("qrot", g), ts)
                    proj_fm(wk, wkk, hT, SEQ, g * 512, 512, 1)
                    qk_path(1, 512, 1, None, None, krot[:, g * 512:(g + 1) * 512], ("krot", g), ts)
                    S.dma(vaug[:, g * 1024:(g + 1) * 1024], vD[pr, :, g * 1024:(g + 1) * 1024], [], [("vaug", g)])

                sc = [0]

                def f3(Ft, hh, jj0):
                    base = Ft[:, hh * FW + jj0 * 64: hh * FW + jj0 * 64 + 256]
                    return bass.AP(base.tensor, base.offset, [[base.ap[0][0], 128], [-128, 2], [1, 256]])

                def att_group(qg):
                    R0 = 4 * qg
                    gq_ = qg // 2
                    qs = slice(R0 * 64, R0 * 64 + 256)
                    if qg == 0:
                        tl = [(a, 6 - a, Ffull, 0, 256) for a in (0, 2, 4, 6)]
                        pairs = [(0, 1), (2, 3)]
                    elif qg == 15:
                        tl = [(a, 6 - (a - 60), Ffull, 0, 256) for a in (56, 58, 60, 62)]
                        pairs = [(0, 1), (2, 3)]
                    else:
                        qr = [(0, 128), (0, 256), (0, 256), (0, 256), (64, 256), (192, 256)]
                        tl = [(R0 - 4 + 2 * i, 10 - 2 * i, Fint, qr[i][0], qr[i][1]) for i in range(6)]
                        pairs = [(1, 2), (3, 4), (0, 5)]
                    i2 = qg % 2
                    for hh in range(2):
                        pb = 64 * hh; ob = 6 + hh; po = PSX[ob][:, 0:256]
                        for pi, (iA, iB) in enumerate(pairs):
                            tA = tl[iA]; tB = tl[iB]
                            wA = tA[4] - tA[3]; wB = tB[4] - tB[3]
                            offs = [256 - wA, 256]
                            lo_ = 256 - wA; hi_ = 256 + wB
                            sb_ = CFG["score_banks"][sc[0] % len(CFG["score_banks"])]; ke = sc[0] % CFG["n_et"]; kp = sc[0] % CFG["n_pt"]; sc[0] += 1
                            for u, tt_ in enumerate((tA, tB)):
                                au, jj0, Ft, qa, qb = tt_
                                S.op("pe", T.matmul, [("krot", au // 8), ("qrot", gq_)], [("ps", sb_)], PSX[sb_][:, offs[u]:offs[u] + (qb - qa)],
                                     krot[pb:pb + 64, au * 64:au * 64 + 128], qrot[pb:pb + 64, R0 * 64 + qa:R0 * 64 + qb], start=True, stop=True)
                            S.op("act", A.activation, [("ps", sb_)], [("et", ke)], out=etb[ke][:, lo_:hi_], in_=PSX[sb_][:, lo_:hi_], func=AF.Exp, scale=0.125)
                            if wA == 256 and wB == 256 and tB[1] == tA[1] - 2:
                                S.op("dve", V.tensor_tensor, [("et", ke), kFf, kFi], [("ptl", kp)],
                                     out=ptb[kp][:].rearrange("p (t c) -> p t c", c=256), in0=etb[ke][:].rearrange("p (t c) -> p t c", c=256),
                                     in1=f3(tA[2], hh, tA[1]), op=ALU.mult)
                            else:
                                for u, tt_ in enumerate((tA, tB)):
                                    au, jj0, Ft, qa, qb = tt_
                                    w_ = qb - qa
                                    S.op("dve", V.tensor_tensor, [("et", ke), kFf, kFi], [("ptl", kp)], out=ptb[kp][:, offs[u]:offs[u] + w_],
                                         in0=etb[ke][:, offs[u]:offs[u] + w_],
                                         in1=Ft[:, hh * FW + jj0 * 64 + qa: hh * FW + jj0 * 64 + qb], op=ALU.mult)
                            for u, tt_ in enumerate((tA, tB)):
                                au, jj0, Ft, qa, qb = tt_
                                tix = au // 2
                                S.op("pe", T.matmul, [("vaug", au // 8), ("ptl", kp)], [("ps", ob)], PSX[ob][:, qa:qb],
                                     vaug[:, tix * 256 + hh * 128: tix * 256 + hh * 128 + 128], ptb[kp][:, offs[u]:offs[u] + (qb - qa)],
                                     start=(pi == 0 and u == 0), stop=False)
                        sb_ = CFG["score_banks"][sc[0] % len(CFG["score_banks"])]; kp = sc[0] % CFG["n_pt"]; sc[0] += 1
                        for j in range(2):
                            S.op("pe", T.matmul, [kkc, ("qpl", gq_)], [("ps", sb_)], PSX[sb_][:, j * 256:(j + 1) * 256], kcn[pb:pb + 64, j * 128:(j + 1) * 128],
                                 qpl[pb:pb + 64, qs], start=True, stop=True)
                        S.op("act", A.activation, [("ps", sb_)], [("ptl", kp)], out=ptb[kp][:], in_=PSX[sb_][:, :], func=AF.Exp, scale=0.125)
                        for j in range(2):
                            S.op("pe", T.matmul, [kvc, ("ptl", kp)], [("ps", ob)], PSX[ob][:, 0:256],
                                 vcaug[:, j * 256 + hh * 128: j * 256 + hh * 128 + 128], ptb[kp][:, j * 256:(j + 1) * 256], start=False, stop=(j == 1))
                        S.op("act", A.activation, [("ps", ob)], [("rd", i2, hh)], out=rdb[i2][pb:pb + 64, :], in_=po[64 - pb:128 - pb, 0:256], func=AF.Ln)
                        S.op("act", A.activation, [("rd", i2, hh)], [("rd", i2, hh)], out=rdb[i2][pb:pb + 64, :], in_=rdb[i2][pb:pb + 64, :],
                             func=AF.Exp, scale=-1.0)
                        S.op("dve", V.tensor_tensor, [("rd", i2, hh), ("sza", gq_)], [("rd", i2, hh)], out=rdb[i2][pb:pb + 64, :], in0=rdb[i2][pb:pb + 64, :],
                             in1=sza[pb:pb + 64, qs], op=ALU.mult)
                        S.op("dve", V.tensor_tensor, [("ps", ob), ("rd", i2, hh)], [("mixc", gq_)], out=mixc[pb:pb + 64, qs], in0=po[pb:pb + 64, 0:256],
                             in1=rdb[i2][pb:pb + 64, :], op=ALU.mult)

                def mix_out(g):
                    S.dma(mixD[pr * 128:(pr + 1) * 128, g * 512:(g + 1) * 512], mixc[:, g * 512:(g + 1) * 512], [("mixc", g)], [("mixD", pr)])

                proj_group(0)
                for g in range(1, 8):
                    proj_group(g)
                    att_group(2 * g - 2)
                    att_group(2 * g - 1)
                    mix_out(g - 1)
                att_group(14)
                att_group(15)
                mix_out(7)

        lru_phase()
        att_phase()

        S.barrier()
        with ExitStack() as e3:
            wo = sbt(e3, "wo", [128, 8 * DM], BF16)
            xtb = [sbt(e3, "xo%d" % i, [128, DM], F32) for i in range(6)]
            otb = [sbt(e3, "ot%d" % i, [128, DM], F32) for i in range(3)]
            hT3 = hT[:].rearrange("p (kc t) -> p kc t", kc=8)
            mixD3 = mixD.rearrange("(kc p) t -> p kc t", p=128)
            wo3 = wo[:].rearrange("p (kc n) -> p kc n", kc=8)
            woD3 = woD.rearrange("(kc p) n -> p kc n", p=128)
            for h4 in range(2):
                S.dma(wo3[:, h4 * 4:(h4 + 1) * 4, :], woD3[:, h4 * 4:(h4 + 1) * 4, :], [], [("wo", h4)])
            for h2 in range(2):
                S.dma(hT[:, 3 * SEQ + h2 * 2048:3 * SEQ + (h2 + 1) * 2048], mixD[384:512, h2 * 2048:(h2 + 1) * 2048], [], [("mx", h2)])
            for i in range(32):
                xt = xtb[i % 6]; xk = ("xo", i % 6)
                S.dma(xt[:], x[i * 128:(i + 1) * 128, :], [], [xk])
                ot = otb[i % 3]; ok = ("ot", i % 3)
                for hf in range(2):
                    b = (2 * i + hf) % 4
                    for kc in range(8):
                        S.op("pe", T.matmul, [("mx", i // 16), ("wo", kc // 4)], [("ps", b)], PS[b][:, :], hT[:, kc * SEQ + i * 128: kc * SEQ + (i + 1) * 128],
                             wo[:, kc * DM + hf * 512: kc * DM + (hf + 1) * 512], start=(kc == 0), stop=(kc == 7))
                    S.op("dve", V.tensor_tensor, [("ps", b), xk], [ok], out=ot[:, hf * 512:(hf + 1) * 512], in0=PS[b][:, :],
                         in1=xt[:, hf * 512:(hf + 1) * 512], op=ALU.add)
                S.dma(out[i * 128:(i + 1) * 128, :], ot[:], [ok], [])
            S.final_wait()
    return nc


def kernel(x, c, ctx, c_ctx, norm_g, w_mod, b_mod, w_in, w_out, q_norm_g, k_norm_g, rpb,
           conv_w, conv_b, lru_wa, lru_ba, lru_wx, lru_bx, lru_lam):
    f = lambda a: np.ascontiguousarray(np.asarray(a, dtype=np.float32))
    x, c, ctx, c_ctx = f(x), f(c), f(ctx), f(c_ctx)
    norm_g, w_mod, b_mod, w_in, w_out = f(norm_g), f(w_mod), f(b_mod), f(w_in), f(w_out)
    q_norm_g, k_norm_g, rpb = f(q_norm_g), f(k_norm_g), f(rpb)
    conv_w, conv_b, lru_wa, lru_ba, lru_wx, lru_bx, lru_lam = (f(conv_w), f(conv_b), f(lru_wa), f(lru_ba),
                                                              f(lru_wx), f(lru_bx), f(lru_lam))
    cosT, sinT, cm, sel, masks, dridx, dcidx = _host_consts()
    rpbG = np.ascontiguousarray(rpb[0][:, dridx, dcidx].reshape(8, 128, FW))
    gqk = np.stack([np.tile(q_norm_g[0], 2), np.tile(k_norm_g[0], 2)], axis=1).astype(np.float32)
    plru = np.zeros((4, 128, 16), np.float32)
    wbd = np.zeros((4, 128, 512), np.float32)
    for ch in range(4):
        sl = slice(ch * 128, (ch + 1) * 128)
        for j in range(4):
            plru[ch, :, j] = conv_w[0, j, sl]
        plru[ch, :, 4] = conv_b[0, sl]
        for d in range(2):
            plru[ch, :, 5 + 3 * d] = lru_ba[0, d, sl]
            plru[ch, :, 6 + 3 * d] = lru_bx[0, d, sl]
            plru[ch, :, 7 + 3 * d] = lru_lam[0, d, sl]
            for m, wsrc in enumerate((lru_wa, lru_wx)):
                col = (2 * d + m) * 128
                for blk in range(2):
                    wbd[ch, blk * 64:(blk + 1) * 64, col + blk * 64: col + (blk + 1) * 64] = wsrc[0, d, 2 * ch + blk]
    common = {
        "norm_g": norm_g[0:1], "w_mod": w_mod[0], "b_mod": b_mod[0:1], "w_in": w_in[0], "w_out": w_out[0],
        "gqk": gqk, "plru": plru, "wbd": wbd, "rpbG": rpbG, "cosT": cosT, "sinT": sinT, "cmat": cm, "sel": sel,
        "masks": masks,
    }
    in_maps = []
    for b in range(8):
        cvb = np.stack([c[b], c_ctx], axis=0)
        cvl = np.ascontiguousarray(cvb.reshape(2, 8, 128).transpose(2, 1, 0).reshape(128, 16))
        m = dict(common)
        m["x"] = x[b]; m["ctx"] = ctx[b]; m["cv"] = cvl
        in_maps.append(m)
    nc = build_nc()
    res = run_bass_kernel_spmd(nc, in_maps, core_ids=list(range(8)))
    return np.stack([np.asarray(r["out"], dtype=np.float32) for r in res.results], axis=0)
```

```python
import numpy as np
from contextlib import ExitStack
import concourse.bass as bass
import concourse.mybir as mybir
from concourse.bass_utils import run_bass_kernel_spmd

F32 = mybir.dt.float32
BF16 = mybir.dt.bfloat16
ALU = mybir.AluOpType
AF = mybir.ActivationFunctionType

CFG = {"score_banks": [3, 4, 5], "n_et": 3, "n_pt": 4, "vbank": 3, "window": 200, "rope_add": "pool", "n_qk": 2, "a2_act": 1, "conv_act": 0, "scan_cost": 2.2, "slack": 0.0, "vevac": "act", "lru_pair": 1, "n_xt": 4}
SEQ = 4096
DM = 1024
CTX = 256
EPS = 1e-6
NJ = 14
FW = NJ * 64


class _Op:
    __slots__ = ("eng", "fn", "args", "kwargs", "preds", "dur", "idx", "tab", "seq", "dsem", "dval", "fin")


_ACT_TAB = {}


def _act_tab(func):
    n = str(func)
    if "Exp" in n:
        return ("exp", "ln")
    if "Tanh" in n:
        return ("exp",)
    if "Ln" in n:
        return ("ln",)
    if "Sqrt" in n:
        return ("sqrt",)
    if "Silu" in n:
        return ("silu",)
    if "Sigmoid" in n:
        return ("sigmoid",)
    return None


class Sched:
    WINDOW = 48

    def __init__(self, nc, es):
        self.nc = nc
        self.engs = {"pe": nc.tensor, "act": nc.scalar, "dve": nc.vector, "pool": nc.gpsimd, "sp": nc.sync}
        self.csem = {e: es.enter_context(nc.semaphore("cs_" + e)) for e in ("pe", "act", "dve", "pool")}
        self.cnt = {e: 0 for e in self.csem}
        self.NDS = 12
        self.dsem = [es.enter_context(nc.semaphore("ds%d" % i)) for i in range(self.NDS)]
        self.dcnt = [0] * self.NDS
        self.dnext = 0
        self.seen = {e: {} for e in self.engs}
        self.ops = []
        self.lastw = {}
        self.readers = {}
        self.last_tab = "none"

    def _est(self, eng, args, kwargs, fn=None):
        out = kwargs.get("out", args[0] if args else None)
        try:
            n = out.free_size()
        except Exception:
            n = 512
        if eng == "pe":
            return 64.0 + n * 0.42
        if eng == "act":
            return max(200.0, 100.0 + n * 0.88)
        if eng == "dve":
            if "scan" in getattr(fn, "__name__", ""):
                return 100.0 + n * CFG["scan_cost"]
            return max(160.0, 60.0 + n * 1.1)
        if eng == "pool":
            return 250.0 + n * 4.5
        try:
            nb = out.nbytes()
        except Exception:
            nb = 65536
        return float(nb)

    def _add(self, eng, fn, r, w, args, kwargs):
        o = _Op()
        o.eng = eng; o.fn = fn; o.args = args; o.kwargs = kwargs; o.idx = len(self.ops)
        o.dur = self._est(eng, args, kwargs, fn)
        o.tab = _act_tab(kwargs.get("func")) if eng == "act" else None
        preds = {}
        for k in r:
            p = self.lastw.get(k)
            if p is not None:
                need = not (p.eng == eng and eng == "pe")
                preds[p.idx] = preds.get(p.idx, False) or need
        for k in w:
            p = self.lastw.get(k)
            if p is not None:
                need = not (p.eng == eng and eng == "pe")
                preds[p.idx] = preds.get(p.idx, False) or need
            for p in self.readers.get(k, ()):
                need = not (p.eng == eng and eng == "pe")
                preds[p.idx] = preds.get(p.idx, False) or need
        o.preds = preds
        for k in r:
            self.readers.setdefault(k, []).append(o)
        for k in w:
            self.lastw[k] = o
            self.readers[k] = []
        self.ops.append(o)
        return o

    def op(self, eng, fn, r, w, *args, **kwargs):
        return self._add(eng, fn, r, w, args, kwargs)

    def dma(self, out, in_, r, w, **kwargs):
        kwargs = dict(kwargs); kwargs["out"] = out; kwargs["in_"] = in_
        return self._add("sp", self.nc.sync.dma_start, r, w, (), kwargs)

    def _wait(self, eng, key, val):
        if self.seen[eng].get(key, 0) >= val:
            return
        sem = self.csem[key[1]] if key[0] == "c" else self.dsem[key[1]]
        self.engs[eng].wait_ge(sem, val)
        self.seen[eng][key] = val

    def flush(self):
        ops = self.ops
        if not ops:
            return
        pend = {e: [] for e in self.engs}
        for o in ops:
            o.fin = None
            pend[o.eng].append(o)
        pos = {e: 0 for e in self.engs}
        free = {e: 0.0 for e in self.engs}
        order = {e: [] for e in self.engs}
        done = {e: set() for e in self.engs}
        remaining = len(ops)
        last_tab = self.last_tab
        dma_free = 0.0
        succ = [[] for _ in ops]
        for o in ops:
            for pi in o.preds:
                succ[pi].append(o.idx)
        rank = [0.0] * len(ops)
        for o in reversed(ops):
            m = 0.0
            for si in succ[o.idx]:
                if rank[si] > m:
                    m = rank[si]
            rank[o.idx] = m + (o.dur if o.eng != "sp" else 2500.0)
        slack = CFG.get("slack", 0.0)
        while remaining:
            best = None
            for e in self.engs:
                lst = pend[e]; p0 = pos[e]
                cnt = 0; i = p0
                cands = []
                while i < len(lst) and cnt < CFG["window"]:
                    o = lst[i]; i += 1
                    if o.idx in done[e]:
                        continue
                    cnt += 1
                    rt = 0.0; ok = True
                    for pi in o.preds:
                        f = ops[pi].fin
                        if f is None:
                            ok = False; break
                        if ops[pi].eng != e:
                            f += 200.0
                        if f > rt:
                            rt = f
                    if not ok:
                        continue
                    st = max(free[e], rt)
                    pen = 0.0
                    if e == "act" and o.tab is not None and last_tab not in o.tab:
                        pen = 1400.0
                    cands.append((st + pen, o, st, pen))
                if not cands:
                    continue
                m = min(c[0] for c in cands)
                pick = None
                for c in cands:
                    if c[0] <= m + slack:
                        if pick is None or (rank[c[1].idx], -c[1].idx) > (rank[pick[1].idx], -pick[1].idx):
                            pick = c
                key = (pick[0], pick[1].idx)
                if best is None or key < best[0]:
                    best = (key, e, pick[1], pick[2], pick[3])
            _, e, o, st, pen = best
            if e == "sp":
                free[e] = st + 300.0
                xs = max(st + 300.0, dma_free)
                dma_free = xs + o.dur / 200.0
                o.fin = dma_free + 1800.0
            else:
                o.fin = st + pen + o.dur
                free[e] = o.fin
            if e == "act" and o.tab is not None and last_tab not in o.tab:
                last_tab = o.tab[0]
            order[e].append(o)
            done[e].add(o.idx)
            while pos[e] < len(pend[e]) and pend[e][pos[e]].idx in done[e]:
                pos[e] += 1
            remaining -= 1
        self.last_tab = last_tab
        for e in self.csem:
            c = self.cnt[e]
            for o in order[e]:
                c += 1; o.seq = c
        dn = self.dnext; dc = list(self.dcnt)
        for o in order["sp"]:
            i = dn % self.NDS; dn += 1
            dc[i] += 16
            o.dsem = i; o.dval = dc[i]
        for e in self.engs:
            for o in order[e]:
                reqs = {}
                for pi, need in o.preds.items():
                    if not need:
                        continue
                    p = ops[pi]
                    if p.eng == "sp":
                        k_ = ("d", p.dsem); v_ = p.dval
                    else:
                        k_ = ("c", p.eng); v_ = p.seq
                    if reqs.get(k_, 0) < v_:
                        reqs[k_] = v_
                for k_, v_ in reqs.items():
                    self._wait(e, k_, v_)
                if e == "sp":
                    i = o.dsem
                    if o.dval > 16:
                        self._wait(e, ("d", i), o.dval - 16)
                    ins = o.fn(*o.args, **o.kwargs)
                    ins.then_inc(self.dsem[i], 16)
                else:
                    ins = o.fn(*o.args, **o.kwargs)
                    ins.then_inc(self.csem[e], 1)
        for e in self.csem:
            self.cnt[e] += len(order[e])
        self.dnext = dn; self.dcnt = dc
        self.ops = []
        self.lastw = {}
        self.readers = {}

    def barrier(self):
        self.flush()
        for eng in self.engs:
            for e in self.csem:
                if e != eng and self.cnt[e] > 0:
                    self._wait(eng, ("c", e), self.cnt[e])
            for i in range(self.NDS):
                if self.dcnt[i] > 0:
                    self._wait(eng, ("d", i), self.dcnt[i])

    def final_wait(self):
        self.flush()
        for i in range(self.NDS):
            if self.dcnt[i] > 0:
                self._wait("sp", ("d", i), self.dcnt[i])


def _host_consts():
    p = np.arange(128)
    f = p % 64
    i16 = (f % 16).astype(np.float32)
    inv = (np.float32(10000.0) ** (-(i16) / np.float32(16.0))).astype(np.float32)
    t = np.arange(SEQ)
    r = (t // 64).astype(np.float32)
    c = (t % 64).astype(np.float32)
    pos = np.where((f < 32)[:, None], r[None, :], c[None, :]).astype(np.float32)
    ang = (pos * inv[:, None]).astype(np.float32)
    cosT = np.cos(ang).astype(np.float32)
    sgn = np.where((f % 32) < 16, -1.0, 1.0).astype(np.float32)
    sinT = (np.sin(ang) * sgn[:, None]).astype(np.float32)
    partner = np.where((f % 32) < 16, p + 16, p - 16)
    cm = np.zeros((128, 384), np.float32)
    cm[p, p] = 1.0
    cm[partner, 128 + p] = 1.0
    blk = (p[:, None] // 64) == (p[None, :] // 64)
    cm[:, 256:384] = blk.astype(np.float32) / 64.0
    sel = np.zeros((3, 384), np.float32)
    for rr in range(3):
        sel[rr, rr * 128:(rr + 1) * 128] = 1.0
    krl = (p // 64)[:, None, None]
    kc = (p % 64)[:, None, None]
    jj = np.arange(NJ)[None, :, None]
    qc = np.arange(64)[None, None, :]
    dr = 6 - jj + krl
    cs = np.clip(qc - 8, 0, 48)
    colm = (kc >= cs) & (kc < cs + 16)
    MC = np.broadcast_to(colm, (128, NJ, 64)).astype(np.float32)
    MI = (colm & (dr >= -4) & (dr <= 3)).astype(np.float32)
    masks = np.concatenate([MC.reshape(128, FW), MI.reshape(128, FW)], axis=1).astype(np.float32)
    dridx = np.broadcast_to(dr + 7, (128, NJ, 64))
    dcidx = np.broadcast_to(np.clip(kc - qc, -15, 15) + 15, (128, NJ, 64))
    return cosT, sinT, cm, sel, masks, dridx, dcidx


def build_nc():
    nc = bass.Bass("TRN2", target_bir_lowering=False)

    def din(name, shape, dt=F32):
        return nc.dram_tensor(name, list(shape), dt, kind="ExternalInput").ap()

    x = din("x", [SEQ, DM]); ctx = din("ctx", [CTX, DM]); cv = din("cv", [128, 16])
    norm_g = din("norm_g", [1, DM]); w_mod = din("w_mod", [DM, 3 * DM]); b_mod = din("b_mod", [1, 3 * DM])
    w_in = din("w_in", [DM, 3 * DM]); w_out = din("w_out", [DM, DM])
    gqk = din("gqk", [128, 2]); plru = din("plru", [4, 128, 16]); wbd = din("wbd", [4, 128, 512])
    rpbG = din("rpbG", [8, 128, FW]); cosT = din("cosT", [128, SEQ]); sinT = din("sinT", [128, SEQ])
    cmat = din("cmat", [128, 384]); sel = din("sel", [3, 384]); masks = din("masks", [128, 2 * FW])
    out = nc.dram_tensor("out", [SEQ, DM], F32, kind="ExternalOutput").ap()
    mixD = nc.dram_tensor("mixD", [DM, SEQ], BF16, kind="Internal").ap()
    vD = nc.dram_tensor("vD", [4, 128, 32 * 256], BF16, kind="Internal").ap()
    woD = nc.dram_tensor("woD", [DM, DM], BF16, kind="Internal").ap()
    gateD = nc.dram_tensor("gateD", [128, DM], F32, kind="Internal").ap()
    w_in_v = w_in.rearrange("(kc p) n -> p kc n", p=128)

    with ExitStack() as es:
        S = Sched(nc, es)
        V, A, G, T = nc.vector, nc.scalar, nc.gpsimd, nc.tensor

        uid = [0]

        def sbt(stack, name, shape, dt):
            uid[0] += 1
            return stack.enter_context(nc.sbuf_tensor("%s_%d" % (name, uid[0]), list(shape), dt))

        hT = sbt(es, "hT", [128, 8 * SEQ], BF16)
        hcT = sbt(es, "hcT", [128, 8 * CTX], BF16)
        cm = sbt(es, "cm", [128, 384], BF16)
        gq = sbt(es, "gq", [128, 2], F32)
        wst = [sbt(es, "wst%d" % i, [128, 1024], F32) for i in range(2)]
        wbf = [sbt(es, "wbf%d" % i, [128, 1024], BF16) for i in range(4)]
        PS = [es.enter_context(nc.psum_tensor("ps%d" % i, [128, 512], F32)) for i in range(6)]
        ident = cm[:, 0:128]; Rm = cm[:, 128:256]; bones = cm[:, 256:384]
        wslot = [0]

        def load_wblock(c0):
            s = wslot[0]; wslot[0] += 1
            st = wst[s % 2]; wb = wbf[s % 4]
            S.dma(st[:].rearrange("p (kc n) -> p kc n", kc=8), w_in_v[:, :, c0:c0 + 128], [], [("wst", s % 2)])
            S.op("dve", V.tensor_copy, [("wst", s % 2)], [("wbf", s % 4)], out=wb[:], in_=st[:])
            return wb, ("wbf", s % 4)

        def proj_fm(wb, wkey, src, src_w, col0, ncols, bank):
            for kc in range(8):
                S.op("pe", T.matmul, [wkey, "hT"], [("ps", bank)], PS[bank][:, 0:ncols],
                     wb[:, kc * 128:(kc + 1) * 128], src[:, kc * src_w + col0: kc * src_w + col0 + ncols],
                     start=(kc == 0), stop=(kc == 7))

        with ExitStack() as e1:
            PT = [e1.enter_context(nc.psum_tensor("pt%d" % i, [128, 1024], BF16)) for i in range(2)]
            cm_f = sbt(e1, "cm_f", [128, 384], F32); gate_bc = sbt(e1, "gate_bc", [128, DM], F32)
            cvt = sbt(e1, "cvt", [128, 16], F32); scv = sbt(e1, "scv", [128, 16], F32)
            M3 = sbt(e1, "M3", [3, 3 * DM], F32); b2 = sbt(e1, "b2", [2, 3 * DM], F32)
            sel_t = sbt(e1, "sel_t", [3, 384], F32)
            wmb = [sbt(e1, "wm%d" % i, [128, 1536], F32) for i in range(4)]
            cols = sbt(e1, "cols", [128, 48], F32); mcol = sbt(e1, "mcol", [128, 32], F32)
            wob = [sbt(e1, "wob%d" % i, [128, DM], BF16) for i in range(2)]
            xtb = [sbt(e1, "xt%d" % i, [128, DM], F32) for i in range(CFG["n_xt"])]
            hbb = [sbt(e1, "hb%d" % i, [128, DM], BF16) for i in range(2)]
            junk = sbt(e1, "junk", [128, DM], BF16)
            ssall = sbt(e1, "ssall", [128, 40], F32); rtall = sbt(e1, "rtall", [128, 40], F32)
            rsall = sbt(e1, "rsall", [128, 40], F32)

            S.dma(cvt[:], cv, [], ["cvt"]); S.dma(cm_f[:], cmat, [], ["cm_f"])
            S.dma(sel_t[:], sel, [], ["sel"])
            S.dma(gq[:], gqk, [], ["gq"])
            S.op("dve", V.tensor_copy, ["cm_f"], ["cm"], out=cm[:], in_=cm_f[:])
            S.op("act", A.activation, ["cvt"], ["scv"], out=scv[:], in_=cvt[:], func=AF.Silu)
            S.op("dve", V.memset, [], ["M3"], M3[:], 0.0)
            S.op("dve", V.memset, [], ["ssall"], ssall[:], 0.0)
            S.dma(M3[2:3, 0:DM], norm_g, [], ["M3"])
            S.dma(b2[0:1, :], b_mod, [], ["b2"]); S.dma(b2[1:2, :], b_mod, [], ["b2"])
            for kc in range(8):
                for hf in range(2):
                    wi = (2 * kc + hf) % 4
                    wm = wmb[wi]
                    S.dma(wm[:], w_mod[kc * 128:(kc + 1) * 128, hf * 1536:(hf + 1) * 1536], [], [("wm", wi)])
                    for n3 in range(3):
                        n = hf * 3 + n3
                        S.op("pe", T.matmul, ["scv", ("wm", wi)], [("ps", n)], PS[n][0:2, :],
                             scv[:, 2 * kc:2 * kc + 2], wm[:, n3 * 512:(n3 + 1) * 512], start=(kc == 0), stop=(kc == 7))
            for n in range(6):
                S.op("dve", V.tensor_tensor, [("ps", n), "b2"], ["M3"], out=M3[0:2, n * 512:(n + 1) * 512],
                     in0=PS[n][0:2, :], in1=b2[0:2, n * 512:(n + 1) * 512], op=ALU.add)
            for j in range(2):
                b = j
                S.op("pe", T.matmul, ["sel", "M3"], [("ps", b)], PS[b][:, :], sel_t[0:3, 0:128], M3[0:3, 2 * DM + j * 512: 2 * DM + (j + 1) * 512],
                     start=True, stop=True)
                S.op("act", A.activation, [("ps", b)], ["gate_bc"], out=gate_bc[:, j * 512:(j + 1) * 512], in_=PS[b][:, :], func=AF.Identity)
            S.dma(gateD, gate_bc[:], ["gate_bc"], [])
            i3 = bass.AP(sel_t[0:3, 0:1].tensor, sel_t[0:3, 0:1].offset, [[sel_t[0:3, 0:1].ap[0][0], 3], [128, 3]])
            for sidx in range(2):
                for kc in range(8):
                    c = (sidx * 8 + kc) * 3
                    S.op("pe", T.matmul, ["sel", "M3"], [("ps", 2)], PS[2][:, c:c + 3],
                         M3[0:3, sidx * DM + kc * 128: sidx * DM + (kc + 1) * 128], i3, start=True, stop=True)
            S.op("dve", V.tensor_copy, [("ps", 2)], ["cols"], out=cols[:, 0:48], in_=PS[2][:, 0:48])

            def colv(sidx, r):
                base = cols[:, sidx * 24 + r: sidx * 24 + r + 1]
                return bass.AP(base.tensor, base.offset, [[base.ap[0][0], 128], [3, 8]])

            S.op("dve", V.scalar_tensor_tensor, ["cols"], ["mcol"], out=mcol[:, 0:8], in0=colv(1, 0), scalar=1.0, in1=colv(0, 2), op0=ALU.add, op1=ALU.mult)
            S.op("dve", V.tensor_copy, ["cols"], ["mcol"], out=mcol[:, 8:16], in_=colv(0, 0))
            S.op("dve", V.scalar_tensor_tensor, ["cols"], ["mcol"], out=mcol[:, 16:24], in0=colv(1, 1), scalar=1.0, in1=colv(0, 2), op0=ALU.add, op1=ALU.mult)
            S.op("dve", V.tensor_copy, ["cols"], ["mcol"], out=mcol[:, 24:32], in_=colv(0, 1))

            def modulate(dst, dstw, c0, n, off, key, eng):
                for kc in range(8):
                    sl = dst[:, kc * dstw + c0: kc * dstw + c0 + n]
                    if True:
                        S.op("dve", V.tensor_scalar, [key, "mcol"], [key], out=sl, in0=sl, scalar1=mcol[:, off + kc:off + kc + 1],
                             scalar2=mcol[:, off + 8 + kc:off + 9 + kc], op0=ALU.mult, op1=ALU.add)
                    else:
                        S.op("act", A.activation, [key, "mcol"], [key], out=sl, in_=sl, func=AF.Identity, scale=mcol[:, off + kc:off + kc + 1],
                             bias=mcol[:, off + 8 + kc:off + 9 + kc])

            tiles = [(ctx[i * 128:(i + 1) * 128, :], hcT, CTX, i * 128, ("hcT", 0)) for i in range(2)]
            tiles += [(x[i * 128:(i + 1) * 128, :], hT, SEQ, i * 128, ("hT", i // 8)) for i in range(32)]
            for idx, (src, dst, dstw, col, hkey) in enumerate(tiles):
                xt = xtb[idx % CFG["n_xt"]]; xk = ("xt", idx % CFG["n_xt"])
                S.dma(xt[:], src, [], [xk])
                S.op("act", A.activation, [xk, "ssall"], ["junk", ("ss", idx)], out=junk[:], in_=xt[:], func=AF.Square,
                     accum_out=ssall[:, idx:idx + 1])
                S.op("act", A.activation, [("ss", idx)], [("rt", idx)], out=rtall[:, idx:idx + 1],
                     in_=ssall[:, idx:idx + 1], func=AF.Sqrt, scale=1.0 / DM, bias=EPS)
                S.op("dve", V.reciprocal, [("rt", idx)], [("rs", idx)], out=rsall[:, idx:idx + 1], in_=rtall[:, idx:idx + 1])
                hb = hbb[idx % 2]
                S.op("act", A.activation, [xk, ("rs", idx)], [("hb", idx % 2)], out=hb[:], in_=xt[:], func=AF.Identity, scale=rsall[:, idx:idx + 1])
                pt = PT[idx % 2]
                for kc in range(8):
                    S.op("pe", T.transpose, [("hb", idx % 2), "cm"], [("pt", idx % 2)], out=pt[:, kc * 128:(kc + 1) * 128],
                         in_=hb[:, kc * 128:(kc + 1) * 128], identity=ident)
                dst_ap = dst[:].rearrange("p (kc t) -> p kc t", kc=8)[:, :, col:col + 128]
                S.op("dve", V.tensor_copy, [("pt", idx % 2)], [hkey], out=dst_ap, in_=pt[:].rearrange("p (kc t) -> p kc t", kc=8))
                if idx == 1:
                    modulate(hcT, CTX, 0, CTX, 16, ("hcT", 0), 0)
                elif idx >= 2 and (idx - 2) % 8 == 7:
                    g4 = (idx - 2) // 8
                    modulate(hT, SEQ, g4 * 1024, 1024, 0, ("hT", g4), g4)
            S.flush()

        def lru_phase():
            S.barrier()
            with ExitStack() as e2:
                NB = 1 if CFG["lru_pair"] else 2
                NWS = 4 if CFG["lru_pair"] else 2
                T0s = [sbt(e2, "T0_%d" % i, [128, SEQ], F32) for i in range(NB)]
                szls = [sbt(e2, "szl_%d" % i, [128, SEQ], BF16) for i in range(NB)]
                T1 = sbt(e2, "T1", [128, SEQ], F32)
                ucbf = sbt(e2, "ucbf", [128, SEQ], BF16)
                QW = 1024
                BAs = [sbt(e2, "BA%d" % i, [128, QW], F32) for i in range(NWS)]
                BBs = [sbt(e2, "BB%d" % i, [128, QW], F32) for i in range(NWS)]
                BCs = [sbt(e2, "BC%d" % i, [128, QW], F32) for i in range(NWS)]
                trb = [sbt(e2, "trb%d" % i, [128, 512], F32) for i in range(2)]
                tib = [sbt(e2, "tib%d" % i, [128, 512], F32) for i in range(2)]
                cx0 = sbt(e2, "cx0", [128, CTX], F32); cx1 = sbt(e2, "cx1", [128, CTX], F32)
                cxbf = sbt(e2, "cxbf", [128, CTX], BF16)
                prms = [sbt(e2, "prm%d" % i, [128, 16], F32) for i in range(2)]
                prhs = [sbt(e2, "prh%d" % i, [128, 16], F32) for i in range(2)]
                clts = [sbt(e2, "clt%d" % i, [128, 8], F32) for i in range(2)]
                gw_f = sbt(e2, "gw_f", [128, 512], F32)
                gws = [sbt(e2, "gw%d" % i, [128, 512], BF16) for i in range(2)]
                carry = sbt(e2, "carry", [128, 2], F32)
                wvf = sbt(e2, "wvf", [128, 8 * 512], BF16)
                wvf3 = wvf[:].rearrange("p (kc n) -> p kc n", kc=8)
                vstage = sbt(e2, "vstage", [128, 1024], BF16)
                psv = e2.enter_context(nc.psum_tensor("psv", [128, 512], F32))
                S.op("pool", G.memset, [], ["vstage"], vstage[:], 1.0)
                for pr_ in range(4):
                    S.dma(wst[pr_ % 2][:].rearrange("p (kc n) -> p kc n", kc=8), w_in_v[:, :, 1024 + 128 * pr_:1024 + 128 * (pr_ + 1)], [], [("wst", pr_ % 2)])
                    S.op("dve", V.tensor_copy, [("wst", pr_ % 2)], ["wvf"], out=wvf3[:, :, pr_ * 128:(pr_ + 1) * 128],
                         in_=wst[pr_ % 2][:].rearrange("p (kc n) -> p kc n", kc=8))
                vD_t = vD.rearrange("r p c -> p r c")
                vst3 = vstage[:].rearrange("p (r c) -> p r c", r=4)
                vb_ = vstage[:]
                vst4 = bass.AP(vb_.tensor, vb_.offset, [[vb_.ap[0][0], 128], [256, 4], [192, 2], [1, 64]])
                pvb_ = psv[:, :]
                psv4 = bass.AP(pvb_.tensor, pvb_.offset, [[pvb_.ap[0][0], 128], [128, 4], [64, 2], [1, 64]])

                def v_tiles(t0, t1):
                    for t_ in range(t0, t1):
                        for kc in range(8):
                            S.op("pe", T.matmul, ["wvf", "hT"], ["psv"], psv[:, :],
                                 hT[:, kc * SEQ + t_ * 128: kc * SEQ + (t_ + 1) * 128], wvf[:, kc * 512:(kc + 1) * 512],
                                 start=(kc == 0), stop=(kc == 7))
                        if CFG["vevac"] == "act":
                            S.op("act", A.activation, ["psv"], ["vstage"], out=vst4, in_=psv4, func=AF.Identity)
                        else:
                            S.op("dve", V.tensor_copy, ["psv"], ["vstage"], out=vst4, in_=psv4)
                        S.dma(vD_t[:, :, t_ * 256:(t_ + 1) * 256], vst3, ["vstage"], [])

                def rev(ap2d, n):
                    pstep = ap2d.ap[0][0]
                    npart = ap2d.ap[0][1]
                    return bass.AP(ap2d.tensor, ap2d.offset + n - 1, [[pstep, npart], [-1, n]])

                def proj_stage(ch):
                    p = ch % 2
                    prm = prms[p]; prh = prhs[p]; clt = clts[p]; gw = gws[p]; pk = p % NB; T0 = T0s[pk]; szl = szls[pk]
                    S.dma(prm[:], plru[ch], [], [("prm", p)])
                    S.dma(gw_f[:], wbd[ch], [], ["gw_f"])
                    S.op("dve", V.tensor_copy, ["gw_f"], [("gw", p)], out=gw[:], in_=gw_f[:])
                    S.op("dve", V.tensor_scalar, [("prm", p)], [("prh", p)], out=prh[:], in0=prm[:], scalar1=0.5, scalar2=None, op0=ALU.mult)
                    for d in range(2):
                        lc = 7 + 3 * d
                        S.op("act", A.activation, [("prm", p)], [("clt", p, 4 + d)], out=clt[:, 4 + d:5 + d], in_=prm[:, lc:lc + 1], func=AF.Exp, scale=-1.0)
                        S.op("act", A.activation, [("clt", p, 4 + d)], [("clt", p, 6 + d)], out=clt[:, 6 + d:7 + d], in_=clt[:, 4 + d:5 + d], func=AF.Ln, bias=1.0)
                        S.op("dve", V.tensor_scalar, [("clt", p, 6 + d)], [("cl", p, d)], out=clt[:, 2 * d:2 * d + 1], in0=clt[:, 6 + d:7 + d],
                             scalar1=-8.0, scalar2=None, op0=ALU.mult)
                        S.op("dve", V.tensor_scalar, [("clt", p, 6 + d)], [("cl", p, d)], out=clt[:, 2 * d + 1:2 * d + 2], in0=clt[:, 6 + d:7 + d],
                             scalar1=-4.0, scalar2=None, op0=ALU.mult)
                    wu, wuk = load_wblock(2048 + 128 * ch)
                    wz, wzk = load_wblock(2560 + 128 * ch)
                    proj_fm(wu, wuk, hcT, CTX, 0, CTX, 5)
                    S.op("act", A.activation, [("ps", 5)], ["cx0"], out=cx0[:], in_=PS[5][:, 0:CTX], func=AF.Identity)
                    for g in range(8):
                        b = 4 + g % 2
                        proj_fm(wu, wuk, hT, SEQ, g * 512, 512, b)
                        S.op("act", A.activation, [("ps", b)], [("T0", pk, g)], out=T0[:, g * 512:(g + 1) * 512], in_=PS[b][:, :], func=AF.Identity)
                    for g in range(8):
                        b = 4 + g % 2
                        proj_fm(wz, wzk, hT, SEQ, g * 512, 512, b)
                        S.op("act", A.activation, [("ps", b)], [("szl", pk, g)], out=szl[:, g * 512:(g + 1) * 512], in_=PS[b][:, :], func=AF.Silu)

                def main_stage(ch, mid_hook):
                    p = ch % 2
                    prm = prms[p]; prh = prhs[p]; clt = clts[p]; gw = gws[p]; pk = p % NB; T0 = T0s[pk]; szl = szls[pk]
                    kprm = ("prm", p)

                    def conv(src, skeys, dst, dkey, lo, hi, n):
                        rk = list(skeys) + [kprm]
                        if CFG["conv_act"] and n > CTX:
                            S.op("act", A.activation, rk, [dkey], out=dst[:, lo:hi], in_=src[:, lo:hi], func=AF.Identity, scale=prm[:, 1:2], bias=prm[:, 4:5])
                        else:
                            S.op("dve", V.tensor_scalar, rk, [dkey], out=dst[:, lo:hi], in0=src[:, lo:hi], scalar1=prm[:, 1:2],
                                 scalar2=prm[:, 4:5], op0=ALU.mult, op1=ALU.add)
                        l0 = max(lo, 1)
                        S.op("dve", V.scalar_tensor_tensor, rk + [dkey], [dkey], out=dst[:, l0:hi], in0=src[:, l0 - 1:hi - 1],
                             scalar=prm[:, 0:1], in1=dst[:, l0:hi], op0=ALU.mult, op1=ALU.add)
                        h2 = min(hi, n - 1)
                        S.op("dve", V.scalar_tensor_tensor, rk + [dkey], [dkey], out=dst[:, lo:h2], in0=src[:, lo + 1:h2 + 1],
                             scalar=prm[:, 2:3], in1=dst[:, lo:h2], op0=ALU.mult, op1=ALU.add)
                        h3 = min(hi, n - 2)
                        S.op("dve", V.scalar_tensor_tensor, rk + [dkey], [dkey], out=dst[:, lo:h3], in0=src[:, lo + 2:h3 + 2],
                             scalar=prm[:, 3:4], in1=dst[:, lo:h3], op0=ALU.mult, op1=ALU.add)

                    conv(cx0, ["cx0"], cx1, "cx1", 0, CTX, CTX)
                    S.op("act", A.activation, ["cx1"], ["cxbf"], out=cxbf[:], in_=cx1[:], func=AF.Identity)
                    for g in range(8):
                        sk = [("T0", pk, j) for j in (g - 1, g, g + 1) if 0 <= j < 8]
                        conv(T0, sk, T1, ("T1", g), g * 512, (g + 1) * 512, SEQ)
                        S.op("act", A.activation, [("T1", g)], [("ucbf", g)], out=ucbf[:, g * 512:(g + 1) * 512], in_=T1[:, g * 512:(g + 1) * 512], func=AF.Identity)

                    def finish(bbuf, bkey, cbuf, ckey, n):
                        S.op("act", A.activation, [bkey], [bkey], out=bbuf[:, 0:n], in_=bbuf[:, 0:n], func=AF.Sqrt, scale=-0.25, bias=0.25)
                        S.op("dve", V.tensor_tensor, [bkey, ckey], [ckey], out=cbuf[:, 0:n], in0=cbuf[:, 0:n], in1=bbuf[:, 0:n], op=ALU.mult)

                    def gates(d, src_bf, sbkeys, src_f, sfkeys, n, abuf, akey, bbuf, bkey, cbuf, ckey, fin=True):
                        wa = gw[:, (2 * d) * 128:(2 * d + 1) * 128]; wx = gw[:, (2 * d + 1) * 128:(2 * d + 2) * 128]
                        ba_c = 5 + 3 * d; bx_c = 6 + 3 * d
                        nseg = (n + 511) // 512
                        for s_ in range(nseg):
                            w_ = min(512, n - s_ * 512); sl = slice(s_ * 512, s_ * 512 + w_)
                            i = gcnt[0] % 2; gcnt[0] += 1
                            b0 = 2 * i; b1 = 2 * i + 1
                            tr = trb[i]; ti = tib[i]
                            sbk = sbkeys[s_]; sfk = sfkeys[s_]
                            S.op("pe", T.matmul, [("gw", p), sbk], [("ps", b0)], PS[b0][:, 0:w_], wa, src_bf[:, sl], start=True, stop=True)
                            S.op("pe", T.matmul, [("gw", p), sbk], [("ps", b1)], PS[b1][:, 0:w_], wx, src_bf[:, sl], start=True, stop=True)
                            S.op("act", A.activation, [("ps", b0), ("prh", p)], [("tr", i)], out=tr[:, 0:w_], in_=PS[b0][:, 0:w_], func=AF.Tanh,
                                 scale=0.5, bias=prh[:, ba_c:ba_c + 1])
                            S.op("act", A.activation, [("ps", b1), ("prh", p)], [("ti", i)], out=ti[:, 0:w_], in_=PS[b1][:, 0:w_], func=AF.Tanh,
                                 scale=0.5, bias=prh[:, bx_c:bx_c + 1])
                            S.op("act", A.activation, [("tr", i), ("cl", p, d)], [akey], out=abuf[:, sl], in_=tr[:, 0:w_], func=AF.Exp,
                                 scale=clt[:, 2 * d + 1:2 * d + 2], bias=clt[:, 2 * d + 1:2 * d + 2])
                            if s_ % 2 < CFG["a2_act"]:
                                S.op("act", A.activation, [("tr", i), ("cl", p, d)], [bkey], out=bbuf[:, sl], in_=tr[:, 0:w_], func=AF.Exp,
                                     scale=clt[:, 2 * d:2 * d + 1], bias=clt[:, 2 * d:2 * d + 1])
                            else:
                                S.op("dve", V.tensor_tensor, [akey], [bkey], out=bbuf[:, sl], in0=abuf[:, sl], in1=abuf[:, sl], op=ALU.mult)
                            S.op("dve", V.scalar_tensor_tensor, [("ti", i), sfk], [ckey], out=cbuf[:, sl], in0=ti[:, 0:w_], scalar=1.0,
                                 in1=src_f[:, sl], op0=ALU.add, op1=ALU.mult)
                        if fin:
                            finish(bbuf, bkey, cbuf, ckey, n)

                    def wset(idx):
                        return BAs[idx], BBs[idx], BCs[idx], ("BA", idx), ("BB", idx), ("BC", idx)

                    if CFG["lru_pair"]:
                        v_tiles(ch * 8, ch * 8 + 8)
                        cs = []
                        for d in range(2):
                            ws = wset((step[0] % 2) * 2 + d)
                            gates(d, cxbf, ["cxbf"], cx1, ["cx1"], CTX, ws[0], ws[3], ws[1], ws[4], ws[2], ws[5], fin=False)
                            cs.append(ws)
                        step[0] += 1
                        for d in range(2):
                            finish(cs[d][1], cs[d][4], cs[d][2], cs[d][5], CTX)
                        for d in range(2):
                            BA, BB, BC, ka, kb, kc_ = cs[d]
                            if d == 0:
                                S.op("dve", V.tensor_tensor_scan, [ka, kc_], [kb], out=BB[:, 0:CTX], data0=BA[:, 0:CTX], data1=BC[:, 0:CTX], initial=0.0,
                                     op0=ALU.mult, op1=ALU.add)
                                S.op("dve", V.tensor_copy, [kb], [("carry", d)], out=carry[:, 0:1], in_=BB[:, CTX - 1:CTX])
                            else:
                                S.op("dve", V.tensor_tensor_scan, [ka, kc_], [kb], out=rev(BB[:, 0:CTX], CTX), data0=rev(BA[:, 0:CTX], CTX),
                                     data1=rev(BC[:, 0:CTX], CTX), initial=0.0, op0=ALU.mult, op1=ALU.add)
                                S.op("dve", V.tensor_copy, [kb], [("carry", d)], out=carry[:, 1:2], in_=BB[:, 0:1])
                        for s4 in range(4):
                            qq = (s4, 3 - s4)
                            cs = []
                            for d in range(2):
                                q = qq[d]; c0 = q * QW
                                ws = wset((step[0] % 2) * 2 + d)
                                blks = [2 * q, 2 * q + 1]
                                gates(d, ucbf[:, c0:c0 + QW], [("ucbf", j) for j in blks], T1[:, c0:c0 + QW], [("T1", j) for j in blks], QW,
                                      ws[0], ws[3], ws[1], ws[4], ws[2], ws[5], fin=False)
                                cs.append(ws)
                            step[0] += 1
                            for d in range(2):
                                finish(cs[d][1], cs[d][4], cs[d][2], cs[d][5], QW)
                            for d in range(2):
                                q = qq[d]; c0 = q * QW
                                BA, BB, BC, ka, kb, kc_ = cs[d]
                                blks = [2 * q, 2 * q + 1]
                                t0keys = [("T0", pk, j) for j in blks]
                                first = s4 < 2
                                if d == 0:
                                    if first:
                                        S.op("dve", V.tensor_tensor_scan, [ka, kc_, ("carry", 0)], t0keys, out=T0[:, c0:c0 + QW], data0=BA[:], data1=BC[:],
                                             initial=carry[:, 0:1], op0=ALU.mult, op1=ALU.add)
                                        S.op("dve", V.tensor_copy, t0keys, [("carry", 0)], out=carry[:, 0:1], in_=T0[:, c0 + QW - 1:c0 + QW])
                                    else:
                                        S.op("dve", V.tensor_tensor_scan, [ka, kc_, ("carry", 0)], [kb], out=BB[:], data0=BA[:], data1=BC[:],
                                             initial=carry[:, 0:1], op0=ALU.mult, op1=ALU.add)
                                        S.op("dve", V.tensor_copy, [kb], [("carry", 0)], out=carry[:, 0:1], in_=BB[:, QW - 1:QW])
                                else:
                                    if first:
                                        S.op("dve", V.tensor_tensor_scan, [ka, kc_, ("carry", 1)], t0keys, out=rev(T0[:, c0:c0 + QW], QW), data0=rev(BA[:], QW),
                                             data1=rev(BC[:], QW), initial=carry[:, 1:2], op0=ALU.mult, op1=ALU.add)
                                        S.op("dve", V.tensor_copy, t0keys, [("carry", 1)], out=carry[:, 1:2], in_=T0[:, c0:c0 + 1])
                                    else:
                                        S.op("dve", V.tensor_tensor_scan, [ka, kc_, ("carry", 1)], [kb], out=rev(BB[:], QW), data0=rev(BA[:], QW),
                                             data1=rev(BC[:], QW), initial=carry[:, 1:2], op0=ALU.mult, op1=ALU.add)
                                        S.op("dve", V.tensor_copy, [kb], [("carry", 1)], out=carry[:, 1:2], in_=BB[:, 0:1])
                                if not first:
                                    S.op("dve", V.tensor_tensor, [kb] + t0keys, [ka], out=BA[:], in0=BB[:], in1=T0[:, c0:c0 + QW], op=ALU.add)
                                    szk = [("szl", pk, j) for j in blks]
                                    S.op("dve", V.tensor_tensor, [ka] + szk, szk, out=szl[:, c0:c0 + QW], in0=BA[:], in1=szl[:, c0:c0 + QW],
                                         op=ALU.mult)
                        for hq in range(2):
                            S.dma(mixD[(4 + ch) * 128:(5 + ch) * 128, hq * 2048:(hq + 1) * 2048], szl[:, hq * 2048:(hq + 1) * 2048],
                                  [("szl", pk, j) for j in range(4 * hq, 4 * hq + 4)], [])
                        if mid_hook is not None:
                            mid_hook()
                        return

                    for d in range(2):
                        v_tiles(ch * 8 + d * 4, ch * 8 + d * 4 + 4)
                        si = step[0] % 2; step[0] += 1
                        BA = BAs[si]; BB = BBs[si]; BC = BCs[si]
                        ka = ("BA", si); kb = ("BB", si); kc_ = ("BC", si)
                        gates(d, cxbf, ["cxbf"], cx1, ["cx1"], CTX, BA, ka, BB, kb, BC, kc_)
                        if d == 0:
                            S.op("dve", V.tensor_tensor_scan, [ka, kc_], [kb], out=BB[:, 0:CTX], data0=BA[:, 0:CTX], data1=BC[:, 0:CTX], initial=0.0,
                                 op0=ALU.mult, op1=ALU.add)
                            S.op("dve", V.tensor_copy, [kb], [("carry", d)], out=carry[:, 0:1], in_=BB[:, CTX - 1:CTX])
                        else:
                            S.op("dve", V.tensor_tensor_scan, [ka, kc_], [kb], out=rev(BB[:, 0:CTX], CTX), data0=rev(BA[:, 0:CTX], CTX),
                                 data1=rev(BC[:, 0:CTX], CTX), initial=0.0, op0=ALU.mult, op1=ALU.add)
                            S.op("dve", V.tensor_copy, [kb], [("carry", d)], out=carry[:, 1:2], in_=BB[:, 0:1])
                        quarters = [0, 1, 2, 3] if d == 0 else [3, 2, 1, 0]
                        for q in quarters:
                            c0 = q * QW
                            si = step[0] % 2; step[0] += 1
                            BA = BAs[si]; BB = BBs[si]; BC = BCs[si]
                            ka = ("BA", si); kb = ("BB", si); kc_ = ("BC", si)
                            blks = [2 * q, 2 * q + 1]
                            gates(d, ucbf[:, c0:c0 + QW], [("ucbf", j) for j in blks], T1[:, c0:c0 + QW], [("T1", j) for j in blks], QW,
                                  BA, ka, BB, kb, BC, kc_)
                            t0keys = [("T0", pk, j) for j in blks]
                            if d == 0:
                                S.op("dve", V.tensor_tensor_scan, [ka, kc_, ("carry", 0)], t0keys, out=T0[:, c0:c0 + QW], data0=BA[:], data1=BC[:],
                                     initial=carry[:, 0:1], op0=ALU.mult, op1=ALU.add)
                                S.op("dve", V.tensor_copy, t0keys, [("carry", 0)], out=carry[:, 0:1], in_=T0[:, c0 + QW - 1:c0 + QW])
                            else:
                                S.op("dve", V.tensor_tensor_scan, [ka, kc_, ("carry", 1)], [kb], out=rev(BB[:], QW), data0=rev(BA[:], QW),
                                     data1=rev(BC[:], QW), initial=carry[:, 1:2], op0=ALU.mult, op1=ALU.add)
                                S.op("dve", V.tensor_copy, [kb], [("carry", 1)], out=carry[:, 1:2], in_=BB[:, 0:1])
                                S.op("dve", V.tensor_tensor, [kb] + t0keys, [ka], out=BA[:], in0=BB[:], in1=T0[:, c0:c0 + QW], op=ALU.add)
                                szk = [("szl", pk, j) for j in blks]
                                S.op("dve", V.tensor_tensor, [ka] + szk, szk, out=szl[:, c0:c0 + QW], in0=BA[:], in1=szl[:, c0:c0 + QW],
                                     op=ALU.mult)
                                if q % 2 == 0:
                                    hq = q // 2
                                    S.dma(mixD[(4 + ch) * 128:(5 + ch) * 128, hq * 2048:(hq + 1) * 2048], szl[:, hq * 2048:(hq + 1) * 2048],
                                          [("szl", pk, j) for j in range(4 * hq, 4 * hq + 4)], [])
                        if d == 0 and mid_hook is not None:
                            mid_hook()

                step = [0]; gcnt = [0]
                proj_stage(0)
                for ch in range(4):
                    main_stage(ch, (lambda c=ch: proj_stage(c + 1)) if ch < 3 else None)
                S.flush()

        def att_phase():
            S.barrier()
            with ExitStack() as e2:
                qrot = sbt(e2, "qrot", [128, SEQ], BF16); qpl = sbt(e2, "qpl", [128, SEQ], BF16)
                krot = sbt(e2, "krot", [128, SEQ], BF16); sza = sbt(e2, "sza", [128, SEQ], BF16)
                vaug = sbt(e2, "vaug", [128, 32 * 256], BF16); mixc = sbt(e2, "mixc", [128, SEQ], BF16)
                kcn2 = [sbt(e2, "kcn%d" % i, [128, CTX], BF16) for i in range(2)]
                vcaug2 = [sbt(e2, "vcaug%d" % i, [128, 2 * 256], BF16) for i in range(2)]
                Ffull2 = [sbt(e2, "Ffull%d" % i, [128, 2 * FW], BF16) for i in range(2)]
                Fint2 = [sbt(e2, "Fint%d" % i, [128, 2 * FW], BF16) for i in range(2)]
                cst = [sbt(e2, "cst%d" % i, [128, 512], F32) for i in range(2)]
                snt = [sbt(e2, "snt%d" % i, [128, 512], F32) for i in range(2)]
                sqb = [sbt(e2, "sqb%d" % i, [128, 512], BF16) for i in range(CFG["n_qk"])]
                qsb = [sbt(e2, "qsb%d" % i, [128, 512], F32) for i in range(CFG["n_qk"])]
                rtb = [sbt(e2, "rtb%d" % i, [128, 512], F32) for i in range(CFG["n_qk"])]
                knbf = [sbt(e2, "knbf%d" % i, [128, 512], BF16) for i in range(CFG["n_qk"])]
                r1b = [sbt(e2, "r1b%d" % i, [128, 512], F32) for i in range(CFG["n_qk"])]
                r2b = [sbt(e2, "r2b%d" % i, [128, 512], F32) for i in range(CFG["n_qk"])]
                etb = [sbt(e2, "etb%d" % i, [128, 512], BF16) for i in range(CFG["n_et"])]
                ptb = [sbt(e2, "ptb%d" % i, [128, 512], BF16) for i in range(CFG["n_pt"])]
                PSX = PS + [e2.enter_context(nc.psum_tensor("psx_%d" % i, [128, 512], F32)) for i in range(2)]
                rdb = [sbt(e2, "rdb%d" % i, [128, 256], F32) for i in range(2)]

                mk = sbt(e2, "mk", [128, 2 * FW], BF16)
                for j_ in range(4):
                    S.dma(r1b[j_ % 2][:, 0:448], masks[:, j_ * 448:(j_ + 1) * 448], [], [("r1b", j_ % 2)])
                    S.op("dve", V.tensor_copy, [("r1b", j_ % 2)], ["mk"], out=mk[:, j_ * 448:(j_ + 1) * 448], in_=r1b[j_ % 2][:, 0:448])
                for i_ in range(2):
                    S.op("pool", G.memset, [], [("vcaug", i_)], vcaug2[i_][:], 1.0)
                gate_a = sbt(e2, "gate_a", [128, DM], F32); wob_a = sbt(e2, "wob_a", [128, DM], BF16)
                for pr in range(4):
                    att_body(pr, locals())
                    if pr == 0:
                        S.dma(gate_a[:], gateD, [], ["gate_a"])
                        for kc in range(8):
                            S.dma(wst[kc % 2][:], w_out[kc * 128:(kc + 1) * 128, :], [], [("wst", kc % 2)])
                            S.op("dve", V.tensor_tensor, [("wst", kc % 2), "gate_a"], ["wob_a"], out=wob_a[:], in0=wst[kc % 2][:], in1=gate_a[:], op=ALU.mult)
                            S.dma(woD[kc * 128:(kc + 1) * 128, :], wob_a[:], ["wob_a"], [])
                for kc in [4, 5, 6, 7, 0, 1, 2]:
                    S.dma(hT[:, kc * SEQ:(kc + 1) * SEQ], mixD[kc * 128:(kc + 1) * 128, :], [("mixD", kc)] if kc < 4 else [], ["hT"])
                S.flush()

        def att_body(pr, L):
            if True:
                if True:
                    pass
                par = pr % 2
                qrot = L["qrot"]; qpl = L["qpl"]; krot = L["krot"]; sza = L["sza"]; vaug = L["vaug"]; mixc = L["mixc"]; mk = L["mk"]
                kcn = L["kcn2"][par]; vcaug = L["vcaug2"][par]; Ffull = L["Ffull2"][par]; Fint = L["Fint2"][par]
                cst = L["cst"]; snt = L["snt"]; sqb = L["sqb"]; rtb = L["rtb"]; knbf = L["knbf"]; qsb = L["qsb"]
                r1b = L["r1b"]; r2b = L["r2b"]; etb = L["etb"]; ptb = L["ptb"]; PSX = L["PSX"]; rdb = L["rdb"]; ttb = L["rdb"]
                kFf = ("Ffull", par); kFi = ("Fint", par); kkc = ("kcn", par); kvc = ("vcaug", par)
                for hh in range(2):
                    for hf in range(2):
                        c0 = hf * 448
                        S.dma(r1b[hf][:, 0:448], rpbG[2 * pr + hh, :, c0:c0 + 448], [], [("r1b", hf)])
                        S.op("act", A.activation, [("r1b", hf)], [("r2b", hf)], out=r2b[hf][:, 0:448], in_=r1b[hf][:, 0:448], func=AF.Exp)
                        S.op("dve", V.tensor_tensor, [("r2b", hf), "mk"], [kFf], out=Ffull[:, hh * FW + c0:hh * FW + c0 + 448], in0=r2b[hf][:, 0:448],
                             in1=mk[:, c0:c0 + 448], op=ALU.mult)
                        S.op("dve", V.tensor_tensor, [("r2b", hf), "mk"], [kFi], out=Fint[:, hh * FW + c0:hh * FW + c0 + 448], in0=r2b[hf][:, 0:448],
                             in1=mk[:, FW + c0:FW + c0 + 448], op=ALU.mult)
                wq, wqk = load_wblock(128 * pr)
                wk, wkk = load_wblock(512 + 128 * pr)
                wv, wvk = load_wblock(1024 + 128 * pr)
                wz, wzk = load_wblock(1536 + 128 * pr)
                for g in range(8):
                    b = g % 2
                    proj_fm(wz, wzk, hT, SEQ, g * 512, 512, b)
                    S.op("act", A.activation, [("ps", b)], [("sza", g)], out=sza[:, g * 512:(g + 1) * 512], in_=PS[b][:, :], func=AF.Silu)

                cnt = [0]

                def qk_path(bank, n, gcol, plain_dst, pkey, rot_dst, rkey, ts):
                    i = cnt[0] % CFG["n_qk"]; cnt[0] += 1
                    S.op("act", A.activation, [("ps", bank)], [("qsb", i)], out=qsb[i][:, 0:n], in_=PS[bank][:, 0:n], func=AF.Identity)
                    S.op("act", A.activation, [("ps", bank)], [("sqb", i)], out=sqb[i][:, 0:n], in_=PS[bank][:, 0:n], func=AF.Square)
                    S.op("pe", T.matmul, ["cm", ("sqb", i)], [("ps", 2)], PS[2][:, 0:n], bones, sqb[i][:, 0:n], start=True, stop=True)
                    S.op("act", A.activation, [("ps", 2)], [("rtb", i)], out=rtb[i][:, 0:n], in_=PS[2][:, 0:n], func=AF.Ln, bias=EPS, scale=1.0)
                    S.op("act", A.activation, [("rtb", i)], [("rtb", i)], out=rtb[i][:, 0:n], in_=rtb[i][:, 0:n], func=AF.Exp, scale=-0.5)
                    if plain_dst is None:
                        plain_dst = knbf[i][:, 0:n]; pkey = ("knbf", i)
                    S.op("dve", V.scalar_tensor_tensor, [("qsb", i), "gq", ("rtb", i)], [pkey], out=plain_dst, in0=qsb[i][:, 0:n],
                         scalar=gq[:, gcol:gcol + 1], in1=rtb[i][:, 0:n], op0=ALU.mult, op1=ALU.mult)
                    if rot_dst is None:
                        return
                    S.op("pe", T.matmul, ["cm", pkey], [("ps", 2)], PS[2][:, 0:n], Rm, plain_dst, start=True, stop=True)
                    S.op("dve", V.tensor_tensor, [pkey, ("cst", ts)], [("r1b", i)], out=r1b[i][:, 0:n], in0=plain_dst, in1=cst[ts][:, 0:n], op=ALU.mult)
                    S.op("dve", V.tensor_tensor, [("ps", 2), ("snt", ts)], [("r2b", i)], out=r2b[i][:, 0:n], in0=PS[2][:, 0:n], in1=snt[ts][:, 0:n], op=ALU.mult)
                    if CFG["rope_add"] == "pool":
                        S.op("pool", G.tensor_tensor, [("r1b", i), ("r2b", i)], [rkey], out=rot_dst, in0=r1b[i][:, 0:n], in1=r2b[i][:, 0:n], op=ALU.add)
                    else:
                        S.op("dve", V.tensor_tensor, [("r1b", i), ("r2b", i)], [rkey], out=rot_dst, in0=r1b[i][:, 0:n], in1=r2b[i][:, 0:n], op=ALU.add)

                proj_fm(wk, wkk, hcT, CTX, 0, CTX, 0)
                qk_path(0, CTX, 1, kcn[:], kkc, None, None, 0)
                for t_ in range(2):
                    for kc in range(8):
                        S.op("pe", T.matmul, [wvk, "hT"], [("ps", 3)], PS[3][:, t_ * 128:(t_ + 1) * 128],
                             hcT[:, kc * CTX + t_ * 128: kc * CTX + (t_ + 1) * 128], wv[:, kc * 128:(kc + 1) * 128], start=(kc == 0), stop=(kc == 7))
                vc3 = vcaug[:].rearrange("p (t c) -> p t c", c=256)
                pv3 = PS[3][:, 0:256].rearrange("p (t c) -> p t c", c=128)
                S.op("act", A.activation, [("ps", 3)], [kvc], out=vc3[:, :, 0:64], in_=pv3[:, :, 0:64], func=AF.Identity)
                S.op("act", A.activation, [("ps", 3)], [kvc], out=vc3[:, :, 192:256], in_=pv3[:, :, 64:128], func=AF.Identity)

                def proj_group(g):
                    ts = g % 2
                    S.dma(cst[ts][:], cosT[:, g * 512:(g + 1) * 512], [], [("cst", ts)])
                    S.dma(snt[ts][:], sinT[:, g * 512:(g + 1) * 512], [], [("snt", ts)])
                    proj_fm(wq, wqk, hT, SEQ, g * 512, 512, 0)
                    qk_path(0, 512, 0, qpl[:, g * 512:(g + 1) * 512], ("qpl", g), qrot[:, g * 512:(g + 1) * 512], ("qrot", g), ts)
                    proj_fm(wk, wkk, hT, SEQ, g * 512, 512, 1)
                    qk_path(1, 512, 1, None, None, krot[:, g * 512:(g + 1) * 512], ("krot", g), ts)
                    S.dma(vaug[:, g * 1024:(g + 1) * 1024], vD[pr, :, g * 1024:(g + 1) * 1024], [], [("vaug", g)])

                sc = [0]

                def f3(Ft, hh, jj0):
                    base = Ft[:, hh * FW + jj0 * 64: hh * FW + jj0 * 64 + 256]
                    return bass.AP(base.tensor, base.offset, [[base.ap[0][0], 128], [-128, 2], [1, 256]])

                def att_group(qg):
                    R0 = 4 * qg
                    gq_ = qg // 2
                    qs = slice(R0 * 64, R0 * 64 + 256)
                    if qg == 0:
                        tl = [(a, 6 - a, Ffull, 0, 256) for a in (0, 2, 4, 6)]
                        pairs = [(0, 1), (2, 3)]
                    elif qg == 15:
                        tl = [(a, 6 - (a - 60), Ffull, 0, 256) for a in (56, 58, 60, 62)]
                        pairs = [(0, 1), (2, 3)]
                    else:
                        qr = [(0, 128), (0, 256), (0, 256), (0, 256), (64, 256), (192, 256)]
                        tl = [(R0 - 4 + 2 * i, 10 - 2 * i, Fint, qr[i][0], qr[i][1]) for i in range(6)]
                        pairs = [(1, 2), (3, 4), (0, 5)]
                    i2 = qg % 2
                    for hh in range(2):
                        pb = 64 * hh; ob = 6 + hh; po = PSX[ob][:, 0:256]
                        for pi, (iA, iB) in enumerate(pairs):
                            tA = tl[iA]; tB = tl[iB]
                            wA = tA[4] - tA[3]; wB = tB[4] - tB[3]
                            offs = [256 - wA, 256]
                            lo_ = 256 - wA; hi_ = 256 + wB
                            sb_ = CFG["score_banks"][sc[0] % len(CFG["score_banks"])]; ke = sc[0] % CFG["n_et"]; kp = sc[0] % CFG["n_pt"]; sc[0] += 1
                            for u, tt_ in enumerate((tA, tB)):
                                au, jj0, Ft, qa, qb = tt_
                                S.op("pe", T.matmul, [("krot", au // 8), ("qrot", gq_)], [("ps", sb_)], PSX[sb_][:, offs[u]:offs[u] + (qb - qa)],
                                     krot[pb:pb + 64, au * 64:au * 64 + 128], qrot[pb:pb + 64, R0 * 64 + qa:R0 * 64 + qb], start=True, stop=True)
                            S.op("act", A.activation, [("ps", sb_)], [("et", ke)], out=etb[ke][:, lo_:hi_], in_=PSX[sb_][:, lo_:hi_], func=AF.Exp, scale=0.125)
                            if wA == 256 and wB == 256 and tB[1] == tA[1] - 2:
                                S.op("dve", V.tensor_tensor, [("et", ke), kFf, kFi], [("ptl", kp)],
                                     out=ptb[kp][:].rearrange("p (t c) -> p t c", c=256), in0=etb[ke][:].rearrange("p (t c) -> p t c", c=256),
                                     in1=f3(tA[2], hh, tA[1]), op=ALU.mult)
                            else:
                                for u, tt_ in enumerate((tA, tB)):
                                    au, jj0, Ft, qa, qb = tt_
                                    w_ = qb - qa
                                    S.op("dve", V.tensor_tensor, [("et", ke), kFf, kFi], [("ptl", kp)], out=ptb[kp][:, offs[u]:offs[u] + w_],
                                         in0=etb[ke][:, offs[u]:offs[u] + w_],
                                         in1=Ft[:, hh * FW + jj0 * 64 + qa: hh * FW + jj0 * 64 + qb], op=ALU.mult)
                            for u, tt_ in enumerate((tA, tB)):
                                au, jj0, Ft, qa, qb = tt_
                                tix = au // 2
                                S.op("pe", T.matmul, [("vaug", au // 8), ("ptl", kp)], [("ps", ob)], PSX[ob][:, qa:qb],
                                     vaug[:, tix * 256 + hh * 128: tix * 256 + hh * 128 + 128], ptb[kp][:, offs[u]:offs[u] + (qb - qa)],
                                     start=(pi == 0 and u == 0), stop=False)
                        sb_ = CFG["score_banks"][sc[0] % len(CFG["score_banks"])]; kp = sc[0] % CFG["n_pt"]; sc[0] += 1
                        for j in range(2):
                            S.op("pe", T.matmul, [kkc, ("qpl", gq_)], [("ps", sb_)], PSX[sb_][:, j * 256:(j + 1) * 256], kcn[pb:pb + 64, j * 128:(j + 1) * 128],
                                 qpl[pb:pb + 64, qs], start=True, stop=True)
                        S.op("act", A.activation, [("ps", sb_)], [("ptl", kp)], out=ptb[kp][:], in_=PSX[sb_][:, :], func=AF.Exp, scale=0.125)
                        for j in range(2):
                            S.op("pe", T.matmul, [kvc, ("ptl", kp)], [("ps", ob)], PSX[ob][:, 0:256],
                                 vcaug[:, j * 256 + hh * 128: j * 256 + hh * 128 + 128], ptb[kp][:, j * 256:(j + 1) * 256], start=False, stop=(j == 1))
                        S.op("act", A.activation, [("ps", ob)], [("rd", i2, hh)], out=rdb[i2][pb:pb + 64, :], in_=po[64 - pb:128 - pb, 0:256], func=AF.Ln)
                        S.op("act", A.activation, [("rd", i2, hh)], [("rd", i2, hh)], out=rdb[i2][pb:pb + 64, :], in_=rdb[i2][pb:pb + 64, :],
                             func=AF.Exp, scale=-1.0)
                        S.op("dve", V.tensor_tensor, [("rd", i2, hh), ("sza", gq_)], [("rd", i2, hh)], out=rdb[i2][pb:pb + 64, :], in0=rdb[i2][pb:pb + 64, :],
                             in1=sza[pb:pb + 64, qs], op=ALU.mult)
                        S.op("dve", V.tensor_tensor, [("ps", ob), ("rd", i2, hh)], [("mixc", gq_)], out=mixc[pb:pb + 64, qs], in0=po[pb:pb + 64, 0:256],
                             in1=rdb[i2][pb:pb + 64, :], op=ALU.mult)

                def mix_out(g):
                    S.dma(mixD[pr * 128:(pr + 1) * 128, g * 512:(g + 1) * 512], mixc[:, g * 512:(g + 1) * 512], [("mixc", g)], [("mixD", pr)])

                proj_group(0)
                for g in range(1, 8):
                    proj_group(g)
                    att_group(2 * g - 2)
                    att_group(2 * g - 1)
                    mix_out(g - 1)
                att_group(14)
                att_group(15)
                mix_out(7)

        lru_phase()
        att_phase()

        S.barrier()
        with ExitStack() as e3:
            wo = sbt(e3, "wo", [128, 8 * DM], BF16)
            xtb = [sbt(e3, "xo%d" % i, [128, DM], F32) for i in range(6)]
            otb = [sbt(e3, "ot%d" % i, [128, DM], F32) for i in range(3)]
            hT3 = hT[:].rearrange("p (kc t) -> p kc t", kc=8)
            mixD3 = mixD.rearrange("(kc p) t -> p kc t", p=128)
            wo3 = wo[:].rearrange("p (kc n) -> p kc n", kc=8)
            woD3 = woD.rearrange("(kc p) n -> p kc n", p=128)
            for h4 in range(2):
                S.dma(wo3[:, h4 * 4:(h4 + 1) * 4, :], woD3[:, h4 * 4:(h4 + 1) * 4, :], [], [("wo", h4)])
            for h2 in range(2):
                S.dma(hT[:, 3 * SEQ + h2 * 2048:3 * SEQ + (h2 + 1) * 2048], mixD[384:512, h2 * 2048:(h2 + 1) * 2048], [], [("mx", h2)])
            for i in range(32):
                xt = xtb[i % 6]; xk = ("xo", i % 6)
                S.dma(xt[:], x[i * 128:(i + 1) * 128, :], [], [xk])
                ot = otb[i % 3]; ok = ("ot", i % 3)
                for hf in range(2):
                    b = (2 * i + hf) % 4
                    for kc in range(8):
                        S.op("pe", T.matmul, [("mx", i // 16), ("wo", kc // 4)], [("ps", b)], PS[b][:, :], hT[:, kc * SEQ + i * 128: kc * SEQ + (i + 1) * 128],
                             wo[:, kc * DM + hf * 512: kc * DM + (hf + 1) * 512], start=(kc == 0), stop=(kc == 7))
                    S.op("dve", V.tensor_tensor, [("ps", b), xk], [ok], out=ot[:, hf * 512:(hf + 1) * 512], in0=PS[b][:, :],
                         in1=xt[:, hf * 512:(hf + 1) * 512], op=ALU.add)
                S.dma(out[i * 128:(i + 1) * 128, :], ot[:], [ok], [])
            S.final_wait()
    return nc


def kernel(x, c, ctx, c_ctx, norm_g, w_mod, b_mod, w_in, w_out, q_norm_g, k_norm_g, rpb,
           conv_w, conv_b, lru_wa, lru_ba, lru_wx, lru_bx, lru_lam):
    f = lambda a: np.ascontiguousarray(np.asarray(a, dtype=np.float32))
    x, c, ctx, c_ctx = f(x), f(c), f(ctx), f(c_ctx)
    norm_g, w_mod, b_mod, w_in, w_out = f(norm_g), f(w_mod), f(b_mod), f(w_in), f(w_out)
    q_norm_g, k_norm_g, rpb = f(q_norm_g), f(k_norm_g), f(rpb)
    conv_w, conv_b, lru_wa, lru_ba, lru_wx, lru_bx, lru_lam = (f(conv_w), f(conv_b), f(lru_wa), f(lru_ba),
                                                              f(lru_wx), f(lru_bx), f(lru_lam))
    cosT, sinT, cm, sel, masks, dridx, dcidx = _host_consts()
    rpbG = np.ascontiguousarray(rpb[0][:, dridx, dcidx].reshape(8, 128, FW))
    gqk = np.stack([np.tile(q_norm_g[0], 2), np.tile(k_norm_g[0], 2)], axis=1).astype(np.float32)
    plru = np.zeros((4, 128, 16), np.float32)
    wbd = np.zeros((4, 128, 512), np.float32)
    for ch in range(4):
        sl = slice(ch * 128, (ch + 1) * 128)
        for j in range(4):
            plru[ch, :, j] = conv_w[0, j, sl]
        plru[ch, :, 4] = conv_b[0, sl]
        for d in range(2):
            plru[ch, :, 5 + 3 * d] = lru_ba[0, d, sl]
            plru[ch, :, 6 + 3 * d] = lru_bx[0, d, sl]
            plru[ch, :, 7 + 3 * d] = lru_lam[0, d, sl]
            for m, wsrc in enumerate((lru_wa, lru_wx)):
                col = (2 * d + m) * 128
                for blk in range(2):
                    wbd[ch, blk * 64:(blk + 1) * 64, col + blk * 64: col + (blk + 1) * 64] = wsrc[0, d, 2 * ch + blk]
    common = {
        "norm_g": norm_g[0:1], "w_mod": w_mod[0], "b_mod": b_mod[0:1], "w_in": w_in[0], "w_out": w_out[0],
        "gqk": gqk, "plru": plru, "wbd": wbd, "rpbG": rpbG, "cosT": cosT, "sinT": sinT, "cmat": cm, "sel": sel,
        "masks": masks,
    }
    in_maps = []
    for b in range(8):
        cvb = np.stack([c[b], c_ctx], axis=0)
        cvl = np.ascontiguousarray(cvb.reshape(2, 8, 128).transpose(2, 1, 0).reshape(128, 16))
        m = dict(common)
        m["x"] = x[b]; m["ctx"] = ctx[b]; m["cv"] = cvl
        in_maps.append(m)
    nc = build_nc()
    res = run_bass_kernel_spmd(nc, in_maps, core_ids=list(range(8)))
    return np.stack([np.asarray(r["out"], dtype=np.float32) for r in res.results], axis=0)
```

```python
import numpy as np
from contextlib import ExitStack
import concourse.bass as bass
import concourse.mybir as mybir
from concourse.bass_utils import run_bass_kernel_spmd

F32 = mybir.dt.float32
BF16 = mybir.dt.bfloat16
ALU = mybir.AluOpType
AF = mybir.ActivationFunctionType

CFG = {"score_banks": [3, 4, 5], "n_et": 3, "n_pt": 4, "vbank": 3, "window": 200, "rope_add": "pool", "n_qk": 2, "a2_act": 1, "conv_act": 1, "scan_cost": 2.2, "slack": 0.0, "vevac": "act", "lru_pair": 1, "n_xt": 4}
SEQ = 4096
DM = 1024
CTX = 256
EPS = 1e-6
NJ = 14
FW = NJ * 64


class _Op:
    __slots__ = ("eng", "fn", "args", "kwargs", "preds", "dur", "idx", "tab", "seq", "dsem", "dval", "fin")


_ACT_TAB = {}


def _act_tab(func):
    n = str(func)
    if "Exp" in n:
        return ("exp", "ln")
    if "Tanh" in n:
        return ("exp",)
    if "Ln" in n:
        return ("ln",)
    if "Sqrt" in n:
        return ("sqrt",)
    if "Silu" in n:
        return ("silu",)
    if "Sigmoid" in n:
        return ("sigmoid",)
    return None


class Sched:
    WINDOW = 48

    def __init__(self, nc, es):
        self.nc = nc
        self.engs = {"pe": nc.tensor, "act": nc.scalar, "dve": nc.vector, "pool": nc.gpsimd, "sp": nc.sync}
        self.csem = {e: es.enter_context(nc.semaphore("cs_" + e)) for e in ("pe", "act", "dve", "pool")}
        self.cnt = {e: 0 for e in self.csem}
        self.NDS = 12
        self.dsem = [es.enter_context(nc.semaphore("ds%d" % i)) for i in range(self.NDS)]
        self.dcnt = [0] * self.NDS
        self.dnext = 0
        self.seen = {e: {} for e in self.engs}
        self.ops = []
        self.lastw = {}
        self.readers = {}
        self.last_tab = "none"

    def _est(self, eng, args, kwargs, fn=None):
        out = kwargs.get("out", args[0] if args else None)
        try:
            n = out.free_size()
        except Exception:
            n = 512
        if eng == "pe":
            return 64.0 + n * 0.42
        if eng == "act":
            return max(200.0, 100.0 + n * 0.88)
        if eng == "dve":
            if "scan" in getattr(fn, "__name__", ""):
                return 100.0 + n * CFG["scan_cost"]
            return max(160.0, 60.0 + n * 1.1)
        if eng == "pool":
            return 250.0 + n * 4.5
        try:
            nb = out.nbytes()
        except Exception:
            nb = 65536
        return float(nb)

    def _add(self, eng, fn, r, w, args, kwargs):
        o = _Op()
        o.eng = eng; o.fn = fn; o.args = args; o.kwargs = kwargs; o.idx = len(self.ops)
        o.dur = self._est(eng, args, kwargs, fn)
        o.tab = _act_tab(kwargs.get("func")) if eng == "act" else None
        preds = {}
        for k in r:
            p = self.lastw.get(k)
            if p is not None:
                need = not (p.eng == eng and eng == "pe")
                preds[p.idx] = preds.get(p.idx, False) or need
        for k in w:
            p = self.lastw.get(k)
            if p is not None:
                need = not (p.eng == eng and eng == "pe")
                preds[p.idx] = preds.get(p.idx, False) or need
            for p in self.readers.get(k, ()):
                need = not (p.eng == eng and eng == "pe")
                preds[p.idx] = preds.get(p.idx, False) or need
        o.preds = preds
        for k in r:
            self.readers.setdefault(k, []).append(o)
        for k in w:
            self.lastw[k] = o
            self.readers[k] = []
        self.ops.append(o)
        return o

    def op(self, eng, fn, r, w, *args, **kwargs):
        return self._add(eng, fn, r, w, args, kwargs)

    def dma(self, out, in_, r, w, **kwargs):
        kwargs = dict(kwargs); kwargs["out"] = out; kwargs["in_"] = in_
        return self._add("sp", self.nc.sync.dma_start, r, w, (), kwargs)

    def _wait(self, eng, key, val):
        if self.seen[eng].get(key, 0) >= val:
            return
        sem = self.csem[key[1]] if key[0] == "c" else self.dsem[key[1]]
        self.engs[eng].wait_ge(sem, val)
        self.seen[eng][key] = val

    def flush(self):
        ops = self.ops
        if not ops:
            return
        pend = {e: [] for e in self.engs}
        for o in ops:
            o.fin = None
            pend[o.eng].append(o)
        pos = {e: 0 for e in self.engs}
        free = {e: 0.0 for e in self.engs}
        order = {e: [] for e in self.engs}
        done = {e: set() for e in self.engs}
        remaining = len(ops)
        last_tab = self.last_tab
        dma_free = 0.0
        succ = [[] for _ in ops]
        for o in ops:
            for pi in o.preds:
                succ[pi].append(o.idx)
        rank = [0.0] * len(ops)
        for o in reversed(ops):
            m = 0.0
            for si in succ[o.idx]:
                if rank[si] > m:
                    m = rank[si]
            rank[o.idx] = m + (o.dur if o.eng != "sp" else 2500.0)
        slack = CFG.get("slack", 0.0)
        while remaining:
            best = None
            for e in self.engs:
                lst = pend[e]; p0 = pos[e]
                cnt = 0; i = p0
                cands = []
                while i < len(lst) and cnt < CFG["window"]:
                    o = lst[i]; i += 1
                    if o.idx in done[e]:
                        continue
                    cnt += 1
                    rt = 0.0; ok = True
                    for pi in o.preds:
                        f = ops[pi].fin
                        if f is None:
                            ok = False; break
                        if ops[pi].eng != e:
                            f += 200.0
                        if f > rt:
                            rt = f
                    if not ok:
                        continue
                    st = max(free[e], rt)
                    pen = 0.0
                    if e == "act" and o.tab is not None and last_tab not in o.tab:
                        pen = 1400.0
                    cands.append((st + pen, o, st, pen))
                if not cands:
                    continue
                m = min(c[0] for c in cands)
                pick = None
                for c in cands:
                    if c[0] <= m + slack:
                        if pick is None or (rank[c[1].idx], -c[1].idx) > (rank[pick[1].idx], -pick[1].idx):
                            pick = c
                key = (pick[0], pick[1].idx)
                if best is None or key < best[0]:
                    best = (key, e, pick[1], pick[2], pick[3])
            _, e, o, st, pen = best
            if e == "sp":
                free[e] = st + 300.0
                xs = max(st + 300.0, dma_free)
                dma_free = xs + o.dur / 200.0
                o.fin = dma_free + 1800.0
            else:
                o.fin = st + pen + o.dur
                free[e] = o.fin
            if e == "act" and o.tab is not None and last_tab not in o.tab:
                last_tab = o.tab[0]
            order[e].append(o)
            done[e].add(o.idx)
            while pos[e] < len(pend[e]) and pend[e][pos[e]].idx in done[e]:
                pos[e] += 1
            remaining -= 1
        self.last_tab = last_tab
        for e in self.csem:
            c = self.cnt[e]
            for o in order[e]:
                c += 1; o.seq = c
        dn = self.dnext; dc = list(self.dcnt)
        for o in order["sp"]:
            i = dn % self.NDS; dn += 1
            dc[i] += 16
            o.dsem = i; o.dval = dc[i]
        for e in self.engs:
            for o in order[e]:
                reqs = {}
                for pi, need in o.preds.items():
                    if not need:
                        continue
                    p = ops[pi]
                    if p.eng == "sp":
                        k_ = ("d", p.dsem); v_ = p.dval
                    else:
                        k_ = ("c", p.eng); v_ = p.seq
                    if reqs.get(k_, 0) < v_:
                        reqs[k_] = v_
                for k_, v_ in reqs.items():
                    self._wait(e, k_, v_)
                if e == "sp":
                    i = o.dsem
                    if o.dval > 16:
                        self._wait(e, ("d", i), o.dval - 16)
                    ins = o.fn(*o.args, **o.kwargs)
                    ins.then_inc(self.dsem[i], 16)
                else:
                    ins = o.fn(*o.args, **o.kwargs)
                    ins.then_inc(self.csem[e], 1)
        for e in self.csem:
            self.cnt[e] += len(order[e])
        self.dnext = dn; self.dcnt = dc
        self.ops = []
        self.lastw = {}
        self.readers = {}

    def barrier(self):
        self.flush()
        for eng in self.engs:
            for e in self.csem:
                if e != eng and self.cnt[e] > 0:
                    self._wait(eng, ("c", e), self.cnt[e])
            for i in range(self.NDS):
                if self.dcnt[i] > 0:
                    self._wait(eng, ("d", i), self.dcnt[i])

    def final_wait(self):
        self.flush()
        for i in range(self.NDS):
            if self.dcnt[i] > 0:
                self._wait("sp", ("d", i), self.dcnt[i])


def _host_consts():
    p = np.arange(128)
    f = p % 64
    i16 = (f % 16).astype(np.float32)
    inv = (np.float32(10000.0) ** (-(i16) / np.float32(16.0))).astype(np.float32)
    t = np.arange(SEQ)
    r = (t // 64).astype(np.float32)
    c = (t % 64).astype(np.float32)
    pos = np.where((f < 32)[:, None], r[None, :], c[None, :]).astype(np.float32)
    ang = (pos * inv[:, None]).astype(np.float32)
    cosT = np.cos(ang).astype(np.float32)
    sgn = np.where((f % 32) < 16, -1.0, 1.0).astype(np.float32)
    sinT = (np.sin(ang) * sgn[:, None]).astype(np.float32)
    partner = np.where((f % 32) < 16, p + 16, p - 16)
    cm = np.zeros((128, 384), np.float32)
    cm[p, p] = 1.0
    cm[partner, 128 + p] = 1.0
    blk = (p[:, None] // 64) == (p[None, :] // 64)
    cm[:, 256:384] = blk.astype(np.float32) / 64.0
    sel = np.zeros((3, 384), np.float32)
    for rr in range(3):
        sel[rr, rr * 128:(rr + 1) * 128] = 1.0
    krl = (p // 64)[:, None, None]
    kc = (p % 64)[:, None, None]
    jj = np.arange(NJ)[None, :, None]
    qc = np.arange(64)[None, None, :]
    dr = 6 - jj + krl
    cs = np.clip(qc - 8, 0, 48)
    colm = (kc >= cs) & (kc < cs + 16)
    MC = np.broadcast_to(colm, (128, NJ, 64)).astype(np.float32)
    MI = (colm & (dr >= -4) & (dr <= 3)).astype(np.float32)
    masks = np.concatenate([MC.reshape(128, FW), MI.reshape(128, FW)], axis=1).astype(np.float32)
    dridx = np.broadcast_to(dr + 7, (128, NJ, 64))
    dcidx = np.broadcast_to(np.clip(kc - qc, -15, 15) + 15, (128, NJ, 64))
    return cosT, sinT, cm, sel, masks, dridx, dcidx


def build_nc():
    nc = bass.Bass("TRN2", target_bir_lowering=False)

    def din(name, shape, dt=F32):
        return nc.dram_tensor(name, list(shape), dt, kind="ExternalInput").ap()

    x = din("x", [SEQ, DM]); ctx = din("ctx", [CTX, DM]); cv = din("cv", [128, 16])
    norm_g = din("norm_g", [1, DM]); w_mod = din("w_mod", [DM, 3 * DM]); b_mod = din("b_mod", [1, 3 * DM])
    w_in = din("w_in", [DM, 3 * DM]); w_out = din("w_out", [DM, DM])
    gqk = din("gqk", [128, 2]); plru = din("plru", [4, 128, 16]); wbd = din("wbd", [4, 128, 512])
    rpbG = din("rpbG", [8, 128, FW]); cosT = din("cosT", [128, SEQ]); sinT = din("sinT", [128, SEQ])
    cmat = din("cmat", [128, 384]); sel = din("sel", [3, 384]); masks = din("masks", [128, 2 * FW])
    out = nc.dram_tensor("out", [SEQ, DM], F32, kind="ExternalOutput").ap()
    mixD = nc.dram_tensor("mixD", [DM, SEQ], BF16, kind="Internal").ap()
    vD = nc.dram_tensor("vD", [4, 128, 32 * 256], BF16, kind="Internal").ap()
    woD = nc.dram_tensor("woD", [DM, DM], BF16, kind="Internal").ap()
    gateD = nc.dram_tensor("gateD", [128, DM], F32, kind="Internal").ap()
    w_in_v = w_in.rearrange("(kc p) n -> p kc n", p=128)

    with ExitStack() as es:
        S = Sched(nc, es)
        V, A, G, T = nc.vector, nc.scalar, nc.gpsimd, nc.tensor

        uid = [0]

        def sbt(stack, name, shape, dt):
            uid[0] += 1
            return stack.enter_context(nc.sbuf_tensor("%s_%d" % (name, uid[0]), list(shape), dt))

        hT = sbt(es, "hT", [128, 8 * SEQ], BF16)
        hcT = sbt(es, "hcT", [128, 8 * CTX], BF16)
        cm = sbt(es, "cm", [128, 384], BF16)
        gq = sbt(es, "gq", [128, 2], F32)
        wst = [sbt(es, "wst%d" % i, [128, 1024], F32) for i in range(2)]
        wbf = [sbt(es, "wbf%d" % i, [128, 1024], BF16) for i in range(4)]
        PS = [es.enter_context(nc.psum_tensor("ps%d" % i, [128, 512], F32)) for i in range(6)]
        ident = cm[:, 0:128]; Rm = cm[:, 128:256]; bones = cm[:, 256:384]
        wslot = [0]

        def load_wblock(c0):
            s = wslot[0]; wslot[0] += 1
            st = wst[s % 2]; wb = wbf[s % 4]
            S.dma(st[:].rearrange("p (kc n) -> p kc n", kc=8), w_in_v[:, :, c0:c0 + 128], [], [("wst", s % 2)])
            S.op("dve", V.tensor_copy, [("wst", s % 2)], [("wbf", s % 4)], out=wb[:], in_=st[:])
            return wb, ("wbf", s % 4)

        def proj_fm(wb, wkey, src, src_w, col0, ncols, bank):
            for kc in range(8):
                S.op("pe", T.matmul, [wkey, "hT"], [("ps", bank)], PS[bank][:, 0:ncols],
                     wb[:, kc * 128:(kc + 1) * 128], src[:, kc * src_w + col0: kc * src_w + col0 + ncols],
                     start=(kc == 0), stop=(kc == 7))

        with ExitStack() as e1:
            PT = [e1.enter_context(nc.psum_tensor("pt%d" % i, [128, 1024], BF16)) for i in range(2)]
            cm_f = sbt(e1, "cm_f", [128, 384], F32); gate_bc = sbt(e1, "gate_bc", [128, DM], F32)
            cvt = sbt(e1, "cvt", [128, 16], F32); scv = sbt(e1, "scv", [128, 16], F32)
            M3 = sbt(e1, "M3", [3, 3 * DM], F32); b2 = sbt(e1, "b2", [2, 3 * DM], F32)
            sel_t = sbt(e1, "sel_t", [3, 384], F32)
            wmb = [sbt(e1, "wm%d" % i, [128, 1536], F32) for i in range(4)]
            cols = sbt(e1, "cols", [128, 48], F32); mcol = sbt(e1, "mcol", [128, 32], F32)
            wob = [sbt(e1, "wob%d" % i, [128, DM], BF16) for i in range(2)]
            xtb = [sbt(e1, "xt%d" % i, [128, DM], F32) for i in range(CFG["n_xt"])]
            hbb = [sbt(e1, "hb%d" % i, [128, DM], BF16) for i in range(2)]
            junk = sbt(e1, "junk", [128, DM], BF16)
            ssall = sbt(e1, "ssall", [128, 40], F32); rtall = sbt(e1, "rtall", [128, 40], F32)
            rsall = sbt(e1, "rsall", [128, 40], F32)

            S.dma(cvt[:], cv, [], ["cvt"]); S.dma(cm_f[:], cmat, [], ["cm_f"])
            S.dma(sel_t[:], sel, [], ["sel"])
            S.dma(gq[:], gqk, [], ["gq"])
            S.op("dve", V.tensor_copy, ["cm_f"], ["cm"], out=cm[:], in_=cm_f[:])
            S.op("act", A.activation, ["cvt"], ["scv"], out=scv[:], in_=cvt[:], func=AF.Silu)
            S.op("dve", V.memset, [], ["M3"], M3[:], 0.0)
            S.op("dve", V.memset, [], ["ssall"], ssall[:], 0.0)
            S.dma(M3[2:3, 0:DM], norm_g, [], ["M3"])
            S.dma(b2[0:1, :], b_mod, [], ["b2"]); S.dma(b2[1:2, :], b_mod, [], ["b2"])
            for kc in range(8):
                for hf in range(2):
                    wi = (2 * kc + hf) % 4
                    wm = wmb[wi]
                    S.dma(wm[:], w_mod[kc * 128:(kc + 1) * 128, hf * 1536:(hf + 1) * 1536], [], [("wm", wi)])
                    for n3 in range(3):
                        n = hf * 3 + n3
                        S.op("pe", T.matmul, ["scv", ("wm", wi)], [("ps", n)], PS[n][0:2, :],
                             scv[:, 2 * kc:2 * kc + 2], wm[:, n3 * 512:(n3 + 1) * 512], start=(kc == 0), stop=(kc == 7))
            for n in range(6):
                S.op("dve", V.tensor_tensor, [("ps", n), "b2"], ["M3"], out=M3[0:2, n * 512:(n + 1) * 512],
                     in0=PS[n][0:2, :], in1=b2[0:2, n * 512:(n + 1) * 512], op=ALU.add)
            for j in range(2):
                b = j
                S.op("pe", T.matmul, ["sel", "M3"], [("ps", b)], PS[b][:, :], sel_t[0:3, 0:128], M3[0:3, 2 * DM + j * 512: 2 * DM + (j + 1) * 512],
                     start=True, stop=True)
                S.op("act", A.activation, [("ps", b)], ["gate_bc"], out=gate_bc[:, j * 512:(j + 1) * 512], in_=PS[b][:, :], func=AF.Identity)
            S.dma(gateD, gate_bc[:], ["gate_bc"], [])
            i3 = bass.AP(sel_t[0:3, 0:1].tensor, sel_t[0:3, 0:1].offset, [[sel_t[0:3, 0:1].ap[0][0], 3], [128, 3]])
            for sidx in range(2):
                for kc in range(8):
                    c = (sidx * 8 + kc) * 3
                    S.op("pe", T.matmul, ["sel", "M3"], [("ps", 2)], PS[2][:, c:c + 3],
                         M3[0:3, sidx * DM + kc * 128: sidx * DM + (kc + 1) * 128], i3, start=True, stop=True)
            S.op("dve", V.tensor_copy, [("ps", 2)], ["cols"], out=cols[:, 0:48], in_=PS[2][:, 0:48])

            def colv(sidx, r):
                base = cols[:, sidx * 24 + r: sidx * 24 + r + 1]
                return bass.AP(base.tensor, base.offset, [[base.ap[0][0], 128], [3, 8]])

            S.op("dve", V.scalar_tensor_tensor, ["cols"], ["mcol"], out=mcol[:, 0:8], in0=colv(1, 0), scalar=1.0, in1=colv(0, 2), op0=ALU.add, op1=ALU.mult)
            S.op("dve", V.tensor_copy, ["cols"], ["mcol"], out=mcol[:, 8:16], in_=colv(0, 0))
            S.op("dve", V.scalar_tensor_tensor, ["cols"], ["mcol"], out=mcol[:, 16:24], in0=colv(1, 1), scalar=1.0, in1=colv(0, 2), op0=ALU.add, op1=ALU.mult)
            S.op("dve", V.tensor_copy, ["cols"], ["mcol"], out=mcol[:, 24:32], in_=colv(0, 1))

            def modulate(dst, dstw, c0, n, off, key, eng):
                for kc in range(8):
                    sl = dst[:, kc * dstw + c0: kc * dstw + c0 + n]
                    if True:
                        S.op("dve", V.tensor_scalar, [key, "mcol"], [key], out=sl, in0=sl, scalar1=mcol[:, off + kc:off + kc + 1],
                             scalar2=mcol[:, off + 8 + kc:off + 9 + kc], op0=ALU.mult, op1=ALU.add)
                    else:
                        S.op("act", A.activation, [key, "mcol"], [key], out=sl, in_=sl, func=AF.Identity, scale=mcol[:, off + kc:off + kc + 1],
                             bias=mcol[:, off + 8 + kc:off + 9 + kc])

            tiles = [(ctx[i * 128:(i + 1) * 128, :], hcT, CTX, i * 128, ("hcT", 0)) for i in range(2)]
            tiles += [(x[i * 128:(i + 1) * 128, :], hT, SEQ, i * 128, ("hT", i // 8)) for i in range(32)]
            for idx, (src, dst, dstw, col, hkey) in enumerate(tiles):
                xt = xtb[idx % CFG["n_xt"]]; xk = ("xt", idx % CFG["n_xt"])
                S.dma(xt[:], src, [], [xk])
                S.op("act", A.activation, [xk, "ssall"], ["junk", ("ss", idx)], out=junk[:], in_=xt[:], func=AF.Square,
                     accum_out=ssall[:, idx:idx + 1])
                S.op("act", A.activation, [("ss", idx)], [("rt", idx)], out=rtall[:, idx:idx + 1],
                     in_=ssall[:, idx:idx + 1], func=AF.Sqrt, scale=1.0 / DM, bias=EPS)
                S.op("dve", V.reciprocal, [("rt", idx)], [("rs", idx)], out=rsall[:, idx:idx + 1], in_=rtall[:, idx:idx + 1])
                hb = hbb[idx % 2]
                S.op("act", A.activation, [xk, ("rs", idx)], [("hb", idx % 2)], out=hb[:], in_=xt[:], func=AF.Identity, scale=rsall[:, idx:idx + 1])
                pt = PT[idx % 2]
                for kc in range(8):
                    S.op("pe", T.transpose, [("hb", idx % 2), "cm"], [("pt", idx % 2)], out=pt[:, kc * 128:(kc + 1) * 128],
                         in_=hb[:, kc * 128:(kc + 1) * 128], identity=ident)
                dst_ap = dst[:].rearrange("p (kc t) -> p kc t", kc=8)[:, :, col:col + 128]
                S.op("dve", V.tensor_copy, [("pt", idx % 2)], [hkey], out=dst_ap, in_=pt[:].rearrange("p (kc t) -> p kc t", kc=8))
                if idx == 1:
                    modulate(hcT, CTX, 0, CTX, 16, ("hcT", 0), 0)
                elif idx >= 2 and (idx - 2) % 8 == 7:
                    g4 = (idx - 2) // 8
                    modulate(hT, SEQ, g4 * 1024, 1024, 0, ("hT", g4), g4)
            S.flush()

        def lru_phase():
            S.barrier()
            with ExitStack() as e2:
                NB = 1 if CFG["lru_pair"] else 2
                NWS = 4 if CFG["lru_pair"] else 2
                T0s = [sbt(e2, "T0_%d" % i, [128, SEQ], F32) for i in range(NB)]
                szls = [sbt(e2, "szl_%d" % i, [128, SEQ], BF16) for i in range(NB)]
                T1 = sbt(e2, "T1", [128, SEQ], F32)
                ucbf = sbt(e2, "ucbf", [128, SEQ], BF16)
                QW = 1024
                BAs = [sbt(e2, "BA%d" % i, [128, QW], F32) for i in range(NWS)]
                BBs = [sbt(e2, "BB%d" % i, [128, QW], F32) for i in range(NWS)]
                BCs = [sbt(e2, "BC%d" % i, [128, QW], F32) for i in range(NWS)]
                trb = [sbt(e2, "trb%d" % i, [128, 512], F32) for i in range(2)]
                tib = [sbt(e2, "tib%d" % i, [128, 512], F32) for i in range(2)]
                cx0 = sbt(e2, "cx0", [128, CTX], F32); cx1 = sbt(e2, "cx1", [128, CTX], F32)
                cxbf = sbt(e2, "cxbf", [128, CTX], BF16)
                prms = [sbt(e2, "prm%d" % i, [128, 16], F32) for i in range(2)]
                prhs = [sbt(e2, "prh%d" % i, [128, 16], F32) for i in range(2)]
                clts = [sbt(e2, "clt%d" % i, [128, 8], F32) for i in range(2)]
                gw_f = sbt(e2, "gw_f", [128, 512], F32)
                gws = [sbt(e2, "gw%d" % i, [128, 512], BF16) for i in range(2)]
                carry = sbt(e2, "carry", [128, 2], F32)
                wvf = sbt(e2, "wvf", [128, 8 * 512], BF16)
                wvf3 = wvf[:].rearrange("p (kc n) -> p kc n", kc=8)
                vstage = sbt(e2, "vstage", [128, 1024], BF16)
                psv = e2.enter_context(nc.psum_tensor("psv", [128, 512], F32))
                S.op("pool", G.memset, [], ["vstage"], vstage[:], 1.0)
                for pr_ in range(4):
                    S.dma(wst[pr_ % 2][:].rearrange("p (kc n) -> p kc n", kc=8), w_in_v[:, :, 1024 + 128 * pr_:1024 + 128 * (pr_ + 1)], [], [("wst", pr_ % 2)])
                    S.op("dve", V.tensor_copy, [("wst", pr_ % 2)], ["wvf"], out=wvf3[:, :, pr_ * 128:(pr_ + 1) * 128],
                         in_=wst[pr_ % 2][:].rearrange("p (kc n) -> p kc n", kc=8))
                vD_t = vD.rearrange("r p c -> p r c")
                vst3 = vstage[:].rearrange("p (r c) -> p r c", r=4)
                vb_ = vstage[:]
                vst4 = bass.AP(vb_.tensor, vb_.offset, [[vb_.ap[0][0], 128], [256, 4], [192, 2], [1, 64]])
                pvb_ = psv[:, :]
                psv4 = bass.AP(pvb_.tensor, pvb_.offset, [[pvb_.ap[0][0], 128], [128, 4], [64, 2], [1, 64]])

                def v_tiles(t0, t1):
                    for t_ in range(t0, t1):
                        for kc in range(8):
                            S.op("pe", T.matmul, ["wvf", "hT"], ["psv"], psv[:, :],
                                 hT[:, kc * SEQ + t_ * 128: kc * SEQ + (t_ + 1) * 128], wvf[:, kc * 512:(kc + 1) * 512],
                                 start=(kc == 0), stop=(kc == 7))
                        if CFG["vevac"] == "act":
                            S.op("act", A.activation, ["psv"], ["vstage"], out=vst4, in_=psv4, func=AF.Identity)
                        else:
                            S.op("dve", V.tensor_copy, ["psv"], ["vstage"], out=vst4, in_=psv4)
                        S.dma(vD_t[:, :, t_ * 256:(t_ + 1) * 256], vst3, ["vstage"], [])

                def rev(ap2d, n):
                    pstep = ap2d.ap[0][0]
                    npart = ap2d.ap[0][1]
                    return bass.AP(ap2d.tensor, ap2d.offset + n - 1, [[pstep, npart], [-1, n]])

                def proj_stage(ch):
                    p = ch % 2
                    prm = prms[p]; prh = prhs[p]; clt = clts[p]; gw = gws[p]; pk = p % NB; T0 = T0s[pk]; szl = szls[pk]
                    S.dma(prm[:], plru[ch], [], [("prm", p)])
                    S.dma(gw_f[:], wbd[ch], [], ["gw_f"])
                    S.op("dve", V.tensor_copy, ["gw_f"], [("gw", p)], out=gw[:], in_=gw_f[:])
                    S.op("dve", V.tensor_scalar, [("prm", p)], [("prh", p)], out=prh[:], in0=prm[:], scalar1=0.5, scalar2=None, op0=ALU.mult)
                    for d in range(2):
                        lc = 7 + 3 * d
                        S.op("act", A.activation, [("prm", p)], [("clt", p, 4 + d)], out=clt[:, 4 + d:5 + d], in_=prm[:, lc:lc + 1], func=AF.Exp, scale=-1.0)
                        S.op("act", A.activation, [("clt", p, 4 + d)], [("clt", p, 6 + d)], out=clt[:, 6 + d:7 + d], in_=clt[:, 4 + d:5 + d], func=AF.Ln, bias=1.0)
                        S.op("dve", V.tensor_scalar, [("clt", p, 6 + d)], [("cl", p, d)], out=clt[:, 2 * d:2 * d + 1], in0=clt[:, 6 + d:7 + d],
                             scalar1=-8.0, scalar2=None, op0=ALU.mult)
                        S.op("dve", V.tensor_scalar, [("clt", p, 6 + d)], [("cl", p, d)], out=clt[:, 2 * d + 1:2 * d + 2], in0=clt[:, 6 + d:7 + d],
                             scalar1=-4.0, scalar2=None, op0=ALU.mult)
                    wu, wuk = load_wblock(2048 + 128 * ch)
                    wz, wzk = load_wblock(2560 + 128 * ch)
                    proj_fm(wu, wuk, hcT, CTX, 0, CTX, 5)
                    S.op("act", A.activation, [("ps", 5)], ["cx0"], out=cx0[:], in_=PS[5][:, 0:CTX], func=AF.Identity)
                    for g in range(8):
                        b = 4 + g % 2
                        proj_fm(wu, wuk, hT, SEQ, g * 512, 512, b)
                        S.op("act", A.activation, [("ps", b)], [("T0", pk, g)], out=T0[:, g * 512:(g + 1) * 512], in_=PS[b][:, :], func=AF.Identity)
                    for g in range(8):
                        b = 4 + g % 2
                        proj_fm(wz, wzk, hT, SEQ, g * 512, 512, b)
                        S.op("act", A.activation, [("ps", b)], [("szl", pk, g)], out=szl[:, g * 512:(g + 1) * 512], in_=PS[b][:, :], func=AF.Silu)

                def main_stage(ch, mid_hook):
                    p = ch % 2
                    prm = prms[p]; prh = prhs[p]; clt = clts[p]; gw = gws[p]; pk = p % NB; T0 = T0s[pk]; szl = szls[pk]
                    kprm = ("prm", p)

                    def conv(src, skeys, dst, dkey, lo, hi, n):
                        rk = list(skeys) + [kprm]
                        if CFG["conv_act"] and n > CTX:
                            S.op("act", A.activation, rk, [dkey], out=dst[:, lo:hi], in_=src[:, lo:hi], func=AF.Identity, scale=prm[:, 1:2], bias=prm[:, 4:5])
                        else:
                            S.op("dve", V.tensor_scalar, rk, [dkey], out=dst[:, lo:hi], in0=src[:, lo:hi], scalar1=prm[:, 1:2],
                                 scalar2=prm[:, 4:5], op0=ALU.mult, op1=ALU.add)
                        l0 = max(lo, 1)
                        S.op("dve", V.scalar_tensor_tensor, rk + [dkey], [dkey], out=dst[:, l0:hi], in0=src[:, l0 - 1:hi - 1],
                             scalar=prm[:, 0:1], in1=dst[:, l0:hi], op0=ALU.mult, op1=ALU.add)
                        h2 = min(hi, n - 1)
                        S.op("dve", V.scalar_tensor_tensor, rk + [dkey], [dkey], out=dst[:, lo:h2], in0=src[:, lo + 1:h2 + 1],
                             scalar=prm[:, 2:3], in1=dst[:, lo:h2], op0=ALU.mult, op1=ALU.add)
                        h3 = min(hi, n - 2)
                        S.op("dve", V.scalar_tensor_tensor, rk + [dkey], [dkey], out=dst[:, lo:h3], in0=src[:, lo + 2:h3 + 2],
                             scalar=prm[:, 3:4], in1=dst[:, lo:h3], op0=ALU.mult, op1=ALU.add)

                    conv(cx0, ["cx0"], cx1, "cx1", 0, CTX, CTX)
                    S.op("act", A.activation, ["cx1"], ["cxbf"], out=cxbf[:], in_=cx1[:], func=AF.Identity)
                    for g in range(8):
                        sk = [("T0", pk, j) for j in (g - 1, g, g + 1) if 0 <= j < 8]
                        conv(T0, sk, T1, ("T1", g), g * 512, (g + 1) * 512, SEQ)
                        S.op("act", A.activation, [("T1", g)], [("ucbf", g)], out=ucbf[:, g * 512:(g + 1) * 512], in_=T1[:, g * 512:(g + 1) * 512], func=AF.Identity)

                    def finish(bbuf, bkey, cbuf, ckey, n):
                        S.op("act", A.activation, [bkey], [bkey], out=bbuf[:, 0:n], in_=bbuf[:, 0:n], func=AF.Sqrt, scale=-0.25, bias=0.25)
                        S.op("dve", V.tensor_tensor, [bkey, ckey], [ckey], out=cbuf[:, 0:n], in0=cbuf[:, 0:n], in1=bbuf[:, 0:n], op=ALU.mult)

                    def gates(d, src_bf, sbkeys, src_f, sfkeys, n, abuf, akey, bbuf, bkey, cbuf, ckey, fin=True):
                        wa = gw[:, (2 * d) * 128:(2 * d + 1) * 128]; wx = gw[:, (2 * d + 1) * 128:(2 * d + 2) * 128]
                        ba_c = 5 + 3 * d; bx_c = 6 + 3 * d
                        nseg = (n + 511) // 512
                        for s_ in range(nseg):
                            w_ = min(512, n - s_ * 512); sl = slice(s_ * 512, s_ * 512 + w_)
                            i = gcnt[0] % 2; gcnt[0] += 1
                            b0 = 2 * i; b1 = 2 * i + 1
                            tr = trb[i]; ti = tib[i]
                            sbk = sbkeys[s_]; sfk = sfkeys[s_]
                            S.op("pe", T.matmul, [("gw", p), sbk], [("ps", b0)], PS[b0][:, 0:w_], wa, src_bf[:, sl], start=True, stop=True)
                            S.op("pe", T.matmul, [("gw", p), sbk], [("ps", b1)], PS[b1][:, 0:w_], wx, src_bf[:, sl], start=True, stop=True)
                            S.op("act", A.activation, [("ps", b0), ("prh", p)], [("tr", i)], out=tr[:, 0:w_], in_=PS[b0][:, 0:w_], func=AF.Tanh,
                                 scale=0.5, bias=prh[:, ba_c:ba_c + 1])
                            S.op("act", A.activation, [("ps", b1), ("prh", p)], [("ti", i)], out=ti[:, 0:w_], in_=PS[b1][:, 0:w_], func=AF.Tanh,
                                 scale=0.5, bias=prh[:, bx_c:bx_c + 1])
                            S.op("act", A.activation, [("tr", i), ("cl", p, d)], [akey], out=abuf[:, sl], in_=tr[:, 0:w_], func=AF.Exp,
                                 scale=clt[:, 2 * d + 1:2 * d + 2], bias=clt[:, 2 * d + 1:2 * d + 2])
                            if s_ % 2 < CFG["a2_act"]:
                                S.op("act", A.activation, [("tr", i), ("cl", p, d)], [bkey], out=bbuf[:, sl], in_=tr[:, 0:w_], func=AF.Exp,
                                     scale=clt[:, 2 * d:2 * d + 1], bias=clt[:, 2 * d:2 * d + 1])
                            else:
                                S.op("dve", V.tensor_tensor, [akey], [bkey], out=bbuf[:, sl], in0=abuf[:, sl], in1=abuf[:, sl], op=ALU.mult)
                            S.op("dve", V.scalar_tensor_tensor, [("ti", i), sfk], [ckey], out=cbuf[:, sl], in0=ti[:, 0:w_], scalar=1.0,
                                 in1=src_f[:, sl], op0=ALU.add, op1=ALU.mult)
                        if fin:
                            finish(bbuf, bkey, cbuf, ckey, n)

                    def wset(idx):
                        return BAs[idx], BBs[idx], BCs[idx], ("BA", idx), ("BB", idx), ("BC", idx)

                    if CFG["lru_pair"]:
                        v_tiles(ch * 8, ch * 8 + 8)
                        cs = []
                        for d in range(2):
                            ws = wset((step[0] % 2) * 2 + d)
                            gates(d, cxbf, ["cxbf"], cx1, ["cx1"], CTX, ws[0], ws[3], ws[1], ws[4], ws[2], ws[5], fin=False)
                            cs.append(ws)
                        step[0] += 1
                        for d in range(2):
                            finish(cs[d][1], cs[d][4], cs[d][2], cs[d][5], CTX)
                        for d in range(2):
                            BA, BB, BC, ka, kb, kc_ = cs[d]
                            if d == 0:
                                S.op("dve", V.tensor_tensor_scan, [ka, kc_], [kb], out=BB[:, 0:CTX], data0=BA[:, 0:CTX], data1=BC[:, 0:CTX], initial=0.0,
                                     op0=ALU.mult, op1=ALU.add)
                                S.op("dve", V.tensor_copy, [kb], [("carry", d)], out=carry[:, 0:1], in_=BB[:, CTX - 1:CTX])
                            else:
                                S.op("dve", V.tensor_tensor_scan, [ka, kc_], [kb], out=rev(BB[:, 0:CTX], CTX), data0=rev(BA[:, 0:CTX], CTX),
                                     data1=rev(BC[:, 0:CTX], CTX), initial=0.0, op0=ALU.mult, op1=ALU.add)
                                S.op("dve", V.tensor_copy, [kb], [("carry", d)], out=carry[:, 1:2], in_=BB[:, 0:1])
                        for s4 in range(4):
                            qq = (s4, 3 - s4)
                            cs = []
                            for d in range(2):
                                q = qq[d]; c0 = q * QW
                                ws = wset((step[0] % 2) * 2 + d)
                                blks = [2 * q, 2 * q + 1]
                                gates(d, ucbf[:, c0:c0 + QW], [("ucbf", j) for j in blks], T1[:, c0:c0 + QW], [("T1", j) for j in blks], QW,
                                      ws[0], ws[3], ws[1], ws[4], ws[2], ws[5], fin=False)
                                cs.append(ws)
                            step[0] += 1
                            for d in range(2):
                                finish(cs[d][1], cs[d][4], cs[d][2], cs[d][5], QW)
                            for d in range(2):
                                q = qq[d]; c0 = q * QW
                                BA, BB, BC, ka, kb, kc_ = cs[d]
                                blks = [2 * q, 2 * q + 1]
                                t0keys = [("T0", pk, j) for j in blks]
                                first = s4 < 2
                                if d == 0:
                                    if first:
                                        S.op("dve", V.tensor_tensor_scan, [ka, kc_, ("carry", 0)], t0keys, out=T0[:, c0:c0 + QW], data0=BA[:], data1=BC[:],
                                             initial=carry[:, 0:1], op0=ALU.mult, op1=ALU.add)
                                        S.op("dve", V.tensor_copy, t0keys, [("carry", 0)], out=carry[:, 0:1], in_=T0[:, c0 + QW - 1:c0 + QW])
                                    else:
                                        S.op("dve", V.tensor_tensor_scan, [ka, kc_, ("carry", 0)], [kb], out=BB[:], data0=BA[:], data1=BC[:],
                                             initial=carry[:, 0:1], op0=ALU.mult, op1=ALU.add)
                                        S.op("dve", V.tensor_copy, [kb], [("carry", 0)], out=carry[:, 0:1], in_=BB[:, QW - 1:QW])
                                else:
                                    if first:
                                        S.op("dve", V.tensor_tensor_scan, [ka, kc_, ("carry", 1)], t0keys, out=rev(T0[:, c0:c0 + QW], QW), data0=rev(BA[:], QW),
                                             data1=rev(BC[:], QW), initial=carry[:, 1:2], op0=ALU.mult, op1=ALU.add)
                                        S.op("dve", V.tensor_copy, t0keys, [("carry", 1)], out=carry[:, 1:2], in_=T0[:, c0:c0 + 1])
                                    else:
                                        S.op("dve", V.tensor_tensor_scan, [ka, kc_, ("carry", 1)], [kb], out=rev(BB[:], QW), data0=rev(BA[:], QW),
                                             data1=rev(BC[:], QW), initial=carry[:, 1:2], op0=ALU.mult, op1=ALU.add)
                                        S.op("dve", V.tensor_copy, [kb], [("carry", 1)], out=carry[:, 1:2], in_=BB[:, 0:1])
                                if not first:
                                    S.op("dve", V.tensor_tensor, [kb] + t0keys, [ka], out=BA[:], in0=BB[:], in1=T0[:, c0:c0 + QW], op=ALU.add)
                                    szk = [("szl", pk, j) for j in blks]
                                    S.op("dve", V.tensor_tensor, [ka] + szk, szk, out=szl[:, c0:c0 + QW], in0=BA[:], in1=szl[:, c0:c0 + QW],
                                         op=ALU.mult)
                        for hq in range(2):
                            S.dma(mixD[(4 + ch) * 128:(5 + ch) * 128, hq * 2048:(hq + 1) * 2048], szl[:, hq * 2048:(hq + 1) * 2048],
                                  [("szl", pk, j) for j in range(4 * hq, 4 * hq + 4)], [])
                        if mid_hook is not None:
                            mid_hook()
                        return

                    for d in range(2):
                        v_tiles(ch * 8 + d * 4, ch * 8 + d * 4 + 4)
                        si = step[0] % 2; step[0] += 1
                        BA = BAs[si]; BB = BBs[si]; BC = BCs[si]
                        ka = ("BA", si); kb = ("BB", si); kc_ = ("BC", si)
                        gates(d, cxbf, ["cxbf"], cx1, ["cx1"], CTX, BA, ka, BB, kb, BC, kc_)
                        if d == 0:
                            S.op("dve", V.tensor_tensor_scan, [ka, kc_], [kb], out=BB[:, 0:CTX], data0=BA[:, 0:CTX], data1=BC[:, 0:CTX], initial=0.0,
                                 op0=ALU.mult, op1=ALU.add)
                            S.op("dve", V.tensor_copy, [kb], [("carry", d)], out=carry[:, 0:1], in_=BB[:, CTX - 1:CTX])
                        else:
                            S.op("dve", V.tensor_tensor_scan, [ka, kc_], [kb], out=rev(BB[:, 0:CTX], CTX), data0=rev(BA[:, 0:CTX], CTX),
                                 data1=rev(BC[:, 0:CTX], CTX), initial=0.0, op0=ALU.mult, op1=ALU.add)
                            S.op("dve", V.tensor_copy, [kb], [("carry", d)], out=carry[:, 1:2], in_=BB[:, 0:1])
                        quarters = [0, 1, 2, 3] if d == 0 else [3, 2, 1, 0]
                        for q in quarters:
                            c0 = q * QW
                            si = step[0] % 2; step[0] += 1
                            BA = BAs[si]; BB = BBs[si]; BC = BCs[si]
                            ka = ("BA", si); kb = ("BB", si); kc_ = ("BC", si)
                            blks = [2 * q, 2 * q + 1]
                            gates(d, ucbf[:, c0:c0 + QW], [("ucbf", j) for j in blks], T1[:, c0:c0 + QW], [("T1", j) for j in blks], QW,
                                  BA, ka, BB, kb, BC, kc_)
                            t0keys = [("T0", pk, j) for j in blks]
                            if d == 0:
                                S.op("dve", V.tensor_tensor_scan, [ka, kc_, ("carry", 0)], t0keys, out=T0[:, c0:c0 + QW], data0=BA[:], data1=BC[:],
                                     initial=carry[:, 0:1], op0=ALU.mult, op1=ALU.add)
                                S.op("dve", V.tensor_copy, t0keys, [("carry", 0)], out=carry[:, 0:1], in_=T0[:, c0 + QW - 1:c0 + QW])
                            else:
                                S.op("dve", V.tensor_tensor_scan, [ka, kc_, ("carry", 1)], [kb], out=rev(BB[:], QW), data0=rev(BA[:], QW),
                                     data1=rev(BC[:], QW), initial=carry[:, 1:2], op0=ALU.mult, op1=ALU.add)
                                S.op("dve", V.tensor_copy, [kb], [("carry", 1)], out=carry[:, 1:2], in_=BB[:, 0:1])
                                S.op("dve", V.tensor_tensor, [kb] + t0keys, [ka], out=BA[:], in0=BB[:], in1=T0[:, c0:c0 + QW], op=ALU.add)
                                szk = [("szl", pk, j) for j in blks]
                                S.op("dve", V.tensor_tensor, [ka] + szk, szk, out=szl[:, c0:c0 + QW], in0=BA[:], in1=szl[:, c0:c0 + QW],
                                     op=ALU.mult)
                                if q % 2 == 0:
                                    hq = q // 2
                                    S.dma(mixD[(4 + ch) * 128:(5 + ch) * 128, hq * 2048:(hq + 1) * 2048], szl[:, hq * 2048:(hq + 1) * 2048],
                                          [("szl", pk, j) for j in range(4 * hq, 4 * hq + 4)], [])
                        if d == 0 and mid_hook is not None:
                            mid_hook()

                step = [0]; gcnt = [0]
                proj_stage(0)
                for ch in range(4):
                    main_stage(ch, (lambda c=ch: proj_stage(c + 1)) if ch < 3 else None)
                S.flush()

        def att_phase():
            S.barrier()
            with ExitStack() as e2:
                qrot = sbt(e2, "qrot", [128, SEQ], BF16); qpl = sbt(e2, "qpl", [128, SEQ], BF16)
                krot = sbt(e2, "krot", [128, SEQ], BF16); sza = sbt(e2, "sza", [128, SEQ], BF16)
                vaug = sbt(e2, "vaug", [128, 32 * 256], BF16); mixc = sbt(e2, "mixc", [128, SEQ], BF16)
                kcn2 = [sbt(e2, "kcn%d" % i, [128, CTX], BF16) for i in range(2)]
                vcaug2 = [sbt(e2, "vcaug%d" % i, [128, 2 * 256], BF16) for i in range(2)]
                Ffull2 = [sbt(e2, "Ffull%d" % i, [128, 2 * FW], BF16) for i in range(2)]
                Fint2 = [sbt(e2, "Fint%d" % i, [128, 2 * FW], BF16) for i in range(2)]
                cst = [sbt(e2, "cst%d" % i, [128, 512], F32) for i in range(2)]
                snt = [sbt(e2, "snt%d" % i, [128, 512], F32) for i in range(2)]
                sqb = [sbt(e2, "sqb%d" % i, [128, 512], BF16) for i in range(CFG["n_qk"])]
                qsb = [sbt(e2, "qsb%d" % i, [128, 512], F32) for i in range(CFG["n_qk"])]
                rtb = [sbt(e2, "rtb%d" % i, [128, 512], F32) for i in range(CFG["n_qk"])]
                knbf = [sbt(e2, "knbf%d" % i, [128, 512], BF16) for i in range(CFG["n_qk"])]
                r1b = [sbt(e2, "r1b%d" % i, [128, 512], F32) for i in range(CFG["n_qk"])]
                r2b = [sbt(e2, "r2b%d" % i, [128, 512], F32) for i in range(CFG["n_qk"])]
                etb = [sbt(e2, "etb%d" % i, [128, 512], BF16) for i in range(CFG["n_et"])]
                ptb = [sbt(e2, "ptb%d" % i, [128, 512], BF16) for i in range(CFG["n_pt"])]
                PSX = PS + [e2.enter_context(nc.psum_tensor("psx_%d" % i, [128, 512], F32)) for i in range(2)]
                rdb = [sbt(e2, "rdb%d" % i, [128, 256], F32) for i in range(2)]

                mk = sbt(e2, "mk", [128, 2 * FW], BF16)
                for j_ in range(4):
                    S.dma(r1b[j_ % 2][:, 0:448], masks[:, j_ * 448:(j_ + 1) * 448], [], [("r1b", j_ % 2)])
                    S.op("dve", V.tensor_copy, [("r1b", j_ % 2)], ["mk"], out=mk[:, j_ * 448:(j_ + 1) * 448], in_=r1b[j_ % 2][:, 0:448])
                for i_ in range(2):
                    S.op("pool", G.memset, [], [("vcaug", i_)], vcaug2[i_][:], 1.0)
                gate_a = sbt(e2, "gate_a", [128, DM], F32); wob_a = sbt(e2, "wob_a", [128, DM], BF16)
                for pr in range(4):
                    att_body(pr, locals())
                    if pr == 0:
                        S.dma(gate_a[:], gateD, [], ["gate_a"])
                        for kc in range(8):
                            S.dma(wst[kc % 2][:], w_out[kc * 128:(kc + 1) * 128, :], [], [("wst", kc % 2)])
                            S.op("dve", V.tensor_tensor, [("wst", kc % 2), "gate_a"], ["wob_a"], out=wob_a[:], in0=wst[kc % 2][:], in1=gate_a[:], op=ALU.mult)
                            S.dma(woD[kc * 128:(kc + 1) * 128, :], wob_a[:], ["wob_a"], [])
                for kc in [4, 5, 6, 7, 0, 1, 2]:
                    S.dma(hT[:, kc * SEQ:(kc + 1) * SEQ], mixD[kc * 128:(kc + 1) * 128, :], [("mixD", kc)] if kc < 4 else [], ["hT"])
                S.flush()

        def att_body(pr, L):
            if True:
                if True:
                    pass
                par = pr % 2
                qrot = L["qrot"]; qpl = L["qpl"]; krot = L["krot"]; sza = L["sza"]; vaug = L["vaug"]; mixc = L["mixc"]; mk = L["mk"]
                kcn = L["kcn2"][par]; vcaug = L["vcaug2"][par]; Ffull = L["Ffull2"][par]; Fint = L["Fint2"][par]
                cst = L["cst"]; snt = L["snt"]; sqb = L["sqb"]; rtb = L["rtb"]; knbf = L["knbf"]; qsb = L["qsb"]
                r1b = L["r1b"]; r2b = L["r2b"]; etb = L["etb"]; ptb = L["ptb"]; PSX = L["PSX"]; rdb = L["rdb"]; ttb = L["rdb"]
                kFf = ("Ffull", par); kFi = ("Fint", par); kkc = ("kcn", par); kvc = ("vcaug", par)
                for hh in range(2):
                    for hf in range(2):
                        c0 = hf * 448
                        S.dma(r1b[hf][:, 0:448], rpbG[2 * pr + hh, :, c0:c0 + 448], [], [("r1b", hf)])
                        S.op("act", A.activation, [("r1b", hf)], [("r2b", hf)], out=r2b[hf][:, 0:448], in_=r1b[hf][:, 0:448], func=AF.Exp)
                        S.op("dve", V.tensor_tensor, [("r2b", hf), "mk"], [kFf], out=Ffull[:, hh * FW + c0:hh * FW + c0 + 448], in0=r2b[hf][:, 0:448],
                             in1=mk[:, c0:c0 + 448], op=ALU.mult)
                        S.op("dve", V.tensor_tensor, [("r2b", hf), "mk"], [kFi], out=Fint[:, hh * FW + c0:hh * FW + c0 + 448], in0=r2b[hf][:, 0:448],
                             in1=mk[:, FW + c0:FW + c0 + 448], op=ALU.mult)
                wq, wqk = load_wblock(128 * pr)
                wk, wkk = load_wblock(512 + 128 * pr)
                wv, wvk = load_wblock(1024 + 128 * pr)
                wz, wzk = load_wblock(1536 + 128 * pr)
                for g in range(8):
                    b = g % 2
                    proj_fm(wz, wzk, hT, SEQ, g * 512, 512, b)
                    S.op("act", A.activation, [("ps", b)], [("sza", g)], out=sza[:, g * 512:(g + 1) * 512], in_=PS[b][:, :], func=AF.Silu)

                cnt = [0]

                def qk_path(bank, n, gcol, plain_dst, pkey, rot_dst, rkey, ts):
                    i = cnt[0] % CFG["n_qk"]; cnt[0] += 1
                    S.op("act", A.activation, [("ps", bank)], [("qsb", i)], out=qsb[i][:, 0:n], in_=PS[bank][:, 0:n], func=AF.Identity)
                    S.op("act", A.activation, [("ps", bank)], [("sqb", i)], out=sqb[i][:, 0:n], in_=PS[bank][:, 0:n], func=AF.Square)
                    S.op("pe", T.matmul, ["cm", ("sqb", i)], [("ps", 2)], PS[2][:, 0:n], bones, sqb[i][:, 0:n], start=True, stop=True)
                    S.op("act", A.activation, [("ps", 2)], [("rtb", i)], out=rtb[i][:, 0:n], in_=PS[2][:, 0:n], func=AF.Ln, bias=EPS, scale=1.0)
                    S.op("act", A.activation, [("rtb", i)], [("rtb", i)], out=rtb[i][:, 0:n], in_=rtb[i][:, 0:n], func=AF.Exp, scale=-0.5)
                    if plain_dst is None:
                        plain_dst = knbf[i][:, 0:n]; pkey = ("knbf", i)
                    S.op("dve", V.scalar_tensor_tensor, [("qsb", i), "gq", ("rtb", i)], [pkey], out=plain_dst, in0=qsb[i][:, 0:n],
                         scalar=gq[:, gcol:gcol + 1], in1=rtb[i][:, 0:n], op0=ALU.mult, op1=ALU.mult)
                    if rot_dst is None:
                        return
                    S.op("pe", T.matmul, ["cm", pkey], [("ps", 2)], PS[2][:, 0:n], Rm, plain_dst, start=True, stop=True)
                    S.op("dve", V.tensor_tensor, [pkey, ("cst", ts)], [("r1b", i)], out=r1b[i][:, 0:n], in0=plain_dst, in1=cst[ts][:, 0:n], op=ALU.mult)
                    S.op("dve", V.tensor_tensor, [("ps", 2), ("snt", ts)], [("r2b", i)], out=r2b[i][:, 0:n], in0=PS[2][:, 0:n], in1=snt[ts][:, 0:n], op=ALU.mult)
                    if CFG["rope_add"] == "pool":
                        S.op("pool", G.tensor_tensor, [("r1b", i), ("r2b", i)], [rkey], out=rot_dst, in0=r1b[i][:, 0:n], in1=r2b[i][:, 0:n], op=ALU.add)
                    else:
                        S.op("dve", V.tensor_tensor, [("r1b", i), ("r2b", i)], [rkey], out=rot_dst, in0=r1b[i][:, 0:n], in1=r2b[i][:, 0:n], op=ALU.add)

                proj_fm(wk, wkk, hcT, CTX, 0, CTX, 0)
                qk_path(0, CTX, 1, kcn[:], kkc, None, None, 0)
                for t_ in range(2):
                    for kc in range(8):
                        S.op("pe", T.matmul, [wvk, "hT"], [("ps", 3)], PS[3][:, t_ * 128:(t_ + 1) * 128],
                             hcT[:, kc * CTX + t_ * 128: kc * CTX + (t_ + 1) * 128], wv[:, kc * 128:(kc + 1) * 128], start=(kc == 0), stop=(kc == 7))
                vc3 = vcaug[:].rearrange("p (t c) -> p t c", c=256)
                pv3 = PS[3][:, 0:256].rearrange("p (t c) -> p t c", c=128)
                S.op("act", A.activation, [("ps", 3)], [kvc], out=vc3[:, :, 0:64], in_=pv3[:, :, 0:64], func=AF.Identity)
                S.op("act", A.activation, [("ps", 3)], [kvc], out=vc3[:, :, 192:256], in_=pv3[:, :, 64:128], func=AF.Identity)

                def proj_group(g):
                    ts = g % 2
                    S.dma(cst[ts][:], cosT[:, g * 512:(g + 1) * 512], [], [("cst", ts)])
                    S.dma(snt[ts][:], sinT[:, g * 512:(g + 1) * 512], [], [("snt", ts)])
                    proj_fm(wq, wqk, hT, SEQ, g * 512, 512, 0)
                    qk_path(0, 512, 0, qpl[:, g * 512:(g + 1) * 512], ("qpl", g), qrot[:, g * 512:(g + 1) * 512], ("qrot", g), ts)
                    proj_fm(wk, wkk, hT, SEQ, g * 512, 512, 1)
                    qk_path(1, 512, 1, None, None, krot[:, g * 512:(g + 1) * 512], ("krot", g), ts)
                    S.dma(vaug[:, g * 1024:(g + 1) * 1024], vD[pr, :, g * 1024:(g + 1) * 1024], [], [("vaug", g)])

                sc = [0]

                def f3(Ft, hh, jj0):
                    base = Ft[:, hh * FW + jj0 * 64: hh * FW + jj0 * 64 + 256]
                    return bass.AP(base.tensor, base.offset, [[base.ap[0][0], 128], [-128, 2], [1, 256]])

                def att_group(qg):
                    R0 = 4 * qg
                    gq_ = qg // 2
                    qs = slice(R0 * 64, R0 * 64 + 256)
                    if qg == 0:
                        tl = [(a, 6 - a, Ffull, 0, 256) for a in (0, 2, 4, 6)]
                        pairs = [(0, 1), (2, 3)]
                    elif qg == 15:
                        tl = [(a, 6 - (a - 60), Ffull, 0, 256) for a in (56, 58, 60, 62)]
                        pairs = [(0, 1), (2, 3)]
                    else:
                        qr = [(0, 128), (0, 256), (0, 256), (0, 256), (64, 256), (192, 256)]
                        tl = [(R0 - 4 + 2 * i, 10 - 2 * i, Fint, qr[i][0], qr[i][1]) for i in range(6)]
                        pairs = [(1, 2), (3, 4), (0, 5)]
                    i2 = qg % 2
                    for hh in range(2):
                        pb = 64 * hh; ob = 6 + hh; po = PSX[ob][:, 0:256]
                        for pi, (iA, iB) in enumerate(pairs):
                            tA = tl[iA]; tB = tl[iB]
                            wA = tA[4] - tA[3]; wB = tB[4] - tB[3]
                            offs = [256 - wA, 256]
                            lo_ = 256 - wA; hi_ = 256 + wB
                            sb_ = CFG["score_banks"][sc[0] % len(CFG["score_banks"])]; ke = sc[0] % CFG["n_et"]; kp = sc[0] % CFG["n_pt"]; sc[0] += 1
                            for u, tt_ in enumerate((tA, tB)):
                                au, jj0, Ft, qa, qb = tt_
                                S.op("pe", T.matmul, [("krot", au // 8), ("qrot", gq_)], [("ps", sb_)], PSX[sb_][:, offs[u]:offs[u] + (qb - qa)],
                                     krot[pb:pb + 64, au * 64:au * 64 + 128], qrot[pb:pb + 64, R0 * 64 + qa:R0 * 64 + qb], start=True, stop=True)
                            S.op("act", A.activation, [("ps", sb_)], [("et", ke)], out=etb[ke][:, lo_:hi_], in_=PSX[sb_][:, lo_:hi_], func=AF.Exp, scale=0.125)
                            if wA == 256 and wB == 256 and tB[1] == tA[1] - 2:
                                S.op("dve", V.tensor_tensor, [("et", ke), kFf, kFi], [("ptl", kp)],
                                     out=ptb[kp][:].rearrange("p (t c) -> p t c", c=256), in0=etb[ke][:].rearrange("p (t c) -> p t c", c=256),
                                     in1=f3(tA[2], hh, tA[1]), op=ALU.mult)
                            else:
                                for u, tt_ in enumerate((tA, tB)):
                                    au, jj0, Ft, qa, qb = tt_
                                    w_ = qb - qa
                                    S.op("dve", V.tensor_tensor, [("et", ke), kFf, kFi], [("ptl", kp)], out=ptb[kp][:, offs[u]:offs[u] + w_],
                                         in0=etb[ke][:, offs[u]:offs[u] + w_],
                                         in1=Ft[:, hh * FW + jj0 * 64 + qa: hh * FW + jj0 * 64 + qb], op=ALU.mult)
                            for u, tt_ in enumerate((tA, tB)):
                                au, jj0, Ft, qa, qb = tt_
                                tix = au // 2
                                S.op("pe", T.matmul, [("vaug", au // 8), ("ptl", kp)], [("ps", ob)], PSX[ob][:, qa:qb],
                                     vaug[:, tix * 256 + hh * 128: tix * 256 + hh * 128 + 128], ptb[kp][:, offs[u]:offs[u] + (qb - qa)],
                                     start=(pi == 0 and u == 0), stop=False)
                        sb_ = CFG["score_banks"][sc[0] % len(CFG["score_banks"])]; kp = sc[0] % CFG["n_pt"]; sc[0] += 1
                        for j in range(2):
                            S.op("pe", T.matmul, [kkc, ("qpl", gq_)], [("ps", sb_)], PSX[sb_][:, j * 256:(j + 1) * 256], kcn[pb:pb + 64, j * 128:(j + 1) * 128],
                                 qpl[pb:pb + 64, qs], start=True, stop=True)
                        S.op("act", A.activation, [("ps", sb_)], [("ptl", kp)], out=ptb[kp][:], in_=PSX[sb_][:, :], func=AF.Exp, scale=0.125)
                        for j in range(2):
                            S.op("pe", T.matmul, [kvc, ("ptl", kp)], [("ps", ob)], PSX[ob][:, 0:256],
                                 vcaug[:, j * 256 + hh * 128: j * 256 + hh * 128 + 128], ptb[kp][:, j * 256:(j + 1) * 256], start=False, stop=(j == 1))
                        S.op("act", A.activation, [("ps", ob)], [("rd", i2, hh)], out=rdb[i2][pb:pb + 64, :], in_=po[64 - pb:128 - pb, 0:256], func=AF.Ln)
                        S.op("act", A.activation, [("rd", i2, hh)], [("rd", i2, hh)], out=rdb[i2][pb:pb + 64, :], in_=rdb[i2][pb:pb + 64, :],
                             func=AF.Exp, scale=-1.0)
                        S.op("dve", V.tensor_tensor, [("rd", i2, hh), ("sza", gq_)], [("rd", i2, hh)], out=rdb[i2][pb:pb + 64, :], in0=rdb[i2][pb:pb + 64, :],
                             in1=sza[pb:pb + 64, qs], op=ALU.mult)
                        S.op("dve", V.tensor_tensor, [("ps", ob), ("rd", i2, hh)], [("mixc", gq_)], out=mixc[pb:pb + 64, qs], in0=po[pb:pb + 64, 0:256],
                             in1=rdb[i2][pb:pb + 64, :], op=ALU.mult)

                def mix_out(g):
                    S.dma(mixD[pr * 128:(pr + 1) * 128, g * 512:(g + 1) * 512], mixc[:, g * 512:(g + 1) * 512], [("mixc", g)], [("mixD", pr)])

                proj_group(0)
                for g in range(1, 8):
                    proj_group(g)
                    att_group(2 * g - 2)
                    att_group(2 * g - 1)
                    mix_out(g - 1)
                att_group(14)
                att_group(15)
                mix_out(7)

        lru_phase()
        att_phase()

        S.barrier()
        with ExitStack() as e3:
            wo = sbt(e3, "wo", [128, 8 * DM], BF16)
            xtb = [sbt(e3, "xo%d" % i, [128, DM], F32) for i in range(6)]
            otb = [sbt(e3, "ot%d" % i, [128, DM], F32) for i in range(3)]
            hT3 = hT[:].rearrange("p (kc t) -> p kc t", kc=8)
            mixD3 = mixD.rearrange("(kc p) t -> p kc t", p=128)
            wo3 = wo[:].rearrange("p (kc n) -> p kc n", kc=8)
            woD3 = woD.rearrange("(kc p) n -> p kc n", p=128)
            for h4 in range(2):
                S.dma(wo3[:, h4 * 4:(h4 + 1) * 4, :], woD3[:, h4 * 4:(h4 + 1) * 4, :], [], [("wo", h4)])
            for h2 in range(2):
                S.dma(hT[:, 3 * SEQ + h2 * 2048:3 * SEQ + (h2 + 1) * 2048], mixD[384:512, h2 * 2048:(h2 + 1) * 2048], [], [("mx", h2)])
            for i in range(32):
                xt = xtb[i % 6]; xk = ("xo", i % 6)
                S.dma(xt[:], x[i * 128:(i + 1) * 128, :], [], [xk])
                ot = otb[i % 3]; ok = ("ot", i % 3)
                for hf in range(2):
                    b = (2 * i + hf) % 4
                    for kc in range(8):
                        S.op("pe", T.matmul, [("mx", i // 16), ("wo", kc // 4)], [("ps", b)], PS[b][:, :], hT[:, kc * SEQ + i * 128: kc * SEQ + (i + 1) * 128],
                             wo[:, kc * DM + hf * 512: kc * DM + (hf + 1) * 512], start=(kc == 0), stop=(kc == 7))
                    S.op("dve", V.tensor_tensor, [("ps", b), xk], [ok], out=ot[:, hf * 512:(hf + 1) * 512], in0=PS[b][:, :],
                         in1=xt[:, hf * 512:(hf + 1) * 512], op=ALU.add)
                S.dma(out[i * 128:(i + 1) * 128, :], ot[:], [ok], [])
            S.final_wait()
    return nc


def kernel(x, c, ctx, c_ctx, norm_g, w_mod, b_mod, w_in, w_out, q_norm_g, k_norm_g, rpb,
           conv_w, conv_b, lru_wa, lru_ba, lru_wx, lru_bx, lru_lam):
    f = lambda a: np.ascontiguousarray(np.asarray(a, dtype=np.float32))
    x, c, ctx, c_ctx = f(x), f(c), f(ctx), f(c_ctx)
    norm_g, w_mod, b_mod, w_in, w_out = f(norm_g), f(w_mod), f(b_mod), f(w_in), f(w_out)
    q_norm_g, k_norm_g, rpb = f(q_norm_g), f(k_norm_g), f(rpb)
    conv_w, conv_b, lru_wa, lru_ba, lru_wx, lru_bx, lru_lam = (f(conv_w), f(conv_b), f(lru_wa), f(lru_ba),
                                                              f(lru_wx), f(lru_bx), f(lru_lam))
    cosT, sinT, cm, sel, masks, dridx, dcidx = _host_consts()
    rpbG = np.ascontiguousarray(rpb[0][:, dridx, dcidx].reshape(8, 128, FW))
    gqk = np.stack([np.tile(q_norm_g[0], 2), np.tile(k_norm_g[0], 2)], axis=1).astype(np.float32)
    plru = np.zeros((4, 128, 16), np.float32)
    wbd = np.zeros((4, 128, 512), np.float32)
    for ch in range(4):
        sl = slice(ch * 128, (ch + 1) * 128)
        for j in range(4):
            plru[ch, :, j] = conv_w[0, j, sl]
        plru[ch, :, 4] = conv_b[0, sl]
        for d in range(2):
            plru[ch, :, 5 + 3 * d] = lru_ba[0, d, sl]
            plru[ch, :, 6 + 3 * d] = lru_bx[0, d, sl]
            plru[ch, :, 7 + 3 * d] = lru_lam[0, d, sl]
            for m, wsrc in enumerate((lru_wa, lru_wx)):
                col = (2 * d + m) * 128
                for blk in range(2):
                    wbd[ch, blk * 64:(blk + 1) * 64, col + blk * 64: col + (blk + 1) * 64] = wsrc[0, d, 2 * ch + blk]
    common = {
        "norm_g": norm_g[0:1], "w_mod": w_mod[0], "b_mod": b_mod[0:1], "w_in": w_in[0], "w_out": w_out[0],
        "gqk": gqk, "plru": plru, "wbd": wbd, "rpbG": rpbG, "cosT": cosT, "sinT": sinT, "cmat": cm, "sel": sel,
        "masks": masks,
    }
    in_maps = []
    for b in range(8):
        cvb = np.stack([c[b], c_ctx], axis=0)
        cvl = np.ascontiguousarray(cvb.reshape(2, 8, 128).transpose(2, 1, 0).reshape(128, 16))
        m = dict(common)
        m["x"] = x[b]; m["ctx"] = ctx[b]; m["cv"] = cvl
        in_maps.append(m)
    nc = build_nc()
    res = run_bass_kernel_spmd(nc, in_maps, core_ids=list(range(8)))
    return np.stack([np.asarray(r["out"], dtype=np.float32) for r in res.results], axis=0)
```
